# Optimizing a Trainium2 kernel written in Bass

```python
import jax, jax.numpy as jnp
from jax import lax
import numpy as np


D_MODEL = 1024
BATCH = 16
SEQ = 2048
DEPTH = 2

CHUNK = 64
Q_BLOCK = 128
D_MIX = D_MODEL
EPS = 1e-6
N_MOD = 6

GDN_HEADS = 4
GDN_DK = D_MIX // 4 // GDN_HEADS
GDN_DV = D_MIX // 4 // GDN_HEADS
GDN_QK = GDN_HEADS * GDN_DK
GDN_VW = GDN_HEADS * GDN_DV
GDN_CONV = 4

RG_WIDTH = D_MIX // 2
RG_BLOCKS = 8
RG_BLOCK = RG_WIDTH // RG_BLOCKS
RG_CONV = 4
RG_C = 8.0

MLA_HEADS = 4
MLA_NOPE = 64
MLA_ROPE = 32
MLA_V = D_MIX // 4 // MLA_HEADS
MLA_Q_RANK = D_MODEL // 4
MLA_KV_RANK = D_MODEL // 8
ROPE_THETA = 10000.0

D_FF = 4 * D_MODEL

IN_SIZES = (GDN_QK, GDN_QK, GDN_VW, GDN_VW, GDN_HEADS, GDN_HEADS,
            RG_WIDTH, RG_WIDTH,
            MLA_Q_RANK, MLA_KV_RANK, MLA_ROPE)
D_IN = sum(IN_SIZES)

kernel_name = 'hybrid_gdn_rglru_mla_adaln_encoder'


def rmsnorm(x, g):
    xf = x.astype(jnp.float32)
    y = xf * lax.rsqrt(jnp.mean(xf * xf, axis=-1, keepdims=True) + EPS)
    return (y * g.astype(jnp.float32)).astype(x.dtype)


def l2norm(x):
    return x * lax.rsqrt(jnp.sum(x * x, axis=-1, keepdims=True) + EPS)


def causal_depthwise_conv(x, w):
    width = w.shape[0]
    return lax.conv_general_dilated(
        x, w[:, None, :].astype(x.dtype), window_strides=(1,),
        padding=[(width - 1, 0)], dimension_numbers=('NWC', 'WIO', 'NWC'),
        feature_group_count=x.shape[-1])


def gated_delta_rule(q, k, v, g, beta):
    bsz, seq, heads, dk = q.shape
    dv = v.shape[-1]
    n = seq // CHUNK

    def to_chunks(t):
        t = t.reshape((bsz, n, CHUNK) + t.shape[2:])
        return jnp.moveaxis(t, 3, 1)

    q, k, v, g, beta = (to_chunks(t) for t in (q, k, v, g, beta))
    q = q * (dk ** -0.5)
    g = jnp.cumsum(g, axis=-1)
    k_beta = k * beta[..., None]
    v_beta = v * beta[..., None]
    lower_incl = jnp.tril(jnp.ones((CHUNK, CHUNK), bool))
    strict_lower = jnp.tril(jnp.ones((CHUNK, CHUNK), bool), -1)
    diff = g[..., :, None] - g[..., None, :]
    decay = jnp.where(lower_incl, jnp.exp(jnp.where(lower_incl, diff, 0.0)), 0.0)
    lmat = jnp.where(strict_lower, jnp.einsum('bhncd,bhnsd->bhncs', k_beta, k) * decay, 0.0)
    tmat = jnp.eye(CHUNK, dtype=lmat.dtype) + lmat
    u = lax.linalg.triangular_solve(tmat, v_beta, left_side=True, lower=True)
    w = lax.linalg.triangular_solve(tmat, k_beta * jnp.exp(g)[..., None], left_side=True, lower=True)
    qk = jnp.einsum('bhncd,bhnsd->bhncs', q, k) * decay
    q_decay = q * jnp.exp(g)[..., None]
    k_tail = k * jnp.exp(g[..., -1:] - g)[..., None]
    chunk_decay = jnp.exp(g[..., -1])

    def step(state, xs):
        qk_c, qd_c, u_c, w_c, kt_c, cd_c = xs
        v_new = u_c - jnp.einsum('bhcd,bhde->bhce', w_c, state)
        o = jnp.einsum('bhcd,bhde->bhce', qd_c, state) + jnp.einsum('bhcs,bhse->bhce', qk_c, v_new)
        state = state * cd_c[..., None, None] + jnp.einsum('bhcd,bhce->bhde', kt_c, v_new)
        return state, o

    xs = tuple(jnp.moveaxis(t, 2, 0) for t in (qk, q_decay, u, w, k_tail, chunk_decay))
    s0 = jnp.zeros((bsz, heads, dk, dv), jnp.float32)
    _, o = lax.scan(step, s0, xs)
    o = jnp.moveaxis(o, 0, 2)
    return jnp.moveaxis(o, 1, 3).reshape(bsz, seq, heads, dv)


def gdn_mixer(q, k, v, z, a, b, conv_w, a_log, dt_bias, norm_g):
    dtype = q.dtype
    bsz, seq, _ = q.shape
    qkv = jax.nn.silu(causal_depthwise_conv(jnp.concatenate([q, k, v], axis=-1), conv_w))
    qkv = qkv.astype(jnp.float32)
    q, k, v = jnp.split(qkv, [GDN_QK, 2 * GDN_QK], axis=-1)
    q = l2norm(q.reshape(bsz, seq, GDN_HEADS, GDN_DK))
    k = l2norm(k.reshape(bsz, seq, GDN_HEADS, GDN_DK))
    v = v.reshape(bsz, seq, GDN_HEADS, GDN_DV)
    g = -jnp.exp(a_log.astype(jnp.float32)) * jax.nn.softplus(a.astype(jnp.float32) + dt_bias.astype(jnp.float32))
    beta = jax.nn.sigmoid(b.astype(jnp.float32))
    o = gated_delta_rule(q, k, v, g, beta)
    zg = jax.nn.silu(z.astype(jnp.float32).reshape(bsz, seq, GDN_HEADS, GDN_DV))
    o = rmsnorm(o, norm_g) * zg
    return o.reshape(bsz, seq, GDN_VW).astype(dtype)


def rglru_mixer(xb, gate, conv_w, conv_b, w_a, b_a, w_x, b_x, lam):
    dtype = xb.dtype
    bsz, seq, _ = xb.shape
    xc = causal_depthwise_conv(xb, conv_w) + conv_b
    xblk = xc.reshape(bsz, seq, RG_BLOCKS, RG_BLOCK)
    r = jax.nn.sigmoid(jnp.einsum('bsgi,gij->bsgj', xblk, w_a).reshape(bsz, seq, RG_WIDTH) + b_a)
    i = jax.nn.sigmoid(jnp.einsum('bsgi,gij->bsgj', xblk, w_x).reshape(bsz, seq, RG_WIDTH) + b_x)
    log_a = -RG_C * r.astype(jnp.float32) * jax.nn.softplus(-lam.astype(jnp.float32))
    a = jnp.exp(log_a)
    mult = jnp.sqrt(-jnp.expm1(2.0 * log_a))
    bterm = mult * (i * xc).astype(jnp.float32)

    def combine(left, right):
        a1, b1 = left
        a2, b2 = right
        return a1 * a2, a2 * b1 + b2

    _, h = lax.associative_scan(combine, (a, bterm), axis=1)
    return (h * jax.nn.gelu(gate.astype(jnp.float32))).astype(dtype)


def rope(x, cos, sin):
    x1, x2 = jnp.split(x, 2, axis=-1)
    return jnp.concatenate([x1 * cos - x2 * sin, x2 * cos + x1 * sin], axis=-1)


def mla_mixer(q_lat, kv_lat, k_rope, positions, q_norm_g, w_qb, kv_norm_g, w_kvb):
    dtype = q_lat.dtype
    bsz, seq, _ = q_lat.shape
    q = (rmsnorm(q_lat, q_norm_g) @ w_qb).reshape(bsz, seq, MLA_HEADS, MLA_NOPE + MLA_ROPE)
    q_nope, q_pe = jnp.split(q.astype(jnp.float32), [MLA_NOPE], axis=-1)
    kv = (rmsnorm(kv_lat, kv_norm_g) @ w_kvb).reshape(bsz, seq, MLA_HEADS, MLA_NOPE + MLA_V)
    k_nope, v = jnp.split(kv.astype(jnp.float32), [MLA_NOPE], axis=-1)
    inv_freq = ROPE_THETA ** (-jnp.arange(0, MLA_ROPE, 2, dtype=jnp.float32) / MLA_ROPE)
    ang = positions.astype(jnp.float32)[..., None] * inv_freq
    cos = jnp.cos(ang)[:, :, None, :]
    sin = jnp.sin(ang)[:, :, None, :]
    q_pe = rope(q_pe, cos, sin)
    k_pe = rope(k_rope.astype(jnp.float32)[:, :, None, :], cos, sin)[:, :, 0]
    scale = (MLA_NOPE + MLA_ROPE) ** -0.5
    chunk_id = jnp.arange(seq) // CHUNK
    outs = []
    for blk in range(seq // Q_BLOCK):
        q0, q1 = blk * Q_BLOCK, (blk + 1) * Q_BLOCK
        s = (jnp.einsum('bqhd,bkhd->bhqk', q_nope[:, q0:q1], k_nope[:, :q1])
             + jnp.einsum('bqhr,bkr->bhqk', q_pe[:, q0:q1], k_pe[:, :q1])) * scale
        mask = chunk_id[None, :q1] <= chunk_id[q0:q1, None]
        p = jax.nn.softmax(jnp.where(mask, s, -jnp.inf), axis=-1)
        outs.append(jnp.einsum('bhqk,bkhd->bqhd', p, v[:, :q1]))
    o = jnp.concatenate(outs, axis=1)
    return o.reshape(bsz, seq, MLA_HEADS * MLA_V).astype(dtype)


def setup_inputs(seed: int = 0) -> dict:
    key = jax.random.key(seed)
    ks = jax.random.split(key, 32)
    f32 = jnp.float32
    nrm = lambda k, shape, s: jax.random.normal(k, shape, f32) * s
    x = jax.random.normal(ks[0], (BATCH, SEQ, D_MODEL), f32)
    c = jax.random.normal(ks[1], (BATCH, D_MODEL), f32)
    offsets = jax.random.randint(ks[2], (BATCH, 1), 0, 4096, dtype=jnp.int32)
    positions = offsets + jnp.arange(SEQ, dtype=jnp.int32)[None, :]
    w_mod = nrm(ks[3], (DEPTH, D_MODEL, N_MOD * D_MODEL), 0.5 * D_MODEL ** -0.5)
    b_mod = nrm(ks[4], (DEPTH, N_MOD * D_MODEL), 0.02)
    norm_mix_g = 1.0 + nrm(ks[5], (DEPTH, D_MODEL), 0.02)
    w_in = nrm(ks[6], (DEPTH, D_MODEL, D_IN), D_MODEL ** -0.5)
    gdn_conv_w = nrm(ks[7], (DEPTH, GDN_CONV, 2 * GDN_QK + GDN_VW), GDN_CONV ** -0.5)
    gdn_a_log = jnp.log(jax.random.uniform(ks[8], (DEPTH, GDN_HEADS), f32, 1.0, 16.0))
    dt = jnp.exp(jax.random.uniform(ks[9], (DEPTH, GDN_HEADS), f32, np.log(1e-3), np.log(1e-1)))
    gdn_dt_bias = dt + jnp.log(-jnp.expm1(-dt))
    gdn_norm_g = 1.0 + nrm(ks[10], (DEPTH, GDN_DV), 0.02)
    rg_conv_w = nrm(ks[11], (DEPTH, RG_CONV, RG_WIDTH), RG_CONV ** -0.5)
    rg_conv_b = nrm(ks[12], (DEPTH, RG_WIDTH), 0.02)
    rg_w_a = nrm(ks[13], (DEPTH, RG_BLOCKS, RG_BLOCK, RG_BLOCK), RG_BLOCK ** -0.5)
    rg_b_a = nrm(ks[14], (DEPTH, RG_WIDTH), 0.02)
    rg_w_x = nrm(ks[15], (DEPTH, RG_BLOCKS, RG_BLOCK, RG_BLOCK), RG_BLOCK ** -0.5)
    rg_b_x = nrm(ks[16], (DEPTH, RG_WIDTH), 0.02)
    a0 = jax.random.uniform(ks[17], (DEPTH, RG_WIDTH), f32, 0.9, 0.999) ** (1.0 / RG_C)
    rg_lambda = jnp.log(a0) - jnp.log1p(-a0)
    mla_q_norm_g = 1.0 + nrm(ks[18], (DEPTH, MLA_Q_RANK), 0.02)
    mla_w_qb = nrm(ks[19], (DEPTH, MLA_Q_RANK, MLA_HEADS * (MLA_NOPE + MLA_ROPE)), MLA_Q_RANK ** -0.5)
    mla_kv_norm_g = 1.0 + nrm(ks[20], (DEPTH, MLA_KV_RANK), 0.02)
    mla_w_kvb = nrm(ks[21], (DEPTH, MLA_KV_RANK, MLA_HEADS * (MLA_NOPE + MLA_V)), MLA_KV_RANK ** -0.5)
    w_out = nrm(ks[22], (DEPTH, D_MIX, D_MODEL), D_MIX ** -0.5)
    norm_mlp_g = 1.0 + nrm(ks[23], (DEPTH, D_MODEL), 0.02)
    w_mlp_in = nrm(ks[24], (DEPTH, D_MODEL, D_FF), D_MODEL ** -0.5)
    w_mlp_out = nrm(ks[25], (DEPTH, D_FF, D_MODEL), D_FF ** -0.5)
    final_norm_g = 1.0 + nrm(ks[26], (D_MODEL,), 0.02)
    return {'x': x, 'c': c, 'positions': positions, 'w_mod': w_mod, 'b_mod': b_mod,
            'norm_mix_g': norm_mix_g, 'w_in': w_in, 'gdn_conv_w': gdn_conv_w,
            'gdn_a_log': gdn_a_log, 'gdn_dt_bias': gdn_dt_bias, 'gdn_norm_g': gdn_norm_g,
            'rg_conv_w': rg_conv_w, 'rg_conv_b': rg_conv_b, 'rg_w_a': rg_w_a, 'rg_b_a': rg_b_a,
            'rg_w_x': rg_w_x, 'rg_b_x': rg_b_x, 'rg_lambda': rg_lambda,
            'mla_q_norm_g': mla_q_norm_g, 'mla_w_qb': mla_w_qb, 'mla_kv_norm_g': mla_kv_norm_g,
            'mla_w_kvb': mla_w_kvb, 'w_out': w_out, 'norm_mlp_g': norm_mlp_g,
            'w_mlp_in': w_mlp_in, 'w_mlp_out': w_mlp_out, 'final_norm_g': final_norm_g}


def reference(x, c, positions, w_mod, b_mod, norm_mix_g, w_in, gdn_conv_w, gdn_a_log, gdn_dt_bias,
              gdn_norm_g, rg_conv_w, rg_conv_b, rg_w_a, rg_b_a, rg_w_x, rg_b_x, rg_lambda,
              mla_q_norm_g, mla_w_qb, mla_kv_norm_g, mla_w_kvb, w_out, norm_mlp_g,
              w_mlp_in, w_mlp_out, final_norm_g):
    split_points = np.cumsum(IN_SIZES)[:-1].tolist()
    h = x
    c_act = jax.nn.silu(c)
    for l in range(DEPTH):
        mod = c_act @ w_mod[l] + b_mod[l]
        sh_m, sc_m, gt_m, sh_f, sc_f, gt_f = (m[:, None, :] for m in jnp.split(mod, N_MOD, axis=-1))
        u = rmsnorm(h, norm_mix_g[l]) * (1.0 + sc_m) + sh_m
        proj = u @ w_in[l]
        gq, gk, gv, gz, ga, gb, rx, rgate, mq, mkv, mkr = jnp.split(proj, split_points, axis=-1)
        o_a = gdn_mixer(gq, gk, gv, gz, ga, gb, gdn_conv_w[l], gdn_a_log[l], gdn_dt_bias[l], gdn_norm_g[l])
        o_b = rglru_mixer(rx, rgate, rg_conv_w[l], rg_conv_b[l], rg_w_a[l], rg_b_a[l],
                          rg_w_x[l], rg_b_x[l], rg_lambda[l])
        o_c = mla_mixer(mq, mkv, mkr, positions, mla_q_norm_g[l], mla_w_qb[l],
                        mla_kv_norm_g[l], mla_w_kvb[l])
        mix = jnp.concatenate([o_a, o_b, o_c], axis=-1) @ w_out[l]
        h = h + gt_m * mix
        u = rmsnorm(h, norm_mlp_g[l]) * (1.0 + sc_f) + sh_f
        f = jnp.square(jax.nn.relu(u @ w_mlp_in[l])) @ w_mlp_out[l]
        h = h + gt_f * f
    return rmsnorm(h, final_norm_g)
```

```python
import numpy as np
import concourse.bass as bass
import concourse.mybir as mybir
from concourse.bass_utils import run_bass_kernel_spmd

F32 = mybir.dt.float32
BF16 = mybir.dt.bfloat16
I32 = mybir.dt.int32
AF = mybir.ActivationFunctionType
ALU = mybir.AluOpType

S = 2048
D = 1024
NSEQ = 2
DEPTH = 2
EPS = 1e-6
GROUPS = {"gdn": True, "rg": True, "mla": True, "mlp": True}

PC = {}
_off = 0


def _pc(name, n):
    global _off
    PC[name] = _off
    _off += n


_pc("c", 16)
_pc("fng", 8)
_pc("invf", 1)
for _l in range(DEPTH):
    _pc(f"bmod{_l}", 48)
    _pc(f"nmg{_l}", 8)
    _pc(f"nfg{_l}", 8)
    _pc(f"gcw{_l}", 24)
    _pc(f"galog{_l}", 4)
    _pc(f"gdtb{_l}", 4)
    _pc(f"gng{_l}", 1)
    _pc(f"rcw{_l}", 16)
    _pc(f"rcb{_l}", 4)
    _pc(f"rba{_l}", 4)
    _pc(f"rbx{_l}", 4)
    _pc(f"rlam{_l}", 4)
    _pc(f"mqg{_l}", 2)
    _pc(f"mkvg{_l}", 1)
NPAR = _off


class Buf:
    __slots__ = ("w", "r")

    def __init__(self):
        self.w = []
        self.r = {}


class Sched:
    EPOCH = 20000

    def __init__(self, nc, sems):
        self.nc = nc
        self.sems = sems
        self.eng = {"pe": nc.tensor, "dve": nc.vector, "act": nc.scalar, "pool": nc.gpsimd, "sp": nc.sync}
        self.q = {e: [] for e in self.eng}
        self.cnt = {e: 0 for e in self.eng}
        self.esem = {e: [] for e in self.eng}
        self.seen = {e: {} for e in self.eng}
        self.pending = {e: [] for e in self.eng}
        self.last = {e: None for e in self.eng}
        self.dsem = {}
        self.dcnt = {}
        self.dval = {}
        for qn in ("sp", "pool"):
            self.dsem[qn] = [self._newsem() for _ in range(6)]
            self.dval[qn] = [0] * 6
            self.dcnt[qn] = 0

    def _newsem(self):
        return self.sems.pop()

    def _deps(self, eng, reads, writes, extra=(), same=True):
        evs = list(extra)
        need = {}
        for b in reads:
            evs.extend(b.w)
        for b in writes:
            evs.extend(b.w)
            evs.extend(b.r.values())
        if self.pending[eng]:
            evs.extend(self.pending[eng])
            self.pending[eng] = []
        for (src, sem, val) in evs:
            if src == eng:
                if eng == "pe" or not same or sem is not (self.esem[eng][-1] if self.esem[eng] else None):
                    continue
                if self.cnt[eng] % self.EPOCH - (val - 1) > 5:
                    continue
                k = id(sem)
                if k not in need or need[k][1] < val:
                    need[k] = (sem, val)
                continue
            k = id(sem)
            if self.seen[eng].get(k, 0) >= val:
                continue
            if k not in need or need[k][1] < val:
                need[k] = (sem, val)
        for k, (sem, val) in need.items():
            self.seen[eng][k] = val
        return list(need.values())

    mute = False

    def op(self, eng, f, *args, reads=(), writes=(), **kw):
        if self.mute:
            return
        fn = (lambda: f(*args, **kw))
        waits = self._deps(eng, reads, writes)
        idx = self.cnt[eng]
        self.cnt[eng] += 1
        ep, v = idx // self.EPOCH, idx % self.EPOCH + 1
        while len(self.esem[eng]) <= ep:
            self.esem[eng].append(self._newsem())
        sem = self.esem[eng][ep]
        self.q[eng].append((waits, fn, sem, 1))
        ev = (eng, sem, v)
        self.last[eng] = ev
        for b in reads:
            b.r[eng] = ev
        for b in writes:
            b.w = [ev]
            b.r = {}

    def dma(self, qn, f, *args, reads=(), writes=(), **kw):
        if self.mute:
            return None
        fn = (lambda: f(*args, **kw))
        i = self.dcnt[qn] % len(self.dsem[qn])
        self.dcnt[qn] += 1
        sem = self.dsem[qn][i]
        prev = ("dma" + qn, sem, self.dval[qn][i])
        waits = self._deps(qn, reads, writes, extra=[prev] if prev[2] > 0 else [])
        self.dval[qn][i] += 16
        self.q[qn].append((waits, fn, sem, 16))
        ev = ("dma" + qn, sem, self.dval[qn][i])
        for b in reads:
            b.r["dma" + qn + str(i)] = ev
        for b in writes:
            b.w = [e for e in b.w if e[0].startswith("dma")][-40:] + [ev]
            b.r = {}
        return ev

    def barrier(self):
        evs = [ev for ev in self.last.values() if ev is not None]
        for qn in self.dsem:
            for i, sem in enumerate(self.dsem[qn]):
                if self.dval[qn][i] > 0:
                    evs.append(("dma" + qn, sem, self.dval[qn][i]))
        for e in self.eng:
            self.pending[e] = list(evs)

    def final_wait(self, eng, bufs):
        waits = self._deps(eng, bufs, bufs)
        self.q[eng].append((waits, None, None, 0))

    def replay(self, eng, h):
        for waits, fn, sem, inc in self.q[eng]:
            for (s, v) in waits:
                h.wait_ge(s, v)
            if fn is not None:
                fn().then_inc(sem, inc)


def build_nc(groups=None, debug_stop=None, lim=None):
    lim = lim or {}
    G = dict(GROUPS)
    if groups:
        G.update(groups)
    nc = bass.Bass("TRN2", target_bir_lowering=False)
    dt_ = nc.dram_tensor
    xT = dt_("xT", [NSEQ, D, S], F32, kind="ExternalInput").ap()
    par_d = dt_("par", [128, NPAR], F32, kind="ExternalInput").ap()
    pos_d = dt_("pos", [NSEQ, 128, S], I32, kind="ExternalInput").ap()
    wmod_d = dt_("wmod", [DEPTH, 128, 8, 6144], F32, kind="ExternalInput").ap()
    win_d = dt_("win", [DEPTH, 128, 8, 2472], F32, kind="ExternalInput").ap()
    wout_d = dt_("wout", [DEPTH, 128, 8, 1024], F32, kind="ExternalInput").ap()
    woutm_d = dt_("woutm", [DEPTH, 64, 4, 1024], F32, kind="ExternalInput").ap()
    w1_d = dt_("w1", [DEPTH, 128, 8, 4096], F32, kind="ExternalInput").ap()
    w2_d = dt_("w2", [DEPTH, 128, 32, 1024], F32, kind="ExternalInput").ap()
    rgw_d = dt_("rgw", [DEPTH, 128, 8, 128], F32, kind="ExternalInput").ap()
    wqb_d = dt_("wqb", [DEPTH, 128, 2, 384], F32, kind="ExternalInput").ap()
    wkvb_d = dt_("wkvb", [DEPTH, 128, 512], F32, kind="ExternalInput").ap()
    yT = dt_("yT", [NSEQ, D, S], F32, kind="ExternalOutput").ap()
    rope_d = dt_("ropescr", [NSEQ, 2, 32, S], F32, kind="Internal").ap()

    from contextlib import ExitStack
    es = ExitStack()
    sb = lambda n, shp, d=F32: es.enter_context(nc.sbuf_tensor(n, shp, d))
    ps = lambda n: es.enter_context(nc.psum_tensor(n, [128, 512], F32))
    with es:
        sems = [es.enter_context(nc.semaphore(f"s{i}")) for i in range(40)]
        sc = Sched(nc, sems)
        op, dma = sc.op, sc.dma
        V, A, P, PE = nc.vector, nc.scalar, nc.gpsimd, nc.tensor

        hT = sb("hT", [128, 8, S])
        uT = sb("uT", [128, 8, S], BF16)
        par = sb("par_sb", [128, NPAR])
        cst = sb("cst", [128, 8])
        ident = sb("ident", [128, 128])
        ones_bf = sb("ones_bf", [128, 128], BF16)
        ones_f = sb("ones_f", [128, 128])
        bones_bf = sb("bones_bf", [128, 128], BF16)
        modT = sb("modT", [128, DEPTH, 48, NSEQ])
        gmix = sb("gmix", [128, 4, 8])
        cact = sb("cact", [128, 8, NSEQ], BF16)
        ctmp = sb("ctmp", [128, 16])
        wsl = [sb("wslA", [128, 12288], BF16), sb("wslB", [128, 12288], BF16)]
        arena = sb("arena", [128, 13000])
        PS = [ps(f"ps{i}") for i in range(8)]

        B = {}

        def bf(name):
            if name not in B:
                B[name] = Buf()
            return B[name]

        hB = [[bf(f"h{k}_{t}") for t in range(4)] for k in range(8)]
        uB = [[bf(f"u{k}_{t}") for t in range(4)] for k in range(8)]
        psB = [bf(f"ps{i}") for i in range(8)]
        wB = [bf("wslA"), bf("wslB")]

        def col(name, j=0):
            c0 = PC[name] + j
            return par[:, c0:c0 + 1]

        op("pool", P.memset, cst[:, 0:1], EPS, writes=[bf("cst")])
        op("pool", P.memset, cst[:, 1:2], 1.0, writes=[bf("cst")])
        op("pool", P.memset, cst[:, 2:3], float(np.log(0.125)), writes=[bf("cst")])
        op("pool", P.memset, cst[:, 3:4], 0.0, writes=[bf("cst")])
        op("pool", P.memset, ones_f[:], 1.0, writes=[bf("ones_f")])
        op("pool", P.memset, ones_bf[:], 1.0, writes=[bf("ones_bf")])
        op("pool", P.memset, bones_bf[:], 0.0, writes=[bf("bones_bf")])
        op("pool", P.memset, bones_bf[0:64, 0:64], 1.0, writes=[bf("bones_bf")])
        op("pool", P.memset, bones_bf[64:128, 64:128], 1.0, writes=[bf("bones_bf")])
        op("pool", P.affine_select, out=ident[:], in_=ones_f[:], pattern=[[-1, 128]], compare_op=ALU.is_equal,
           fill=0.0, base=0, channel_multiplier=1, reads=[bf("ones_f")], writes=[bf("ident")])
        dma("sp", nc.sync.dma_start, out=par[:], in_=par_d, writes=[bf("par")])

        def col(name, j=0):
            c0 = PC[name] + j
            return par[:, c0:c0 + 1]

        def act(out, in_, func, reads=(), writes=(), **kw):
            op("act", A.activation, out=out, in_=in_, func=func, reads=reads, writes=writes, **kw)

        def tt(eng, out, in0, in1, o, reads=(), writes=()):
            op(eng, (V if eng == "dve" else P).tensor_tensor, out=out, in0=in0, in1=in1, op=o, reads=reads, writes=writes)

        def ts(eng, out, in0, s1, o0, s2=None, o1=None, reads=(), writes=()):
            kw = {} if o1 is None else {"op1": o1}
            op(eng, (V if eng == "dve" else P).tensor_scalar, out=out, in0=in0, scalar1=s1, scalar2=s2, op0=o0, reads=reads, writes=writes, **kw)

        def stt(out, in0, scalar, in1, o0, o1, reads=(), writes=()):
            op("dve", V.scalar_tensor_tensor, out=out, in0=in0, scalar=scalar, in1=in1, op0=o0, op1=o1, reads=reads, writes=writes)

        def mm(out, lhsT, rhs, start, stop, reads=(), writes=()):
            op("pe", PE.matmul, out, lhsT, rhs, start=start, stop=stop, reads=reads, writes=writes)

        def sigm(out, in_, reads, wb, scale=1.0, nbias=None):
            kw = {} if nbias is None else {"bias": nbias}
            act(out, in_, AF.Exp, reads=reads, writes=[wb], scale=-scale, **kw)
            act(out, out, AF.Ln, reads=[bf("cst")], writes=[wb], bias=cst[:, 1:2])
            act(out, out, AF.Exp, writes=[wb], scale=-1.0)

        MUL, ADD, SUB, MAX, MIN = ALU.mult, ALU.add, ALU.subtract, ALU.max, ALU.min

        cc = par[:, PC["c"]:PC["c"] + 16]
        sigm(ctmp[:], cc, [bf("par")], bf("ctmp"))
        for s_ in range(NSEQ):
            tt("dve", cact[:, :, s_], ctmp[:, s_ * 8:(s_ + 1) * 8], par[:, PC["c"] + s_ * 8:PC["c"] + (s_ + 1) * 8], MUL,
               reads=[bf("ctmp"), bf("par")], writes=[bf("cact")])
        wcount = [0]

        def next_slot():
            s = wcount[0] % 2
            wcount[0] += 1
            return s

        for l in range(DEPTH):
            for pc_ in range(12):
                slot = next_slot()
                wv = wsl[slot][:, 0:4096].rearrange("p (k n) -> p k n", k=8)
                dma("pool", P.dma_start, out=wv, in_=wmod_d[l, :, :, pc_ * 512:(pc_ + 1) * 512], writes=[wB[slot]])
                for oc in range(4):
                    pb = (pc_ * 4 + oc) % 8
                    for kc in range(8):
                        mm(PS[pb][:, 0:NSEQ], wv[:, kc, oc * 128:(oc + 1) * 128], cact[:, kc, :], kc == 0, kc == 7,
                           reads=[wB[slot], bf("cact")], writes=[psB[pb]])
                    ch = pc_ * 4 + oc
                    ts("dve", modT[:, l, ch, :], PS[pb][:, 0:NSEQ], col(f"bmod{l}", ch), ADD, reads=[psB[pb], bf("par")], writes=[bf("modT")])

        def norm_stats(t, nchunk_src, scale):
            sq = arena[:, 0:1024].bitcast(BF16).rearrange("p (b n) -> p b n", b=4)
            rs = arena[:, 1024:2048].rearrange("p (b n) -> p b n", b=2)
            tsl = slice(t * 512, (t + 1) * 512)
            pb = t % 2
            for k in range(8):
                j = (t * 8 + k) % 4
                act(sq[:, j, :], hT[:, k, tsl], AF.Square, reads=[hB[k][t]], writes=[bf(f"nsq{j}")])
                mm(PS[pb][:], ones_bf[:], sq[:, j, :], k == 0, k == 7, reads=[bf(f"nsq{j}"), bf("ones_bf")], writes=[psB[pb]])
            rb = bf(f"nrs{t % 2}")
            act(rs[:, t % 2, :], PS[pb][:], AF.Ln, reads=[psB[pb], bf("cst")], writes=[rb], scale=1.0 / D, bias=cst[:, 0:1])
            act(rs[:, t % 2, :], rs[:, t % 2, :], AF.Exp, writes=[rb], scale=-0.5)
            return rs[:, t % 2, :], rb

        def modnorm(l, s_, which):
            sc.barrier()
            gname = f"nmg{l}" if which == 0 else f"nfg{l}"
            sh0, sc0 = (0, 8) if which == 0 else (24, 32)
            stt(gmix[:, which, :], modT[:, l, sc0:sc0 + 8, s_], 1.0, par[:, PC[gname]:PC[gname] + 8], ADD, MUL,
                reads=[bf("modT"), bf("par")], writes=[bf("gmix")])
            tm = arena[:, 2048:3072].rearrange("p (b n) -> p b n", b=2)
            for t in range(4):
                tsl = slice(t * 512, (t + 1) * 512)
                rs, rb = norm_stats(t, 8, 1.0 / D)
                for k in range(8):
                    tb = bf(f"ntm{k % 2}")
                    tt("dve", tm[:, k % 2, :], hT[:, k, tsl], rs, MUL, reads=[hB[k][t], rb], writes=[tb])
                    act(uT[:, k, tsl], tm[:, k % 2, :], AF.Identity, reads=[tb, bf("gmix"), bf("modT")], writes=[uB[k][t]],
                        scale=gmix[:, which, k:k + 1], bias=modT[:, l, sh0 + k, s_:s_ + 1])

        def final_norm(s_):
            sc.barrier()
            ot = arena[:, 2048:4096].rearrange("p (b n) -> p b n", b=4)
            for t in range(4):
                tsl = slice(t * 512, (t + 1) * 512)
                rs, rb = norm_stats(t, 8, 1.0 / D)
                for k in range(8):
                    ob = bf(f"fot{k % 4}")
                    stt(ot[:, k % 4, :], hT[:, k, tsl], col("fng", k), rs, MUL, MUL, reads=[hB[k][t], rb, bf("par")], writes=[ob])
                    dma("sp", nc.sync.dma_start, out=yT[s_, k * 128:(k + 1) * 128, tsl], in_=ot[:, k % 4, :], reads=[ob], writes=[bf("yout")])

        def mlp(l, s_):
            sc.barrier()
            fbuf = arena[:, 0:2048].bitcast(BF16).rearrange("p (b n) -> p b n", b=8)
            rl = arena[:, 2048:4096].rearrange("p (b n) -> p b n", b=4)
            cnt = 0
            for fc in range(8):
                slot = next_slot()
                w1v = wsl[slot][:, 0:4096].rearrange("p (k n) -> p k n", k=8)
                w2v = wsl[slot][:, 4096:8192].rearrange("p (k n) -> p k n", k=4)
                dma("pool", P.dma_start, out=w1v, in_=w1_d[l, :, :, fc * 512:(fc + 1) * 512], writes=[wB[slot]])
                dma("pool", P.dma_start, out=w2v, in_=w2_d[l, :, fc * 4:(fc + 1) * 4, :], writes=[wB[slot]])
                for t in range(4):
                    tsl = slice(t * 512, (t + 1) * 512)
                    par_ = (fc * 4 + t) % 2
                    for fs in range(4):
                        pb = cnt % 4
                        cnt += 1
                        for kc in range(8):
                            mm(PS[pb][:], w1v[:, kc, fs * 128:(fs + 1) * 128], uT[:, kc, tsl], kc == 0, kc == 7,
                               reads=[wB[slot], uB[kc][t]], writes=[psB[pb]])
                        rb = bf(f"rl{pb}")
                        fb = bf(f"f1_{par_}_{fs}")
                        act(rl[:, pb, :], PS[pb][:], AF.Relu, reads=[psB[pb]], writes=[rb])
                        tt("pool", fbuf[:, par_ * 4 + fs, :], rl[:, pb, :], rl[:, pb, :], MUL, reads=[rb], writes=[fb])
                    for oc in range(8):
                        pb = 4 + oc % 4
                        for fs in range(4):
                            mm(PS[pb][:], w2v[:, fs, oc * 128:(oc + 1) * 128], fbuf[:, par_ * 4 + fs, :], fs == 0, fs == 3,
                               reads=[wB[slot], bf(f"f1_{par_}_{fs}")], writes=[psB[pb]])
                        stt(hT[:, oc, tsl], PS[pb][:], modT[:, l, 40 + oc, s_:s_ + 1], hT[:, oc, tsl], MUL, ADD,
                            reads=[psB[pb], bf("modT"), hB[oc][t]], writes=[hB[oc][t]])

        def accum_wout(l, s_, t0, n, rhs_fn, nk, wv_fn, rd):
            tq = t0 // 512
            for oc in range(8):
                pb = 4 + oc % 4
                for kc in range(nk):
                    mm(PS[pb][:, 0:n], wv_fn(kc, oc), rhs_fn(kc), kc == 0, kc == nk - 1, reads=rd, writes=[psB[pb]])
                stt(hT[:, oc, t0:t0 + n], PS[pb][:, 0:n], modT[:, l, 16 + oc, s_:s_ + 1], hT[:, oc, t0:t0 + n], MUL, ADD,
                    reads=[psB[pb], bf("modT"), hB[oc][tq]], writes=[hB[oc][tq]])

        def rg_group(l, s_):
            sc.barrier()
            T = 256
            slot = next_slot()
            wv = wsl[slot][:, 0:8192].rearrange("p (k n) -> p k n", k=8)
            wo = wsl[slot][:, 8192:12288].rearrange("p (k n) -> p k n", k=4)
            dma("pool", P.dma_start, out=wv, in_=win_d[l, :, :, 1032:2056], writes=[wB[slot]])
            dma("pool", P.dma_start, out=wo, in_=wout_d[l, :, 2:6, :], writes=[wB[slot]])
            rgw = arena[:, 0:1024].rearrange("p (k n) -> p k n", k=8)
            dma("sp", nc.sync.dma_start, out=rgw, in_=rgw_d[l], writes=[bf("rgw")])
            prm = arena[:, 1024:1040]
            hst = arena[:, 1040:1044]
            xin = arena[:, 1048:1048 + 4 * 259].rearrange("p (j n) -> p j n", j=4)
            o0 = 1048 + 4 * 259 + 4
            gate = arena[:, o0:o0 + 4 * T].rearrange("p (j n) -> p j n", j=4)
            o1 = o0 + 4 * T
            tmpv = arena[:, o1:o1 + 10 * T].rearrange("p (j n) -> p j n", j=10)
            o2 = o1 + 10 * T
            ob = arena[:, o2:o2 + 2 * T].bitcast(BF16).rearrange("p (j n) -> p j n", j=4)
            pb_ = bf("rgprm")
            lam = par[:, PC[f"rlam{l}"]:PC[f"rlam{l}"] + 4]
            act(prm[:, 0:4], lam, AF.Exp, reads=[bf("par")], writes=[pb_], scale=-1.0)
            act(prm[:, 0:4], prm[:, 0:4], AF.Ln, reads=[bf("cst")], writes=[pb_], bias=cst[:, 1:2])
            ts("dve", prm[:, 4:8], prm[:, 0:4], -16.0, MUL, writes=[pb_])
            ts("dve", prm[:, 0:4], prm[:, 0:4], -8.0, MUL, writes=[pb_])
            ts("dve", prm[:, 8:12], par[:, PC[f"rba{l}"]:PC[f"rba{l}"] + 4], -1.0, MUL, reads=[bf("par")], writes=[pb_])
            ts("dve", prm[:, 12:16], par[:, PC[f"rbx{l}"]:PC[f"rbx{l}"] + 4], -1.0, MUL, reads=[bf("par")], writes=[pb_])
            op("dve", V.memset, hst, 0.0, writes=[bf("rghst")])
            op("dve", V.memset, xin[:, :, 0:3], 0.0, writes=[bf("rgxin")])
            C_G = 0.7978845608028654
            xb, gb, tb = bf("rgxin"), bf("rggate"), bf("rgtmp")
            for tq in range(S // T):
                t0 = tq * T
                for j in range(4):
                    pbx, pbg = (2 * j) % 4, (2 * j + 1) % 4
                    for kc in range(8):
                        mm(PS[pbx][:, 0:T], wv[:, kc, j * 128:(j + 1) * 128], uT[:, kc, t0:t0 + T], kc == 0, kc == 7,
                           reads=[wB[slot], uB[kc][t0 // 512]], writes=[psB[pbx]])
                    for kc in range(8):
                        mm(PS[pbg][:, 0:T], wv[:, kc, 512 + j * 128:512 + (j + 1) * 128], uT[:, kc, t0:t0 + T], kc == 0, kc == 7,
                           reads=[wB[slot], uB[kc][t0 // 512]], writes=[psB[pbg]])
                    act(xin[:, j, 3:3 + T], PS[pbx][:, 0:T], AF.Copy, reads=[psB[pbx]], writes=[xb])
                    act(gate[:, j, :], PS[pbg][:, 0:T], AF.Copy, reads=[psB[pbg]], writes=[gb])
                    xc = tmpv[:, 0, :]
                    ts("dve", xc, xin[:, j, 0:T], col(f"rcw{l}", j), MUL, col(f"rcb{l}", j), ADD, reads=[xb, bf("par")], writes=[tb])
                    for tap in range(1, 4):
                        stt(xc, xin[:, j, tap:tap + T], col(f"rcw{l}", tap * 4 + j), xc, MUL, ADD, reads=[xb, bf("par")], writes=[tb])
                    op("dve", V.tensor_copy, out=xin[:, j, 0:3], in_=xin[:, j, T:T + 3], reads=[xb], writes=[xb])
                    pr, pi = 4 + (2 * j) % 4, 4 + (2 * j + 1) % 4
                    mm(PS[pr][:, 0:T], rgw[:, j, :], xc, True, True, reads=[bf("rgw"), tb], writes=[psB[pr]])
                    mm(PS[pi][:, 0:T], rgw[:, 4 + j, :], xc, True, True, reads=[bf("rgw"), tb], writes=[psB[pi]])
                    r_, i_, a_, a2_, ml_ = (tmpv[:, q, :] for q in (1, 2, 3, 4, 5))
                    sigm(r_, PS[pr][:, 0:T], [psB[pr], pb_], tb, nbias=prm[:, 8 + j:9 + j])
                    sigm(i_, PS[pi][:, 0:T], [psB[pi], pb_], tb, nbias=prm[:, 12 + j:13 + j])
                    act(a_, r_, AF.Exp, reads=[pb_], writes=[tb], scale=prm[:, j:j + 1])
                    act(a2_, r_, AF.Exp, reads=[pb_], writes=[tb], scale=prm[:, 4 + j:5 + j])
                    act(ml_, a2_, AF.Ln, reads=[bf("cst")], writes=[tb], scale=-1.0, bias=cst[:, 1:2])
                    act(ml_, ml_, AF.Exp, writes=[tb], scale=0.5)
                    tt("dve", i_, i_, xc, MUL, writes=[tb])
                    tt("dve", i_, i_, ml_, MUL, writes=[tb])
                    hs_ = tmpv[:, 6, :]
                    op("dve", V.tensor_tensor_scan, out=hs_, data0=a_, data1=i_, initial=hst[:, j:j + 1], op0=MUL, op1=ADD,
                       reads=[bf("rghst")], writes=[tb])
                    op("pool", P.tensor_copy, out=hst[:, j:j + 1], in_=hs_[:, T - 1:T], reads=[tb], writes=[bf("rghst")])
                    g2, sg = tmpv[:, 7, :], tmpv[:, 8, :]
                    act(g2, gate[:, j, :], AF.Square, reads=[gb], writes=[tb])
                    ts("pool", g2, g2, 0.044715, MUL, 1.0, ADD, writes=[tb])
                    tt("pool", g2, g2, gate[:, j, :], MUL, reads=[gb], writes=[tb])
                    sigm(sg, g2, [tb], tb, scale=2.0 * C_G)
                    tt("pool", sg, sg, gate[:, j, :], MUL, reads=[gb], writes=[tb])
                    tt("dve", ob[:, j, :], hs_, sg, MUL, reads=[tb], writes=[bf("rgob")])
                accum_wout(l, s_, t0, T, lambda kc: ob[:, kc, :], 4, lambda kc, oc: wo[:, kc, oc * 128:(oc + 1) * 128], [wB[slot], bf("rgob")])

        def rope_tables(s_):
            sc.barrier()
            pi_ = arena[:, 0:2048].bitcast(I32)
            ang = arena[:, 2048:4096]
            kf = arena[:, 4096:6144]
            ki = arena[:, 6144:8192].bitcast(I32)
            r0 = arena[:, 8192:10240]
            m_ = arena[:, 10240:12288]
            rb = bf("ropew")
            dma("sp", nc.sync.dma_start, out=pi_[0:32, :], in_=pos_d[s_, 0:32, :], writes=[rb])
            op("dve", V.tensor_copy, out=ang[0:32, :], in_=pi_[0:32, :], reads=[rb], writes=[rb])
            ts("dve", ang[0:32, :], ang[0:32, :], par[0:32, PC["invf"]:PC["invf"] + 1], MUL, reads=[bf("par")], writes=[rb])
            TWO_PI = 2.0 * np.pi
            C1 = 6.28125
            C2 = TWO_PI - C1
            for which, shift in ((0, np.pi / 2.0), (1, 0.0)):
                ts("dve", kf[0:32, :], ang[0:32, :], float(shift), ADD, 1.0 / TWO_PI, MUL, writes=[rb])
                op("dve", V.tensor_copy, out=ki[0:32, :], in_=kf[0:32, :], writes=[rb])
                op("dve", V.tensor_copy, out=kf[0:32, :], in_=ki[0:32, :], writes=[rb])
                ts("dve", r0[0:32, :], ang[0:32, :], float(shift), ADD, writes=[rb])
                stt(r0[0:32, :], kf[0:32, :], -C1, r0[0:32, :], MUL, ADD, writes=[rb])
                stt(r0[0:32, :], kf[0:32, :], -C2, r0[0:32, :], MUL, ADD, writes=[rb])
                ts("dve", m_[0:32, :], r0[0:32, :], float(np.pi), ALU.is_gt, -TWO_PI, MUL, writes=[rb])
                tt("dve", r0[0:32, :], r0[0:32, :], m_[0:32, :], ADD, writes=[rb])
                ts("dve", m_[0:32, :], r0[0:32, :], float(-np.pi), ALU.is_lt, TWO_PI, MUL, writes=[rb])
                tt("dve", r0[0:32, :], r0[0:32, :], m_[0:32, :], ADD, writes=[rb])
                ts("dve", r0[0:32, :], r0[0:32, :], 3.1415925, MIN, -3.1415925, MAX, writes=[rb])
                act(m_[0:32, :], r0[0:32, :], AF.Sin, writes=[rb])
                dma("sp", nc.sync.dma_start, out=rope_d[s_, which], in_=m_[0:32, :], reads=[rb], writes=[bf(f"roped{which}")])

        def mla_group(l, s_):
            sc.barrier()
            T = 512
            slot = next_slot()
            W = wsl[slot]
            wm = W[:, 0:3328].rearrange("p (k n) -> p k n", k=8)
            wks = W[:, 3328:3584].rearrange("p (k n) -> p k n", k=8)
            wom = W[:, 3584:7680].rearrange("p (h n) -> p h n", h=4)
            wq = W[:, 7680:8448].rearrange("p (k n) -> p k n", k=2)
            wqs = W[:, 8448:8704].rearrange("p (k h n) -> p k h n", k=2, h=4)
            wkv = W[:, 8704:9216]
            wb_ = wB[slot]
            dma("pool", P.dma_start, out=wm, in_=win_d[l, :, :, 2056:2472], writes=[wb_])
            dma("pool", P.dma_start, out=wom[0:64], in_=woutm_d[l], writes=[wb_])
            dma("pool", P.dma_start, out=wq, in_=wqb_d[l], writes=[wb_])
            dma("pool", P.dma_start, out=wkv, in_=wkvb_d[l], writes=[wb_])
            op("act", A.mul, out=wks[:, :, 0:16], in_=wm[:, :, 400:416], mul=-1.0, reads=[wb_], writes=[wb_])
            op("act", A.copy, out=wks[:, :, 16:32], in_=wm[:, :, 384:400], reads=[wb_], writes=[wb_])
            wq4 = wq.rearrange("p k (h n) -> p k h n", h=4)
            for kc in range(2):
                op("act", A.mul, out=wqs[:, kc, :, 0:16], in_=wq4[:, kc, :, 80:96], mul=-1.0, reads=[wb_], writes=[wb_])
                op("act", A.copy, out=wqs[:, kc, :, 16:32], in_=wq4[:, kc, :, 64:80], reads=[wb_], writes=[wb_])
            a0 = 0
            Kc = arena[:, 0:4096].bitcast(BF16).rearrange("p (h n) -> p h n", h=4)
            Vc = arena[:, 4096:6176].bitcast(BF16).rearrange("p (j h n) -> p j h n", j=16, h=4)
            o = 6176

            def take(n):
                nonlocal o
                r = arena[:, o:o + n]
                o += n
                return r
            qn = take(512).bitcast(BF16).rearrange("p (k n) -> p k n", k=2)
            kvn = take(256).bitcast(BF16)
            Qh = take(256).bitcast(BF16)
            ET = take(512).bitcast(BF16).rearrange("p (k n) -> p k n", k=2)
            cos_ = take(512)
            sin_ = take(512)
            rs = take(512)
            sq = take(256).bitcast(BF16)
            t1 = take(512)
            t2 = take(512)
            osb = take(512)
            rec = take(512)
            oT = take(1024).bitcast(BF16).rearrange("p (h n) -> p h n", h=4)
            assert o <= 13000
            kb, vb_, wk = bf("mlaK"), bf("mlaV"), bf("mlawk")
            op("pool", P.memset, Vc[:, :, :, 64:65], 1.0, writes=[vb_])
            SCALE = float(96 ** -0.5)
            bank = [0]

            def nb():
                bank[0] = (bank[0] + 1) % 4
                return bank[0]
            for t in range(4):
                tsl = slice(t * T, (t + 1) * T)
                ub = [uB[kc][t] for kc in range(8)]
                dma("sp", nc.sync.dma_start, out=cos_[64:96, :], in_=rope_d[s_, 0, :, tsl], reads=[bf("roped0")], writes=[bf("mlacs")])
                dma("sp", nc.sync.dma_start, out=sin_[64:96, :], in_=rope_d[s_, 1, :, tsl], reads=[bf("roped1")], writes=[bf("mlacs")])
                for c in range(2):
                    for kc in range(8):
                        mm(PS[c][:], wm[:, kc, c * 128:(c + 1) * 128], uT[:, kc, tsl], kc == 0, kc == 7, reads=[wb_] + ub, writes=[psB[c]])
                for kc in range(8):
                    mm(PS[2][:], wm[:, kc, 256:384], uT[:, kc, tsl], kc == 0, kc == 7, reads=[wb_] + ub, writes=[psB[2]])
                for kc in range(8):
                    mm(PS[3][64:96, :], wm[:, kc, 384:416], uT[:, kc, tsl], kc == 0, kc == 7, reads=[wb_] + ub, writes=[psB[3]])
                for kc in range(8):
                    mm(PS[4][64:96, :], wks[:, kc, :], uT[:, kc, tsl], kc == 0, kc == 7, reads=[wb_] + ub, writes=[psB[4]])
                for c in range(2):
                    act(sq, PS[c][:], AF.Square, reads=[psB[c]], writes=[wk])
                    mm(PS[5][:], ones_bf[:], sq, c == 0, c == 1, reads=[wk, bf("ones_bf")], writes=[psB[5]])
                act(rs, PS[5][:], AF.Ln, reads=[psB[5], bf("cst")], writes=[wk], scale=1.0 / 256, bias=cst[:, 0:1])
                act(rs, rs, AF.Exp, writes=[wk], scale=-0.5)
                for c in range(2):
                    stt(qn[:, c, :], PS[c][:], col(f"mqg{l}", c), rs, MUL, MUL, reads=[psB[c], bf("par"), wk], writes=[wk])
                act(sq, PS[2][:], AF.Square, reads=[psB[2]], writes=[wk])
                mm(PS[5][:], ones_bf[:], sq, True, True, reads=[wk, bf("ones_bf")], writes=[psB[5]])
                act(rs, PS[5][:], AF.Ln, reads=[psB[5], bf("cst")], writes=[wk], scale=1.0 / 128, bias=cst[:, 0:1])
                act(rs, rs, AF.Exp, writes=[wk], scale=-0.5)
                stt(kvn, PS[2][:], col(f"mkvg{l}", 0), rs, MUL, MUL, reads=[psB[2], bf("par"), wk], writes=[wk])
                tt("dve", t1[64:96, :], PS[3][64:96, :], cos_[64:96, :], MUL, reads=[psB[3], bf("mlacs")], writes=[wk])
                tt("dve", t2[64:96, :], PS[4][64:96, :], sin_[64:96, :], MUL, reads=[psB[4], bf("mlacs")], writes=[wk])
                for h in range(4):
                    tt("dve", Kc[64:96, h, tsl], t1[64:96, :], t2[64:96, :], ADD, reads=[wk], writes=[kb])
                for h in range(4):
                    pb = 6 + h % 2
                    mm(PS[pb][0:64, :], wkv[:, h * 128:h * 128 + 64], kvn, True, True, reads=[wb_, wk], writes=[psB[pb]])
                    act(Kc[0:64, h, tsl], PS[pb][0:64, :], AF.Copy, reads=[psB[pb]], writes=[kb])
                for kt in range(4):
                    pb = 6 + kt % 2
                    for h in range(4):
                        mm(PS[pb][:, h * 64:(h + 1) * 64], kvn[:, kt * 128:(kt + 1) * 128], wkv[:, h * 128 + 64:h * 128 + 128], True, True,
                           reads=[wb_, wk], writes=[psB[pb]])
                    op("dve", V.tensor_copy, out=Vc[:, t * 4 + kt, :, 0:64], in_=PS[pb][:, 0:256].rearrange("p (h n) -> p h n", h=4),
                       reads=[psB[pb]], writes=[vb_])
                for h in range(4):
                    pa, pbq = 0, 1
                    for c in range(2):
                        mm(PS[pa][0:96, :], wq[:, c, h * 96:(h + 1) * 96], qn[:, c, :], c == 0, c == 1, reads=[wb_, wk], writes=[psB[pa]])
                    for c in range(2):
                        mm(PS[pbq][64:96, :], wqs[:, c, h, :], qn[:, c, :], c == 0, c == 1, reads=[wb_, wk], writes=[psB[pbq]])
                    qb = bf("mlaQ")
                    act(Qh[0:64, :], PS[pa][0:64, :], AF.Copy, reads=[psB[pa]], writes=[qb])
                    tt("dve", t1[64:96, :], PS[pa][64:96, :], cos_[64:96, :], MUL, reads=[psB[pa], bf("mlacs")], writes=[wk])
                    tt("dve", t2[64:96, :], PS[pbq][64:96, :], sin_[64:96, :], MUL, reads=[psB[pbq], bf("mlacs")], writes=[wk])
                    tt("dve", Qh[64:96, :], t1[64:96, :], t2[64:96, :], ADD, reads=[wk], writes=[qb])
                    nkt = 4 * t + 4
                    pacc = 5
                    for j in range(nkt):
                        r = j - 4 * t
                        qs = 0 if r < 0 else r * 128
                        n = T - qs
                        pst = 2 + j % 2
                        e = j % 2
                        eb = bf(f"mlaE{e}")
                        mm(PS[pst][:, 0:n], Kc[0:96, h, j * 128:(j + 1) * 128], Qh[0:96, qs:T], True, True, reads=[kb, qb], writes=[psB[pst]])
                        act(ET[:, e, 0:n], PS[pst][:, 0:n], AF.Exp, reads=[psB[pst]], writes=[eb], scale=SCALE)
                        if r >= 0:
                            op("pool", P.memset, ET[64:128, e, 0:64], 0.0, writes=[eb])
                        mm(PS[pacc][0:65, qs:T], Vc[:, j, h, :], ET[:, e, 0:n], j == 0, j == nkt - 1, reads=[vb_, eb], writes=[psB[pacc]])
                    act(osb[0:65, :], PS[pacc][0:65, :], AF.Copy, reads=[psB[pacc]], writes=[wk])
                    op("dve", V.reciprocal, out=rec[64:65, :], in_=osb[64:65, :], reads=[wk], writes=[wk])
                    mm(PS[4][0:64, :], ones_f[64:65, 0:64], rec[64:65, :], True, True, reads=[wk, bf("ones_f")], writes=[psB[4]])
                    tt("dve", oT[0:64, h, :], osb[0:64, :], PS[4][0:64, :], MUL, reads=[wk, psB[4]], writes=[bf("mlaoT")])
                accum_wout(l, s_, t * T, T, lambda kc: oT[0:64, kc, :], 4, lambda kc, oc: wom[0:64, kc, oc * 128:(oc + 1) * 128], [wb_, bf("mlaoT")])

        def gdn_group(l, s_):
            sc.barrier()
            T = 256
            slot = next_slot()
            wg = wsl[slot][:, 0:8256].rearrange("p (k n) -> p k n", k=8)
            wo = wsl[slot][:, 8256:10304].rearrange("p (k n) -> p k n", k=2)
            wb_ = wB[slot]
            dma("pool", P.dma_start, out=wg, in_=win_d[l, :, :, 0:1032], writes=[wb_])
            dma("pool", P.dma_start, out=wo, in_=wout_d[l, :, 0:2, :], writes=[wb_])
            o = [0]

            def take(n):
                r = arena[:, o[0]:o[0] + n]
                o[0] += n
                return r
            Ubd, BD1, Mls, MuiT = take(128), take(128), take(128), take(128)
            xin = take(6 * 259).rearrange("p (j n) -> p j n", j=6)
            yq = take(6 * T).rearrange("p (j n) -> p j n", j=6)
            zs = take(2 * T).rearrange("p (j n) -> p j n", j=2)
            tmp = take(2 * T).rearrange("p (j n) -> p j n", j=2)
            Sst = take(128).rearrange("p (c n) -> p c n", c=2)
            gt_ = take(40)
            M = [take(128) for _ in range(12)]
            ktail, kbg, vbt, u_, vnew = take(64), take(64), take(64), take(64), take(64)
            otok = take(256).rearrange("p (h n) -> p h n", h=4)
            on_ = take(256)
            oT = take(T).bitcast(BF16).rearrange("p (c n) -> p c n", c=2)
            assert o[0] <= 13000, o[0]
            cb, wk, sb_, ob_ = bf("gdnc"), bf("gdnw"), bf("gdnS"), bf("gdno")
            op("pool", P.memset, BD1, 0.0, writes=[cb])
            op("pool", P.memset, BD1[0:64, 0:64], 1.0, writes=[cb])
            op("pool", P.memset, BD1[64:128, 64:128], 1.0, writes=[cb])
            op("pool", P.affine_select, out=Ubd, in_=BD1, pattern=[[1, 128]], compare_op=ALU.is_ge, fill=0.0, base=0, channel_multiplier=-1,
               reads=[cb], writes=[cb])
            op("pool", P.tensor_copy, out=MuiT, in_=Ubd, reads=[cb], writes=[cb])
            op("pool", P.affine_select, out=Mls, in_=BD1, pattern=[[-1, 128]], compare_op=ALU.is_gt, fill=0.0, base=0, channel_multiplier=1,
               reads=[cb], writes=[cb])
            op("pool", P.memset, Sst, 0.0, writes=[sb_])
            op("pool", P.memset, xin[:, :, 0:3], 0.0, writes=[bf("gdnx")])
            nA = gt_[:, 36:40]
            act(nA, par[:, PC[f"galog{l}"]:PC[f"galog{l}"] + 4], AF.Exp, reads=[bf("par")], writes=[cb])
            ts("dve", nA, nA, -1.0, MUL, writes=[cb])
            xb = bf("gdnx")
            stage = lim.get('gdn_stage', 99)

            def cut(n):
                if stage <= n:
                    sc.mute = True
            for tq in range(lim.get('gdn_tiles', S // T)):
                sc.mute = False
                t0 = tq * T
                ub = [uB[kc][t0 // 512] for kc in range(8)]
                for j in range(8):
                    pb = j % 4
                    for kc in range(8):
                        mm(PS[pb][:, 0:T], wg[:, kc, j * 128:(j + 1) * 128], uT[:, kc, t0:t0 + T], kc == 0, kc == 7, reads=[wb_] + ub, writes=[psB[pb]])
                    if j < 6:
                        act(xin[:, j, 3:3 + T], PS[pb][:, 0:T], AF.Copy, reads=[psB[pb]], writes=[xb])
                        y = yq[:, j, :]
                        ts("dve", y, xin[:, j, 0:T], col(f"gcw{l}", j), MUL, reads=[xb, bf("par")], writes=[wk])
                        for tap in range(1, 4):
                            stt(y, xin[:, j, tap:tap + T], col(f"gcw{l}", tap * 6 + j), y, MUL, ADD, reads=[xb, bf("par")], writes=[wk])
                        op("pool", P.tensor_copy, out=xin[:, j, 0:3], in_=xin[:, j, T:T + 3], reads=[xb], writes=[xb])
                        sigm(tmp[:, 0, :], y, [wk], wk)
                        tt("dve", y, y, tmp[:, 0, :], MUL, writes=[wk])
                        if j < 4:
                            act(tmp[:, 1, :].bitcast(BF16)[:, 0:T], y, AF.Square, writes=[wk])
                            mm(PS[4][:, 0:T], bones_bf[:], tmp[:, 1, :].bitcast(BF16)[:, 0:T], True, True, reads=[wk, bf("bones_bf")], writes=[psB[4]])
                            act(tmp[:, 0, :], PS[4][:, 0:T], AF.Ln, reads=[psB[4], bf("cst")], writes=[wk], bias=cst[:, 0:1])
                            if j < 2:
                                act(tmp[:, 0, :], tmp[:, 0, :], AF.Exp, reads=[bf("cst")], writes=[wk], scale=-0.5, bias=cst[:, 2:3])
                            else:
                                act(tmp[:, 0, :], tmp[:, 0, :], AF.Exp, writes=[wk], scale=-0.5)
                            tt("dve", y, y, tmp[:, 0, :], MUL, writes=[wk])
                    else:
                        zc = zs[:, j - 6, :]
                        act(zc, PS[pb][:, 0:T], AF.Copy, reads=[psB[pb]], writes=[wk])
                        sigm(tmp[:, 0, :], zc, [wk], wk)
                        tt("dve", zc, zc, tmp[:, 0, :], MUL, writes=[wk])
                cut(1)
                for blk in range(2):
                    b0 = blk * 128
                    bs = slice(b0, b0 + 128)
                    for kc in range(8):
                        mm(PS[5][:, 0:8], uT[:, kc, t0 + b0:t0 + b0 + 128], wg[:, kc, 1024:1032], kc == 0, kc == 7, reads=[wb_] + ub, writes=[psB[5]])
                    gcol, beta, nbeta, Gc, eG, eTl, bg, sp = (gt_[:, 4 * i:4 * i + 4] for i in range(8))
                    tt("dve", sp, PS[5][:, 0:4], par[:, PC[f"gdtb{l}"]:PC[f"gdtb{l}"] + 4], ADD, reads=[psB[5], bf("par")], writes=[wk])
                    act(sp, sp, AF.Exp, writes=[wk])
                    act(sp, sp, AF.Ln, reads=[bf("cst")], writes=[wk], bias=cst[:, 1:2])
                    tt("dve", gcol, sp, nA, MUL, reads=[cb], writes=[wk])
                    sigm(beta, PS[5][:, 4:8], [psB[5]], wk)
                    ts("dve", nbeta, beta, -1.0, MUL, writes=[wk])
                    mm(PS[6][:, 0:4], Ubd, gcol, True, True, reads=[cb, wk], writes=[psB[6]])
                    mm(PS[6][:, 4:8], BD1, gcol, True, True, reads=[cb, wk], writes=[psB[6]])
                    op("dve", V.tensor_copy, out=Gc, in_=PS[6][:, 0:4], reads=[psB[6]], writes=[wk])
                    act(eG, PS[6][:, 0:4], AF.Exp, reads=[psB[6]], writes=[wk])
                    tt("dve", eTl, PS[6][:, 4:8], Gc, SUB, reads=[psB[6]], writes=[wk])
                    act(eTl, eTl, AF.Exp, writes=[wk])
                    tt("dve", bg, beta, eG, MUL, writes=[wk])
                    cut(2)
                    for h in range(4):
                        c2, po = h // 2, (h % 2) * 64
                        hp = slice(po, po + 64)
                        qT, kT, vT = yq[hp, c2, bs], yq[hp, 2 + c2, bs], yq[hp, 4 + c2, bs]
                        ngbc, E1, E2, eGr, A_, B_, Y_, A2, B2, QK = M[0:10]
                        ts("dve", ngbc, ones_f[:], gt_[:, h:h + 1], MUL, -1.0, MUL, reads=[bf("ones_f")], writes=[wk])
                        mm(PS[0][:, 0:128], ngbc, Ubd, True, True, reads=[wk, cb], writes=[psB[0]])
                        ts("dve", E2, PS[0][:, 0:128], gt_[:, 12 + h:13 + h], ADD, 0.0, MIN, reads=[psB[0]], writes=[wk])
                        act(E2, E2, AF.Exp, writes=[wk])
                        tt("pool", E2, E2, Mls, MUL, reads=[cb], writes=[wk])
                        ts("dve", E1, PS[0][:, 0:128], gt_[:, 12 + h:13 + h], ADD, 0.0, MAX, reads=[psB[0]], writes=[wk])
                        act(E1, E1, AF.Exp, writes=[wk], scale=-1.0)
                        tt("pool", E1, E1, MuiT, MUL, reads=[cb], writes=[wk])
                        act(eGr, PS[0][:, 0:128], AF.Exp, reads=[psB[0]], writes=[wk], scale=-1.0)
                        cut(3)
                        mm(PS[1][:, 0:128], kT, kT, True, True, reads=[wk], writes=[psB[1]])
                        stt(A_, PS[1][:, 0:128], gt_[:, 8 + h:9 + h], E2, MUL, MUL, reads=[psB[1]], writes=[wk])
                        mm(PS[2][:, 0:128], kT, qT, True, True, reads=[wk], writes=[psB[2]])
                        tt("dve", QK, PS[2][:, 0:128], E1, MUL, reads=[psB[2]], writes=[wk])
                        cut(4)
                        op("pe", PE.transpose, PS[3][:, 0:128], A_, ident[:], reads=[wk, bf("ident")], writes=[psB[3]])
                        op("dve", V.tensor_copy, out=B_, in_=PS[3][:, 0:128], reads=[psB[3]], writes=[wk])
                        tt("dve", Y_, B_, ident[:], ADD, reads=[bf("ident")], writes=[wk])
                        Am, Bm, An, Bn = A_, B_, A2, B2
                        for lvl in range(5):
                            mm(PS[0][:, 0:128], Bm, Am, True, True, reads=[wk], writes=[psB[0]])
                            op("act", A.copy, out=An, in_=PS[0][:, 0:128], reads=[psB[0]], writes=[wk])
                            if lvl < 4:
                                mm(PS[1][:, 0:128], Am, Bm, True, True, reads=[wk], writes=[psB[1]])
                                op("dve", V.tensor_copy, out=Bn, in_=PS[1][:, 0:128], reads=[psB[1]], writes=[wk])
                            mm(PS[2][:, 0:128], An, Y_, True, True, reads=[wk], writes=[psB[2]])
                            tt("dve", Y_, Y_, PS[2][:, 0:128], ADD, reads=[psB[2]], writes=[wk])
                            Am, Bm, An, Bn = An, Bn, Am, Bm
                        cut(5)
                        idh = ident[hp, po:po + 64]
                        op("pe", PE.transpose, PS[3][:, 0:64], kT, idh, reads=[wk, bf("ident")], writes=[psB[3]])
                        ts("dve", ktail, PS[3][:, 0:64], gt_[:, 20 + h:21 + h], MUL, reads=[psB[3]], writes=[wk])
                        ts("dve", kbg, PS[3][:, 0:64], gt_[:, 24 + h:25 + h], MUL, reads=[psB[3]], writes=[wk])
                        op("pe", PE.transpose, PS[0][:, 0:64], vT, idh, reads=[wk, bf("ident")], writes=[psB[0]])
                        ts("dve", vbt, PS[0][:, 0:64], gt_[:, 4 + h:5 + h], MUL, reads=[psB[0]], writes=[wk])
                        cut(6)
                        mm(PS[1][:, 0:64], Y_, vbt, True, True, reads=[wk], writes=[psB[1]])
                        op("act", A.copy, out=u_, in_=PS[1][:, 0:64], reads=[psB[1]], writes=[wk])
                        wT, qdT = M[10], M[11]
                        mm(PS[2][hp, 0:128], kbg, Y_, True, True, reads=[wk], writes=[psB[2]])
                        op("act", A.copy, out=wT[hp, :], in_=PS[2][hp, 0:128], reads=[psB[2]], writes=[wk])
                        tt("dve", qdT[hp, :], qT, eGr[hp, :], MUL, writes=[wk])
                        cut(7)
                        Sh = Sst[hp, c2, :]
                        for c in range(2):
                            tc = slice(c * 64, c * 64 + 64)
                            mm(PS[3][tc, 0:64], wT[hp, tc], Sh, True, True, reads=[wk, sb_], writes=[psB[3]])
                            tt("dve", vnew[tc, :], u_[tc, :], PS[3][tc, 0:64], SUB, reads=[psB[3]], writes=[wk])
                            mm(PS[0][tc, 0:64], qdT[hp, tc], Sh, True, True, reads=[wk, sb_], writes=[psB[0]])
                            mm(PS[4][tc, 0:64], QK[tc, tc], vnew[tc, :], True, True, reads=[wk], writes=[psB[4]])
                            op("act", A.copy, out=otok[tc, h, :], in_=PS[0][tc, 0:64], reads=[psB[0]], writes=[ob_])
                            tt("dve", otok[tc, h, :], otok[tc, h, :], PS[4][tc, 0:64], ADD, reads=[psB[4], ob_], writes=[ob_])
                            mm(PS[1][hp, 0:64], ktail[tc, :], vnew[tc, :], True, True, reads=[wk], writes=[psB[1]])
                            stt(Sh, Sh, eGr[hp, c * 64 + 63:c * 64 + 64], PS[1][hp, 0:64], MUL, ADD, reads=[psB[1], wk], writes=[sb_])
                    cut(8)
                    ss = gt_[:, 32:36]
                    for h in range(4):
                        tt("dve", on_[:, h * 64:(h + 1) * 64], otok[:, h, :], otok[:, h, :], MUL, reads=[ob_], writes=[wk])
                    op("dve", V.tensor_reduce, out=ss, in_=on_.rearrange("p (h n) -> p h n", h=4), axis=mybir.AxisListType.X, op=ADD, writes=[wk])
                    act(ss, ss, AF.Ln, reads=[bf("cst")], writes=[wk], scale=1.0 / 64, bias=cst[:, 0:1])
                    act(ss, ss, AF.Exp, writes=[wk], scale=-0.5)
                    for h in range(4):
                        ts("dve", on_[:, h * 64:(h + 1) * 64], otok[:, h, :], gt_[:, 32 + h:33 + h], MUL, reads=[ob_], writes=[wk])
                    for c2 in range(2):
                        op("pe", PE.transpose, PS[2 + c2][:, 0:128], on_[:, c2 * 128:(c2 + 1) * 128], ident[:], reads=[wk, bf("ident")], writes=[psB[2 + c2]])
                        stt(oT[:, c2, bs], PS[2 + c2][:, 0:128], col(f"gng{l}"), zs[:, c2, bs], MUL, MUL, reads=[psB[2 + c2], bf("par"), wk], writes=[bf("gdnoT")])
                cut(9)
                sc.mute = False
                accum_wout(l, s_, t0, T, lambda kc: oT[:, kc, :], 2, lambda kc, oc: wo[:, kc, oc * 128:(oc + 1) * 128], [wb_, bf("gdnoT")])

        for s_ in range(lim.get('nseq', NSEQ)):
            for k in range(8):
                for t in range(4):
                    dma("sp", nc.sync.dma_start, out=hT[:, k, t * 512:(t + 1) * 512], in_=xT[s_, k * 128:(k + 1) * 128, t * 512:(t + 1) * 512],
                        writes=[hB[k][t]])
            if G["mla"]:
                rope_tables(s_)
            for l in range(lim.get('depth', DEPTH)):
                modnorm(l, s_, 0)
                if G["gdn"]:
                    gdn_group(l, s_)
                if G["rg"]:
                    rg_group(l, s_)
                if G["mla"]:
                    mla_group(l, s_)
                if G["mlp"]:
                    modnorm(l, s_, 1)
                    mlp(l, s_)
            final_norm(s_)
        if debug_stop == "mod":
            dma("sp", nc.sync.dma_start, out=yT[1, 0:128, 0:DEPTH * 48 * NSEQ], in_=modT[:].rearrange("p l c s -> p (l c s)"),
                reads=[bf("modT")], writes=[bf("yout")])
        sc.final_wait("sp", [bf("yout")])
        build_nc.last_counts = dict(sc.cnt)

        with nc.Block() as block:
            @block.tensor
            def _(e):
                sc.replay("pe", e)

            @block.vector
            def _(e):
                sc.replay("dve", e)

            @block.scalar
            def _(e):
                sc.replay("act", e)

            @block.gpsimd
            def _(e):
                sc.replay("pool", e)

            @block.sync
            def _(e):
                sc.replay("sp", e)
    return nc


def _pack_params(inp, core):
    par = np.zeros((128, NPAR), np.float32)
    f = lambda v: np.asarray(v, np.float32)
    cm = lambda v: f(v).reshape(-1, 128).T
    for s_ in range(NSEQ):
        par[:, PC["c"] + 8 * s_:PC["c"] + 8 * (s_ + 1)] = cm(inp["c"][core * NSEQ + s_])
    par[:, PC["fng"]:PC["fng"] + 8] = cm(inp["final_norm_g"])
    invf = (np.float32(10000.0) ** (-np.arange(0, 32, 2, dtype=np.float32) / np.float32(32))).astype(np.float32)
    par[:, PC["invf"]] = np.tile(invf, 8)
    for l in range(DEPTH):
        par[:, PC[f"bmod{l}"]:PC[f"bmod{l}"] + 48] = cm(inp["b_mod"][l])
        par[:, PC[f"nmg{l}"]:PC[f"nmg{l}"] + 8] = cm(inp["norm_mix_g"][l])
        par[:, PC[f"nfg{l}"]:PC[f"nfg{l}"] + 8] = cm(inp["norm_mlp_g"][l])
        gcw = f(inp["gdn_conv_w"][l])
        for tap in range(4):
            par[:, PC[f"gcw{l}"] + tap * 6:PC[f"gcw{l}"] + tap * 6 + 6] = cm(gcw[tap])
        par[:, PC[f"galog{l}"]:PC[f"galog{l}"] + 4] = f(inp["gdn_a_log"][l])[None, :]
        par[:, PC[f"gdtb{l}"]:PC[f"gdtb{l}"] + 4] = f(inp["gdn_dt_bias"][l])[None, :]
        par[:, PC[f"gng{l}"]] = np.tile(f(inp["gdn_norm_g"][l]), 2)
        rcw = f(inp["rg_conv_w"][l])
        for tap in range(4):
            par[:, PC[f"rcw{l}"] + tap * 4:PC[f"rcw{l}"] + tap * 4 + 4] = cm(rcw[tap])
        par[:, PC[f"rcb{l}"]:PC[f"rcb{l}"] + 4] = cm(inp["rg_conv_b"][l])
        par[:, PC[f"rba{l}"]:PC[f"rba{l}"] + 4] = cm(inp["rg_b_a"][l])
        par[:, PC[f"rbx{l}"]:PC[f"rbx{l}"] + 4] = cm(inp["rg_b_x"][l])
        par[:, PC[f"rlam{l}"]:PC[f"rlam{l}"] + 4] = cm(inp["rg_lambda"][l])
        par[:, PC[f"mqg{l}"]:PC[f"mqg{l}"] + 2] = cm(inp["mla_q_norm_g"][l])
        par[:, PC[f"mkvg{l}"]:PC[f"mkvg{l}"] + 1] = cm(inp["mla_kv_norm_g"][l])
    return par


def _shared_weights(inp):
    f = lambda v: np.ascontiguousarray(np.asarray(v, np.float32))
    kp = lambda w, k: f(w.reshape(w.shape[0], k, 128, w.shape[2]).transpose(0, 2, 1, 3))
    sh = {}
    sh["wmod"] = kp(inp["w_mod"], 8)
    sh["win"] = kp(inp["w_in"], 8)
    sh["wout"] = kp(inp["w_out"], 8)
    wo = np.asarray(inp["w_out"], np.float32)
    sh["woutm"] = f(wo[:, 768:1024, :].reshape(DEPTH, 4, 64, 1024).transpose(0, 2, 1, 3))
    sh["w1"] = kp(inp["w_mlp_in"], 8)
    sh["w2"] = kp(inp["w_mlp_out"], 32)
    rgw = np.zeros((DEPTH, 128, 8, 128), np.float32)
    for l in range(DEPTH):
        for wi_, nm in enumerate(("rg_w_a", "rg_w_x")):
            w = np.asarray(inp[nm][l], np.float32)
            for j in range(4):
                rgw[l, 0:64, wi_ * 4 + j, 0:64] = w[2 * j]
                rgw[l, 64:128, wi_ * 4 + j, 64:128] = w[2 * j + 1]
    sh["rgw"] = rgw
    sh["wqb"] = kp(inp["mla_w_qb"], 2)
    sh["wkvb"] = f(inp["mla_w_kvb"])
    return sh


def make_in_maps(inp, cores):
    sh = _shared_weights(inp)
    maps = []
    for c in cores:
        m = dict(sh)
        xs = np.asarray(inp["x"][c * NSEQ:(c + 1) * NSEQ], np.float32)
        m["xT"] = np.ascontiguousarray(xs.transpose(0, 2, 1))
        m["par"] = _pack_params(inp, c)
        pos = np.asarray(inp["positions"][c * NSEQ:(c + 1) * NSEQ], np.int32)
        m["pos"] = np.ascontiguousarray(np.broadcast_to(pos[:, None, :], (NSEQ, 128, S)))
        maps.append(m)
    return maps


def kernel(**inputs):
    nc = build_nc()
    cores = list(range(8))
    in_maps = make_in_maps(inputs, cores)
    res = run_bass_kernel_spmd(nc, in_maps, core_ids=cores)
    out = np.empty((16, S, D), np.float32)
    for c in cores:
        y = res.results[c]["yT"]
        for s_ in range(NSEQ):
            out[c * NSEQ + s_] = y[s_].T
    return out
```

```python
import numpy as np
import concourse.bass as bass
import concourse.mybir as mybir
from concourse.bass_utils import run_bass_kernel_spmd

F32 = mybir.dt.float32
BF16 = mybir.dt.bfloat16
I32 = mybir.dt.int32
AF = mybir.ActivationFunctionType
ALU = mybir.AluOpType

S = 2048
D = 1024
NSEQ = 2
DEPTH = 2
EPS = 1e-6
GROUPS = {"gdn": True, "rg": True, "mla": True, "mlp": True}

PC = {}
_off = 0


def _pc(name, n):
    global _off
    PC[name] = _off
    _off += n


_pc("c", 16)
_pc("fng", 8)
_pc("invf", 1)
for _l in range(DEPTH):
    _pc(f"bmod{_l}", 48)
    _pc(f"nmg{_l}", 8)
    _pc(f"nfg{_l}", 8)
    _pc(f"gcw{_l}", 24)
    _pc(f"galog{_l}", 4)
    _pc(f"gdtb{_l}", 4)
    _pc(f"gng{_l}", 1)
    _pc(f"rcw{_l}", 16)
    _pc(f"rcb{_l}", 4)
    _pc(f"rba{_l}", 4)
    _pc(f"rbx{_l}", 4)
    _pc(f"rlam{_l}", 4)
    _pc(f"mqg{_l}", 2)
    _pc(f"mkvg{_l}", 1)
NPAR = _off


class Buf:
    __slots__ = ("w", "r", "excl")

    def __init__(self):
        self.w = []
        self.r = {}
        self.excl = False


class Sched:
    EPOCH = 20000

    def __init__(self, nc, sems):
        self.nc = nc
        self.sems = sems
        self.eng = {"pe": nc.tensor, "dve": nc.vector, "act": nc.scalar, "pool": nc.gpsimd, "sp": nc.sync}
        self.q = {e: [] for e in self.eng}
        self.cnt = {e: 0 for e in self.eng}
        self.esem = {e: [] for e in self.eng}
        self.seen = {e: {} for e in self.eng}
        self.pending = {e: [] for e in self.eng}
        self.last = {e: None for e in self.eng}
        self.dsem = {}
        self.dcnt = {}
        self.dval = {}
        for qn in ("sp", "pool"):
            self.dsem[qn] = [self._newsem() for _ in range(6)]
            self.dval[qn] = [0] * 6
            self.dcnt[qn] = 0

    def _newsem(self):
        return self.sems.pop()

    def _deps(self, eng, reads, writes, extra=(), same=True):
        evs = list(extra)
        need = {}
        for b in reads:
            evs.extend(b.w)
            if b.excl:
                evs.extend(b.r.values())
        for b in writes:
            evs.extend(b.w)
            evs.extend(b.r.values())
        if self.pending[eng]:
            evs.extend(self.pending[eng])
            self.pending[eng] = []
        for (src, sem, val) in evs:
            if src == eng:
                if eng == "pe" or not same or sem is not (self.esem[eng][-1] if self.esem[eng] else None):
                    continue
                if self.cnt[eng] % self.EPOCH - (val - 1) > 5:
                    continue
                k = id(sem)
                if k not in need or need[k][1] < val:
                    need[k] = (sem, val)
                continue
            k = id(sem)
            if self.seen[eng].get(k, 0) >= val:
                continue
            if k not in need or need[k][1] < val:
                need[k] = (sem, val)
        for k, (sem, val) in need.items():
            self.seen[eng][k] = val
        return list(need.values())

    mute = False

    def op(self, eng, f, *args, reads=(), writes=(), **kw):
        if self.mute:
            return
        fn = (lambda: f(*args, **kw))
        waits = self._deps(eng, reads, writes)
        idx = self.cnt[eng]
        self.cnt[eng] += 1
        ep, v = idx // self.EPOCH, idx % self.EPOCH + 1
        while len(self.esem[eng]) <= ep:
            self.esem[eng].append(self._newsem())
        sem = self.esem[eng][ep]
        self.q[eng].append((waits, fn, sem, 1))
        ev = (eng, sem, v)
        self.last[eng] = ev
        for b in reads:
            b.r[eng] = ev
        for b in writes:
            b.w = [ev]
            b.r = {}

    def dma(self, qn, f, *args, reads=(), writes=(), **kw):
        if self.mute:
            return None
        fn = (lambda: f(*args, **kw))
        i = self.dcnt[qn] % len(self.dsem[qn])
        self.dcnt[qn] += 1
        sem = self.dsem[qn][i]
        prev = ("dma" + qn, sem, self.dval[qn][i])
        waits = self._deps(qn, reads, writes, extra=[prev] if prev[2] > 0 else [])
        self.dval[qn][i] += 16
        self.q[qn].append((waits, fn, sem, 16))
        ev = ("dma" + qn, sem, self.dval[qn][i])
        for b in reads:
            b.r["dma" + qn + str(i)] = ev
        for b in writes:
            b.w = [e for e in b.w if e[0].startswith("dma")][-40:] + [ev]
            b.r = {}
        return ev

    def barrier(self):
        evs = [ev for ev in self.last.values() if ev is not None]
        for qn in self.dsem:
            for i, sem in enumerate(self.dsem[qn]):
                if self.dval[qn][i] > 0:
                    evs.append(("dma" + qn, sem, self.dval[qn][i]))
        for e in self.eng:
            self.pending[e] = list(evs)

    def final_wait(self, eng, bufs):
        waits = self._deps(eng, bufs, bufs)
        self.q[eng].append((waits, None, None, 0))

    def replay(self, eng, h):
        for waits, fn, sem, inc in self.q[eng]:
            for (s, v) in waits:
                h.wait_ge(s, v)
            if fn is not None:
                fn().then_inc(sem, inc)


def build_nc(groups=None, debug_stop=None, lim=None):
    lim = lim or {}
    G = dict(GROUPS)
    if groups:
        G.update(groups)
    nc = bass.Bass("TRN2", target_bir_lowering=False)
    dt_ = nc.dram_tensor
    xT = dt_("xT", [NSEQ, D, S], F32, kind="ExternalInput").ap()
    par_d = dt_("par", [128, NPAR], F32, kind="ExternalInput").ap()
    pos_d = dt_("pos", [NSEQ, 128, S], I32, kind="ExternalInput").ap()
    wmod_d = dt_("wmod", [DEPTH, 128, 8, 6144], F32, kind="ExternalInput").ap()
    win_d = dt_("win", [DEPTH, 128, 8, 2472], F32, kind="ExternalInput").ap()
    wout_d = dt_("wout", [DEPTH, 128, 8, 1024], F32, kind="ExternalInput").ap()
    woutm_d = dt_("woutm", [DEPTH, 64, 4, 1024], F32, kind="ExternalInput").ap()
    w1_d = dt_("w1", [DEPTH, 128, 8, 4096], F32, kind="ExternalInput").ap()
    w2_d = dt_("w2", [DEPTH, 128, 32, 1024], F32, kind="ExternalInput").ap()
    rgw_d = dt_("rgw", [DEPTH, 128, 8, 128], F32, kind="ExternalInput").ap()
    wqb_d = dt_("wqb", [DEPTH, 128, 2, 384], F32, kind="ExternalInput").ap()
    wkvb_d = dt_("wkvb", [DEPTH, 128, 512], F32, kind="ExternalInput").ap()
    yT = dt_("yT", [NSEQ, D, S], F32, kind="ExternalOutput").ap()
    rope_d = dt_("ropescr", [NSEQ, 2, 32, S], F32, kind="Internal").ap()

    from contextlib import ExitStack
    es = ExitStack()
    sb = lambda n, shp, d=F32: es.enter_context(nc.sbuf_tensor(n, shp, d))
    ps = lambda n: es.enter_context(nc.psum_tensor(n, [128, 512], F32))
    with es:
        sems = [es.enter_context(nc.semaphore(f"s{i}")) for i in range(40)]
        sc = Sched(nc, sems)
        op, dma = sc.op, sc.dma
        V, A, P, PE = nc.vector, nc.scalar, nc.gpsimd, nc.tensor

        hT = sb("hT", [128, 8, S])
        uT = sb("uT", [128, 8, S], BF16)
        par = sb("par_sb", [128, NPAR])
        cst = sb("cst", [128, 8])
        ident = sb("ident", [128, 128])
        ones_bf = sb("ones_bf", [128, 128], BF16)
        ones_f = sb("ones_f", [128, 128])
        bones_bf = sb("bones_bf", [128, 128], BF16)
        modT = sb("modT", [128, DEPTH, 48, NSEQ])
        gmix = sb("gmix", [128, 4, 8])
        cact = sb("cact", [128, 8, NSEQ], BF16)
        ctmp = sb("ctmp", [128, 16])
        wsl = [sb("wslA", [128, 12288], BF16), sb("wslB", [128, 12288], BF16)]
        arena = sb("arena", [128, 13000])
        PS = [ps(f"ps{i}") for i in range(8)]

        B = {}

        def bf(name):
            if name not in B:
                B[name] = Buf()
            return B[name]

        hB = [[bf(f"h{k}_{t}") for t in range(4)] for k in range(8)]
        uB = [[bf(f"u{k}_{t}") for t in range(4)] for k in range(8)]
        psB = [bf(f"ps{i}") for i in range(8)]
        for b_ in psB:
            b_.excl = True
        wB = [bf("wslA"), bf("wslB")]

        def col(name, j=0):
            c0 = PC[name] + j
            return par[:, c0:c0 + 1]

        op("pool", P.memset, cst[:, 0:1], EPS, writes=[bf("cst")])
        op("pool", P.memset, cst[:, 1:2], 1.0, writes=[bf("cst")])
        op("pool", P.memset, cst[:, 2:3], float(np.log(0.125)), writes=[bf("cst")])
        op("pool", P.memset, cst[:, 3:4], 0.0, writes=[bf("cst")])
        op("pool", P.memset, ones_f[:], 1.0, writes=[bf("ones_f")])
        op("pool", P.memset, ones_bf[:], 1.0, writes=[bf("ones_bf")])
        op("pool", P.memset, bones_bf[:], 0.0, writes=[bf("bones_bf")])
        op("pool", P.memset, bones_bf[0:64, 0:64], 1.0, writes=[bf("bones_bf")])
        op("pool", P.memset, bones_bf[64:128, 64:128], 1.0, writes=[bf("bones_bf")])
        op("pool", P.affine_select, out=ident[:], in_=ones_f[:], pattern=[[-1, 128]], compare_op=ALU.is_equal,
           fill=0.0, base=0, channel_multiplier=1, reads=[bf("ones_f")], writes=[bf("ident")])
        dma("sp", nc.sync.dma_start, out=par[:], in_=par_d, writes=[bf("par")])

        def col(name, j=0):
            c0 = PC[name] + j
            return par[:, c0:c0 + 1]

        def act(out, in_, func, reads=(), writes=(), **kw):
            op("act", A.activation, out=out, in_=in_, func=func, reads=reads, writes=writes, **kw)

        def tt(eng, out, in0, in1, o, reads=(), writes=()):
            op(eng, (V if eng == "dve" else P).tensor_tensor, out=out, in0=in0, in1=in1, op=o, reads=reads, writes=writes)

        def ts(eng, out, in0, s1, o0, s2=None, o1=None, reads=(), writes=()):
            kw = {} if o1 is None else {"op1": o1}
            op(eng, (V if eng == "dve" else P).tensor_scalar, out=out, in0=in0, scalar1=s1, scalar2=s2, op0=o0, reads=reads, writes=writes, **kw)

        def stt(out, in0, scalar, in1, o0, o1, reads=(), writes=()):
            op("dve", V.scalar_tensor_tensor, out=out, in0=in0, scalar=scalar, in1=in1, op0=o0, op1=o1, reads=reads, writes=writes)

        def mm(out, lhsT, rhs, start, stop, reads=(), writes=()):
            op("pe", PE.matmul, out, lhsT, rhs, start=start, stop=stop, reads=reads, writes=writes)

        def sigm(out, in_, reads, wb, scale=1.0, nbias=None):
            kw = {} if nbias is None else {"bias": nbias}
            act(out, in_, AF.Exp, reads=reads, writes=[wb], scale=-scale, **kw)
            act(out, out, AF.Ln, reads=[bf("cst")], writes=[wb], bias=cst[:, 1:2])
            act(out, out, AF.Exp, writes=[wb], scale=-1.0)

        MUL, ADD, SUB, MAX, MIN = ALU.mult, ALU.add, ALU.subtract, ALU.max, ALU.min

        cc = par[:, PC["c"]:PC["c"] + 16]
        sigm(ctmp[:], cc, [bf("par")], bf("ctmp"))
        for s_ in range(NSEQ):
            tt("dve", cact[:, :, s_], ctmp[:, s_ * 8:(s_ + 1) * 8], par[:, PC["c"] + s_ * 8:PC["c"] + (s_ + 1) * 8], MUL,
               reads=[bf("ctmp"), bf("par")], writes=[bf("cact")])
        wcount = [0]

        def next_slot():
            s = wcount[0] % 2
            wcount[0] += 1
            return s

        for l in range(DEPTH):
            for pc_ in range(12):
                slot = next_slot()
                wv = wsl[slot][:, 0:4096].rearrange("p (k n) -> p k n", k=8)
                dma("pool", P.dma_start, out=wv, in_=wmod_d[l, :, :, pc_ * 512:(pc_ + 1) * 512], writes=[wB[slot]])
                for oc in range(4):
                    pb = (pc_ * 4 + oc) % 8
                    for kc in range(8):
                        mm(PS[pb][:, 0:NSEQ], wv[:, kc, oc * 128:(oc + 1) * 128], cact[:, kc, :], kc == 0, kc == 7,
                           reads=[wB[slot], bf("cact")], writes=[psB[pb]])
                    ch = pc_ * 4 + oc
                    ts("dve", modT[:, l, ch, :], PS[pb][:, 0:NSEQ], col(f"bmod{l}", ch), ADD, reads=[psB[pb], bf("par")], writes=[bf("modT")])

        def norm_stats(t, nchunk_src, scale):
            sq = arena[:, 0:1024].bitcast(BF16).rearrange("p (b n) -> p b n", b=4)
            rs = arena[:, 1024:2048].rearrange("p (b n) -> p b n", b=2)
            tsl = slice(t * 512, (t + 1) * 512)
            pb = t % 2
            for k in range(8):
                j = (t * 8 + k) % 4
                act(sq[:, j, :], hT[:, k, tsl], AF.Square, reads=[hB[k][t]], writes=[bf(f"nsq{j}")])
                mm(PS[pb][:], ones_bf[:], sq[:, j, :], k == 0, k == 7, reads=[bf(f"nsq{j}"), bf("ones_bf")], writes=[psB[pb]])
            rb = bf(f"nrs{t % 2}")
            act(rs[:, t % 2, :], PS[pb][:], AF.Ln, reads=[psB[pb], bf("cst")], writes=[rb], scale=1.0 / D, bias=cst[:, 0:1])
            act(rs[:, t % 2, :], rs[:, t % 2, :], AF.Exp, writes=[rb], scale=-0.5)
            return rs[:, t % 2, :], rb

        def modnorm(l, s_, which):
            sc.barrier()
            gname = f"nmg{l}" if which == 0 else f"nfg{l}"
            sh0, sc0 = (0, 8) if which == 0 else (24, 32)
            stt(gmix[:, which, :], modT[:, l, sc0:sc0 + 8, s_], 1.0, par[:, PC[gname]:PC[gname] + 8], ADD, MUL,
                reads=[bf("modT"), bf("par")], writes=[bf("gmix")])
            tm = arena[:, 2048:3072].rearrange("p (b n) -> p b n", b=2)
            for t in range(4):
                tsl = slice(t * 512, (t + 1) * 512)
                rs, rb = norm_stats(t, 8, 1.0 / D)
                for k in range(8):
                    tb = bf(f"ntm{k % 2}")
                    tt("dve", tm[:, k % 2, :], hT[:, k, tsl], rs, MUL, reads=[hB[k][t], rb], writes=[tb])
                    act(uT[:, k, tsl], tm[:, k % 2, :], AF.Identity, reads=[tb, bf("gmix"), bf("modT")], writes=[uB[k][t]],
                        scale=gmix[:, which, k:k + 1], bias=modT[:, l, sh0 + k, s_:s_ + 1])

        def final_norm(s_):
            sc.barrier()
            ot = arena[:, 2048:4096].rearrange("p (b n) -> p b n", b=4)
            for t in range(4):
                tsl = slice(t * 512, (t + 1) * 512)
                rs, rb = norm_stats(t, 8, 1.0 / D)
                for k in range(8):
                    ob = bf(f"fot{k % 4}")
                    stt(ot[:, k % 4, :], hT[:, k, tsl], col("fng", k), rs, MUL, MUL, reads=[hB[k][t], rb, bf("par")], writes=[ob])
                    dma("sp", nc.sync.dma_start, out=yT[s_, k * 128:(k + 1) * 128, tsl], in_=ot[:, k % 4, :], reads=[ob], writes=[bf("yout")])

        def mlp(l, s_):
            sc.barrier()
            fbuf = arena[:, 0:2048].bitcast(BF16).rearrange("p (b n) -> p b n", b=8)
            rl = arena[:, 2048:4096].rearrange("p (b n) -> p b n", b=4)
            cnt = 0
            for fc in range(8):
                slot = next_slot()
                w1v = wsl[slot][:, 0:4096].rearrange("p (k n) -> p k n", k=8)
                w2v = wsl[slot][:, 4096:8192].rearrange("p (k n) -> p k n", k=4)
                dma("pool", P.dma_start, out=w1v, in_=w1_d[l, :, :, fc * 512:(fc + 1) * 512], writes=[wB[slot]])
                dma("pool", P.dma_start, out=w2v, in_=w2_d[l, :, fc * 4:(fc + 1) * 4, :], writes=[wB[slot]])
                for t in range(4):
                    tsl = slice(t * 512, (t + 1) * 512)
                    par_ = (fc * 4 + t) % 2
                    for fs in range(4):
                        pb = cnt % 4
                        cnt += 1
                        for kc in range(8):
                            mm(PS[pb][:], w1v[:, kc, fs * 128:(fs + 1) * 128], uT[:, kc, tsl], kc == 0, kc == 7,
                               reads=[wB[slot], uB[kc][t]], writes=[psB[pb]])
                        rb = bf(f"rl{pb}")
                        fb = bf(f"f1_{par_}_{fs}")
                        act(rl[:, pb, :], PS[pb][:], AF.Relu, reads=[psB[pb]], writes=[rb])
                        tt("pool", fbuf[:, par_ * 4 + fs, :], rl[:, pb, :], rl[:, pb, :], MUL, reads=[rb], writes=[fb])
                    for oc in range(8):
                        pb = 4 + oc % 4
                        for fs in range(4):
                            mm(PS[pb][:], w2v[:, fs, oc * 128:(oc + 1) * 128], fbuf[:, par_ * 4 + fs, :], fs == 0, fs == 3,
                               reads=[wB[slot], bf(f"f1_{par_}_{fs}")], writes=[psB[pb]])
                        stt(hT[:, oc, tsl], PS[pb][:], modT[:, l, 40 + oc, s_:s_ + 1], hT[:, oc, tsl], MUL, ADD,
                            reads=[psB[pb], bf("modT"), hB[oc][t]], writes=[hB[oc][t]])

        def accum_wout(l, s_, t0, n, rhs_fn, nk, wv_fn, rd):
            tq = t0 // 512
            for oc in range(8):
                pb = 4 + oc % 4
                for kc in range(nk):
                    mm(PS[pb][:, 0:n], wv_fn(kc, oc), rhs_fn(kc), kc == 0, kc == nk - 1, reads=rd, writes=[psB[pb]])
                stt(hT[:, oc, t0:t0 + n], PS[pb][:, 0:n], modT[:, l, 16 + oc, s_:s_ + 1], hT[:, oc, t0:t0 + n], MUL, ADD,
                    reads=[psB[pb], bf("modT"), hB[oc][tq]], writes=[hB[oc][tq]])

        def rg_group(l, s_):
            sc.barrier()
            T = 256
            slot = next_slot()
            wv = wsl[slot][:, 0:8192].rearrange("p (k n) -> p k n", k=8)
            wo = wsl[slot][:, 8192:12288].rearrange("p (k n) -> p k n", k=4)
            dma("pool", P.dma_start, out=wv, in_=win_d[l, :, :, 1032:2056], writes=[wB[slot]])
            dma("pool", P.dma_start, out=wo, in_=wout_d[l, :, 2:6, :], writes=[wB[slot]])
            rgw = arena[:, 0:1024].rearrange("p (k n) -> p k n", k=8)
            dma("sp", nc.sync.dma_start, out=rgw, in_=rgw_d[l], writes=[bf("rgw")])
            prm = arena[:, 1024:1040]
            hst = arena[:, 1040:1044]
            xin = arena[:, 1048:1048 + 4 * 259].rearrange("p (j n) -> p j n", j=4)
            o0 = 1048 + 4 * 259 + 4
            gate = arena[:, o0:o0 + 4 * T].rearrange("p (j n) -> p j n", j=4)
            o1 = o0 + 4 * T
            tmpv = arena[:, o1:o1 + 10 * T].rearrange("p (j n) -> p j n", j=10)
            o2 = o1 + 10 * T
            ob = arena[:, o2:o2 + 2 * T].bitcast(BF16).rearrange("p (j n) -> p j n", j=4)
            pb_ = bf("rgprm")
            lam = par[:, PC[f"rlam{l}"]:PC[f"rlam{l}"] + 4]
            act(prm[:, 0:4], lam, AF.Exp, reads=[bf("par")], writes=[pb_], scale=-1.0)
            act(prm[:, 0:4], prm[:, 0:4], AF.Ln, reads=[bf("cst")], writes=[pb_], bias=cst[:, 1:2])
            ts("dve", prm[:, 4:8], prm[:, 0:4], -16.0, MUL, writes=[pb_])
            ts("dve", prm[:, 0:4], prm[:, 0:4], -8.0, MUL, writes=[pb_])
            ts("dve", prm[:, 8:12], par[:, PC[f"rba{l}"]:PC[f"rba{l}"] + 4], -1.0, MUL, reads=[bf("par")], writes=[pb_])
            ts("dve", prm[:, 12:16], par[:, PC[f"rbx{l}"]:PC[f"rbx{l}"] + 4], -1.0, MUL, reads=[bf("par")], writes=[pb_])
            op("dve", V.memset, hst, 0.0, writes=[bf("rghst")])
            op("dve", V.memset, xin[:, :, 0:3], 0.0, writes=[bf("rgxin")])
            C_G = 0.7978845608028654
            xb, gb, tb = bf("rgxin"), bf("rggate"), bf("rgtmp")
            for tq in range(S // T):
                t0 = tq * T
                for j in range(4):
                    pbx, pbg = (2 * j) % 4, (2 * j + 1) % 4
                    for kc in range(8):
                        mm(PS[pbx][:, 0:T], wv[:, kc, j * 128:(j + 1) * 128], uT[:, kc, t0:t0 + T], kc == 0, kc == 7,
                           reads=[wB[slot], uB[kc][t0 // 512]], writes=[psB[pbx]])
                    for kc in range(8):
                        mm(PS[pbg][:, 0:T], wv[:, kc, 512 + j * 128:512 + (j + 1) * 128], uT[:, kc, t0:t0 + T], kc == 0, kc == 7,
                           reads=[wB[slot], uB[kc][t0 // 512]], writes=[psB[pbg]])
                    act(xin[:, j, 3:3 + T], PS[pbx][:, 0:T], AF.Copy, reads=[psB[pbx]], writes=[xb])
                    act(gate[:, j, :], PS[pbg][:, 0:T], AF.Copy, reads=[psB[pbg]], writes=[gb])
                    xc = tmpv[:, 0, :]
                    ts("dve", xc, xin[:, j, 0:T], col(f"rcw{l}", j), MUL, col(f"rcb{l}", j), ADD, reads=[xb, bf("par")], writes=[tb])
                    for tap in range(1, 4):
                        stt(xc, xin[:, j, tap:tap + T], col(f"rcw{l}", tap * 4 + j), xc, MUL, ADD, reads=[xb, bf("par")], writes=[tb])
                    op("dve", V.tensor_copy, out=xin[:, j, 0:3], in_=xin[:, j, T:T + 3], reads=[xb], writes=[xb])
                    pr, pi = 4 + (2 * j) % 4, 4 + (2 * j + 1) % 4
                    mm(PS[pr][:, 0:T], rgw[:, j, :], xc, True, True, reads=[bf("rgw"), tb], writes=[psB[pr]])
                    mm(PS[pi][:, 0:T], rgw[:, 4 + j, :], xc, True, True, reads=[bf("rgw"), tb], writes=[psB[pi]])
                    r_, i_, a_, a2_, ml_ = (tmpv[:, q, :] for q in (1, 2, 3, 4, 5))
                    sigm(r_, PS[pr][:, 0:T], [psB[pr], pb_], tb, nbias=prm[:, 8 + j:9 + j])
                    sigm(i_, PS[pi][:, 0:T], [psB[pi], pb_], tb, nbias=prm[:, 12 + j:13 + j])
                    act(a_, r_, AF.Exp, reads=[pb_], writes=[tb], scale=prm[:, j:j + 1])
                    act(a2_, r_, AF.Exp, reads=[pb_], writes=[tb], scale=prm[:, 4 + j:5 + j])
                    act(ml_, a2_, AF.Ln, reads=[bf("cst")], writes=[tb], scale=-1.0, bias=cst[:, 1:2])
                    act(ml_, ml_, AF.Exp, writes=[tb], scale=0.5)
                    tt("dve", i_, i_, xc, MUL, writes=[tb])
                    tt("dve", i_, i_, ml_, MUL, writes=[tb])
                    hs_ = tmpv[:, 6, :]
                    op("dve", V.tensor_tensor_scan, out=hs_, data0=a_, data1=i_, initial=hst[:, j:j + 1], op0=MUL, op1=ADD,
                       reads=[bf("rghst")], writes=[tb])
                    op("pool", P.tensor_copy, out=hst[:, j:j + 1], in_=hs_[:, T - 1:T], reads=[tb], writes=[bf("rghst")])
                    g2, sg = tmpv[:, 7, :], tmpv[:, 8, :]
                    act(g2, gate[:, j, :], AF.Square, reads=[gb], writes=[tb])
                    ts("pool", g2, g2, 0.044715, MUL, 1.0, ADD, writes=[tb])
                    tt("pool", g2, g2, gate[:, j, :], MUL, reads=[gb], writes=[tb])
                    sigm(sg, g2, [tb], tb, scale=2.0 * C_G)
                    tt("pool", sg, sg, gate[:, j, :], MUL, reads=[gb], writes=[tb])
                    tt("dve", ob[:, j, :], hs_, sg, MUL, reads=[tb], writes=[bf("rgob")])
                accum_wout(l, s_, t0, T, lambda kc: ob[:, kc, :], 4, lambda kc, oc: wo[:, kc, oc * 128:(oc + 1) * 128], [wB[slot], bf("rgob")])

        def rope_tables(s_):
            sc.barrier()
            pi_ = arena[:, 0:2048].bitcast(I32)
            ang = arena[:, 2048:4096]
            kf = arena[:, 4096:6144]
            ki = arena[:, 6144:8192].bitcast(I32)
            r0 = arena[:, 8192:10240]
            m_ = arena[:, 10240:12288]
            rb = bf("ropew")
            dma("sp", nc.sync.dma_start, out=pi_[0:32, :], in_=pos_d[s_, 0:32, :], writes=[rb])
            op("dve", V.tensor_copy, out=ang[0:32, :], in_=pi_[0:32, :], reads=[rb], writes=[rb])
            ts("dve", ang[0:32, :], ang[0:32, :], par[0:32, PC["invf"]:PC["invf"] + 1], MUL, reads=[bf("par")], writes=[rb])
            TWO_PI = 2.0 * np.pi
            C1 = 6.28125
            C2 = TWO_PI - C1
            for which, shift in ((0, np.pi / 2.0), (1, 0.0)):
                ts("dve", kf[0:32, :], ang[0:32, :], float(shift), ADD, 1.0 / TWO_PI, MUL, writes=[rb])
                op("dve", V.tensor_copy, out=ki[0:32, :], in_=kf[0:32, :], writes=[rb])
                op("dve", V.tensor_copy, out=kf[0:32, :], in_=ki[0:32, :], writes=[rb])
                ts("dve", r0[0:32, :], ang[0:32, :], float(shift), ADD, writes=[rb])
                stt(r0[0:32, :], kf[0:32, :], -C1, r0[0:32, :], MUL, ADD, writes=[rb])
                stt(r0[0:32, :], kf[0:32, :], -C2, r0[0:32, :], MUL, ADD, writes=[rb])
                ts("dve", m_[0:32, :], r0[0:32, :], float(np.pi), ALU.is_gt, -TWO_PI, MUL, writes=[rb])
                tt("dve", r0[0:32, :], r0[0:32, :], m_[0:32, :], ADD, writes=[rb])
                ts("dve", m_[0:32, :], r0[0:32, :], float(-np.pi), ALU.is_lt, TWO_PI, MUL, writes=[rb])
                tt("dve", r0[0:32, :], r0[0:32, :], m_[0:32, :], ADD, writes=[rb])
                ts("dve", r0[0:32, :], r0[0:32, :], 3.1415925, MIN, -3.1415925, MAX, writes=[rb])
                act(m_[0:32, :], r0[0:32, :], AF.Sin, writes=[rb])
                dma("sp", nc.sync.dma_start, out=rope_d[s_, which], in_=m_[0:32, :], reads=[rb], writes=[bf(f"roped{which}")])

        def mla_group(l, s_):
            sc.barrier()
            T = 512
            slot = next_slot()
            W = wsl[slot]
            wm = W[:, 0:3328].rearrange("p (k n) -> p k n", k=8)
            wks = W[:, 3328:3584].rearrange("p (k n) -> p k n", k=8)
            wom = W[:, 3584:7680].rearrange("p (h n) -> p h n", h=4)
            wq = W[:, 7680:8448].rearrange("p (k n) -> p k n", k=2)
            wqs = W[:, 8448:8704].rearrange("p (k h n) -> p k h n", k=2, h=4)
            wkv = W[:, 8704:9216]
            wb_ = wB[slot]
            dma("pool", P.dma_start, out=wm, in_=win_d[l, :, :, 2056:2472], writes=[wb_])
            dma("pool", P.dma_start, out=wom[0:64], in_=woutm_d[l], writes=[wb_])
            dma("pool", P.dma_start, out=wq, in_=wqb_d[l], writes=[wb_])
            dma("pool", P.dma_start, out=wkv, in_=wkvb_d[l], writes=[wb_])
            op("act", A.mul, out=wks[:, :, 0:16], in_=wm[:, :, 400:416], mul=-1.0, reads=[wb_], writes=[wb_])
            op("act", A.copy, out=wks[:, :, 16:32], in_=wm[:, :, 384:400], reads=[wb_], writes=[wb_])
            wq4 = wq.rearrange("p k (h n) -> p k h n", h=4)
            for kc in range(2):
                op("act", A.mul, out=wqs[:, kc, :, 0:16], in_=wq4[:, kc, :, 80:96], mul=-1.0, reads=[wb_], writes=[wb_])
                op("act", A.copy, out=wqs[:, kc, :, 16:32], in_=wq4[:, kc, :, 64:80], reads=[wb_], writes=[wb_])
            a0 = 0
            Kc = arena[:, 0:4096].bitcast(BF16).rearrange("p (h n) -> p h n", h=4)
            Vc = arena[:, 4096:6176].bitcast(BF16).rearrange("p (j h n) -> p j h n", j=16, h=4)
            o = 6176

            def take(n):
                nonlocal o
                r = arena[:, o:o + n]
                o += n
                return r
            qn = take(512).bitcast(BF16).rearrange("p (k n) -> p k n", k=2)
            kvn = take(256).bitcast(BF16)
            Qh = take(256).bitcast(BF16)
            ET = take(512).bitcast(BF16).rearrange("p (k n) -> p k n", k=2)
            cos_ = take(512)
            sin_ = take(512)
            rs = take(512)
            sq = take(256).bitcast(BF16)
            t1 = take(512)
            t2 = take(512)
            osb = take(512)
            rec = take(512)
            oT = take(1024).bitcast(BF16).rearrange("p (h n) -> p h n", h=4)
            assert o <= 13000
            kb, vb_, wk = bf("mlaK"), bf("mlaV"), bf("mlawk")
            op("pool", P.memset, Vc[:, :, :, 64:65], 1.0, writes=[vb_])
            SCALE = float(96 ** -0.5)
            bank = [0]

            def nb():
                bank[0] = (bank[0] + 1) % 4
                return bank[0]
            for t in range(4):
                tsl = slice(t * T, (t + 1) * T)
                ub = [uB[kc][t] for kc in range(8)]
                dma("sp", nc.sync.dma_start, out=cos_[64:96, :], in_=rope_d[s_, 0, :, tsl], reads=[bf("roped0")], writes=[bf("mlacs")])
                dma("sp", nc.sync.dma_start, out=sin_[64:96, :], in_=rope_d[s_, 1, :, tsl], reads=[bf("roped1")], writes=[bf("mlacs")])
                for c in range(2):
                    for kc in range(8):
                        mm(PS[c][:], wm[:, kc, c * 128:(c + 1) * 128], uT[:, kc, tsl], kc == 0, kc == 7, reads=[wb_] + ub, writes=[psB[c]])
                for kc in range(8):
                    mm(PS[2][:], wm[:, kc, 256:384], uT[:, kc, tsl], kc == 0, kc == 7, reads=[wb_] + ub, writes=[psB[2]])
                for kc in range(8):
                    mm(PS[3][64:96, :], wm[:, kc, 384:416], uT[:, kc, tsl], kc == 0, kc == 7, reads=[wb_] + ub, writes=[psB[3]])
                for kc in range(8):
                    mm(PS[4][64:96, :], wks[:, kc, :], uT[:, kc, tsl], kc == 0, kc == 7, reads=[wb_] + ub, writes=[psB[4]])
                for c in range(2):
                    act(sq, PS[c][:], AF.Square, reads=[psB[c]], writes=[wk])
                    mm(PS[5][:], ones_bf[:], sq, c == 0, c == 1, reads=[wk, bf("ones_bf")], writes=[psB[5]])
                act(rs, PS[5][:], AF.Ln, reads=[psB[5], bf("cst")], writes=[wk], scale=1.0 / 256, bias=cst[:, 0:1])
                act(rs, rs, AF.Exp, writes=[wk], scale=-0.5)
                for c in range(2):
                    stt(qn[:, c, :], PS[c][:], col(f"mqg{l}", c), rs, MUL, MUL, reads=[psB[c], bf("par"), wk], writes=[wk])
                act(sq, PS[2][:], AF.Square, reads=[psB[2]], writes=[wk])
                mm(PS[5][:], ones_bf[:], sq, True, True, reads=[wk, bf("ones_bf")], writes=[psB[5]])
                act(rs, PS[5][:], AF.Ln, reads=[psB[5], bf("cst")], writes=[wk], scale=1.0 / 128, bias=cst[:, 0:1])
                act(rs, rs, AF.Exp, writes=[wk], scale=-0.5)
                stt(kvn, PS[2][:], col(f"mkvg{l}", 0), rs, MUL, MUL, reads=[psB[2], bf("par"), wk], writes=[wk])
                tt("dve", t1[64:96, :], PS[3][64:96, :], cos_[64:96, :], MUL, reads=[psB[3], bf("mlacs")], writes=[wk])
                tt("dve", t2[64:96, :], PS[4][64:96, :], sin_[64:96, :], MUL, reads=[psB[4], bf("mlacs")], writes=[wk])
                for h in range(4):
                    tt("dve", Kc[64:96, h, tsl], t1[64:96, :], t2[64:96, :], ADD, reads=[wk], writes=[kb])
                for h in range(4):
                    pb = 6 + h % 2
                    mm(PS[pb][0:64, :], wkv[:, h * 128:h * 128 + 64], kvn, True, True, reads=[wb_, wk], writes=[psB[pb]])
                    act(Kc[0:64, h, tsl], PS[pb][0:64, :], AF.Copy, reads=[psB[pb]], writes=[kb])
                for kt in range(4):
                    pb = 6 + kt % 2
                    for h in range(4):
                        mm(PS[pb][:, h * 64:(h + 1) * 64], kvn[:, kt * 128:(kt + 1) * 128], wkv[:, h * 128 + 64:h * 128 + 128], True, True,
                           reads=[wb_, wk], writes=[psB[pb]])
                    op("dve", V.tensor_copy, out=Vc[:, t * 4 + kt, :, 0:64], in_=PS[pb][:, 0:256].rearrange("p (h n) -> p h n", h=4),
                       reads=[psB[pb]], writes=[vb_])
                for h in range(4):
                    pa, pbq = 0, 1
                    for c in range(2):
                        mm(PS[pa][0:96, :], wq[:, c, h * 96:(h + 1) * 96], qn[:, c, :], c == 0, c == 1, reads=[wb_, wk], writes=[psB[pa]])
                    for c in range(2):
                        mm(PS[pbq][64:96, :], wqs[:, c, h, :], qn[:, c, :], c == 0, c == 1, reads=[wb_, wk], writes=[psB[pbq]])
                    qb = bf("mlaQ")
                    act(Qh[0:64, :], PS[pa][0:64, :], AF.Copy, reads=[psB[pa]], writes=[qb])
                    tt("dve", t1[64:96, :], PS[pa][64:96, :], cos_[64:96, :], MUL, reads=[psB[pa], bf("mlacs")], writes=[wk])
                    tt("dve", t2[64:96, :], PS[pbq][64:96, :], sin_[64:96, :], MUL, reads=[psB[pbq], bf("mlacs")], writes=[wk])
                    tt("dve", Qh[64:96, :], t1[64:96, :], t2[64:96, :], ADD, reads=[wk], writes=[qb])
                    nkt = 4 * t + 4
                    pacc = 5
                    for j in range(nkt):
                        r = j - 4 * t
                        qs = 0 if r < 0 else r * 128
                        n = T - qs
                        pst = 2 + j % 2
                        e = j % 2
                        eb = bf(f"mlaE{e}")
                        mm(PS[pst][:, 0:n], Kc[0:96, h, j * 128:(j + 1) * 128], Qh[0:96, qs:T], True, True, reads=[kb, qb], writes=[psB[pst]])
                        act(ET[:, e, 0:n], PS[pst][:, 0:n], AF.Exp, reads=[psB[pst]], writes=[eb], scale=SCALE)
                        if r >= 0:
                            op("pool", P.memset, ET[64:128, e, 0:64], 0.0, writes=[eb])
                        mm(PS[pacc][0:65, qs:T], Vc[:, j, h, :], ET[:, e, 0:n], j == 0, j == nkt - 1, reads=[vb_, eb], writes=[psB[pacc]])
                    act(osb[0:65, :], PS[pacc][0:65, :], AF.Copy, reads=[psB[pacc]], writes=[wk])
                    op("dve", V.reciprocal, out=rec[64:65, :], in_=osb[64:65, :], reads=[wk], writes=[wk])
                    mm(PS[4][0:64, :], ones_f[64:65, 0:64], rec[64:65, :], True, True, reads=[wk, bf("ones_f")], writes=[psB[4]])
                    tt("dve", oT[0:64, h, :], osb[0:64, :], PS[4][0:64, :], MUL, reads=[wk, psB[4]], writes=[bf("mlaoT")])
                accum_wout(l, s_, t * T, T, lambda kc: oT[0:64, kc, :], 4, lambda kc, oc: wom[0:64, kc, oc * 128:(oc + 1) * 128], [wb_, bf("mlaoT")])

        def gdn_group(l, s_):
            sc.barrier()
            T = 256
            slot = next_slot()
            wg = wsl[slot][:, 0:8256].rearrange("p (k n) -> p k n", k=8)
            wo = wsl[slot][:, 8256:10304].rearrange("p (k n) -> p k n", k=2)
            wb_ = wB[slot]
            dma("pool", P.dma_start, out=wg, in_=win_d[l, :, :, 0:1032], writes=[wb_])
            dma("pool", P.dma_start, out=wo, in_=wout_d[l, :, 0:2, :], writes=[wb_])
            o = [0]

            def take(n):
                r = arena[:, o[0]:o[0] + n]
                o[0] += n
                return r
            Ubd, BD1, Mls, MuiT = take(128), take(128), take(128), take(128)
            xin = take(6 * 259).rearrange("p (j n) -> p j n", j=6)
            yq = take(6 * T).rearrange("p (j n) -> p j n", j=6)
            zs = take(2 * T).rearrange("p (j n) -> p j n", j=2)
            tmp = take(4 * T).rearrange("p (j n) -> p j n", j=4)
            Sst = take(128).rearrange("p (c n) -> p c n", c=2)
            gt_ = take(40)
            HM = [[take(128) for _ in range(9)] for _ in range(4)]
            HS = [[take(64) for _ in range(5)] for _ in range(4)]
            otok = take(256).rearrange("p (h n) -> p h n", h=4)
            on_ = take(256)
            oT = take(T).bitcast(BF16).rearrange("p (c n) -> p c n", c=2)
            assert o[0] <= 13000, o[0]
            cb, sb_ = bf("gdnc"), [bf(f"gdnS{h}") for h in range(4)]
            gB = bf("gdngate")
            op("pool", P.memset, BD1, 0.0, writes=[cb])
            op("pool", P.memset, BD1[0:64, 0:64], 1.0, writes=[cb])
            op("pool", P.memset, BD1[64:128, 64:128], 1.0, writes=[cb])
            op("pool", P.affine_select, out=Ubd, in_=BD1, pattern=[[1, 128]], compare_op=ALU.is_ge, fill=0.0, base=0, channel_multiplier=-1,
               reads=[cb], writes=[cb])
            op("pool", P.tensor_copy, out=MuiT, in_=Ubd, reads=[cb], writes=[cb])
            op("pool", P.affine_select, out=Mls, in_=BD1, pattern=[[-1, 128]], compare_op=ALU.is_gt, fill=0.0, base=0, channel_multiplier=1,
               reads=[cb], writes=[cb])
            op("pool", P.memset, Sst, 0.0, writes=sb_)
            xb = [bf(f"gdnx{j}") for j in range(6)]
            yb = [bf(f"gdny{j}") for j in range(6)]
            zb = [bf(f"gdnz{j}") for j in range(2)]
            tb4 = [bf(f"gdnt{j}") for j in range(4)]
            op("pool", P.memset, xin[:, :, 0:3], 0.0, writes=xb)
            nA = gt_[:, 36:40]
            act(nA, par[:, PC[f"galog{l}"]:PC[f"galog{l}"] + 4], AF.Exp, reads=[bf("par")], writes=[cb])
            ts("dve", nA, nA, -1.0, MUL, writes=[cb])
            otB, onB, oTB = bf("gdnotok"), bf("gdnon"), bf("gdnoT")

            def head_gen(h, bs):
                c2, po = h // 2, (h % 2) * 64
                hp = slice(po, po + 64)
                pa, pbk = 2 * h, 2 * h + 1
                qT, kT, vT = yq[hp, c2, bs], yq[hp, 2 + c2, bs], yq[hp, 4 + c2, bs]
                rq, rk, rv = [yb[c2]], [yb[2 + c2]], [yb[4 + c2]]
                NQ, E1, E2, eGr, A_, B_, Y_, wT, qdT = HM[h]
                ktail, kbg, vbt, u_, vnew = HS[h]
                bN = [bf(f"gh{h}_{i}") for i in range(9)]
                bNQ, bE1, bE2, beG, bA, bB, bY, bwT, bqd = bN
                bkt, bkb, bvb, bu, bvn = [bf(f"gs{h}_{i}") for i in range(5)]
                ts("dve", NQ, ones_f[:], gt_[:, h:h + 1], MUL, -1.0, MUL, reads=[bf("ones_f"), gB], writes=[bNQ])
                mm(PS[pa][:, 0:128], NQ, Ubd, True, True, reads=[bNQ, cb], writes=[psB[pa]])
                yield
                ts("dve", E2, PS[pa][:, 0:128], gt_[:, 12 + h:13 + h], ADD, 0.0, MIN, reads=[psB[pa], gB], writes=[bE2])
                act(E2, E2, AF.Exp, writes=[bE2])
                tt("pool", E2, E2, Mls, MUL, reads=[cb], writes=[bE2])
                ts("dve", E1, PS[pa][:, 0:128], gt_[:, 12 + h:13 + h], ADD, 0.0, MAX, reads=[psB[pa], gB], writes=[bE1])
                act(E1, E1, AF.Exp, writes=[bE1], scale=-1.0)
                tt("pool", E1, E1, MuiT, MUL, reads=[cb], writes=[bE1])
                act(eGr, PS[pa][:, 0:128], AF.Exp, reads=[psB[pa]], writes=[beG], scale=-1.0)
                mm(PS[pbk][:, 0:128], kT, kT, True, True, reads=rk, writes=[psB[pbk]])
                yield
                stt(A_, PS[pbk][:, 0:128], gt_[:, 8 + h:9 + h], E2, MUL, MUL, reads=[psB[pbk], gB, bE2], writes=[bA])
                mm(PS[pa][:, 0:128], kT, qT, True, True, reads=rk + rq, writes=[psB[pa]])
                op("pe", PE.transpose, PS[pbk][:, 0:128], A_, ident[:], reads=[bA, bf("ident")], writes=[psB[pbk]])
                yield
                tt("dve", NQ, PS[pa][:, 0:128], E1, MUL, reads=[psB[pa], bE1], writes=[bNQ])
                op("act", A.copy, out=B_, in_=PS[pbk][:, 0:128], reads=[psB[pbk]], writes=[bB])
                tt("dve", Y_, B_, ident[:], ADD, reads=[bB, bf("ident")], writes=[bY])
                yield
                Am, Bm, An, Bn = A_, B_, E2, E1
                bAm, bBm, bAn, bBn = bA, bB, bE2, bE1
                for lvl in range(5):
                    mm(PS[pa][:, 0:128], Bm, Am, True, True, reads=[bAm, bBm], writes=[psB[pa]])
                    if lvl < 4:
                        mm(PS[pbk][:, 0:128], Am, Bm, True, True, reads=[bAm, bBm], writes=[psB[pbk]])
                    yield
                    op("act", A.copy, out=An, in_=PS[pa][:, 0:128], reads=[psB[pa]], writes=[bAn])
                    if lvl < 4:
                        op("dve", V.tensor_copy, out=Bn, in_=PS[pbk][:, 0:128], reads=[psB[pbk]], writes=[bBn])
                    mm(PS[pa][:, 0:128], An, Y_, True, True, reads=[bAn, bY], writes=[psB[pa]])
                    yield
                    tt("dve", Y_, Y_, PS[pa][:, 0:128], ADD, reads=[psB[pa]], writes=[bY])
                    Am, Bm, An, Bn = An, Bn, Am, Bm
                    bAm, bBm, bAn, bBn = bAn, bBn, bAm, bBm
                idh = ident[hp, po:po + 64]
                op("pe", PE.transpose, PS[pbk][:, 0:64], kT, idh, reads=rk + [bf("ident")], writes=[psB[pbk]])
                op("pe", PE.transpose, PS[pa][:, 0:64], vT, idh, reads=rv + [bf("ident")], writes=[psB[pa]])
                yield
                ts("dve", ktail, PS[pbk][:, 0:64], gt_[:, 20 + h:21 + h], MUL, reads=[psB[pbk], gB], writes=[bkt])
                ts("dve", kbg, PS[pbk][:, 0:64], gt_[:, 24 + h:25 + h], MUL, reads=[psB[pbk], gB], writes=[bkb])
                ts("dve", vbt, PS[pa][:, 0:64], gt_[:, 4 + h:5 + h], MUL, reads=[psB[pa], gB], writes=[bvb])
                mm(PS[pbk][:, 0:64], Y_, vbt, True, True, reads=[bY, bvb], writes=[psB[pbk]])
                mm(PS[pa][hp, 0:128], kbg, Y_, True, True, reads=[bkb, bY], writes=[psB[pa]])
                tt("dve", qdT[hp, :], qT, eGr[hp, :], MUL, reads=rq + [beG], writes=[bqd])
                yield
                op("act", A.copy, out=u_, in_=PS[pbk][:, 0:64], reads=[psB[pbk]], writes=[bu])
                op("act", A.copy, out=wT[hp, :], in_=PS[pa][hp, 0:128], reads=[psB[pa]], writes=[bwT])
                Sh = Sst[hp, c2, :]
                for c in range(2):
                    tc = slice(c * 64, c * 64 + 64)
                    mm(PS[pa][tc, 0:64], wT[hp, tc], Sh, True, True, reads=[bwT, sb_[h]], writes=[psB[pa]])
                    mm(PS[pbk][tc, 0:64], qdT[hp, tc], Sh, True, True, reads=[bqd, sb_[h]], writes=[psB[pbk]])
                    yield
                    tt("dve", vnew[tc, :], u_[tc, :], PS[pa][tc, 0:64], SUB, reads=[psB[pa], bu], writes=[bvn])
                    op("act", A.copy, out=otok[tc, h, :], in_=PS[pbk][tc, 0:64], reads=[psB[pbk]], writes=[otB])
                    mm(PS[pa][tc, 0:64], NQ[tc, tc], vnew[tc, :], True, True, reads=[bNQ, bvn], writes=[psB[pa]])
                    mm(PS[pbk][hp, 0:64], ktail[tc, :], vnew[tc, :], True, True, reads=[bkt, bvn], writes=[psB[pbk]])
                    yield
                    tt("dve", otok[tc, h, :], otok[tc, h, :], PS[pa][tc, 0:64], ADD, reads=[psB[pa]], writes=[otB])
                    stt(Sh, Sh, eGr[hp, c * 64 + 63:c * 64 + 64], PS[pbk][hp, 0:64], MUL, ADD, reads=[psB[pbk], beG], writes=[sb_[h]])
                    yield

            for tq in range(lim.get('gdn_tiles', S // T)):
                t0 = tq * T
                ub = [uB[kc][t0 // 512] for kc in range(8)]
                for j in range(8):
                    pb = j % 4
                    for kc in range(8):
                        mm(PS[pb][:, 0:T], wg[:, kc, j * 128:(j + 1) * 128], uT[:, kc, t0:t0 + T], kc == 0, kc == 7, reads=[wb_] + ub, writes=[psB[pb]])
                    tj = tmp[:, j % 2, :]
                    tjb = tb4[j % 2]
                    if j < 6:
                        act(xin[:, j, 3:3 + T], PS[pb][:, 0:T], AF.Copy, reads=[psB[pb]], writes=[xb[j]])
                        y = yq[:, j, :]
                        ts("dve", y, xin[:, j, 0:T], col(f"gcw{l}", j), MUL, reads=[xb[j], bf("par")], writes=[yb[j]])
                        for tap in range(1, 4):
                            stt(y, xin[:, j, tap:tap + T], col(f"gcw{l}", tap * 6 + j), y, MUL, ADD, reads=[xb[j], bf("par")], writes=[yb[j]])
                        op("pool", P.tensor_copy, out=xin[:, j, 0:3], in_=xin[:, j, T:T + 3], reads=[xb[j]], writes=[xb[j]])
                        sigm(tj, y, [yb[j]], tjb)
                        tt("dve", y, y, tj, MUL, reads=[tjb], writes=[yb[j]])
                        if j < 4:
                            t2_ = tmp[:, 2 + j % 2, :]
                            t2b = tb4[2 + j % 2]
                            act(t2_.bitcast(BF16)[:, 0:T], y, AF.Square, reads=[yb[j]], writes=[t2b])
                            mm(PS[4 + j][:, 0:T], bones_bf[:], t2_.bitcast(BF16)[:, 0:T], True, True, reads=[t2b, bf("bones_bf")], writes=[psB[4 + j]])
                            act(tj, PS[4 + j][:, 0:T], AF.Ln, reads=[psB[4 + j], bf("cst")], writes=[tjb], bias=cst[:, 0:1])
                            if j < 2:
                                act(tj, tj, AF.Exp, reads=[bf("cst")], writes=[tjb], scale=-0.5, bias=cst[:, 2:3])
                            else:
                                act(tj, tj, AF.Exp, writes=[tjb], scale=-0.5)
                            tt("dve", y, y, tj, MUL, reads=[tjb], writes=[yb[j]])
                    else:
                        zc = zs[:, j - 6, :]
                        act(zc, PS[pb][:, 0:T], AF.Copy, reads=[psB[pb]], writes=[zb[j - 6]])
                        sigm(tj, zc, [zb[j - 6]], tjb)
                        tt("dve", zc, zc, tj, MUL, reads=[tjb], writes=[zb[j - 6]])
                for blk in range(2):
                    b0 = blk * 128
                    bs = slice(b0, b0 + 128)
                    for kc in range(8):
                        mm(PS[5][:, 0:8], uT[:, kc, t0 + b0:t0 + b0 + 128], wg[:, kc, 1024:1032], kc == 0, kc == 7, reads=[wb_] + ub, writes=[psB[5]])
                    gcol, beta, nbeta, Gc, eG, eTl, bg, sp = (gt_[:, 4 * i:4 * i + 4] for i in range(8))
                    tt("dve", sp, PS[5][:, 0:4], par[:, PC[f"gdtb{l}"]:PC[f"gdtb{l}"] + 4], ADD, reads=[psB[5], bf("par")], writes=[gB])
                    act(sp, sp, AF.Exp, writes=[gB])
                    act(sp, sp, AF.Ln, reads=[bf("cst")], writes=[gB], bias=cst[:, 1:2])
                    tt("dve", gcol, sp, nA, MUL, reads=[cb], writes=[gB])
                    sigm(beta, PS[5][:, 4:8], [psB[5]], gB)
                    ts("dve", nbeta, beta, -1.0, MUL, writes=[gB])
                    mm(PS[6][:, 0:4], Ubd, gcol, True, True, reads=[cb, gB], writes=[psB[6]])
                    mm(PS[6][:, 4:8], BD1, gcol, True, True, reads=[cb, gB], writes=[psB[6]])
                    op("dve", V.tensor_copy, out=Gc, in_=PS[6][:, 0:4], reads=[psB[6]], writes=[gB])
                    act(eG, PS[6][:, 0:4], AF.Exp, reads=[psB[6]], writes=[gB])
                    tt("dve", eTl, PS[6][:, 4:8], Gc, SUB, reads=[psB[6]], writes=[gB])
                    act(eTl, eTl, AF.Exp, writes=[gB])
                    tt("dve", bg, beta, eG, MUL, writes=[gB])
                    for grp in lim.get('gdn_il', [[0, 1, 2, 3]]):
                      alive = [head_gen(h, bs) for h in grp]
                      nsteps = 0
                      while alive and nsteps < lim.get('gdn_steps', 999):
                          nsteps += 1
                          nxt = []
                          for g_ in alive:
                              try:
                                  next(g_)
                                  nxt.append(g_)
                              except StopIteration:
                                  pass
                          alive = nxt
                    ss = gt_[:, 32:36]
                    for h in range(4):
                        tt("pool", on_[:, h * 64:(h + 1) * 64], otok[:, h, :], otok[:, h, :], MUL, reads=[otB], writes=[onB])
                    op("dve", V.tensor_reduce, out=ss, in_=on_.rearrange("p (h n) -> p h n", h=4), axis=mybir.AxisListType.X, op=ADD,
                       reads=[onB], writes=[gB])
                    act(ss, ss, AF.Ln, reads=[bf("cst")], writes=[gB], scale=1.0 / 64, bias=cst[:, 0:1])
                    act(ss, ss, AF.Exp, writes=[gB], scale=-0.5)
                    for h in range(4):
                        ts("dve", on_[:, h * 64:(h + 1) * 64], otok[:, h, :], gt_[:, 32 + h:33 + h], MUL, reads=[otB, gB], writes=[onB])
                    for c2 in range(2):
                        op("pe", PE.transpose, PS[2 + c2][:, 0:128], on_[:, c2 * 128:(c2 + 1) * 128], ident[:], reads=[onB, bf("ident")], writes=[psB[2 + c2]])
                        stt(oT[:, c2, bs], PS[2 + c2][:, 0:128], col(f"gng{l}"), zs[:, c2, bs], MUL, MUL, reads=[psB[2 + c2], bf("par"), zb[c2]], writes=[oTB])
                accum_wout(l, s_, t0, T, lambda kc: oT[:, kc, :], 2, lambda kc, oc: wo[:, kc, oc * 128:(oc + 1) * 128], [wb_, oTB])

        for s_ in range(lim.get('nseq', NSEQ)):
            for k in range(8):
                for t in range(4):
                    dma("sp", nc.sync.dma_start, out=hT[:, k, t * 512:(t + 1) * 512], in_=xT[s_, k * 128:(k + 1) * 128, t * 512:(t + 1) * 512],
                        writes=[hB[k][t]])
            if G["mla"]:
                rope_tables(s_)
            for l in range(lim.get('depth', DEPTH)):
                modnorm(l, s_, 0)
                if G["gdn"]:
                    gdn_group(l, s_)
                if G["rg"]:
                    rg_group(l, s_)
                if G["mla"]:
                    mla_group(l, s_)
                if G["mlp"]:
                    modnorm(l, s_, 1)
                    mlp(l, s_)
            final_norm(s_)
        if debug_stop == "mod":
            dma("sp", nc.sync.dma_start, out=yT[1, 0:128, 0:DEPTH * 48 * NSEQ], in_=modT[:].rearrange("p l c s -> p (l c s)"),
                reads=[bf("modT")], writes=[bf("yout")])
        sc.final_wait("sp", [bf("yout")])
        build_nc.last_counts = dict(sc.cnt)

        with nc.Block() as block:
            @block.tensor
            def _(e):
                sc.replay("pe", e)

            @block.vector
            def _(e):
                sc.replay("dve", e)

            @block.scalar
            def _(e):
                sc.replay("act", e)

            @block.gpsimd
            def _(e):
                sc.replay("pool", e)

            @block.sync
            def _(e):
                sc.replay("sp", e)
    return nc


def _pack_params(inp, core):
    par = np.zeros((128, NPAR), np.float32)
    f = lambda v: np.asarray(v, np.float32)
    cm = lambda v: f(v).reshape(-1, 128).T
    for s_ in range(NSEQ):
        par[:, PC["c"] + 8 * s_:PC["c"] + 8 * (s_ + 1)] = cm(inp["c"][core * NSEQ + s_])
    par[:, PC["fng"]:PC["fng"] + 8] = cm(inp["final_norm_g"])
    invf = (np.float32(10000.0) ** (-np.arange(0, 32, 2, dtype=np.float32) / np.float32(32))).astype(np.float32)
    par[:, PC["invf"]] = np.tile(invf, 8)
    for l in range(DEPTH):
        par[:, PC[f"bmod{l}"]:PC[f"bmod{l}"] + 48] = cm(inp["b_mod"][l])
        par[:, PC[f"nmg{l}"]:PC[f"nmg{l}"] + 8] = cm(inp["norm_mix_g"][l])
        par[:, PC[f"nfg{l}"]:PC[f"nfg{l}"] + 8] = cm(inp["norm_mlp_g"][l])
        gcw = f(inp["gdn_conv_w"][l])
        for tap in range(4):
            par[:, PC[f"gcw{l}"] + tap * 6:PC[f"gcw{l}"] + tap * 6 + 6] = cm(gcw[tap])
        par[:, PC[f"galog{l}"]:PC[f"galog{l}"] + 4] = f(inp["gdn_a_log"][l])[None, :]
        par[:, PC[f"gdtb{l}"]:PC[f"gdtb{l}"] + 4] = f(inp["gdn_dt_bias"][l])[None, :]
        par[:, PC[f"gng{l}"]] = np.tile(f(inp["gdn_norm_g"][l]), 2)
        rcw = f(inp["rg_conv_w"][l])
        for tap in range(4):
            par[:, PC[f"rcw{l}"] + tap * 4:PC[f"rcw{l}"] + tap * 4 + 4] = cm(rcw[tap])
        par[:, PC[f"rcb{l}"]:PC[f"rcb{l}"] + 4] = cm(inp["rg_conv_b"][l])
        par[:, PC[f"rba{l}"]:PC[f"rba{l}"] + 4] = cm(inp["rg_b_a"][l])
        par[:, PC[f"rbx{l}"]:PC[f"rbx{l}"] + 4] = cm(inp["rg_b_x"][l])
        par[:, PC[f"rlam{l}"]:PC[f"rlam{l}"] + 4] = cm(inp["rg_lambda"][l])
        par[:, PC[f"mqg{l}"]:PC[f"mqg{l}"] + 2] = cm(inp["mla_q_norm_g"][l])
        par[:, PC[f"mkvg{l}"]:PC[f"mkvg{l}"] + 1] = cm(inp["mla_kv_norm_g"][l])
    return par


def _shared_weights(inp):
    f = lambda v: np.ascontiguousarray(np.asarray(v, np.float32))
    kp = lambda w, k: f(w.reshape(w.shape[0], k, 128, w.shape[2]).transpose(0, 2, 1, 3))
    sh = {}
    sh["wmod"] = kp(inp["w_mod"], 8)
    sh["win"] = kp(inp["w_in"], 8)
    sh["wout"] = kp(inp["w_out"], 8)
    wo = np.asarray(inp["w_out"], np.float32)
    sh["woutm"] = f(wo[:, 768:1024, :].reshape(DEPTH, 4, 64, 1024).transpose(0, 2, 1, 3))
    sh["w1"] = kp(inp["w_mlp_in"], 8)
    sh["w2"] = kp(inp["w_mlp_out"], 32)
    rgw = np.zeros((DEPTH, 128, 8, 128), np.float32)
    for l in range(DEPTH):
        for wi_, nm in enumerate(("rg_w_a", "rg_w_x")):
            w = np.asarray(inp[nm][l], np.float32)
            for j in range(4):
                rgw[l, 0:64, wi_ * 4 + j, 0:64] = w[2 * j]
                rgw[l, 64:128, wi_ * 4 + j, 64:128] = w[2 * j + 1]
    sh["rgw"] = rgw
    sh["wqb"] = kp(inp["mla_w_qb"], 2)
    sh["wkvb"] = f(inp["mla_w_kvb"])
    return sh


def make_in_maps(inp, cores):
    sh = _shared_weights(inp)
    maps = []
    for c in cores:
        m = dict(sh)
        xs = np.asarray(inp["x"][c * NSEQ:(c + 1) * NSEQ], np.float32)
        m["xT"] = np.ascontiguousarray(xs.transpose(0, 2, 1))
        m["par"] = _pack_params(inp, c)
        pos = np.asarray(inp["positions"][c * NSEQ:(c + 1) * NSEQ], np.int32)
        m["pos"] = np.ascontiguousarray(np.broadcast_to(pos[:, None, :], (NSEQ, 128, S)))
        maps.append(m)
    return maps


def kernel(**inputs):
    nc = build_nc()
    cores = list(range(8))
    in_maps = make_in_maps(inputs, cores)
    res = run_bass_kernel_spmd(nc, in_maps, core_ids=cores)
    out = np.empty((16, S, D), np.float32)
    for c in cores:
        y = res.results[c]["yT"]
        for s_ in range(NSEQ):
            out[c * NSEQ + s_] = y[s_].T
    return out
```

```python
import numpy as np
import concourse.bass as bass
import concourse.mybir as mybir
from concourse.bass_utils import run_bass_kernel_spmd

F32 = mybir.dt.float32
BF16 = mybir.dt.bfloat16
I32 = mybir.dt.int32
AF = mybir.ActivationFunctionType
ALU = mybir.AluOpType

S = 2048
D = 1024
NSEQ = 2
DEPTH = 2
EPS = 1e-6
GROUPS = {"gdn": True, "rg": True, "mla": True, "mlp": True}

PC = {}
_off = 0


def _pc(name, n):
    global _off
    PC[name] = _off
    _off += n


_pc("c", 16)
_pc("fng", 8)
_pc("invf", 1)
for _l in range(DEPTH):
    _pc(f"bmod{_l}", 48)
    _pc(f"nmg{_l}", 8)
    _pc(f"nfg{_l}", 8)
    _pc(f"gcw{_l}", 24)
    _pc(f"galog{_l}", 4)
    _pc(f"gdtb{_l}", 4)
    _pc(f"gng{_l}", 1)
    _pc(f"rcw{_l}", 16)
    _pc(f"rcb{_l}", 4)
    _pc(f"rba{_l}", 4)
    _pc(f"rbx{_l}", 4)
    _pc(f"rlam{_l}", 4)
    _pc(f"mqg{_l}", 2)
    _pc(f"mkvg{_l}", 1)
NPAR = _off


class Buf:
    __slots__ = ("w", "r", "excl")

    def __init__(self):
        self.w = []
        self.r = {}
        self.excl = False


class Sched:
    EPOCH = 20000

    def __init__(self, nc, sems):
        self.nc = nc
        self.sems = sems
        self.eng = {"pe": nc.tensor, "dve": nc.vector, "act": nc.scalar, "pool": nc.gpsimd, "sp": nc.sync}
        self.q = {e: [] for e in self.eng}
        self.cnt = {e: 0 for e in self.eng}
        self.esem = {e: [] for e in self.eng}
        self.seen = {e: {} for e in self.eng}
        self.pending = {e: [] for e in self.eng}
        self.last = {e: None for e in self.eng}
        self.dsem = {}
        self.dcnt = {}
        self.dval = {}
        for qn in ("sp", "pool"):
            self.dsem[qn] = [self._newsem() for _ in range(6)]
            self.dval[qn] = [0] * 6
            self.dcnt[qn] = 0

    def _newsem(self):
        return self.sems.pop()

    def _deps(self, eng, reads, writes, extra=(), same=True):
        evs = list(extra)
        need = {}
        for b in reads:
            evs.extend(b.w)
            if b.excl:
                evs.extend(b.r.values())
        for b in writes:
            evs.extend(b.w)
            evs.extend(b.r.values())
        if self.pending[eng]:
            evs.extend(self.pending[eng])
            self.pending[eng] = []
        for (src, sem, val) in evs:
            if src == eng:
                if eng == "pe" or not same or sem is not (self.esem[eng][-1] if self.esem[eng] else None):
                    continue
                if self.cnt[eng] % self.EPOCH - (val - 1) > 5:
                    continue
                k = id(sem)
                if k not in need or need[k][1] < val:
                    need[k] = (sem, val)
                continue
            k = id(sem)
            if self.seen[eng].get(k, 0) >= val:
                continue
            if k not in need or need[k][1] < val:
                need[k] = (sem, val)
        for k, (sem, val) in need.items():
            self.seen[eng][k] = val
        return list(need.values())

    mute = False

    def op(self, eng, f, *args, reads=(), writes=(), **kw):
        if self.mute:
            return
        fn = (lambda: f(*args, **kw))
        waits = self._deps(eng, reads, writes)
        idx = self.cnt[eng]
        self.cnt[eng] += 1
        ep, v = idx // self.EPOCH, idx % self.EPOCH + 1
        while len(self.esem[eng]) <= ep:
            self.esem[eng].append(self._newsem())
        sem = self.esem[eng][ep]
        self.q[eng].append((waits, fn, sem, 1))
        ev = (eng, sem, v)
        self.last[eng] = ev
        for b in reads:
            b.r[eng] = ev
        for b in writes:
            b.w = [ev]
            b.r = {}

    def dma(self, qn, f, *args, reads=(), writes=(), **kw):
        if self.mute:
            return None
        fn = (lambda: f(*args, **kw))
        i = self.dcnt[qn] % len(self.dsem[qn])
        self.dcnt[qn] += 1
        sem = self.dsem[qn][i]
        prev = ("dma" + qn, sem, self.dval[qn][i])
        waits = self._deps(qn, reads, writes, extra=[prev] if prev[2] > 0 else [])
        self.dval[qn][i] += 16
        self.q[qn].append((waits, fn, sem, 16))
        ev = ("dma" + qn, sem, self.dval[qn][i])
        for b in reads:
            b.r["dma" + qn + str(i)] = ev
        for b in writes:
            b.w = [e for e in b.w if e[0].startswith("dma")][-40:] + [ev]
            b.r = {}
        return ev

    def barrier(self):
        evs = [ev for ev in self.last.values() if ev is not None]
        for qn in self.dsem:
            for i, sem in enumerate(self.dsem[qn]):
                if self.dval[qn][i] > 0:
                    evs.append(("dma" + qn, sem, self.dval[qn][i]))
        for e in self.eng:
            self.pending[e] = list(evs)

    def final_wait(self, eng, bufs):
        waits = self._deps(eng, bufs, bufs)
        self.q[eng].append((waits, None, None, 0))

    def replay(self, eng, h):
        for waits, fn, sem, inc in self.q[eng]:
            for (s, v) in waits:
                h.wait_ge(s, v)
            if fn is not None:
                fn().then_inc(sem, inc)


def build_nc(groups=None, debug_stop=None, lim=None):
    lim = lim or {}
    G = dict(GROUPS)
    if groups:
        G.update(groups)
    nc = bass.Bass("TRN2", target_bir_lowering=False)
    dt_ = nc.dram_tensor
    xT = dt_("xT", [NSEQ, D, S], F32, kind="ExternalInput").ap()
    par_d = dt_("par", [128, NPAR], F32, kind="ExternalInput").ap()
    pos_d = dt_("pos", [NSEQ, 128, S], I32, kind="ExternalInput").ap()
    wmod_d = dt_("wmod", [DEPTH, 128, 8, 6144], F32, kind="ExternalInput").ap()
    win_d = dt_("win", [DEPTH, 128, 8, 2472], F32, kind="ExternalInput").ap()
    wout_d = dt_("wout", [DEPTH, 128, 8, 1024], F32, kind="ExternalInput").ap()
    woutm_d = dt_("woutm", [DEPTH, 64, 4, 1024], F32, kind="ExternalInput").ap()
    w1_d = dt_("w1", [DEPTH, 128, 8, 4096], F32, kind="ExternalInput").ap()
    w2_d = dt_("w2", [DEPTH, 128, 32, 1024], F32, kind="ExternalInput").ap()
    rgw_d = dt_("rgw", [DEPTH, 128, 8, 128], F32, kind="ExternalInput").ap()
    wqb_d = dt_("wqb", [DEPTH, 128, 2, 384], F32, kind="ExternalInput").ap()
    wkvb_d = dt_("wkvb", [DEPTH, 128, 512], F32, kind="ExternalInput").ap()
    yT = dt_("yT", [NSEQ, D, S], F32, kind="ExternalOutput").ap()
    rope_d = dt_("ropescr", [NSEQ, 2, 32, S], F32, kind="Internal").ap()

    from contextlib import ExitStack
    es = ExitStack()
    sb = lambda n, shp, d=F32: es.enter_context(nc.sbuf_tensor(n, shp, d))
    ps = lambda n: es.enter_context(nc.psum_tensor(n, [128, 512], F32))
    with es:
        sems = [es.enter_context(nc.semaphore(f"s{i}")) for i in range(40)]
        sc = Sched(nc, sems)
        op, dma = sc.op, sc.dma
        V, A, P, PE = nc.vector, nc.scalar, nc.gpsimd, nc.tensor

        hT = sb("hT", [128, 8, S])
        uT = sb("uT", [128, 8, S], BF16)
        par = sb("par_sb", [128, NPAR])
        cst = sb("cst", [128, 8])
        ident = sb("ident", [128, 128])
        ones_bf = sb("ones_bf", [128, 128], BF16)
        ones_f = sb("ones_f", [128, 128])
        bones_bf = sb("bones_bf", [128, 128], BF16)
        modT = sb("modT", [128, DEPTH, 48, NSEQ])
        gmix = sb("gmix", [128, 4, 8])
        cact = sb("cact", [128, 8, NSEQ], BF16)
        ctmp = sb("ctmp", [128, 16])
        wsl = [sb("wslA", [128, 12288], BF16), sb("wslB", [128, 12288], BF16)]
        arena = sb("arena", [128, 13000])
        PS = [ps(f"ps{i}") for i in range(8)]

        B = {}

        def bf(name):
            if name not in B:
                B[name] = Buf()
            return B[name]

        hB = [[bf(f"h{k}_{t}") for t in range(4)] for k in range(8)]
        uB = [[bf(f"u{k}_{t}") for t in range(4)] for k in range(8)]
        psB = [bf(f"ps{i}") for i in range(8)]
        for b_ in psB:
            b_.excl = True
        wB = [bf("wslA"), bf("wslB")]

        def col(name, j=0):
            c0 = PC[name] + j
            return par[:, c0:c0 + 1]

        op("pool", P.memset, cst[:, 0:1], EPS, writes=[bf("cst")])
        op("pool", P.memset, cst[:, 1:2], 1.0, writes=[bf("cst")])
        op("pool", P.memset, cst[:, 2:3], float(np.log(0.125)), writes=[bf("cst")])
        op("pool", P.memset, cst[:, 3:4], 0.0, writes=[bf("cst")])
        op("pool", P.memset, ones_f[:], 1.0, writes=[bf("ones_f")])
        op("pool", P.memset, ones_bf[:], 1.0, writes=[bf("ones_bf")])
        op("pool", P.memset, bones_bf[:], 0.0, writes=[bf("bones_bf")])
        op("pool", P.memset, bones_bf[0:64, 0:64], 1.0, writes=[bf("bones_bf")])
        op("pool", P.memset, bones_bf[64:128, 64:128], 1.0, writes=[bf("bones_bf")])
        op("pool", P.affine_select, out=ident[:], in_=ones_f[:], pattern=[[-1, 128]], compare_op=ALU.is_equal,
           fill=0.0, base=0, channel_multiplier=1, reads=[bf("ones_f")], writes=[bf("ident")])
        dma("sp", nc.sync.dma_start, out=par[:], in_=par_d, writes=[bf("par")])

        def col(name, j=0):
            c0 = PC[name] + j
            return par[:, c0:c0 + 1]

        def act(out, in_, func, reads=(), writes=(), **kw):
            op("act", A.activation, out=out, in_=in_, func=func, reads=reads, writes=writes, **kw)

        def tt(eng, out, in0, in1, o, reads=(), writes=()):
            op(eng, (V if eng == "dve" else P).tensor_tensor, out=out, in0=in0, in1=in1, op=o, reads=reads, writes=writes)

        def ts(eng, out, in0, s1, o0, s2=None, o1=None, reads=(), writes=()):
            kw = {} if o1 is None else {"op1": o1}
            op(eng, (V if eng == "dve" else P).tensor_scalar, out=out, in0=in0, scalar1=s1, scalar2=s2, op0=o0, reads=reads, writes=writes, **kw)

        def stt(out, in0, scalar, in1, o0, o1, reads=(), writes=()):
            op("dve", V.scalar_tensor_tensor, out=out, in0=in0, scalar=scalar, in1=in1, op0=o0, op1=o1, reads=reads, writes=writes)

        def mm(out, lhsT, rhs, start, stop, reads=(), writes=()):
            op("pe", PE.matmul, out, lhsT, rhs, start=start, stop=stop, reads=reads, writes=writes)

        def sigm(out, in_, reads, wb, scale=1.0, nbias=None):
            kw = {} if nbias is None else {"bias": nbias}
            act(out, in_, AF.Exp, reads=reads, writes=[wb], scale=-scale, **kw)
            act(out, out, AF.Ln, reads=[bf("cst")], writes=[wb], bias=cst[:, 1:2])
            act(out, out, AF.Exp, writes=[wb], scale=-1.0)

        MUL, ADD, SUB, MAX, MIN = ALU.mult, ALU.add, ALU.subtract, ALU.max, ALU.min

        cc = par[:, PC["c"]:PC["c"] + 16]
        sigm(ctmp[:], cc, [bf("par")], bf("ctmp"))
        for s_ in range(NSEQ):
            tt("dve", cact[:, :, s_], ctmp[:, s_ * 8:(s_ + 1) * 8], par[:, PC["c"] + s_ * 8:PC["c"] + (s_ + 1) * 8], MUL,
               reads=[bf("ctmp"), bf("par")], writes=[bf("cact")])
        wcount = [0]

        def next_slot():
            s = wcount[0] % 2
            wcount[0] += 1
            return s

        for l in range(DEPTH):
            for pc_ in range(12):
                slot = next_slot()
                wv = wsl[slot][:, 0:4096].rearrange("p (k n) -> p k n", k=8)
                dma("pool", P.dma_start, out=wv, in_=wmod_d[l, :, :, pc_ * 512:(pc_ + 1) * 512], writes=[wB[slot]])
                for oc in range(4):
                    pb = (pc_ * 4 + oc) % 8
                    for kc in range(8):
                        mm(PS[pb][:, 0:NSEQ], wv[:, kc, oc * 128:(oc + 1) * 128], cact[:, kc, :], kc == 0, kc == 7,
                           reads=[wB[slot], bf("cact")], writes=[psB[pb]])
                    ch = pc_ * 4 + oc
                    ts("dve", modT[:, l, ch, :], PS[pb][:, 0:NSEQ], col(f"bmod{l}", ch), ADD, reads=[psB[pb], bf("par")], writes=[bf("modT")])

        def norm_stats(t, nchunk_src, scale):
            sq = arena[:, 0:1024].bitcast(BF16).rearrange("p (b n) -> p b n", b=4)
            rs = arena[:, 1024:2048].rearrange("p (b n) -> p b n", b=2)
            tsl = slice(t * 512, (t + 1) * 512)
            pb = t % 2
            for k in range(8):
                j = (t * 8 + k) % 4
                act(sq[:, j, :], hT[:, k, tsl], AF.Square, reads=[hB[k][t]], writes=[bf(f"nsq{j}")])
                mm(PS[pb][:], ones_bf[:], sq[:, j, :], k == 0, k == 7, reads=[bf(f"nsq{j}"), bf("ones_bf")], writes=[psB[pb]])
            rb = bf(f"nrs{t % 2}")
            act(rs[:, t % 2, :], PS[pb][:], AF.Ln, reads=[psB[pb], bf("cst")], writes=[rb], scale=1.0 / D, bias=cst[:, 0:1])
            act(rs[:, t % 2, :], rs[:, t % 2, :], AF.Exp, writes=[rb], scale=-0.5)
            return rs[:, t % 2, :], rb

        def modnorm(l, s_, which):
            sc.barrier()
            gname = f"nmg{l}" if which == 0 else f"nfg{l}"
            sh0, sc0 = (0, 8) if which == 0 else (24, 32)
            stt(gmix[:, which, :], modT[:, l, sc0:sc0 + 8, s_], 1.0, par[:, PC[gname]:PC[gname] + 8], ADD, MUL,
                reads=[bf("modT"), bf("par")], writes=[bf("gmix")])
            tm = arena[:, 2048:3072].rearrange("p (b n) -> p b n", b=2)
            for t in range(4):
                tsl = slice(t * 512, (t + 1) * 512)
                rs, rb = norm_stats(t, 8, 1.0 / D)
                for k in range(8):
                    tb = bf(f"ntm{k % 2}")
                    tt("dve", tm[:, k % 2, :], hT[:, k, tsl], rs, MUL, reads=[hB[k][t], rb], writes=[tb])
                    act(uT[:, k, tsl], tm[:, k % 2, :], AF.Identity, reads=[tb, bf("gmix"), bf("modT")], writes=[uB[k][t]],
                        scale=gmix[:, which, k:k + 1], bias=modT[:, l, sh0 + k, s_:s_ + 1])

        def final_norm(s_):
            sc.barrier()
            ot = arena[:, 2048:4096].rearrange("p (b n) -> p b n", b=4)
            for t in range(4):
                tsl = slice(t * 512, (t + 1) * 512)
                rs, rb = norm_stats(t, 8, 1.0 / D)
                for k in range(8):
                    ob = bf(f"fot{k % 4}")
                    stt(ot[:, k % 4, :], hT[:, k, tsl], col("fng", k), rs, MUL, MUL, reads=[hB[k][t], rb, bf("par")], writes=[ob])
                    dma("sp", nc.sync.dma_start, out=yT[s_, k * 128:(k + 1) * 128, tsl], in_=ot[:, k % 4, :], reads=[ob], writes=[bf("yout")])

        def mlp(l, s_):
            sc.barrier()
            fbuf = arena[:, 0:2048].bitcast(BF16).rearrange("p (b n) -> p b n", b=8)
            rl = arena[:, 2048:4096].rearrange("p (b n) -> p b n", b=4)
            cnt = 0
            for fc in range(8):
                slot = next_slot()
                w1v = wsl[slot][:, 0:4096].rearrange("p (k n) -> p k n", k=8)
                w2v = wsl[slot][:, 4096:8192].rearrange("p (k n) -> p k n", k=4)
                dma("pool", P.dma_start, out=w1v, in_=w1_d[l, :, :, fc * 512:(fc + 1) * 512], writes=[wB[slot]])
                dma("pool", P.dma_start, out=w2v, in_=w2_d[l, :, fc * 4:(fc + 1) * 4, :], writes=[wB[slot]])
                for t in range(4):
                    tsl = slice(t * 512, (t + 1) * 512)
                    par_ = (fc * 4 + t) % 2
                    for fs in range(4):
                        pb = cnt % 4
                        cnt += 1
                        for kc in range(8):
                            mm(PS[pb][:], w1v[:, kc, fs * 128:(fs + 1) * 128], uT[:, kc, tsl], kc == 0, kc == 7,
                               reads=[wB[slot], uB[kc][t]], writes=[psB[pb]])
                        rb = bf(f"rl{pb}")
                        fb = bf(f"f1_{par_}_{fs}")
                        act(rl[:, pb, :], PS[pb][:], AF.Relu, reads=[psB[pb]], writes=[rb])
                        tt("pool", fbuf[:, par_ * 4 + fs, :], rl[:, pb, :], rl[:, pb, :], MUL, reads=[rb], writes=[fb])
                    for oc in range(8):
                        pb = 4 + oc % 4
                        for fs in range(4):
                            mm(PS[pb][:], w2v[:, fs, oc * 128:(oc + 1) * 128], fbuf[:, par_ * 4 + fs, :], fs == 0, fs == 3,
                               reads=[wB[slot], bf(f"f1_{par_}_{fs}")], writes=[psB[pb]])
                        stt(hT[:, oc, tsl], PS[pb][:], modT[:, l, 40 + oc, s_:s_ + 1], hT[:, oc, tsl], MUL, ADD,
                            reads=[psB[pb], bf("modT"), hB[oc][t]], writes=[hB[oc][t]])

        def accum_wout(l, s_, t0, n, rhs_fn, nk, wv_fn, rd):
            tq = t0 // 512
            for oc in range(8):
                pb = 4 + oc % 4
                for kc in range(nk):
                    mm(PS[pb][:, 0:n], wv_fn(kc, oc), rhs_fn(kc), kc == 0, kc == nk - 1, reads=rd, writes=[psB[pb]])
                stt(hT[:, oc, t0:t0 + n], PS[pb][:, 0:n], modT[:, l, 16 + oc, s_:s_ + 1], hT[:, oc, t0:t0 + n], MUL, ADD,
                    reads=[psB[pb], bf("modT"), hB[oc][tq]], writes=[hB[oc][tq]])

        def rg_group(l, s_):
            sc.barrier()
            T = 256
            slot = next_slot()
            wv = wsl[slot][:, 0:8192].rearrange("p (k n) -> p k n", k=8)
            wo = wsl[slot][:, 8192:12288].rearrange("p (k n) -> p k n", k=4)
            dma("pool", P.dma_start, out=wv, in_=win_d[l, :, :, 1032:2056], writes=[wB[slot]])
            dma("pool", P.dma_start, out=wo, in_=wout_d[l, :, 2:6, :], writes=[wB[slot]])
            rgw = arena[:, 0:1024].rearrange("p (k n) -> p k n", k=8)
            dma("sp", nc.sync.dma_start, out=rgw, in_=rgw_d[l], writes=[bf("rgw")])
            prm = arena[:, 1024:1040]
            hst = arena[:, 1040:1044]
            xin = arena[:, 1048:1048 + 4 * 259].rearrange("p (j n) -> p j n", j=4)
            o0 = 1048 + 4 * 259 + 4
            gate = arena[:, o0:o0 + 4 * T].rearrange("p (j n) -> p j n", j=4)
            o1 = o0 + 4 * T
            tmpv = arena[:, o1:o1 + 28 * T].rearrange("p (j q n) -> p j q n", j=4, q=7)
            o2 = o1 + 28 * T
            ob = arena[:, o2:o2 + 2 * T].bitcast(BF16).rearrange("p (j n) -> p j n", j=4)
            assert o2 + 2 * T <= 13000
            pb_ = bf("rgprm")
            lam = par[:, PC[f"rlam{l}"]:PC[f"rlam{l}"] + 4]
            act(prm[:, 0:4], lam, AF.Exp, reads=[bf("par")], writes=[pb_], scale=-1.0)
            act(prm[:, 0:4], prm[:, 0:4], AF.Ln, reads=[bf("cst")], writes=[pb_], bias=cst[:, 1:2])
            ts("dve", prm[:, 4:8], prm[:, 0:4], -16.0, MUL, writes=[pb_])
            ts("dve", prm[:, 0:4], prm[:, 0:4], -8.0, MUL, writes=[pb_])
            ts("dve", prm[:, 8:12], par[:, PC[f"rba{l}"]:PC[f"rba{l}"] + 4], -1.0, MUL, reads=[bf("par")], writes=[pb_])
            ts("dve", prm[:, 12:16], par[:, PC[f"rbx{l}"]:PC[f"rbx{l}"] + 4], -1.0, MUL, reads=[bf("par")], writes=[pb_])
            hB_ = [bf(f"rgh{j}") for j in range(4)]
            xB_ = [bf(f"rgx{j}") for j in range(4)]
            gB_ = [bf(f"rgg{j}") for j in range(4)]
            oB_ = [bf(f"rgo{j}") for j in range(4)]
            tB_ = [[bf(f"rgt{j}_{q}") for q in range(7)] for j in range(4)]
            op("dve", V.memset, hst, 0.0, writes=hB_)
            op("dve", V.memset, xin[:, :, 0:3], 0.0, writes=xB_)
            C_G = 0.7978845608028654

            def chunk_gen(j, t0):
                pbx, pbg = 2 * j, 2 * j + 1
                ubs = [uB[kc][t0 // 512] for kc in range(8)]
                xc, r_, i_, a_, m_, hs_, g_ = (tmpv[:, j, q, :] for q in range(7))
                bxc, br, bi, ba, bm, bh, bg_ = tB_[j]
                for kc in range(8):
                    mm(PS[pbx][:, 0:T], wv[:, kc, j * 128:(j + 1) * 128], uT[:, kc, t0:t0 + T], kc == 0, kc == 7,
                       reads=[wB[slot]] + ubs, writes=[psB[pbx]])
                for kc in range(8):
                    mm(PS[pbg][:, 0:T], wv[:, kc, 512 + j * 128:512 + (j + 1) * 128], uT[:, kc, t0:t0 + T], kc == 0, kc == 7,
                       reads=[wB[slot]] + ubs, writes=[psB[pbg]])
                yield
                act(xin[:, j, 3:3 + T], PS[pbx][:, 0:T], AF.Copy, reads=[psB[pbx]], writes=[xB_[j]])
                act(gate[:, j, :], PS[pbg][:, 0:T], AF.Copy, reads=[psB[pbg]], writes=[gB_[j]])
                ts("dve", xc, xin[:, j, 0:T], col(f"rcw{l}", j), MUL, col(f"rcb{l}", j), ADD, reads=[xB_[j], bf("par")], writes=[bxc])
                for tap in range(1, 4):
                    stt(xc, xin[:, j, tap:tap + T], col(f"rcw{l}", tap * 4 + j), xc, MUL, ADD, reads=[xB_[j], bf("par")], writes=[bxc])
                op("pool", P.tensor_copy, out=xin[:, j, 0:3], in_=xin[:, j, T:T + 3], reads=[xB_[j]], writes=[xB_[j]])
                mm(PS[pbx][:, 0:T], rgw[:, j, :], xc, True, True, reads=[bf("rgw"), bxc], writes=[psB[pbx]])
                mm(PS[pbg][:, 0:T], rgw[:, 4 + j, :], xc, True, True, reads=[bf("rgw"), bxc], writes=[psB[pbg]])
                act(g_, gate[:, j, :], AF.Square, reads=[gB_[j]], writes=[bg_])
                ts("pool", g_, g_, 0.044715, MUL, 1.0, ADD, writes=[bg_])
                tt("pool", g_, g_, gate[:, j, :], MUL, reads=[gB_[j]], writes=[bg_])
                yield
                sigm(r_, PS[pbx][:, 0:T], [psB[pbx], pb_], br, nbias=prm[:, 8 + j:9 + j])
                sigm(i_, PS[pbg][:, 0:T], [psB[pbg], pb_], bi, nbias=prm[:, 12 + j:13 + j])
                yield
                act(a_, r_, AF.Exp, reads=[pb_, br], writes=[ba], scale=prm[:, j:j + 1])
                act(m_, r_, AF.Exp, reads=[pb_, br], writes=[bm], scale=prm[:, 4 + j:5 + j])
                act(m_, m_, AF.Ln, reads=[bf("cst")], writes=[bm], scale=-1.0, bias=cst[:, 1:2])
                act(m_, m_, AF.Exp, writes=[bm], scale=0.5)
                tt("pool", i_, i_, xc, MUL, reads=[bxc], writes=[bi])
                yield
                tt("dve", i_, i_, m_, MUL, reads=[bm], writes=[bi])
                op("dve", V.tensor_tensor_scan, out=hs_, data0=a_, data1=i_, initial=hst[:, j:j + 1], op0=MUL, op1=ADD,
                   reads=[hB_[j], ba, bi], writes=[bh])
                op("pool", P.tensor_copy, out=hst[:, j:j + 1], in_=hs_[:, T - 1:T], reads=[bh], writes=[hB_[j]])
                sigm(g_, g_, [bg_], bg_, scale=2.0 * C_G)
                yield
                tt("pool", g_, g_, gate[:, j, :], MUL, reads=[gB_[j]], writes=[bg_])
                tt("dve", ob[:, j, :], hs_, g_, MUL, reads=[bh, bg_], writes=[oB_[j]])

            for tq in range(S // T):
                t0 = tq * T
                alive = [chunk_gen(j, t0) for j in range(4)]
                while alive:
                    nxt = []
                    for g2_ in alive:
                        try:
                            next(g2_)
                            nxt.append(g2_)
                        except StopIteration:
                            pass
                    alive = nxt
                accum_wout(l, s_, t0, T, lambda kc: ob[:, kc, :], 4, lambda kc, oc: wo[:, kc, oc * 128:(oc + 1) * 128], [wB[slot]] + oB_)

        def rope_tables(s_):
            sc.barrier()
            pi_ = arena[:, 0:2048].bitcast(I32)
            ang = arena[:, 2048:4096]
            kf = arena[:, 4096:6144]
            ki = arena[:, 6144:8192].bitcast(I32)
            r0 = arena[:, 8192:10240]
            m_ = arena[:, 10240:12288]
            rb = bf("ropew")
            dma("sp", nc.sync.dma_start, out=pi_[0:32, :], in_=pos_d[s_, 0:32, :], writes=[rb])
            op("dve", V.tensor_copy, out=ang[0:32, :], in_=pi_[0:32, :], reads=[rb], writes=[rb])
            ts("dve", ang[0:32, :], ang[0:32, :], par[0:32, PC["invf"]:PC["invf"] + 1], MUL, reads=[bf("par")], writes=[rb])
            TWO_PI = 2.0 * np.pi
            C1 = 6.28125
            C2 = TWO_PI - C1
            for which, shift in ((0, np.pi / 2.0), (1, 0.0)):
                ts("dve", kf[0:32, :], ang[0:32, :], float(shift), ADD, 1.0 / TWO_PI, MUL, writes=[rb])
                op("dve", V.tensor_copy, out=ki[0:32, :], in_=kf[0:32, :], writes=[rb])
                op("dve", V.tensor_copy, out=kf[0:32, :], in_=ki[0:32, :], writes=[rb])
                ts("dve", r0[0:32, :], ang[0:32, :], float(shift), ADD, writes=[rb])
                stt(r0[0:32, :], kf[0:32, :], -C1, r0[0:32, :], MUL, ADD, writes=[rb])
                stt(r0[0:32, :], kf[0:32, :], -C2, r0[0:32, :], MUL, ADD, writes=[rb])
                ts("dve", m_[0:32, :], r0[0:32, :], float(np.pi), ALU.is_gt, -TWO_PI, MUL, writes=[rb])
                tt("dve", r0[0:32, :], r0[0:32, :], m_[0:32, :], ADD, writes=[rb])
                ts("dve", m_[0:32, :], r0[0:32, :], float(-np.pi), ALU.is_lt, TWO_PI, MUL, writes=[rb])
                tt("dve", r0[0:32, :], r0[0:32, :], m_[0:32, :], ADD, writes=[rb])
                ts("dve", r0[0:32, :], r0[0:32, :], 3.1415925, MIN, -3.1415925, MAX, writes=[rb])
                act(m_[0:32, :], r0[0:32, :], AF.Sin, writes=[rb])
                dma("sp", nc.sync.dma_start, out=rope_d[s_, which], in_=m_[0:32, :], reads=[rb], writes=[bf(f"roped{which}")])

        def mla_group(l, s_):
            sc.barrier()
            T = 512
            slot = next_slot()
            W = wsl[slot]
            wm = W[:, 0:3328].rearrange("p (k n) -> p k n", k=8)
            wks = W[:, 3328:3584].rearrange("p (k n) -> p k n", k=8)
            wom = W[:, 3584:7680].rearrange("p (h n) -> p h n", h=4)
            wq = W[:, 7680:8448].rearrange("p (k n) -> p k n", k=2)
            wqs = W[:, 8448:8704].rearrange("p (k h n) -> p k h n", k=2, h=4)
            wkv = W[:, 8704:9216]
            wb_ = wB[slot]
            dma("pool", P.dma_start, out=wm, in_=win_d[l, :, :, 2056:2472], writes=[wb_])
            dma("pool", P.dma_start, out=wom[0:64], in_=woutm_d[l], writes=[wb_])
            dma("pool", P.dma_start, out=wq, in_=wqb_d[l], writes=[wb_])
            dma("pool", P.dma_start, out=wkv, in_=wkvb_d[l], writes=[wb_])
            op("act", A.mul, out=wks[:, :, 0:16], in_=wm[:, :, 400:416], mul=-1.0, reads=[wb_], writes=[wb_])
            op("act", A.copy, out=wks[:, :, 16:32], in_=wm[:, :, 384:400], reads=[wb_], writes=[wb_])
            wq4 = wq.rearrange("p k (h n) -> p k h n", h=4)
            for kc in range(2):
                op("act", A.mul, out=wqs[:, kc, :, 0:16], in_=wq4[:, kc, :, 80:96], mul=-1.0, reads=[wb_], writes=[wb_])
                op("act", A.copy, out=wqs[:, kc, :, 16:32], in_=wq4[:, kc, :, 64:80], reads=[wb_], writes=[wb_])
            a0 = 0
            Kc = arena[:, 0:4096].bitcast(BF16).rearrange("p (h n) -> p h n", h=4)
            Vc = arena[:, 4096:6176].bitcast(BF16).rearrange("p (j h n) -> p j h n", j=16, h=4)
            o = 6176

            def take(n):
                nonlocal o
                r = arena[:, o:o + n]
                o += n
                return r
            qn = take(512).bitcast(BF16).rearrange("p (k n) -> p k n", k=2)
            kvn = take(256).bitcast(BF16)
            Qh = take(256).bitcast(BF16)
            ET = take(512).bitcast(BF16).rearrange("p (k n) -> p k n", k=2)
            cos_ = take(512)
            sin_ = take(512)
            rs = take(512)
            sq = take(256).bitcast(BF16)
            t1 = take(512)
            t2 = take(512)
            osb = take(512)
            rec = take(512)
            oT = take(1024).bitcast(BF16).rearrange("p (h n) -> p h n", h=4)
            assert o <= 13000
            kb, vb_ = bf("mlaK"), bf("mlaV")
            bsq, brs, bqn, bkvn, bt1, bt2 = (bf(f"mla_{n}") for n in ("sq", "rs", "qn", "kvn", "t1", "t2"))
            bosb, brec, bQ = [bf("mla_osb0"), bf("mla_osb1")], [bf("mla_rec0"), bf("mla_rec1")], [bf("mla_Q0"), bf("mla_Q1")]
            boT = [bf(f"mla_oT{h}") for h in range(4)]
            op("pool", P.memset, Vc[:, :, :, 64:65], 1.0, writes=[vb_])
            SCALE = float(96 ** -0.5)
            Qh2 = [Qh, sq]
            osb2 = [osb, t1]
            rec2 = [rec, t2]
            for t in range(4):
                tsl = slice(t * T, (t + 1) * T)
                ub = [uB[kc][t] for kc in range(8)]
                dma("sp", nc.sync.dma_start, out=cos_[64:96, :], in_=rope_d[s_, 0, :, tsl], reads=[bf("roped0")], writes=[bf("mlacs")])
                dma("sp", nc.sync.dma_start, out=sin_[64:96, :], in_=rope_d[s_, 1, :, tsl], reads=[bf("roped1")], writes=[bf("mlacs")])
                for c in range(2):
                    for kc in range(8):
                        mm(PS[c][:], wm[:, kc, c * 128:(c + 1) * 128], uT[:, kc, tsl], kc == 0, kc == 7, reads=[wb_] + ub, writes=[psB[c]])
                for kc in range(8):
                    mm(PS[2][:], wm[:, kc, 256:384], uT[:, kc, tsl], kc == 0, kc == 7, reads=[wb_] + ub, writes=[psB[2]])
                for kc in range(8):
                    mm(PS[3][64:96, :], wm[:, kc, 384:416], uT[:, kc, tsl], kc == 0, kc == 7, reads=[wb_] + ub, writes=[psB[3]])
                for kc in range(8):
                    mm(PS[4][64:96, :], wks[:, kc, :], uT[:, kc, tsl], kc == 0, kc == 7, reads=[wb_] + ub, writes=[psB[4]])
                for c in range(2):
                    act(sq, PS[c][:], AF.Square, reads=[psB[c]], writes=[bsq] + bQ)
                    mm(PS[5][:], ones_bf[:], sq, c == 0, c == 1, reads=[bsq, bf("ones_bf")], writes=[psB[5]])
                act(rs, PS[5][:], AF.Ln, reads=[psB[5], bf("cst")], writes=[brs], scale=1.0 / 256, bias=cst[:, 0:1])
                act(rs, rs, AF.Exp, writes=[brs], scale=-0.5)
                for c in range(2):
                    stt(qn[:, c, :], PS[c][:], col(f"mqg{l}", c), rs, MUL, MUL, reads=[psB[c], bf("par"), brs], writes=[bqn])
                act(sq, PS[2][:], AF.Square, reads=[psB[2]], writes=[bsq] + bQ)
                mm(PS[5][:], ones_bf[:], sq, True, True, reads=[bsq, bf("ones_bf")], writes=[psB[5]])
                act(rs, PS[5][:], AF.Ln, reads=[psB[5], bf("cst")], writes=[brs], scale=1.0 / 128, bias=cst[:, 0:1])
                act(rs, rs, AF.Exp, writes=[brs], scale=-0.5)
                stt(kvn, PS[2][:], col(f"mkvg{l}", 0), rs, MUL, MUL, reads=[psB[2], bf("par"), brs], writes=[bkvn])
                tt("dve", t1[64:96, :], PS[3][64:96, :], cos_[64:96, :], MUL, reads=[psB[3], bf("mlacs")], writes=[bt1] + bosb)
                tt("dve", t2[64:96, :], PS[4][64:96, :], sin_[64:96, :], MUL, reads=[psB[4], bf("mlacs")], writes=[bt2] + brec)
                for h in range(4):
                    tt("pool", Kc[64:96, h, tsl], t1[64:96, :], t2[64:96, :], ADD, reads=[bt1, bt2], writes=[kb])
                for h in range(4):
                    pb = 6 + h % 2
                    mm(PS[pb][0:64, :], wkv[:, h * 128:h * 128 + 64], kvn, True, True, reads=[wb_, bkvn], writes=[psB[pb]])
                    act(Kc[0:64, h, tsl], PS[pb][0:64, :], AF.Copy, reads=[psB[pb]], writes=[kb])
                for kt in range(4):
                    pb = 6 + kt % 2
                    for h in range(4):
                        mm(PS[pb][:, h * 64:(h + 1) * 64], kvn[:, kt * 128:(kt + 1) * 128], wkv[:, h * 128 + 64:h * 128 + 128], True, True,
                           reads=[wb_, bkvn], writes=[psB[pb]])
                    op("dve", V.tensor_copy, out=Vc[:, t * 4 + kt, :, 0:64], in_=PS[pb][:, 0:256].rearrange("p (h n) -> p h n", h=4),
                       reads=[psB[pb]], writes=[vb_])
                for h in range(4):
                    hh = h % 2
                    Qh_, osb_, rec_ = Qh2[hh], osb2[hh], rec2[hh]
                    pa, pbq = 0, 1
                    for c in range(2):
                        mm(PS[pa][0:96, :], wq[:, c, h * 96:(h + 1) * 96], qn[:, c, :], c == 0, c == 1, reads=[wb_, bqn], writes=[psB[pa]])
                    for c in range(2):
                        mm(PS[pbq][64:96, :], wqs[:, c, h, :], qn[:, c, :], c == 0, c == 1, reads=[wb_, bqn], writes=[psB[pbq]])
                    qb = bQ[hh]
                    act(Qh_[0:64, :], PS[pa][0:64, :], AF.Copy, reads=[psB[pa]], writes=[qb, bsq] if hh else [qb])
                    act(Qh_[64:96, :], PS[pa][64:96, :], AF.Copy, reads=[psB[pa]], writes=[qb, bsq] if hh else [qb])
                    tt("dve", rs[64:96, :], PS[pbq][64:96, :], sin_[64:96, :], MUL, reads=[psB[pbq], bf("mlacs")], writes=[brs])
                    tt("pool", Qh_[64:96, :], Qh_[64:96, :], cos_[64:96, :], MUL, reads=[bf("mlacs")], writes=[qb])
                    tt("pool", Qh_[64:96, :], Qh_[64:96, :], rs[64:96, :], ADD, reads=[brs], writes=[qb])
                    nkt = 4 * t + 4
                    pacc = 4 + hh
                    def qk_exp(j):
                        r = j - 4 * t
                        qs = 0 if r < 0 else r * 128
                        n = T - qs
                        pst = 2 + j % 2
                        e = j % 2
                        eb = bf(f"mlaE{e}")
                        mm(PS[pst][:, 0:n], Kc[0:96, h, j * 128:(j + 1) * 128], Qh_[0:96, qs:T], True, True, reads=[kb, qb], writes=[psB[pst]])
                        act(ET[:, e, 0:n], PS[pst][:, 0:n], AF.Exp, reads=[psB[pst]], writes=[eb], scale=SCALE)
                        if r >= 0:
                            op("pool", P.memset, ET[64:128, e, 0:64], 0.0, writes=[eb])

                    def pv(j):
                        r = j - 4 * t
                        qs = 0 if r < 0 else r * 128
                        n = T - qs
                        e = j % 2
                        mm(PS[pacc][0:65, qs:T], Vc[:, j, h, :], ET[:, e, 0:n], j == 0, j == nkt - 1, reads=[vb_, bf(f"mlaE{e}")], writes=[psB[pacc]])
                    qk_exp(0)
                    for j in range(nkt):
                        if j + 1 < nkt:
                            qk_exp(j + 1)
                        pv(j)
                    wr = [bosb[hh]] + ([bt1] if hh else [])
                    op("dve", V.tensor_copy, out=osb_[0:65, :], in_=PS[pacc][0:65, :], reads=[psB[pacc]], writes=wr)
                    wr2 = [brec[hh]] + ([bt2] if hh else [])
                    op("dve", V.reciprocal, out=rec_[64:65, :], in_=osb_[64:65, :], reads=[bosb[hh]], writes=wr2)
                    pbc = 6 + hh
                    mm(PS[pbc][0:64, :], ones_f[64:65, 0:64], rec_[64:65, :], True, True, reads=[brec[hh], bf("ones_f")], writes=[psB[pbc]])
                    tt("dve", oT[0:64, h, :], osb_[0:64, :], PS[pbc][0:64, :], MUL, reads=[bosb[hh], psB[pbc]], writes=[boT[h]])
                accum_wout(l, s_, t * T, T, lambda kc: oT[0:64, kc, :], 4, lambda kc, oc: wom[0:64, kc, oc * 128:(oc + 1) * 128], [wb_] + boT)

        def gdn_group(l, s_):
            sc.barrier()
            T = 256
            slot = next_slot()
            wg = wsl[slot][:, 0:8256].rearrange("p (k n) -> p k n", k=8)
            wo = wsl[slot][:, 8256:10304].rearrange("p (k n) -> p k n", k=2)
            wb_ = wB[slot]
            dma("pool", P.dma_start, out=wg, in_=win_d[l, :, :, 0:1032], writes=[wb_])
            dma("pool", P.dma_start, out=wo, in_=wout_d[l, :, 0:2, :], writes=[wb_])
            o = [0]

            def take(n):
                r = arena[:, o[0]:o[0] + n]
                o[0] += n
                return r
            Ubd, BD1, Mls, MuiT = take(128), take(128), take(128), take(128)
            xin = take(6 * 259).rearrange("p (j n) -> p j n", j=6)
            yq = take(6 * T).rearrange("p (j n) -> p j n", j=6)
            zs = take(2 * T).rearrange("p (j n) -> p j n", j=2)
            tmp = take(4 * T).rearrange("p (j n) -> p j n", j=4)
            Sst = take(128).rearrange("p (c n) -> p c n", c=2)
            gt_ = take(40)
            HM = [[take(128) for _ in range(9)] for _ in range(4)]
            HS = [[take(64) for _ in range(5)] for _ in range(4)]
            otok = take(256).rearrange("p (h n) -> p h n", h=4)
            on_ = take(256)
            oT = take(T).bitcast(BF16).rearrange("p (c n) -> p c n", c=2)
            assert o[0] <= 13000, o[0]
            cb, sb_ = bf("gdnc"), [bf(f"gdnS{h}") for h in range(4)]
            gB = bf("gdngate")
            op("pool", P.memset, BD1, 0.0, writes=[cb])
            op("pool", P.memset, BD1[0:64, 0:64], 1.0, writes=[cb])
            op("pool", P.memset, BD1[64:128, 64:128], 1.0, writes=[cb])
            op("pool", P.affine_select, out=Ubd, in_=BD1, pattern=[[1, 128]], compare_op=ALU.is_ge, fill=0.0, base=0, channel_multiplier=-1,
               reads=[cb], writes=[cb])
            op("pool", P.tensor_copy, out=MuiT, in_=Ubd, reads=[cb], writes=[cb])
            op("pool", P.affine_select, out=Mls, in_=BD1, pattern=[[-1, 128]], compare_op=ALU.is_gt, fill=0.0, base=0, channel_multiplier=1,
               reads=[cb], writes=[cb])
            op("pool", P.memset, Sst, 0.0, writes=sb_)
            xb = [bf(f"gdnx{j}") for j in range(6)]
            yb = [bf(f"gdny{j}") for j in range(6)]
            zb = [bf(f"gdnz{j}") for j in range(2)]
            tb4 = [bf(f"gdnt{j}") for j in range(4)]
            op("pool", P.memset, xin[:, :, 0:3], 0.0, writes=xb)
            nA = gt_[:, 36:40]
            act(nA, par[:, PC[f"galog{l}"]:PC[f"galog{l}"] + 4], AF.Exp, reads=[bf("par")], writes=[cb])
            ts("dve", nA, nA, -1.0, MUL, writes=[cb])
            otB, onB, oTB = bf("gdnotok"), bf("gdnon"), bf("gdnoT")

            def head_gen(h, bs):
                c2, po = h // 2, (h % 2) * 64
                hp = slice(po, po + 64)
                pa, pbk = 2 * h, 2 * h + 1
                qT, kT, vT = yq[hp, c2, bs], yq[hp, 2 + c2, bs], yq[hp, 4 + c2, bs]
                rq, rk, rv = [yb[c2]], [yb[2 + c2]], [yb[4 + c2]]
                NQ, E1, E2, eGr, A_, B_, Y_, wT, qdT = HM[h]
                ktail, kbg, vbt, u_, vnew = HS[h]
                bN = [bf(f"gh{h}_{i}") for i in range(9)]
                bNQ, bE1, bE2, beG, bA, bB, bY, bwT, bqd = bN
                bkt, bkb, bvb, bu, bvn = [bf(f"gs{h}_{i}") for i in range(5)]
                ts("dve", NQ, ones_f[:], gt_[:, h:h + 1], MUL, -1.0, MUL, reads=[bf("ones_f"), gB], writes=[bNQ])
                mm(PS[pa][:, 0:128], NQ, Ubd, True, True, reads=[bNQ, cb], writes=[psB[pa]])
                yield
                ts("dve", E2, PS[pa][:, 0:128], gt_[:, 12 + h:13 + h], ADD, 0.0, MIN, reads=[psB[pa], gB], writes=[bE2])
                act(E2, E2, AF.Exp, writes=[bE2])
                tt("pool", E2, E2, Mls, MUL, reads=[cb], writes=[bE2])
                ts("dve", E1, PS[pa][:, 0:128], gt_[:, 12 + h:13 + h], ADD, 0.0, MAX, reads=[psB[pa], gB], writes=[bE1])
                act(E1, E1, AF.Exp, writes=[bE1], scale=-1.0)
                tt("pool", E1, E1, MuiT, MUL, reads=[cb], writes=[bE1])
                act(eGr, PS[pa][:, 0:128], AF.Exp, reads=[psB[pa]], writes=[beG], scale=-1.0)
                mm(PS[pbk][:, 0:128], kT, kT, True, True, reads=rk, writes=[psB[pbk]])
                yield
                stt(A_, PS[pbk][:, 0:128], gt_[:, 8 + h:9 + h], E2, MUL, MUL, reads=[psB[pbk], gB, bE2], writes=[bA])
                mm(PS[pa][:, 0:128], kT, qT, True, True, reads=rk + rq, writes=[psB[pa]])
                op("pe", PE.transpose, PS[pbk][:, 0:128], A_, ident[:], reads=[bA, bf("ident")], writes=[psB[pbk]])
                yield
                tt("dve", NQ, PS[pa][:, 0:128], E1, MUL, reads=[psB[pa], bE1], writes=[bNQ])
                op("act", A.copy, out=B_, in_=PS[pbk][:, 0:128], reads=[psB[pbk]], writes=[bB])
                tt("dve", Y_, B_, ident[:], ADD, reads=[bB, bf("ident")], writes=[bY])
                yield
                Am, Bm, An, Bn = A_, B_, E2, E1
                bAm, bBm, bAn, bBn = bA, bB, bE2, bE1
                for lvl in range(5):
                    mm(PS[pa][:, 0:128], Bm, Am, True, True, reads=[bAm, bBm], writes=[psB[pa]])
                    if lvl < 4:
                        mm(PS[pbk][:, 0:128], Am, Bm, True, True, reads=[bAm, bBm], writes=[psB[pbk]])
                    yield
                    op("act", A.copy, out=An, in_=PS[pa][:, 0:128], reads=[psB[pa]], writes=[bAn])
                    if lvl < 4:
                        op("dve", V.tensor_copy, out=Bn, in_=PS[pbk][:, 0:128], reads=[psB[pbk]], writes=[bBn])
                    mm(PS[pa][:, 0:128], An, Y_, True, True, reads=[bAn, bY], writes=[psB[pa]])
                    yield
                    tt("dve", Y_, Y_, PS[pa][:, 0:128], ADD, reads=[psB[pa]], writes=[bY])
                    Am, Bm, An, Bn = An, Bn, Am, Bm
                    bAm, bBm, bAn, bBn = bAn, bBn, bAm, bBm
                idh = ident[hp, po:po + 64]
                op("pe", PE.transpose, PS[pbk][:, 0:64], kT, idh, reads=rk + [bf("ident")], writes=[psB[pbk]])
                op("pe", PE.transpose, PS[pa][:, 0:64], vT, idh, reads=rv + [bf("ident")], writes=[psB[pa]])
                yield
                ts("dve", ktail, PS[pbk][:, 0:64], gt_[:, 20 + h:21 + h], MUL, reads=[psB[pbk], gB], writes=[bkt])
                ts("dve", kbg, PS[pbk][:, 0:64], gt_[:, 24 + h:25 + h], MUL, reads=[psB[pbk], gB], writes=[bkb])
                ts("dve", vbt, PS[pa][:, 0:64], gt_[:, 4 + h:5 + h], MUL, reads=[psB[pa], gB], writes=[bvb])
                mm(PS[pbk][:, 0:64], Y_, vbt, True, True, reads=[bY, bvb], writes=[psB[pbk]])
                mm(PS[pa][hp, 0:128], kbg, Y_, True, True, reads=[bkb, bY], writes=[psB[pa]])
                tt("dve", qdT[hp, :], qT, eGr[hp, :], MUL, reads=rq + [beG], writes=[bqd])
                yield
                op("act", A.copy, out=u_, in_=PS[pbk][:, 0:64], reads=[psB[pbk]], writes=[bu])
                op("act", A.copy, out=wT[hp, :], in_=PS[pa][hp, 0:128], reads=[psB[pa]], writes=[bwT])
                Sh = Sst[hp, c2, :]
                for c in range(2):
                    tc = slice(c * 64, c * 64 + 64)
                    mm(PS[pa][tc, 0:64], wT[hp, tc], Sh, True, True, reads=[bwT, sb_[h]], writes=[psB[pa]])
                    mm(PS[pbk][tc, 0:64], qdT[hp, tc], Sh, True, True, reads=[bqd, sb_[h]], writes=[psB[pbk]])
                    yield
                    tt("dve", vnew[tc, :], u_[tc, :], PS[pa][tc, 0:64], SUB, reads=[psB[pa], bu], writes=[bvn])
                    op("act", A.copy, out=otok[tc, h, :], in_=PS[pbk][tc, 0:64], reads=[psB[pbk]], writes=[otB])
                    mm(PS[pa][tc, 0:64], NQ[tc, tc], vnew[tc, :], True, True, reads=[bNQ, bvn], writes=[psB[pa]])
                    mm(PS[pbk][hp, 0:64], ktail[tc, :], vnew[tc, :], True, True, reads=[bkt, bvn], writes=[psB[pbk]])
                    yield
                    tt("dve", otok[tc, h, :], otok[tc, h, :], PS[pa][tc, 0:64], ADD, reads=[psB[pa]], writes=[otB])
                    stt(Sh, Sh, eGr[hp, c * 64 + 63:c * 64 + 64], PS[pbk][hp, 0:64], MUL, ADD, reads=[psB[pbk], beG], writes=[sb_[h]])
                    yield

            for tq in range(lim.get('gdn_tiles', S // T)):
                t0 = tq * T
                ub = [uB[kc][t0 // 512] for kc in range(8)]
                for j in range(8):
                    pb = j % 4
                    for kc in range(8):
                        mm(PS[pb][:, 0:T], wg[:, kc, j * 128:(j + 1) * 128], uT[:, kc, t0:t0 + T], kc == 0, kc == 7, reads=[wb_] + ub, writes=[psB[pb]])
                    tj = tmp[:, j % 2, :]
                    tjb = tb4[j % 2]
                    if j < 6:
                        act(xin[:, j, 3:3 + T], PS[pb][:, 0:T], AF.Copy, reads=[psB[pb]], writes=[xb[j]])
                        y = yq[:, j, :]
                        ts("dve", y, xin[:, j, 0:T], col(f"gcw{l}", j), MUL, reads=[xb[j], bf("par")], writes=[yb[j]])
                        for tap in range(1, 4):
                            stt(y, xin[:, j, tap:tap + T], col(f"gcw{l}", tap * 6 + j), y, MUL, ADD, reads=[xb[j], bf("par")], writes=[yb[j]])
                        op("pool", P.tensor_copy, out=xin[:, j, 0:3], in_=xin[:, j, T:T + 3], reads=[xb[j]], writes=[xb[j]])
                        sigm(tj, y, [yb[j]], tjb)
                        tt("dve", y, y, tj, MUL, reads=[tjb], writes=[yb[j]])
                        if j < 4:
                            t2_ = tmp[:, 2 + j % 2, :]
                            t2b = tb4[2 + j % 2]
                            act(t2_.bitcast(BF16)[:, 0:T], y, AF.Square, reads=[yb[j]], writes=[t2b])
                            mm(PS[4 + j][:, 0:T], bones_bf[:], t2_.bitcast(BF16)[:, 0:T], True, True, reads=[t2b, bf("bones_bf")], writes=[psB[4 + j]])
                            act(tj, PS[4 + j][:, 0:T], AF.Ln, reads=[psB[4 + j], bf("cst")], writes=[tjb], bias=cst[:, 0:1])
                            if j < 2:
                                act(tj, tj, AF.Exp, reads=[bf("cst")], writes=[tjb], scale=-0.5, bias=cst[:, 2:3])
                            else:
                                act(tj, tj, AF.Exp, writes=[tjb], scale=-0.5)
                            tt("dve", y, y, tj, MUL, reads=[tjb], writes=[yb[j]])
                    else:
                        zc = zs[:, j - 6, :]
                        act(zc, PS[pb][:, 0:T], AF.Copy, reads=[psB[pb]], writes=[zb[j - 6]])
                        sigm(tj, zc, [zb[j - 6]], tjb)
                        tt("dve", zc, zc, tj, MUL, reads=[tjb], writes=[zb[j - 6]])
                for blk in range(2):
                    b0 = blk * 128
                    bs = slice(b0, b0 + 128)
                    for kc in range(8):
                        mm(PS[5][:, 0:8], uT[:, kc, t0 + b0:t0 + b0 + 128], wg[:, kc, 1024:1032], kc == 0, kc == 7, reads=[wb_] + ub, writes=[psB[5]])
                    gcol, beta, nbeta, Gc, eG, eTl, bg, sp = (gt_[:, 4 * i:4 * i + 4] for i in range(8))
                    tt("dve", sp, PS[5][:, 0:4], par[:, PC[f"gdtb{l}"]:PC[f"gdtb{l}"] + 4], ADD, reads=[psB[5], bf("par")], writes=[gB])
                    act(sp, sp, AF.Exp, writes=[gB])
                    act(sp, sp, AF.Ln, reads=[bf("cst")], writes=[gB], bias=cst[:, 1:2])
                    tt("dve", gcol, sp, nA, MUL, reads=[cb], writes=[gB])
                    sigm(beta, PS[5][:, 4:8], [psB[5]], gB)
                    ts("dve", nbeta, beta, -1.0, MUL, writes=[gB])
                    mm(PS[6][:, 0:4], Ubd, gcol, True, True, reads=[cb, gB], writes=[psB[6]])
                    mm(PS[6][:, 4:8], BD1, gcol, True, True, reads=[cb, gB], writes=[psB[6]])
                    op("dve", V.tensor_copy, out=Gc, in_=PS[6][:, 0:4], reads=[psB[6]], writes=[gB])
                    act(eG, PS[6][:, 0:4], AF.Exp, reads=[psB[6]], writes=[gB])
                    tt("dve", eTl, PS[6][:, 4:8], Gc, SUB, reads=[psB[6]], writes=[gB])
                    act(eTl, eTl, AF.Exp, writes=[gB])
                    tt("dve", bg, beta, eG, MUL, writes=[gB])
                    for grp in lim.get('gdn_il', [[0, 1, 2, 3]]):
                      alive = [head_gen(h, bs) for h in grp]
                      nsteps = 0
                      while alive and nsteps < lim.get('gdn_steps', 999):
                          nsteps += 1
                          nxt = []
                          for g_ in alive:
                              try:
                                  next(g_)
                                  nxt.append(g_)
                              except StopIteration:
                                  pass
                          alive = nxt
                    ss = gt_[:, 32:36]
                    for h in range(4):
                        tt("pool", on_[:, h * 64:(h + 1) * 64], otok[:, h, :], otok[:, h, :], MUL, reads=[otB], writes=[onB])
                    op("dve", V.tensor_reduce, out=ss, in_=on_.rearrange("p (h n) -> p h n", h=4), axis=mybir.AxisListType.X, op=ADD,
                       reads=[onB], writes=[gB])
                    act(ss, ss, AF.Ln, reads=[bf("cst")], writes=[gB], scale=1.0 / 64, bias=cst[:, 0:1])
                    act(ss, ss, AF.Exp, writes=[gB], scale=-0.5)
                    for h in range(4):
                        ts("dve", on_[:, h * 64:(h + 1) * 64], otok[:, h, :], gt_[:, 32 + h:33 + h], MUL, reads=[otB, gB], writes=[onB])
                    for c2 in range(2):
                        op("pe", PE.transpose, PS[2 + c2][:, 0:128], on_[:, c2 * 128:(c2 + 1) * 128], ident[:], reads=[onB, bf("ident")], writes=[psB[2 + c2]])
                        stt(oT[:, c2, bs], PS[2 + c2][:, 0:128], col(f"gng{l}"), zs[:, c2, bs], MUL, MUL, reads=[psB[2 + c2], bf("par"), zb[c2]], writes=[oTB])
                accum_wout(l, s_, t0, T, lambda kc: oT[:, kc, :], 2, lambda kc, oc: wo[:, kc, oc * 128:(oc + 1) * 128], [wb_, oTB])

        for s_ in range(lim.get('nseq', NSEQ)):
            for k in range(8):
                for t in range(4):
                    dma("sp", nc.sync.dma_start, out=hT[:, k, t * 512:(t + 1) * 512], in_=xT[s_, k * 128:(k + 1) * 128, t * 512:(t + 1) * 512],
                        writes=[hB[k][t]])
            if G["mla"]:
                rope_tables(s_)
            for l in range(lim.get('depth', DEPTH)):
                modnorm(l, s_, 0)
                if G["gdn"]:
                    gdn_group(l, s_)
                if G["rg"]:
                    rg_group(l, s_)
                if G["mla"]:
                    mla_group(l, s_)
                if G["mlp"]:
                    modnorm(l, s_, 1)
                    mlp(l, s_)
            final_norm(s_)
        if debug_stop == "mod":
            dma("sp", nc.sync.dma_start, out=yT[1, 0:128, 0:DEPTH * 48 * NSEQ], in_=modT[:].rearrange("p l c s -> p (l c s)"),
                reads=[bf("modT")], writes=[bf("yout")])
        sc.final_wait("sp", [bf("yout")])
        build_nc.last_counts = dict(sc.cnt)

        with nc.Block() as block:
            @block.tensor
            def _(e):
                sc.replay("pe", e)

            @block.vector
            def _(e):
                sc.replay("dve", e)

            @block.scalar
            def _(e):
                sc.replay("act", e)

            @block.gpsimd
            def _(e):
                sc.replay("pool", e)

            @block.sync
            def _(e):
                sc.replay("sp", e)
    return nc


def _pack_params(inp, core):
    par = np.zeros((128, NPAR), np.float32)
    f = lambda v: np.asarray(v, np.float32)
    cm = lambda v: f(v).reshape(-1, 128).T
    for s_ in range(NSEQ):
        par[:, PC["c"] + 8 * s_:PC["c"] + 8 * (s_ + 1)] = cm(inp["c"][core * NSEQ + s_])
    par[:, PC["fng"]:PC["fng"] + 8] = cm(inp["final_norm_g"])
    invf = (np.float32(10000.0) ** (-np.arange(0, 32, 2, dtype=np.float32) / np.float32(32))).astype(np.float32)
    par[:, PC["invf"]] = np.tile(invf, 8)
    for l in range(DEPTH):
        par[:, PC[f"bmod{l}"]:PC[f"bmod{l}"] + 48] = cm(inp["b_mod"][l])
        par[:, PC[f"nmg{l}"]:PC[f"nmg{l}"] + 8] = cm(inp["norm_mix_g"][l])
        par[:, PC[f"nfg{l}"]:PC[f"nfg{l}"] + 8] = cm(inp["norm_mlp_g"][l])
        gcw = f(inp["gdn_conv_w"][l])
        for tap in range(4):
            par[:, PC[f"gcw{l}"] + tap * 6:PC[f"gcw{l}"] + tap * 6 + 6] = cm(gcw[tap])
        par[:, PC[f"galog{l}"]:PC[f"galog{l}"] + 4] = f(inp["gdn_a_log"][l])[None, :]
        par[:, PC[f"gdtb{l}"]:PC[f"gdtb{l}"] + 4] = f(inp["gdn_dt_bias"][l])[None, :]
        par[:, PC[f"gng{l}"]] = np.tile(f(inp["gdn_norm_g"][l]), 2)
        rcw = f(inp["rg_conv_w"][l])
        for tap in range(4):
            par[:, PC[f"rcw{l}"] + tap * 4:PC[f"rcw{l}"] + tap * 4 + 4] = cm(rcw[tap])
        par[:, PC[f"rcb{l}"]:PC[f"rcb{l}"] + 4] = cm(inp["rg_conv_b"][l])
        par[:, PC[f"rba{l}"]:PC[f"rba{l}"] + 4] = cm(inp["rg_b_a"][l])
        par[:, PC[f"rbx{l}"]:PC[f"rbx{l}"] + 4] = cm(inp["rg_b_x"][l])
        par[:, PC[f"rlam{l}"]:PC[f"rlam{l}"] + 4] = cm(inp["rg_lambda"][l])
        par[:, PC[f"mqg{l}"]:PC[f"mqg{l}"] + 2] = cm(inp["mla_q_norm_g"][l])
        par[:, PC[f"mkvg{l}"]:PC[f"mkvg{l}"] + 1] = cm(inp["mla_kv_norm_g"][l])
    return par


def _shared_weights(inp):
    f = lambda v: np.ascontiguousarray(np.asarray(v, np.float32))
    kp = lambda w, k: f(w.reshape(w.shape[0], k, 128, w.shape[2]).transpose(0, 2, 1, 3))
    sh = {}
    sh["wmod"] = kp(inp["w_mod"], 8)
    sh["win"] = kp(inp["w_in"], 8)
    sh["wout"] = kp(inp["w_out"], 8)
    wo = np.asarray(inp["w_out"], np.float32)
    sh["woutm"] = f(wo[:, 768:1024, :].reshape(DEPTH, 4, 64, 1024).transpose(0, 2, 1, 3))
    sh["w1"] = kp(inp["w_mlp_in"], 8)
    sh["w2"] = kp(inp["w_mlp_out"], 32)
    rgw = np.zeros((DEPTH, 128, 8, 128), np.float32)
    for l in range(DEPTH):
        for wi_, nm in enumerate(("rg_w_a", "rg_w_x")):
            w = np.asarray(inp[nm][l], np.float32)
            for j in range(4):
                rgw[l, 0:64, wi_ * 4 + j, 0:64] = w[2 * j]
                rgw[l, 64:128, wi_ * 4 + j, 64:128] = w[2 * j + 1]
    sh["rgw"] = rgw
    sh["wqb"] = kp(inp["mla_w_qb"], 2)
    sh["wkvb"] = f(inp["mla_w_kvb"])
    return sh


def make_in_maps(inp, cores):
    sh = _shared_weights(inp)
    maps = []
    for c in cores:
        m = dict(sh)
        xs = np.asarray(inp["x"][c * NSEQ:(c + 1) * NSEQ], np.float32)
        m["xT"] = np.ascontiguousarray(xs.transpose(0, 2, 1))
        m["par"] = _pack_params(inp, c)
        pos = np.asarray(inp["positions"][c * NSEQ:(c + 1) * NSEQ], np.int32)
        m["pos"] = np.ascontiguousarray(np.broadcast_to(pos[:, None, :], (NSEQ, 128, S)))
        maps.append(m)
    return maps


def kernel(**inputs):
    nc = build_nc()
    cores = list(range(8))
    in_maps = make_in_maps(inputs, cores)
    res = run_bass_kernel_spmd(nc, in_maps, core_ids=cores)
    out = np.empty((16, S, D), np.float32)
    for c in cores:
        y = res.results[c]["yT"]
        for s_ in range(NSEQ):
            out[c * NSEQ + s_] = y[s_].T
    return out
```

```python
import numpy as np
import concourse.bass as bass
import concourse.mybir as mybir
from concourse.bass_utils import run_bass_kernel_spmd

F32 = mybir.dt.float32
BF16 = mybir.dt.bfloat16
I32 = mybir.dt.int32
AF = mybir.ActivationFunctionType
ALU = mybir.AluOpType

S = 2048
D = 1024
NSEQ = 2
DEPTH = 2
EPS = 1e-6
GROUPS = {"gdn": True, "rg": True, "mla": True, "mlp": True}

PC = {}
_off = 0


def _pc(name, n):
    global _off
    PC[name] = _off
    _off += n


_pc("c", 16)
_pc("fng", 8)
_pc("invf", 1)
for _l in range(DEPTH):
    _pc(f"bmod{_l}", 48)
    _pc(f"nmg{_l}", 8)
    _pc(f"nfg{_l}", 8)
    _pc(f"gcw{_l}", 24)
    _pc(f"galog{_l}", 4)
    _pc(f"gdtb{_l}", 4)
    _pc(f"gng{_l}", 1)
    _pc(f"rcw{_l}", 16)
    _pc(f"rcb{_l}", 4)
    _pc(f"rba{_l}", 4)
    _pc(f"rbx{_l}", 4)
    _pc(f"rlam{_l}", 4)
    _pc(f"mqg{_l}", 2)
    _pc(f"mkvg{_l}", 1)
NPAR = _off


class Buf:
    __slots__ = ("w", "r", "excl")

    def __init__(self):
        self.w = []
        self.r = {}
        self.excl = False


class Sched:
    EPOCH = 20000

    def __init__(self, nc, sems):
        self.nc = nc
        self.sems = sems
        self.eng = {"pe": nc.tensor, "dve": nc.vector, "act": nc.scalar, "pool": nc.gpsimd, "sp": nc.sync}
        self.q = {e: [] for e in self.eng}
        self.cnt = {e: 0 for e in self.eng}
        self.esem = {e: [] for e in self.eng}
        self.seen = {e: {} for e in self.eng}
        self.pending = {e: [] for e in self.eng}
        self.last = {e: None for e in self.eng}
        self.dsem = {}
        self.dcnt = {}
        self.dval = {}
        for qn in ("sp", "pool"):
            self.dsem[qn] = [self._newsem() for _ in range(6)]
            self.dval[qn] = [0] * 6
            self.dcnt[qn] = 0

    def _newsem(self):
        return self.sems.pop()

    def _deps(self, eng, reads, writes, extra=(), same=True):
        evs = list(extra)
        need = {}
        for b in reads:
            evs.extend(b.w)
            if b.excl:
                evs.extend(b.r.values())
        for b in writes:
            evs.extend(b.w)
            evs.extend(b.r.values())
        if self.pending[eng]:
            evs.extend(self.pending[eng])
            self.pending[eng] = []
        for (src, sem, val) in evs:
            if src == eng:
                if eng == "pe" or not same or sem is not (self.esem[eng][-1] if self.esem[eng] else None):
                    continue
                if self.cnt[eng] % self.EPOCH - (val - 1) > 5:
                    continue
                k = id(sem)
                if k not in need or need[k][1] < val:
                    need[k] = (sem, val)
                continue
            k = id(sem)
            if self.seen[eng].get(k, 0) >= val:
                continue
            if k not in need or need[k][1] < val:
                need[k] = (sem, val)
        for k, (sem, val) in need.items():
            self.seen[eng][k] = val
        return list(need.values())

    mute = False

    def op(self, eng, f, *args, reads=(), writes=(), **kw):
        if self.mute:
            return
        fn = (lambda: f(*args, **kw))
        waits = self._deps(eng, reads, writes)
        idx = self.cnt[eng]
        self.cnt[eng] += 1
        ep, v = idx // self.EPOCH, idx % self.EPOCH + 1
        while len(self.esem[eng]) <= ep:
            self.esem[eng].append(self._newsem())
        sem = self.esem[eng][ep]
        self.q[eng].append((waits, fn, sem, 1))
        ev = (eng, sem, v)
        self.last[eng] = ev
        for b in reads:
            b.r[eng] = ev
        for b in writes:
            b.w = [ev]
            b.r = {}

    def dma(self, qn, f, *args, reads=(), writes=(), **kw):
        if self.mute:
            return None
        fn = (lambda: f(*args, **kw))
        i = self.dcnt[qn] % len(self.dsem[qn])
        self.dcnt[qn] += 1
        sem = self.dsem[qn][i]
        prev = ("dma" + qn, sem, self.dval[qn][i])
        waits = self._deps(qn, reads, writes, extra=[prev] if prev[2] > 0 else [])
        self.dval[qn][i] += 16
        self.q[qn].append((waits, fn, sem, 16))
        ev = ("dma" + qn, sem, self.dval[qn][i])
        for b in reads:
            b.r["dma" + qn + str(i)] = ev
        for b in writes:
            b.w = [e for e in b.w if e[0].startswith("dma")][-40:] + [ev]
            b.r = {}
        return ev

    def barrier(self):
        evs = [ev for ev in self.last.values() if ev is not None]
        for qn in self.dsem:
            for i, sem in enumerate(self.dsem[qn]):
                if self.dval[qn][i] > 0:
                    evs.append(("dma" + qn, sem, self.dval[qn][i]))
        for e in self.eng:
            self.pending[e] = list(evs)

    def final_wait(self, eng, bufs):
        waits = self._deps(eng, bufs, bufs)
        self.q[eng].append((waits, None, None, 0))

    def replay(self, eng, h):
        for waits, fn, sem, inc in self.q[eng]:
            for (s, v) in waits:
                h.wait_ge(s, v)
            if fn is not None:
                fn().then_inc(sem, inc)


def build_nc(groups=None, debug_stop=None, lim=None):
    lim = lim or {}
    G = dict(GROUPS)
    if groups:
        G.update(groups)
    nc = bass.Bass("TRN2", target_bir_lowering=False)
    dt_ = nc.dram_tensor
    xT = dt_("xT", [NSEQ, D, S], F32, kind="ExternalInput").ap()
    par_d = dt_("par", [128, NPAR], F32, kind="ExternalInput").ap()
    pos_d = dt_("pos", [NSEQ, 128, S], I32, kind="ExternalInput").ap()
    wmod_d = dt_("wmod", [DEPTH, 128, 8, 6144], F32, kind="ExternalInput").ap()
    win_d = dt_("win", [DEPTH, 128, 8, 2472], F32, kind="ExternalInput").ap()
    wout_d = dt_("wout", [DEPTH, 128, 8, 1024], F32, kind="ExternalInput").ap()
    woutm_d = dt_("woutm", [DEPTH, 64, 4, 1024], F32, kind="ExternalInput").ap()
    w1_d = dt_("w1", [DEPTH, 128, 8, 4096], F32, kind="ExternalInput").ap()
    w2_d = dt_("w2", [DEPTH, 128, 32, 1024], F32, kind="ExternalInput").ap()
    rgw_d = dt_("rgw", [DEPTH, 128, 8, 128], F32, kind="ExternalInput").ap()
    wqb_d = dt_("wqb", [DEPTH, 128, 2, 384], F32, kind="ExternalInput").ap()
    wkvb_d = dt_("wkvb", [DEPTH, 128, 512], F32, kind="ExternalInput").ap()
    yT = dt_("yT", [NSEQ, D, S], F32, kind="ExternalOutput").ap()
    rope_d = dt_("ropescr", [NSEQ, 2, 32, S], F32, kind="Internal").ap()

    from contextlib import ExitStack
    es = ExitStack()
    sb = lambda n, shp, d=F32: es.enter_context(nc.sbuf_tensor(n, shp, d))
    ps = lambda n: es.enter_context(nc.psum_tensor(n, [128, 512], F32))
    with es:
        sems = [es.enter_context(nc.semaphore(f"s{i}")) for i in range(40)]
        sc = Sched(nc, sems)
        op, dma = sc.op, sc.dma
        V, A, P, PE = nc.vector, nc.scalar, nc.gpsimd, nc.tensor

        hT = sb("hT", [128, 8, S])
        uT = sb("uT", [128, 8, S], BF16)
        par = sb("par_sb", [128, NPAR])
        cst = sb("cst", [128, 8])
        ident = sb("ident", [128, 128])
        ones_bf = sb("ones_bf", [128, 128], BF16)
        ones_f = sb("ones_f", [128, 128])
        bones_bf = sb("bones_bf", [128, 128], BF16)
        modT = sb("modT", [128, DEPTH, 48, NSEQ])
        gmix = sb("gmix", [128, 4, 8])
        cact = sb("cact", [128, 8, NSEQ], BF16)
        ctmp = sb("ctmp", [128, 16])
        wsl = [sb("wslA", [128, 12288], BF16), sb("wslB", [128, 12288], BF16)]
        arena = sb("arena", [128, 13000])
        PS = [ps(f"ps{i}") for i in range(8)]

        B = {}

        def bf(name):
            if name not in B:
                B[name] = Buf()
            return B[name]

        hB = [[bf(f"h{k}_{t}") for t in range(4)] for k in range(8)]
        uB = [[bf(f"u{k}_{t}") for t in range(4)] for k in range(8)]
        psB = [bf(f"ps{i}") for i in range(8)]
        for b_ in psB:
            b_.excl = True
        wB = [bf("wslA"), bf("wslB")]

        def col(name, j=0):
            c0 = PC[name] + j
            return par[:, c0:c0 + 1]

        op("pool", P.memset, cst[:, 0:1], EPS, writes=[bf("cst")])
        op("pool", P.memset, cst[:, 1:2], 1.0, writes=[bf("cst")])
        op("pool", P.memset, cst[:, 2:3], float(np.log(0.125)), writes=[bf("cst")])
        op("pool", P.memset, cst[:, 3:4], 0.0, writes=[bf("cst")])
        op("pool", P.memset, ones_f[:], 1.0, writes=[bf("ones_f")])
        op("pool", P.memset, ones_bf[:], 1.0, writes=[bf("ones_bf")])
        op("pool", P.memset, bones_bf[:], 0.0, writes=[bf("bones_bf")])
        op("pool", P.memset, bones_bf[0:64, 0:64], 1.0, writes=[bf("bones_bf")])
        op("pool", P.memset, bones_bf[64:128, 64:128], 1.0, writes=[bf("bones_bf")])
        op("pool", P.affine_select, out=ident[:], in_=ones_f[:], pattern=[[-1, 128]], compare_op=ALU.is_equal,
           fill=0.0, base=0, channel_multiplier=1, reads=[bf("ones_f")], writes=[bf("ident")])
        dma("sp", nc.sync.dma_start, out=par[:], in_=par_d, writes=[bf("par")])

        def col(name, j=0):
            c0 = PC[name] + j
            return par[:, c0:c0 + 1]

        def act(out, in_, func, reads=(), writes=(), **kw):
            op("act", A.activation, out=out, in_=in_, func=func, reads=reads, writes=writes, **kw)

        def tt(eng, out, in0, in1, o, reads=(), writes=()):
            op(eng, (V if eng == "dve" else P).tensor_tensor, out=out, in0=in0, in1=in1, op=o, reads=reads, writes=writes)

        def ts(eng, out, in0, s1, o0, s2=None, o1=None, reads=(), writes=()):
            kw = {} if o1 is None else {"op1": o1}
            op(eng, (V if eng == "dve" else P).tensor_scalar, out=out, in0=in0, scalar1=s1, scalar2=s2, op0=o0, reads=reads, writes=writes, **kw)

        def stt(out, in0, scalar, in1, o0, o1, reads=(), writes=()):
            op("dve", V.scalar_tensor_tensor, out=out, in0=in0, scalar=scalar, in1=in1, op0=o0, op1=o1, reads=reads, writes=writes)

        def mm(out, lhsT, rhs, start, stop, reads=(), writes=()):
            op("pe", PE.matmul, out, lhsT, rhs, start=start, stop=stop, reads=reads, writes=writes)

        def sigm(out, in_, reads, wb, scale=1.0, nbias=None):
            kw = {} if nbias is None else {"bias": nbias}
            act(out, in_, AF.Exp, reads=reads, writes=[wb], scale=-scale, **kw)
            act(out, out, AF.Ln, reads=[bf("cst")], writes=[wb], bias=cst[:, 1:2])
            act(out, out, AF.Exp, writes=[wb], scale=-1.0)

        MUL, ADD, SUB, MAX, MIN = ALU.mult, ALU.add, ALU.subtract, ALU.max, ALU.min

        cc = par[:, PC["c"]:PC["c"] + 16]
        sigm(ctmp[:], cc, [bf("par")], bf("ctmp"))
        for s_ in range(NSEQ):
            tt("dve", cact[:, :, s_], ctmp[:, s_ * 8:(s_ + 1) * 8], par[:, PC["c"] + s_ * 8:PC["c"] + (s_ + 1) * 8], MUL,
               reads=[bf("ctmp"), bf("par")], writes=[bf("cact")])
        wcount = [0]

        def next_slot():
            s = wcount[0] % 2
            wcount[0] += 1
            return s

        for l in range(DEPTH):
            for pc_ in range(12):
                slot = next_slot()
                wv = wsl[slot][:, 0:4096].rearrange("p (k n) -> p k n", k=8)
                dma("pool", P.dma_start, out=wv, in_=wmod_d[l, :, :, pc_ * 512:(pc_ + 1) * 512], writes=[wB[slot]])
                for oc in range(4):
                    pb = (pc_ * 4 + oc) % 8
                    for kc in range(8):
                        mm(PS[pb][:, 0:NSEQ], wv[:, kc, oc * 128:(oc + 1) * 128], cact[:, kc, :], kc == 0, kc == 7,
                           reads=[wB[slot], bf("cact")], writes=[psB[pb]])
                    ch = pc_ * 4 + oc
                    ts("dve", modT[:, l, ch, :], PS[pb][:, 0:NSEQ], col(f"bmod{l}", ch), ADD, reads=[psB[pb], bf("par")], writes=[bf("modT")])

        def norm_stats(t, nchunk_src, scale):
            sq = arena[:, 0:1024].bitcast(BF16).rearrange("p (b n) -> p b n", b=4)
            rs = arena[:, 1024:2048].rearrange("p (b n) -> p b n", b=2)
            tsl = slice(t * 512, (t + 1) * 512)
            pb = t % 2
            for k in range(8):
                j = (t * 8 + k) % 4
                act(sq[:, j, :], hT[:, k, tsl], AF.Square, reads=[hB[k][t]], writes=[bf(f"nsq{j}")])
                mm(PS[pb][:], ones_bf[:], sq[:, j, :], k == 0, k == 7, reads=[bf(f"nsq{j}"), bf("ones_bf")], writes=[psB[pb]])
            rb = bf(f"nrs{t % 2}")
            act(rs[:, t % 2, :], PS[pb][:], AF.Ln, reads=[psB[pb], bf("cst")], writes=[rb], scale=1.0 / D, bias=cst[:, 0:1])
            act(rs[:, t % 2, :], rs[:, t % 2, :], AF.Exp, writes=[rb], scale=-0.5)
            return rs[:, t % 2, :], rb

        def modnorm(l, s_, which):
            sc.barrier()
            gname = f"nmg{l}" if which == 0 else f"nfg{l}"
            sh0, sc0 = (0, 8) if which == 0 else (24, 32)
            stt(gmix[:, which, :], modT[:, l, sc0:sc0 + 8, s_], 1.0, par[:, PC[gname]:PC[gname] + 8], ADD, MUL,
                reads=[bf("modT"), bf("par")], writes=[bf("gmix")])
            tm = arena[:, 2048:3072].rearrange("p (b n) -> p b n", b=2)
            for t in range(4):
                tsl = slice(t * 512, (t + 1) * 512)
                rs, rb = norm_stats(t, 8, 1.0 / D)
                for k in range(8):
                    tb = bf(f"ntm{k % 2}")
                    tt("dve", tm[:, k % 2, :], hT[:, k, tsl], rs, MUL, reads=[hB[k][t], rb], writes=[tb])
                    act(uT[:, k, tsl], tm[:, k % 2, :], AF.Identity, reads=[tb, bf("gmix"), bf("modT")], writes=[uB[k][t]],
                        scale=gmix[:, which, k:k + 1], bias=modT[:, l, sh0 + k, s_:s_ + 1])

        def final_norm(s_):
            sc.barrier()
            ot = arena[:, 2048:4096].rearrange("p (b n) -> p b n", b=4)
            for t in range(4):
                tsl = slice(t * 512, (t + 1) * 512)
                rs, rb = norm_stats(t, 8, 1.0 / D)
                for k in range(8):
                    ob = bf(f"fot{k % 4}")
                    stt(ot[:, k % 4, :], hT[:, k, tsl], col("fng", k), rs, MUL, MUL, reads=[hB[k][t], rb, bf("par")], writes=[ob])
                    dma("sp", nc.sync.dma_start, out=yT[s_, k * 128:(k + 1) * 128, tsl], in_=ot[:, k % 4, :], reads=[ob], writes=[bf("yout")])

        prefetched = {}

        def load_mlp0(l):
            slot = next_slot()
            w1v = wsl[slot][:, 0:4096].rearrange("p (k n) -> p k n", k=8)
            w2v = wsl[slot][:, 4096:8192].rearrange("p (k n) -> p k n", k=4)
            dma("pool", P.dma_start, out=w1v, in_=w1_d[l, :, :, 0:512], writes=[wB[slot]])
            dma("pool", P.dma_start, out=w2v, in_=w2_d[l, :, 0:4, :], writes=[wB[slot]])
            return (slot, w1v, w2v)

        def load_rg(l):
            slot = next_slot()
            wv = wsl[slot][:, 0:8192].rearrange("p (k n) -> p k n", k=8)
            wo = wsl[slot][:, 8192:12288].rearrange("p (k n) -> p k n", k=4)
            dma("pool", P.dma_start, out=wv, in_=win_d[l, :, :, 1032:2056], writes=[wB[slot]])
            dma("pool", P.dma_start, out=wo, in_=wout_d[l, :, 2:6, :], writes=[wB[slot]])
            return (slot, wv, wo)

        def load_mla(l):
            slot = next_slot()
            W = wsl[slot]
            wm = W[:, 0:3328].rearrange("p (k n) -> p k n", k=8)
            wom = W[:, 3584:7680].rearrange("p (h n) -> p h n", h=4)
            wq = W[:, 7680:8448].rearrange("p (k n) -> p k n", k=2)
            wkv = W[:, 8704:9216]
            dma("pool", P.dma_start, out=wm, in_=win_d[l, :, :, 2056:2472], writes=[wB[slot]])
            dma("pool", P.dma_start, out=wom[0:64], in_=woutm_d[l], writes=[wB[slot]])
            dma("pool", P.dma_start, out=wq, in_=wqb_d[l], writes=[wB[slot]])
            dma("pool", P.dma_start, out=wkv, in_=wkvb_d[l], writes=[wB[slot]])
            return slot

        def mlp(l, s_):
            sc.barrier()
            fbuf = arena[:, 0:2048].bitcast(BF16).rearrange("p (b n) -> p b n", b=8)
            rl = arena[:, 2048:4096].rearrange("p (b n) -> p b n", b=4)
            views = {}
            if ("mlp0", l) in prefetched:
                views[0] = prefetched.pop(("mlp0", l))
            cnt = [0]

            def load(fc):
                slot = next_slot()
                w1v = wsl[slot][:, 0:4096].rearrange("p (k n) -> p k n", k=8)
                w2v = wsl[slot][:, 4096:8192].rearrange("p (k n) -> p k n", k=4)
                dma("pool", P.dma_start, out=w1v, in_=w1_d[l, :, :, fc * 512:(fc + 1) * 512], writes=[wB[slot]])
                dma("pool", P.dma_start, out=w2v, in_=w2_d[l, :, fc * 4:(fc + 1) * 4, :], writes=[wB[slot]])
                views[fc] = (slot, w1v, w2v)

            def w1_stage(i):
                fc, t = divmod(i, 4)
                if fc not in views:
                    load(fc)
                if t == 1 and fc + 1 < 8 and (fc + 1) not in views:
                    load(fc + 1)
                slot, w1v, w2v = views[fc]
                tsl = slice(t * 512, (t + 1) * 512)
                par_ = i % 2
                for fs in range(4):
                    pb = cnt[0] % 4
                    cnt[0] += 1
                    for kc in range(8):
                        mm(PS[pb][:], w1v[:, kc, fs * 128:(fs + 1) * 128], uT[:, kc, tsl], kc == 0, kc == 7,
                           reads=[wB[slot], uB[kc][t]], writes=[psB[pb]])
                    rb = bf(f"rl{pb}")
                    fb = bf(f"f1_{par_}_{fs}")
                    act(rl[:, pb, :], PS[pb][:], AF.Relu, reads=[psB[pb]], writes=[rb])
                    tt("dve" if fs % 2 == 0 else "pool", fbuf[:, par_ * 4 + fs, :], rl[:, pb, :], rl[:, pb, :], MUL, reads=[rb], writes=[fb])

            def w2_stage(i):
                fc, t = divmod(i, 4)
                slot, w1v, w2v = views[fc]
                tsl = slice(t * 512, (t + 1) * 512)
                par_ = i % 2
                for oc in range(8):
                    pb = 4 + oc % 4
                    for fs in range(4):
                        mm(PS[pb][:], w2v[:, fs, oc * 128:(oc + 1) * 128], fbuf[:, par_ * 4 + fs, :], fs == 0, fs == 3,
                           reads=[wB[slot], bf(f"f1_{par_}_{fs}")], writes=[psB[pb]])
                    stt(hT[:, oc, tsl], PS[pb][:], modT[:, l, 40 + oc, s_:s_ + 1], hT[:, oc, tsl], MUL, ADD,
                        reads=[psB[pb], bf("modT"), hB[oc][t]], writes=[hB[oc][t]])

            NI = 32
            w1_stage(0)
            for i in range(NI):
                if i + 1 < NI:
                    w1_stage(i + 1)
                w2_stage(i)

        def accum_wout(l, s_, t0, n, rhs_fn, nk, wv_fn, rd):
            tq = t0 // 512
            for oc in range(8):
                pb = 4 + oc % 4
                for kc in range(nk):
                    mm(PS[pb][:, 0:n], wv_fn(kc, oc), rhs_fn(kc), kc == 0, kc == nk - 1, reads=rd, writes=[psB[pb]])
                stt(hT[:, oc, t0:t0 + n], PS[pb][:, 0:n], modT[:, l, 16 + oc, s_:s_ + 1], hT[:, oc, t0:t0 + n], MUL, ADD,
                    reads=[psB[pb], bf("modT"), hB[oc][tq]], writes=[hB[oc][tq]])

        def rg_group(l, s_):
            sc.barrier()
            T = 256
            slot, wv, wo = prefetched.pop(("rg", l)) if ("rg", l) in prefetched else load_rg(l)
            if G["mla"]:
                prefetched[("mla", l)] = load_mla(l)
            rgw = arena[:, 0:1024].rearrange("p (k n) -> p k n", k=8)
            dma("sp", nc.sync.dma_start, out=rgw, in_=rgw_d[l], writes=[bf("rgw")])
            prm = arena[:, 1024:1040]
            hst = arena[:, 1040:1044]
            xin = arena[:, 1048:1048 + 4 * 259].rearrange("p (j n) -> p j n", j=4)
            o0 = 1048 + 4 * 259 + 4
            gate = arena[:, o0:o0 + 4 * T].rearrange("p (j n) -> p j n", j=4)
            o1 = o0 + 4 * T
            tmpv = arena[:, o1:o1 + 28 * T].rearrange("p (j q n) -> p j q n", j=4, q=7)
            o2 = o1 + 28 * T
            ob = arena[:, o2:o2 + 2 * T].bitcast(BF16).rearrange("p (j n) -> p j n", j=4)
            assert o2 + 2 * T <= 13000
            pb_ = bf("rgprm")
            lam = par[:, PC[f"rlam{l}"]:PC[f"rlam{l}"] + 4]
            act(prm[:, 0:4], lam, AF.Exp, reads=[bf("par")], writes=[pb_], scale=-1.0)
            act(prm[:, 0:4], prm[:, 0:4], AF.Ln, reads=[bf("cst")], writes=[pb_], bias=cst[:, 1:2])
            ts("dve", prm[:, 4:8], prm[:, 0:4], -16.0, MUL, writes=[pb_])
            ts("dve", prm[:, 0:4], prm[:, 0:4], -8.0, MUL, writes=[pb_])
            ts("dve", prm[:, 8:12], par[:, PC[f"rba{l}"]:PC[f"rba{l}"] + 4], -1.0, MUL, reads=[bf("par")], writes=[pb_])
            ts("dve", prm[:, 12:16], par[:, PC[f"rbx{l}"]:PC[f"rbx{l}"] + 4], -1.0, MUL, reads=[bf("par")], writes=[pb_])
            hB_ = [bf(f"rgh{j}") for j in range(4)]
            xB_ = [bf(f"rgx{j}") for j in range(4)]
            gB_ = [bf(f"rgg{j}") for j in range(4)]
            oB_ = [bf(f"rgo{j}") for j in range(4)]
            tB_ = [[bf(f"rgt{j}_{q}") for q in range(7)] for j in range(4)]
            op("dve", V.memset, hst, 0.0, writes=hB_)
            op("dve", V.memset, xin[:, :, 0:3], 0.0, writes=xB_)
            C_G = 0.7978845608028654

            def chunk_gen(j, t0):
                pbx, pbg = 2 * j, 2 * j + 1
                ubs = [uB[kc][t0 // 512] for kc in range(8)]
                xc, r_, i_, a_, m_, hs_, g_ = (tmpv[:, j, q, :] for q in range(7))
                bxc, br, bi, ba, bm, bh, bg_ = tB_[j]
                for kc in range(8):
                    mm(PS[pbx][:, 0:T], wv[:, kc, j * 128:(j + 1) * 128], uT[:, kc, t0:t0 + T], kc == 0, kc == 7,
                       reads=[wB[slot]] + ubs, writes=[psB[pbx]])
                for kc in range(8):
                    mm(PS[pbg][:, 0:T], wv[:, kc, 512 + j * 128:512 + (j + 1) * 128], uT[:, kc, t0:t0 + T], kc == 0, kc == 7,
                       reads=[wB[slot]] + ubs, writes=[psB[pbg]])
                yield
                act(xin[:, j, 3:3 + T], PS[pbx][:, 0:T], AF.Copy, reads=[psB[pbx]], writes=[xB_[j]])
                act(gate[:, j, :], PS[pbg][:, 0:T], AF.Copy, reads=[psB[pbg]], writes=[gB_[j]])
                ts("dve", xc, xin[:, j, 0:T], col(f"rcw{l}", j), MUL, col(f"rcb{l}", j), ADD, reads=[xB_[j], bf("par")], writes=[bxc])
                for tap in range(1, 4):
                    stt(xc, xin[:, j, tap:tap + T], col(f"rcw{l}", tap * 4 + j), xc, MUL, ADD, reads=[xB_[j], bf("par")], writes=[bxc])
                op("pool", P.tensor_copy, out=xin[:, j, 0:3], in_=xin[:, j, T:T + 3], reads=[xB_[j]], writes=[xB_[j]])
                mm(PS[pbx][:, 0:T], rgw[:, j, :], xc, True, True, reads=[bf("rgw"), bxc], writes=[psB[pbx]])
                mm(PS[pbg][:, 0:T], rgw[:, 4 + j, :], xc, True, True, reads=[bf("rgw"), bxc], writes=[psB[pbg]])
                act(g_, gate[:, j, :], AF.Square, reads=[gB_[j]], writes=[bg_])
                ts("pool", g_, g_, 0.044715, MUL, 1.0, ADD, writes=[bg_])
                tt("pool", g_, g_, gate[:, j, :], MUL, reads=[gB_[j]], writes=[bg_])
                yield
                sigm(r_, PS[pbx][:, 0:T], [psB[pbx], pb_], br, nbias=prm[:, 8 + j:9 + j])
                sigm(i_, PS[pbg][:, 0:T], [psB[pbg], pb_], bi, nbias=prm[:, 12 + j:13 + j])
                yield
                act(a_, r_, AF.Exp, reads=[pb_, br], writes=[ba], scale=prm[:, j:j + 1])
                act(m_, r_, AF.Exp, reads=[pb_, br], writes=[bm], scale=prm[:, 4 + j:5 + j])
                act(m_, m_, AF.Ln, reads=[bf("cst")], writes=[bm], scale=-1.0, bias=cst[:, 1:2])
                act(m_, m_, AF.Exp, writes=[bm], scale=0.5)
                tt("pool", i_, i_, xc, MUL, reads=[bxc], writes=[bi])
                yield
                tt("dve", i_, i_, m_, MUL, reads=[bm], writes=[bi])
                op("dve", V.tensor_tensor_scan, out=hs_, data0=a_, data1=i_, initial=hst[:, j:j + 1], op0=MUL, op1=ADD,
                   reads=[hB_[j], ba, bi], writes=[bh])
                op("pool", P.tensor_copy, out=hst[:, j:j + 1], in_=hs_[:, T - 1:T], reads=[bh], writes=[hB_[j]])
                sigm(g_, g_, [bg_], bg_, scale=2.0 * C_G)
                yield
                tt("pool", g_, g_, gate[:, j, :], MUL, reads=[gB_[j]], writes=[bg_])
                tt("dve", ob[:, j, :], hs_, g_, MUL, reads=[bh, bg_], writes=[oB_[j]])

            for tq in range(S // T):
                t0 = tq * T
                alive = [chunk_gen(j, t0) for j in range(4)]
                while alive:
                    nxt = []
                    for g2_ in alive:
                        try:
                            next(g2_)
                            nxt.append(g2_)
                        except StopIteration:
                            pass
                    alive = nxt
                accum_wout(l, s_, t0, T, lambda kc: ob[:, kc, :], 4, lambda kc, oc: wo[:, kc, oc * 128:(oc + 1) * 128], [wB[slot]] + oB_)

        def rope_tables(s_):
            sc.barrier()
            pi_ = arena[:, 0:2048].bitcast(I32)
            ang = arena[:, 2048:4096]
            kf = arena[:, 4096:6144]
            ki = arena[:, 6144:8192].bitcast(I32)
            r0 = arena[:, 8192:10240]
            m_ = arena[:, 10240:12288]
            rb = bf("ropew")
            dma("sp", nc.sync.dma_start, out=pi_[0:32, :], in_=pos_d[s_, 0:32, :], writes=[rb])
            op("dve", V.tensor_copy, out=ang[0:32, :], in_=pi_[0:32, :], reads=[rb], writes=[rb])
            ts("dve", ang[0:32, :], ang[0:32, :], par[0:32, PC["invf"]:PC["invf"] + 1], MUL, reads=[bf("par")], writes=[rb])
            TWO_PI = 2.0 * np.pi
            C1 = 6.28125
            C2 = TWO_PI - C1
            for which, shift in ((0, np.pi / 2.0), (1, 0.0)):
                ts("dve", kf[0:32, :], ang[0:32, :], float(shift), ADD, 1.0 / TWO_PI, MUL, writes=[rb])
                op("dve", V.tensor_copy, out=ki[0:32, :], in_=kf[0:32, :], writes=[rb])
                op("dve", V.tensor_copy, out=kf[0:32, :], in_=ki[0:32, :], writes=[rb])
                ts("dve", r0[0:32, :], ang[0:32, :], float(shift), ADD, writes=[rb])
                stt(r0[0:32, :], kf[0:32, :], -C1, r0[0:32, :], MUL, ADD, writes=[rb])
                stt(r0[0:32, :], kf[0:32, :], -C2, r0[0:32, :], MUL, ADD, writes=[rb])
                ts("dve", m_[0:32, :], r0[0:32, :], float(np.pi), ALU.is_gt, -TWO_PI, MUL, writes=[rb])
                tt("dve", r0[0:32, :], r0[0:32, :], m_[0:32, :], ADD, writes=[rb])
                ts("dve", m_[0:32, :], r0[0:32, :], float(-np.pi), ALU.is_lt, TWO_PI, MUL, writes=[rb])
                tt("dve", r0[0:32, :], r0[0:32, :], m_[0:32, :], ADD, writes=[rb])
                ts("dve", r0[0:32, :], r0[0:32, :], 3.1415925, MIN, -3.1415925, MAX, writes=[rb])
                act(m_[0:32, :], r0[0:32, :], AF.Sin, writes=[rb])
                dma("sp", nc.sync.dma_start, out=rope_d[s_, which], in_=m_[0:32, :], reads=[rb], writes=[bf(f"roped{which}")])

        def mla_group(l, s_):
            sc.barrier()
            T = 512
            slot = prefetched.pop(("mla", l)) if ("mla", l) in prefetched else load_mla(l)
            if G["mlp"]:
                prefetched[("mlp0", l)] = load_mlp0(l)
            W = wsl[slot]
            wm = W[:, 0:3328].rearrange("p (k n) -> p k n", k=8)
            wks = W[:, 3328:3584].rearrange("p (k n) -> p k n", k=8)
            wom = W[:, 3584:7680].rearrange("p (h n) -> p h n", h=4)
            wq = W[:, 7680:8448].rearrange("p (k n) -> p k n", k=2)
            wqs = W[:, 8448:8704].rearrange("p (k h n) -> p k h n", k=2, h=4)
            wkv = W[:, 8704:9216]
            wb_ = wB[slot]
            op("act", A.mul, out=wks[:, :, 0:16], in_=wm[:, :, 400:416], mul=-1.0, reads=[wb_], writes=[wb_])
            op("act", A.copy, out=wks[:, :, 16:32], in_=wm[:, :, 384:400], reads=[wb_], writes=[wb_])
            wq4 = wq.rearrange("p k (h n) -> p k h n", h=4)
            for kc in range(2):
                op("act", A.mul, out=wqs[:, kc, :, 0:16], in_=wq4[:, kc, :, 80:96], mul=-1.0, reads=[wb_], writes=[wb_])
                op("act", A.copy, out=wqs[:, kc, :, 16:32], in_=wq4[:, kc, :, 64:80], reads=[wb_], writes=[wb_])
            a0 = 0
            Kc = arena[:, 0:4096].bitcast(BF16).rearrange("p (h n) -> p h n", h=4)
            Vc = arena[:, 4096:6176].bitcast(BF16).rearrange("p (j h n) -> p j h n", j=16, h=4)
            o = 6176

            def take(n):
                nonlocal o
                r = arena[:, o:o + n]
                o += n
                return r
            qn = take(512).bitcast(BF16).rearrange("p (k n) -> p k n", k=2)
            kvn = take(256).bitcast(BF16)
            Qh = take(256).bitcast(BF16)
            ET = take(512).bitcast(BF16).rearrange("p (k n) -> p k n", k=2)
            cos_ = take(512)
            sin_ = take(512)
            rs = take(512)
            sq = take(256).bitcast(BF16)
            t1 = take(512)
            t2 = take(512)
            osb = take(512)
            rec = take(512)
            oT = take(1024).bitcast(BF16).rearrange("p (h n) -> p h n", h=4)
            assert o <= 13000
            kb, vb_ = bf("mlaK"), bf("mlaV")
            bsq, brs, bqn, bkvn, bt1, bt2 = (bf(f"mla_{n}") for n in ("sq", "rs", "qn", "kvn", "t1", "t2"))
            bosb, brec, bQ = [bf("mla_osb0"), bf("mla_osb1")], [bf("mla_rec0"), bf("mla_rec1")], [bf("mla_Q0"), bf("mla_Q1")]
            boT = [bf(f"mla_oT{h}") for h in range(4)]
            op("pool", P.memset, Vc[:, :, :, 64:65], 1.0, writes=[vb_])
            SCALE = float(96 ** -0.5)
            Qh2 = [Qh, sq]
            osb2 = [osb, t1]
            rec2 = [rec, t2]
            for t in range(4):
                tsl = slice(t * T, (t + 1) * T)
                ub = [uB[kc][t] for kc in range(8)]
                dma("sp", nc.sync.dma_start, out=cos_[64:96, :], in_=rope_d[s_, 0, :, tsl], reads=[bf("roped0")], writes=[bf("mlacs")])
                dma("sp", nc.sync.dma_start, out=sin_[64:96, :], in_=rope_d[s_, 1, :, tsl], reads=[bf("roped1")], writes=[bf("mlacs")])
                for c in range(2):
                    for kc in range(8):
                        mm(PS[c][:], wm[:, kc, c * 128:(c + 1) * 128], uT[:, kc, tsl], kc == 0, kc == 7, reads=[wb_] + ub, writes=[psB[c]])
                for kc in range(8):
                    mm(PS[2][:], wm[:, kc, 256:384], uT[:, kc, tsl], kc == 0, kc == 7, reads=[wb_] + ub, writes=[psB[2]])
                for kc in range(8):
                    mm(PS[3][64:96, :], wm[:, kc, 384:416], uT[:, kc, tsl], kc == 0, kc == 7, reads=[wb_] + ub, writes=[psB[3]])
                for kc in range(8):
                    mm(PS[4][64:96, :], wks[:, kc, :], uT[:, kc, tsl], kc == 0, kc == 7, reads=[wb_] + ub, writes=[psB[4]])
                for c in range(2):
                    act(sq, PS[c][:], AF.Square, reads=[psB[c]], writes=[bsq] + bQ)
                    mm(PS[5][:], ones_bf[:], sq, c == 0, c == 1, reads=[bsq, bf("ones_bf")], writes=[psB[5]])
                act(rs, PS[5][:], AF.Ln, reads=[psB[5], bf("cst")], writes=[brs], scale=1.0 / 256, bias=cst[:, 0:1])
                act(rs, rs, AF.Exp, writes=[brs], scale=-0.5)
                for c in range(2):
                    stt(qn[:, c, :], PS[c][:], col(f"mqg{l}", c), rs, MUL, MUL, reads=[psB[c], bf("par"), brs], writes=[bqn])
                act(sq, PS[2][:], AF.Square, reads=[psB[2]], writes=[bsq] + bQ)
                mm(PS[5][:], ones_bf[:], sq, True, True, reads=[bsq, bf("ones_bf")], writes=[psB[5]])
                act(rs, PS[5][:], AF.Ln, reads=[psB[5], bf("cst")], writes=[brs], scale=1.0 / 128, bias=cst[:, 0:1])
                act(rs, rs, AF.Exp, writes=[brs], scale=-0.5)
                stt(kvn, PS[2][:], col(f"mkvg{l}", 0), rs, MUL, MUL, reads=[psB[2], bf("par"), brs], writes=[bkvn])
                tt("dve", t1[64:96, :], PS[3][64:96, :], cos_[64:96, :], MUL, reads=[psB[3], bf("mlacs")], writes=[bt1] + bosb)
                tt("dve", t2[64:96, :], PS[4][64:96, :], sin_[64:96, :], MUL, reads=[psB[4], bf("mlacs")], writes=[bt2] + brec)
                for h in range(4):
                    tt("pool", Kc[64:96, h, tsl], t1[64:96, :], t2[64:96, :], ADD, reads=[bt1, bt2], writes=[kb])
                for h in range(4):
                    pb = 6 + h % 2
                    mm(PS[pb][0:64, :], wkv[:, h * 128:h * 128 + 64], kvn, True, True, reads=[wb_, bkvn], writes=[psB[pb]])
                    act(Kc[0:64, h, tsl], PS[pb][0:64, :], AF.Copy, reads=[psB[pb]], writes=[kb])
                for kt in range(4):
                    pb = 6 + kt % 2
                    for h in range(4):
                        mm(PS[pb][:, h * 64:(h + 1) * 64], kvn[:, kt * 128:(kt + 1) * 128], wkv[:, h * 128 + 64:h * 128 + 128], True, True,
                           reads=[wb_, bkvn], writes=[psB[pb]])
                    op("dve", V.tensor_copy, out=Vc[:, t * 4 + kt, :, 0:64], in_=PS[pb][:, 0:256].rearrange("p (h n) -> p h n", h=4),
                       reads=[psB[pb]], writes=[vb_])
                def stage_a(h):
                    hh = h % 2
                    Qh_ = Qh2[hh]
                    pa, pbq = 0, 1
                    for c in range(2):
                        mm(PS[pa][0:96, :], wq[:, c, h * 96:(h + 1) * 96], qn[:, c, :], c == 0, c == 1, reads=[wb_, bqn], writes=[psB[pa]])
                    for c in range(2):
                        mm(PS[pbq][64:96, :], wqs[:, c, h, :], qn[:, c, :], c == 0, c == 1, reads=[wb_, bqn], writes=[psB[pbq]])
                    qb = bQ[hh]
                    act(Qh_[0:64, :], PS[pa][0:64, :], AF.Copy, reads=[psB[pa]], writes=[qb, bsq] if hh else [qb])
                    rsb = rs.bitcast(BF16)
                    scr = rsb[64:96, hh * 512:(hh + 1) * 512]
                    tt("dve", scr, PS[pbq][64:96, :], sin_[64:96, :], MUL, reads=[psB[pbq], bf("mlacs")], writes=[brs])
                    tt("dve", Qh_[64:96, :], PS[pa][64:96, :], cos_[64:96, :], MUL, reads=[psB[pa], bf("mlacs")], writes=[qb, bsq] if hh else [qb])
                    tt("pool", Qh_[64:96, :], Qh_[64:96, :], scr, ADD, reads=[brs], writes=[qb])

                def stage_b(h):
                    hh = h % 2
                    Qh_, osb_, rec_ = Qh2[hh], osb2[hh], rec2[hh]
                    qb = bQ[hh]
                    nkt = 4 * t + 4
                    pacc = 4 + hh
                    def qk_exp(j):
                        r = j - 4 * t
                        qs = 0 if r < 0 else r * 128
                        n = T - qs
                        pst = 2 + j % 2
                        e = j % 2
                        eb = bf(f"mlaE{e}")
                        mm(PS[pst][:, 0:n], Kc[0:96, h, j * 128:(j + 1) * 128], Qh_[0:96, qs:T], True, True, reads=[kb, qb], writes=[psB[pst]])
                        act(ET[:, e, 0:n], PS[pst][:, 0:n], AF.Exp, reads=[psB[pst]], writes=[eb], scale=SCALE)
                        if r >= 0:
                            op("pool", P.memset, ET[64:128, e, 0:64], 0.0, writes=[eb])

                    def pv(j):
                        r = j - 4 * t
                        qs = 0 if r < 0 else r * 128
                        n = T - qs
                        e = j % 2
                        mm(PS[pacc][0:65, qs:T], Vc[:, j, h, :], ET[:, e, 0:n], j == 0, j == nkt - 1, reads=[vb_, bf(f"mlaE{e}")], writes=[psB[pacc]])
                    qk_exp(0)
                    for j in range(nkt):
                        if j + 1 < nkt:
                            qk_exp(j + 1)
                        pv(j)
                    wr = [bosb[hh]] + ([bt1] if hh else [])
                    op("dve", V.tensor_copy, out=osb_[0:65, :], in_=PS[pacc][0:65, :], reads=[psB[pacc]], writes=wr)
                    wr2 = [brec[hh]] + ([bt2] if hh else [])
                    op("dve", V.reciprocal, out=rec_[64:65, :], in_=osb_[64:65, :], reads=[bosb[hh]], writes=wr2)
                    pbc = 6 + hh
                    mm(PS[pbc][0:64, :], ones_f[64:65, 0:64], rec_[64:65, :], True, True, reads=[brec[hh], bf("ones_f")], writes=[psB[pbc]])
                    tt("dve", oT[0:64, h, :], osb_[0:64, :], PS[pbc][0:64, :], MUL, reads=[bosb[hh], psB[pbc]], writes=[boT[h]])
                stage_a(0)
                for h in range(4):
                    if h + 1 < 4:
                        stage_a(h + 1)
                    stage_b(h)
                accum_wout(l, s_, t * T, T, lambda kc: oT[0:64, kc, :], 4, lambda kc, oc: wom[0:64, kc, oc * 128:(oc + 1) * 128], [wb_] + boT)

        def gdn_group(l, s_):
            sc.barrier()
            T = 256
            slot = next_slot()
            wg = wsl[slot][:, 0:8256].rearrange("p (k n) -> p k n", k=8)
            wo = wsl[slot][:, 8256:10304].rearrange("p (k n) -> p k n", k=2)
            wb_ = wB[slot]
            dma("pool", P.dma_start, out=wg, in_=win_d[l, :, :, 0:1032], writes=[wb_])
            dma("pool", P.dma_start, out=wo, in_=wout_d[l, :, 0:2, :], writes=[wb_])
            if G["rg"]:
                prefetched[("rg", l)] = load_rg(l)
            o = [0]

            def take(n):
                r = arena[:, o[0]:o[0] + n]
                o[0] += n
                return r
            Ubd, BD1, Mls, MuiT = take(128), take(128), take(128), take(128)
            xin = take(6 * 259).rearrange("p (j n) -> p j n", j=6)
            yq = take(6 * T).rearrange("p (j n) -> p j n", j=6)
            zs = take(2 * T).rearrange("p (j n) -> p j n", j=2)
            tmp = take(4 * T).rearrange("p (j n) -> p j n", j=4)
            Sst = take(128).rearrange("p (c n) -> p c n", c=2)
            gt_ = take(40)
            HM = [[take(128) for _ in range(9)] for _ in range(4)]
            HS = [[take(64) for _ in range(5)] for _ in range(4)]
            otok = take(256).rearrange("p (h n) -> p h n", h=4)
            on_ = take(256)
            oT = take(T).bitcast(BF16).rearrange("p (c n) -> p c n", c=2)
            assert o[0] <= 13000, o[0]
            cb, sb_ = bf("gdnc"), [bf(f"gdnS{h}") for h in range(4)]
            gB = bf("gdngate")
            op("pool", P.memset, BD1, 0.0, writes=[cb])
            op("pool", P.memset, BD1[0:64, 0:64], 1.0, writes=[cb])
            op("pool", P.memset, BD1[64:128, 64:128], 1.0, writes=[cb])
            op("pool", P.affine_select, out=Ubd, in_=BD1, pattern=[[1, 128]], compare_op=ALU.is_ge, fill=0.0, base=0, channel_multiplier=-1,
               reads=[cb], writes=[cb])
            op("pool", P.tensor_copy, out=MuiT, in_=Ubd, reads=[cb], writes=[cb])
            op("pool", P.affine_select, out=Mls, in_=BD1, pattern=[[-1, 128]], compare_op=ALU.is_gt, fill=0.0, base=0, channel_multiplier=1,
               reads=[cb], writes=[cb])
            op("pool", P.memset, Sst, 0.0, writes=sb_)
            xb = [bf(f"gdnx{j}") for j in range(6)]
            yb = [bf(f"gdny{j}") for j in range(6)]
            zb = [bf(f"gdnz{j}") for j in range(2)]
            tb4 = [bf(f"gdnt{j}") for j in range(4)]
            op("pool", P.memset, xin[:, :, 0:3], 0.0, writes=xb)
            nA = gt_[:, 36:40]
            act(nA, par[:, PC[f"galog{l}"]:PC[f"galog{l}"] + 4], AF.Exp, reads=[bf("par")], writes=[cb])
            ts("dve", nA, nA, -1.0, MUL, writes=[cb])
            otB, onB, oTB = bf("gdnotok"), bf("gdnon"), bf("gdnoT")

            def head_gen(h, bs):
                c2, po = h // 2, (h % 2) * 64
                hp = slice(po, po + 64)
                pa, pbk = 2 * h, 2 * h + 1
                qT, kT, vT = yq[hp, c2, bs], yq[hp, 2 + c2, bs], yq[hp, 4 + c2, bs]
                rq, rk, rv = [yb[c2]], [yb[2 + c2]], [yb[4 + c2]]
                NQ, E1, E2, eGr, A_, B_, Y_, wT, qdT = HM[h]
                ktail, kbg, vbt, u_, vnew = HS[h]
                bN = [bf(f"gh{h}_{i}") for i in range(9)]
                bNQ, bE1, bE2, beG, bA, bB, bY, bwT, bqd = bN
                bkt, bkb, bvb, bu, bvn = [bf(f"gs{h}_{i}") for i in range(5)]
                ts("dve", NQ, ones_f[:], gt_[:, h:h + 1], MUL, -1.0, MUL, reads=[bf("ones_f"), gB], writes=[bNQ])
                mm(PS[pa][:, 0:128], NQ, Ubd, True, True, reads=[bNQ, cb], writes=[psB[pa]])
                yield
                ts("dve", E2, PS[pa][:, 0:128], gt_[:, 12 + h:13 + h], ADD, 0.0, MIN, reads=[psB[pa], gB], writes=[bE2])
                act(E2, E2, AF.Exp, writes=[bE2])
                tt("pool", E2, E2, Mls, MUL, reads=[cb], writes=[bE2])
                ts("dve", E1, PS[pa][:, 0:128], gt_[:, 12 + h:13 + h], ADD, 0.0, MAX, reads=[psB[pa], gB], writes=[bE1])
                act(E1, E1, AF.Exp, writes=[bE1], scale=-1.0)
                tt("pool", E1, E1, MuiT, MUL, reads=[cb], writes=[bE1])
                act(eGr, PS[pa][:, 0:128], AF.Exp, reads=[psB[pa]], writes=[beG], scale=-1.0)
                mm(PS[pbk][:, 0:128], kT, kT, True, True, reads=rk, writes=[psB[pbk]])
                yield
                stt(A_, PS[pbk][:, 0:128], gt_[:, 8 + h:9 + h], E2, MUL, MUL, reads=[psB[pbk], gB, bE2], writes=[bA])
                mm(PS[pa][:, 0:128], kT, qT, True, True, reads=rk + rq, writes=[psB[pa]])
                op("pe", PE.transpose, PS[pbk][:, 0:128], A_, ident[:], reads=[bA, bf("ident")], writes=[psB[pbk]])
                yield
                tt("dve", NQ, PS[pa][:, 0:128], E1, MUL, reads=[psB[pa], bE1], writes=[bNQ])
                op("act", A.copy, out=B_, in_=PS[pbk][:, 0:128], reads=[psB[pbk]], writes=[bB])
                tt("dve", Y_, B_, ident[:], ADD, reads=[bB, bf("ident")], writes=[bY])
                yield
                Am, Bm, An, Bn = A_, B_, E2, E1
                bAm, bBm, bAn, bBn = bA, bB, bE2, bE1
                for lvl in range(5):
                    mm(PS[pa][:, 0:128], Bm, Am, True, True, reads=[bAm, bBm], writes=[psB[pa]])
                    if lvl < 4:
                        mm(PS[pbk][:, 0:128], Am, Bm, True, True, reads=[bAm, bBm], writes=[psB[pbk]])
                    yield
                    op("act", A.copy, out=An, in_=PS[pa][:, 0:128], reads=[psB[pa]], writes=[bAn])
                    if lvl < 4:
                        op("dve", V.tensor_copy, out=Bn, in_=PS[pbk][:, 0:128], reads=[psB[pbk]], writes=[bBn])
                    mm(PS[pa][:, 0:128], An, Y_, True, True, reads=[bAn, bY], writes=[psB[pa]])
                    yield
                    tt("dve", Y_, Y_, PS[pa][:, 0:128], ADD, reads=[psB[pa]], writes=[bY])
                    Am, Bm, An, Bn = An, Bn, Am, Bm
                    bAm, bBm, bAn, bBn = bAn, bBn, bAm, bBm
                idh = ident[hp, po:po + 64]
                op("pe", PE.transpose, PS[pbk][:, 0:64], kT, idh, reads=rk + [bf("ident")], writes=[psB[pbk]])
                op("pe", PE.transpose, PS[pa][:, 0:64], vT, idh, reads=rv + [bf("ident")], writes=[psB[pa]])
                yield
                ts("dve", ktail, PS[pbk][:, 0:64], gt_[:, 20 + h:21 + h], MUL, reads=[psB[pbk], gB], writes=[bkt])
                ts("dve", kbg, PS[pbk][:, 0:64], gt_[:, 24 + h:25 + h], MUL, reads=[psB[pbk], gB], writes=[bkb])
                ts("dve", vbt, PS[pa][:, 0:64], gt_[:, 4 + h:5 + h], MUL, reads=[psB[pa], gB], writes=[bvb])
                mm(PS[pbk][:, 0:64], Y_, vbt, True, True, reads=[bY, bvb], writes=[psB[pbk]])
                mm(PS[pa][hp, 0:128], kbg, Y_, True, True, reads=[bkb, bY], writes=[psB[pa]])
                tt("dve", qdT[hp, :], qT, eGr[hp, :], MUL, reads=rq + [beG], writes=[bqd])
                yield
                op("act", A.copy, out=u_, in_=PS[pbk][:, 0:64], reads=[psB[pbk]], writes=[bu])
                op("act", A.copy, out=wT[hp, :], in_=PS[pa][hp, 0:128], reads=[psB[pa]], writes=[bwT])
                Sh = Sst[hp, c2, :]
                for c in range(2):
                    tc = slice(c * 64, c * 64 + 64)
                    mm(PS[pa][tc, 0:64], wT[hp, tc], Sh, True, True, reads=[bwT, sb_[h]], writes=[psB[pa]])
                    mm(PS[pbk][tc, 0:64], qdT[hp, tc], Sh, True, True, reads=[bqd, sb_[h]], writes=[psB[pbk]])
                    yield
                    tt("dve", vnew[tc, :], u_[tc, :], PS[pa][tc, 0:64], SUB, reads=[psB[pa], bu], writes=[bvn])
                    op("act", A.copy, out=otok[tc, h, :], in_=PS[pbk][tc, 0:64], reads=[psB[pbk]], writes=[otB])
                    mm(PS[pa][tc, 0:64], NQ[tc, tc], vnew[tc, :], True, True, reads=[bNQ, bvn], writes=[psB[pa]])
                    mm(PS[pbk][hp, 0:64], ktail[tc, :], vnew[tc, :], True, True, reads=[bkt, bvn], writes=[psB[pbk]])
                    yield
                    tt("dve", otok[tc, h, :], otok[tc, h, :], PS[pa][tc, 0:64], ADD, reads=[psB[pa]], writes=[otB])
                    stt(Sh, Sh, eGr[hp, c * 64 + 63:c * 64 + 64], PS[pbk][hp, 0:64], MUL, ADD, reads=[psB[pbk], beG], writes=[sb_[h]])
                    yield

            for tq in range(lim.get('gdn_tiles', S // T)):
                t0 = tq * T
                ub = [uB[kc][t0 // 512] for kc in range(8)]
                for j in range(8):
                    pb = j % 4
                    for kc in range(8):
                        mm(PS[pb][:, 0:T], wg[:, kc, j * 128:(j + 1) * 128], uT[:, kc, t0:t0 + T], kc == 0, kc == 7, reads=[wb_] + ub, writes=[psB[pb]])
                    tj = tmp[:, j % 2, :]
                    tjb = tb4[j % 2]
                    if j < 6:
                        act(xin[:, j, 3:3 + T], PS[pb][:, 0:T], AF.Copy, reads=[psB[pb]], writes=[xb[j]])
                        y = yq[:, j, :]
                        ts("dve", y, xin[:, j, 0:T], col(f"gcw{l}", j), MUL, reads=[xb[j], bf("par")], writes=[yb[j]])
                        for tap in range(1, 4):
                            stt(y, xin[:, j, tap:tap + T], col(f"gcw{l}", tap * 6 + j), y, MUL, ADD, reads=[xb[j], bf("par")], writes=[yb[j]])
                        op("pool", P.tensor_copy, out=xin[:, j, 0:3], in_=xin[:, j, T:T + 3], reads=[xb[j]], writes=[xb[j]])
                        sigm(tj, y, [yb[j]], tjb)
                        tt("dve", y, y, tj, MUL, reads=[tjb], writes=[yb[j]])
                        if j < 4:
                            t2_ = tmp[:, 2 + j % 2, :]
                            t2b = tb4[2 + j % 2]
                            act(t2_.bitcast(BF16)[:, 0:T], y, AF.Square, reads=[yb[j]], writes=[t2b])
                            mm(PS[4 + j][:, 0:T], bones_bf[:], t2_.bitcast(BF16)[:, 0:T], True, True, reads=[t2b, bf("bones_bf")], writes=[psB[4 + j]])
                            act(tj, PS[4 + j][:, 0:T], AF.Ln, reads=[psB[4 + j], bf("cst")], writes=[tjb], bias=cst[:, 0:1])
                            if j < 2:
                                act(tj, tj, AF.Exp, reads=[bf("cst")], writes=[tjb], scale=-0.5, bias=cst[:, 2:3])
                            else:
                                act(tj, tj, AF.Exp, writes=[tjb], scale=-0.5)
                            tt("dve", y, y, tj, MUL, reads=[tjb], writes=[yb[j]])
                    else:
                        zc = zs[:, j - 6, :]
                        act(zc, PS[pb][:, 0:T], AF.Copy, reads=[psB[pb]], writes=[zb[j - 6]])
                        sigm(tj, zc, [zb[j - 6]], tjb)
                        tt("dve", zc, zc, tj, MUL, reads=[tjb], writes=[zb[j - 6]])
                for blk in range(2):
                    b0 = blk * 128
                    bs = slice(b0, b0 + 128)
                    for kc in range(8):
                        mm(PS[5][:, 0:8], uT[:, kc, t0 + b0:t0 + b0 + 128], wg[:, kc, 1024:1032], kc == 0, kc == 7, reads=[wb_] + ub, writes=[psB[5]])
                    gcol, beta, nbeta, Gc, eG, eTl, bg, sp = (gt_[:, 4 * i:4 * i + 4] for i in range(8))
                    tt("dve", sp, PS[5][:, 0:4], par[:, PC[f"gdtb{l}"]:PC[f"gdtb{l}"] + 4], ADD, reads=[psB[5], bf("par")], writes=[gB])
                    act(sp, sp, AF.Exp, writes=[gB])
                    act(sp, sp, AF.Ln, reads=[bf("cst")], writes=[gB], bias=cst[:, 1:2])
                    tt("dve", gcol, sp, nA, MUL, reads=[cb], writes=[gB])
                    sigm(beta, PS[5][:, 4:8], [psB[5]], gB)
                    ts("dve", nbeta, beta, -1.0, MUL, writes=[gB])
                    mm(PS[6][:, 0:4], Ubd, gcol, True, True, reads=[cb, gB], writes=[psB[6]])
                    mm(PS[6][:, 4:8], BD1, gcol, True, True, reads=[cb, gB], writes=[psB[6]])
                    op("dve", V.tensor_copy, out=Gc, in_=PS[6][:, 0:4], reads=[psB[6]], writes=[gB])
                    act(eG, PS[6][:, 0:4], AF.Exp, reads=[psB[6]], writes=[gB])
                    tt("dve", eTl, PS[6][:, 4:8], Gc, SUB, reads=[psB[6]], writes=[gB])
                    act(eTl, eTl, AF.Exp, writes=[gB])
                    tt("dve", bg, beta, eG, MUL, writes=[gB])
                    for grp in lim.get('gdn_il', [[0, 1, 2, 3]]):
                      alive = [head_gen(h, bs) for h in grp]
                      nsteps = 0
                      while alive and nsteps < lim.get('gdn_steps', 999):
                          nsteps += 1
                          nxt = []
                          for g_ in alive:
                              try:
                                  next(g_)
                                  nxt.append(g_)
                              except StopIteration:
                                  pass
                          alive = nxt
                    ss = gt_[:, 32:36]
                    for h in range(4):
                        tt("pool", on_[:, h * 64:(h + 1) * 64], otok[:, h, :], otok[:, h, :], MUL, reads=[otB], writes=[onB])
                    op("dve", V.tensor_reduce, out=ss, in_=on_.rearrange("p (h n) -> p h n", h=4), axis=mybir.AxisListType.X, op=ADD,
                       reads=[onB], writes=[gB])
                    act(ss, ss, AF.Ln, reads=[bf("cst")], writes=[gB], scale=1.0 / 64, bias=cst[:, 0:1])
                    act(ss, ss, AF.Exp, writes=[gB], scale=-0.5)
                    for h in range(4):
                        ts("dve", on_[:, h * 64:(h + 1) * 64], otok[:, h, :], gt_[:, 32 + h:33 + h], MUL, reads=[otB, gB], writes=[onB])
                    for c2 in range(2):
                        op("pe", PE.transpose, PS[2 + c2][:, 0:128], on_[:, c2 * 128:(c2 + 1) * 128], ident[:], reads=[onB, bf("ident")], writes=[psB[2 + c2]])
                        stt(oT[:, c2, bs], PS[2 + c2][:, 0:128], col(f"gng{l}"), zs[:, c2, bs], MUL, MUL, reads=[psB[2 + c2], bf("par"), zb[c2]], writes=[oTB])
                accum_wout(l, s_, t0, T, lambda kc: oT[:, kc, :], 2, lambda kc, oc: wo[:, kc, oc * 128:(oc + 1) * 128], [wb_, oTB])

        for s_ in range(lim.get('nseq', NSEQ)):
            for k in range(8):
                for t in range(4):
                    dma("sp", nc.sync.dma_start, out=hT[:, k, t * 512:(t + 1) * 512], in_=xT[s_, k * 128:(k + 1) * 128, t * 512:(t + 1) * 512],
                        writes=[hB[k][t]])
            if G["mla"]:
                rope_tables(s_)
            for l in range(lim.get('depth', DEPTH)):
                modnorm(l, s_, 0)
                if G["gdn"]:
                    gdn_group(l, s_)
                if G["rg"]:
                    rg_group(l, s_)
                if G["mla"]:
                    mla_group(l, s_)
                if G["mlp"]:
                    modnorm(l, s_, 1)
                    mlp(l, s_)
            final_norm(s_)
        if debug_stop == "mod":
            dma("sp", nc.sync.dma_start, out=yT[1, 0:128, 0:DEPTH * 48 * NSEQ], in_=modT[:].rearrange("p l c s -> p (l c s)"),
                reads=[bf("modT")], writes=[bf("yout")])
        sc.final_wait("sp", [bf("yout")])
        build_nc.last_counts = dict(sc.cnt)

        with nc.Block() as block:
            @block.tensor
            def _(e):
                sc.replay("pe", e)

            @block.vector
            def _(e):
                sc.replay("dve", e)

            @block.scalar
            def _(e):
                sc.replay("act", e)

            @block.gpsimd
            def _(e):
                sc.replay("pool", e)

            @block.sync
            def _(e):
                sc.replay("sp", e)
    return nc


def _pack_params(inp, core):
    par = np.zeros((128, NPAR), np.float32)
    f = lambda v: np.asarray(v, np.float32)
    cm = lambda v: f(v).reshape(-1, 128).T
    for s_ in range(NSEQ):
        par[:, PC["c"] + 8 * s_:PC["c"] + 8 * (s_ + 1)] = cm(inp["c"][core * NSEQ + s_])
    par[:, PC["fng"]:PC["fng"] + 8] = cm(inp["final_norm_g"])
    invf = (np.float32(10000.0) ** (-np.arange(0, 32, 2, dtype=np.float32) / np.float32(32))).astype(np.float32)
    par[:, PC["invf"]] = np.tile(invf, 8)
    for l in range(DEPTH):
        par[:, PC[f"bmod{l}"]:PC[f"bmod{l}"] + 48] = cm(inp["b_mod"][l])
        par[:, PC[f"nmg{l}"]:PC[f"nmg{l}"] + 8] = cm(inp["norm_mix_g"][l])
        par[:, PC[f"nfg{l}"]:PC[f"nfg{l}"] + 8] = cm(inp["norm_mlp_g"][l])
        gcw = f(inp["gdn_conv_w"][l])
        for tap in range(4):
            par[:, PC[f"gcw{l}"] + tap * 6:PC[f"gcw{l}"] + tap * 6 + 6] = cm(gcw[tap])
        par[:, PC[f"galog{l}"]:PC[f"galog{l}"] + 4] = f(inp["gdn_a_log"][l])[None, :]
        par[:, PC[f"gdtb{l}"]:PC[f"gdtb{l}"] + 4] = f(inp["gdn_dt_bias"][l])[None, :]
        par[:, PC[f"gng{l}"]] = np.tile(f(inp["gdn_norm_g"][l]), 2)
        rcw = f(inp["rg_conv_w"][l])
        for tap in range(4):
            par[:, PC[f"rcw{l}"] + tap * 4:PC[f"rcw{l}"] + tap * 4 + 4] = cm(rcw[tap])
        par[:, PC[f"rcb{l}"]:PC[f"rcb{l}"] + 4] = cm(inp["rg_conv_b"][l])
        par[:, PC[f"rba{l}"]:PC[f"rba{l}"] + 4] = cm(inp["rg_b_a"][l])
        par[:, PC[f"rbx{l}"]:PC[f"rbx{l}"] + 4] = cm(inp["rg_b_x"][l])
        par[:, PC[f"rlam{l}"]:PC[f"rlam{l}"] + 4] = cm(inp["rg_lambda"][l])
        par[:, PC[f"mqg{l}"]:PC[f"mqg{l}"] + 2] = cm(inp["mla_q_norm_g"][l])
        par[:, PC[f"mkvg{l}"]:PC[f"mkvg{l}"] + 1] = cm(inp["mla_kv_norm_g"][l])
    return par


def _shared_weights(inp):
    f = lambda v: np.ascontiguousarray(np.asarray(v, np.float32))
    kp = lambda w, k: f(w.reshape(w.shape[0], k, 128, w.shape[2]).transpose(0, 2, 1, 3))
    sh = {}
    sh["wmod"] = kp(inp["w_mod"], 8)
    sh["win"] = kp(inp["w_in"], 8)
    sh["wout"] = kp(inp["w_out"], 8)
    wo = np.asarray(inp["w_out"], np.float32)
    sh["woutm"] = f(wo[:, 768:1024, :].reshape(DEPTH, 4, 64, 1024).transpose(0, 2, 1, 3))
    sh["w1"] = kp(inp["w_mlp_in"], 8)
    sh["w2"] = kp(inp["w_mlp_out"], 32)
    rgw = np.zeros((DEPTH, 128, 8, 128), np.float32)
    for l in range(DEPTH):
        for wi_, nm in enumerate(("rg_w_a", "rg_w_x")):
            w = np.asarray(inp[nm][l], np.float32)
            for j in range(4):
                rgw[l, 0:64, wi_ * 4 + j, 0:64] = w[2 * j]
                rgw[l, 64:128, wi_ * 4 + j, 64:128] = w[2 * j + 1]
    sh["rgw"] = rgw
    sh["wqb"] = kp(inp["mla_w_qb"], 2)
    sh["wkvb"] = f(inp["mla_w_kvb"])
    return sh


def make_in_maps(inp, cores):
    sh = _shared_weights(inp)
    maps = []
    for c in cores:
        m = dict(sh)
        xs = np.asarray(inp["x"][c * NSEQ:(c + 1) * NSEQ], np.float32)
        m["xT"] = np.ascontiguousarray(xs.transpose(0, 2, 1))
        m["par"] = _pack_params(inp, c)
        pos = np.asarray(inp["positions"][c * NSEQ:(c + 1) * NSEQ], np.int32)
        m["pos"] = np.ascontiguousarray(np.broadcast_to(pos[:, None, :], (NSEQ, 128, S)))
        maps.append(m)
    return maps


def kernel(**inputs):
    nc = build_nc()
    cores = list(range(8))
    in_maps = make_in_maps(inputs, cores)
    res = run_bass_kernel_spmd(nc, in_maps, core_ids=cores)
    out = np.empty((16, S, D), np.float32)
    for c in cores:
        y = res.results[c]["yT"]
        for s_ in range(NSEQ):
            out[c * NSEQ + s_] = y[s_].T
    return out
```

```python
import numpy as np
import concourse.bass as bass
import concourse.mybir as mybir
from concourse.bass_utils import run_bass_kernel_spmd

F32 = mybir.dt.float32
BF16 = mybir.dt.bfloat16
I32 = mybir.dt.int32
AF = mybir.ActivationFunctionType
ALU = mybir.AluOpType

S = 2048
D = 1024
NSEQ = 2
DEPTH = 2
EPS = 1e-6
GROUPS = {"gdn": True, "rg": True, "mla": True, "mlp": True}

PC = {}
_off = 0


def _pc(name, n):
    global _off
    PC[name] = _off
    _off += n


_pc("c", 16)
_pc("fng", 8)
_pc("invf", 1)
for _l in range(DEPTH):
    _pc(f"bmod{_l}", 48)
    _pc(f"nmg{_l}", 8)
    _pc(f"nfg{_l}", 8)
    _pc(f"gcw{_l}", 24)
    _pc(f"galog{_l}", 4)
    _pc(f"gdtb{_l}", 4)
    _pc(f"gng{_l}", 1)
    _pc(f"rcw{_l}", 16)
    _pc(f"rcb{_l}", 4)
    _pc(f"rba{_l}", 4)
    _pc(f"rbx{_l}", 4)
    _pc(f"rlam{_l}", 4)
    _pc(f"mqg{_l}", 2)
    _pc(f"mkvg{_l}", 1)
NPAR = _off


class Buf:
    __slots__ = ("w", "r", "excl")

    def __init__(self):
        self.w = []
        self.r = {}
        self.excl = False


class Sched:
    EPOCH = 20000

    def __init__(self, nc, sems):
        self.nc = nc
        self.sems = sems
        self.eng = {"pe": nc.tensor, "dve": nc.vector, "act": nc.scalar, "pool": nc.gpsimd, "sp": nc.sync}
        self.q = {e: [] for e in self.eng}
        self.cnt = {e: 0 for e in self.eng}
        self.esem = {e: [] for e in self.eng}
        self.seen = {e: {} for e in self.eng}
        self.pending = {e: [] for e in self.eng}
        self.last = {e: None for e in self.eng}
        self.dsem = {}
        self.dcnt = {}
        self.dval = {}
        for qn in ("sp", "pool"):
            self.dsem[qn] = [self._newsem() for _ in range(6)]
            self.dval[qn] = [0] * 6
            self.dcnt[qn] = 0

    def _newsem(self):
        return self.sems.pop()

    def _deps(self, eng, reads, writes, extra=(), same=True):
        evs = list(extra)
        need = {}
        for b in reads:
            evs.extend(b.w)
            if b.excl:
                evs.extend(b.r.values())
        for b in writes:
            evs.extend(b.w)
            evs.extend(b.r.values())
        if self.pending[eng]:
            evs.extend(self.pending[eng])
            self.pending[eng] = []
        for (src, sem, val) in evs:
            if src == eng:
                if eng == "pe" or not same or sem is not (self.esem[eng][-1] if self.esem[eng] else None):
                    continue
                if self.cnt[eng] % self.EPOCH - (val - 1) > 5:
                    continue
                k = id(sem)
                if k not in need or need[k][1] < val:
                    need[k] = (sem, val)
                continue
            k = id(sem)
            if self.seen[eng].get(k, 0) >= val:
                continue
            if k not in need or need[k][1] < val:
                need[k] = (sem, val)
        for k, (sem, val) in need.items():
            self.seen[eng][k] = val
        return list(need.values())

    mute = False

    def op(self, eng, f, *args, reads=(), writes=(), **kw):
        if self.mute:
            return
        fn = (lambda: f(*args, **kw))
        waits = self._deps(eng, reads, writes)
        idx = self.cnt[eng]
        self.cnt[eng] += 1
        ep, v = idx // self.EPOCH, idx % self.EPOCH + 1
        while len(self.esem[eng]) <= ep:
            self.esem[eng].append(self._newsem())
        sem = self.esem[eng][ep]
        self.q[eng].append((waits, fn, sem, 1))
        ev = (eng, sem, v)
        self.last[eng] = ev
        for b in reads:
            b.r[eng] = ev
        for b in writes:
            b.w = [ev]
            b.r = {}

    def dma(self, qn, f, *args, reads=(), writes=(), **kw):
        if self.mute:
            return None
        fn = (lambda: f(*args, **kw))
        i = self.dcnt[qn] % len(self.dsem[qn])
        self.dcnt[qn] += 1
        sem = self.dsem[qn][i]
        prev = ("dma" + qn, sem, self.dval[qn][i])
        waits = self._deps(qn, reads, writes, extra=[prev] if prev[2] > 0 else [])
        self.dval[qn][i] += 16
        self.q[qn].append((waits, fn, sem, 16))
        ev = ("dma" + qn, sem, self.dval[qn][i])
        for b in reads:
            b.r["dma" + qn + str(i)] = ev
        for b in writes:
            b.w = [e for e in b.w if e[0].startswith("dma")][-40:] + [ev]
            b.r = {}
        return ev

    def barrier(self):
        evs = [ev for ev in self.last.values() if ev is not None]
        for qn in self.dsem:
            for i, sem in enumerate(self.dsem[qn]):
                if self.dval[qn][i] > 0:
                    evs.append(("dma" + qn, sem, self.dval[qn][i]))
        for e in self.eng:
            self.pending[e] = list(evs)

    def final_wait(self, eng, bufs):
        waits = self._deps(eng, bufs, bufs)
        self.q[eng].append((waits, None, None, 0))

    def replay(self, eng, h):
        for waits, fn, sem, inc in self.q[eng]:
            for (s, v) in waits:
                h.wait_ge(s, v)
            if fn is not None:
                fn().then_inc(sem, inc)


def build_nc(groups=None, debug_stop=None, lim=None):
    lim = lim or {}
    G = dict(GROUPS)
    if groups:
        G.update(groups)
    nc = bass.Bass("TRN2", target_bir_lowering=False)
    dt_ = nc.dram_tensor
    xT = dt_("xT", [NSEQ, D, S], F32, kind="ExternalInput").ap()
    par_d = dt_("par", [128, NPAR], F32, kind="ExternalInput").ap()
    pos_d = dt_("pos", [NSEQ, 128, S], I32, kind="ExternalInput").ap()
    wmod_d = dt_("wmod", [DEPTH, 128, 8, 6144], F32, kind="ExternalInput").ap()
    win_d = dt_("win", [DEPTH, 128, 8, 2472], F32, kind="ExternalInput").ap()
    wout_d = dt_("wout", [DEPTH, 128, 8, 1024], F32, kind="ExternalInput").ap()
    woutm_d = dt_("woutm", [DEPTH, 64, 4, 1024], F32, kind="ExternalInput").ap()
    w1_d = dt_("w1", [DEPTH, 128, 8, 4096], F32, kind="ExternalInput").ap()
    w2_d = dt_("w2", [DEPTH, 128, 32, 1024], F32, kind="ExternalInput").ap()
    rgw_d = dt_("rgw", [DEPTH, 128, 8, 128], F32, kind="ExternalInput").ap()
    wqb_d = dt_("wqb", [DEPTH, 128, 2, 384], F32, kind="ExternalInput").ap()
    wkvb_d = dt_("wkvb", [DEPTH, 128, 512], F32, kind="ExternalInput").ap()
    yT = dt_("yT", [NSEQ, D, S], F32, kind="ExternalOutput").ap()
    rope_d = dt_("ropescr", [NSEQ, 2, 32, S], F32, kind="Internal").ap()

    from contextlib import ExitStack
    es = ExitStack()
    sb = lambda n, shp, d=F32: es.enter_context(nc.sbuf_tensor(n, shp, d))
    ps = lambda n: es.enter_context(nc.psum_tensor(n, [128, 512], F32))
    with es:
        sems = [es.enter_context(nc.semaphore(f"s{i}")) for i in range(40)]
        sc = Sched(nc, sems)
        op, dma = sc.op, sc.dma
        V, A, P, PE = nc.vector, nc.scalar, nc.gpsimd, nc.tensor

        hT = sb("hT", [128, 8, S])
        uT = sb("uT", [128, 8, S], BF16)
        par = sb("par_sb", [128, NPAR])
        cst = sb("cst", [128, 8])
        ident = sb("ident", [128, 128])
        ones_bf = sb("ones_bf", [128, 128], BF16)
        ones_f = sb("ones_f", [128, 128])
        bones_bf = sb("bones_bf", [128, 128], BF16)
        modT = sb("modT", [128, DEPTH, 48, NSEQ])
        gmix = sb("gmix", [128, 4, 8])
        cact = sb("cact", [128, 8, NSEQ], BF16)
        ctmp = sb("ctmp", [128, 16])
        wsl = [sb("wslA", [128, 12288], BF16), sb("wslB", [128, 12288], BF16)]
        arena = sb("arena", [128, 13000])
        PS = [ps(f"ps{i}") for i in range(8)]

        B = {}

        def bf(name):
            if name not in B:
                B[name] = Buf()
            return B[name]

        hB = [[bf(f"h{k}_{t}") for t in range(4)] for k in range(8)]
        uB = [[bf(f"u{k}_{t}") for t in range(4)] for k in range(8)]
        psB = [bf(f"ps{i}") for i in range(8)]
        for b_ in psB:
            b_.excl = True
        wB = [bf("wslA"), bf("wslB")]

        def col(name, j=0):
            c0 = PC[name] + j
            return par[:, c0:c0 + 1]

        op("pool", P.memset, cst[:, 0:1], EPS, writes=[bf("cst")])
        op("pool", P.memset, cst[:, 1:2], 1.0, writes=[bf("cst")])
        op("pool", P.memset, cst[:, 2:3], float(np.log(0.125)), writes=[bf("cst")])
        op("pool", P.memset, cst[:, 3:4], 0.0, writes=[bf("cst")])
        op("pool", P.memset, ones_f[:], 1.0, writes=[bf("ones_f")])
        op("pool", P.memset, ones_bf[:], 1.0, writes=[bf("ones_bf")])
        op("pool", P.memset, bones_bf[:], 0.0, writes=[bf("bones_bf")])
        op("pool", P.memset, bones_bf[0:64, 0:64], 1.0, writes=[bf("bones_bf")])
        op("pool", P.memset, bones_bf[64:128, 64:128], 1.0, writes=[bf("bones_bf")])
        op("pool", P.affine_select, out=ident[:], in_=ones_f[:], pattern=[[-1, 128]], compare_op=ALU.is_equal,
           fill=0.0, base=0, channel_multiplier=1, reads=[bf("ones_f")], writes=[bf("ident")])
        dma("sp", nc.sync.dma_start, out=par[:], in_=par_d, writes=[bf("par")])

        def col(name, j=0):
            c0 = PC[name] + j
            return par[:, c0:c0 + 1]

        def act(out, in_, func, reads=(), writes=(), **kw):
            op("act", A.activation, out=out, in_=in_, func=func, reads=reads, writes=writes, **kw)

        def tt(eng, out, in0, in1, o, reads=(), writes=()):
            op(eng, (V if eng == "dve" else P).tensor_tensor, out=out, in0=in0, in1=in1, op=o, reads=reads, writes=writes)

        def ts(eng, out, in0, s1, o0, s2=None, o1=None, reads=(), writes=()):
            kw = {} if o1 is None else {"op1": o1}
            op(eng, (V if eng == "dve" else P).tensor_scalar, out=out, in0=in0, scalar1=s1, scalar2=s2, op0=o0, reads=reads, writes=writes, **kw)

        def stt(out, in0, scalar, in1, o0, o1, reads=(), writes=()):
            op("dve", V.scalar_tensor_tensor, out=out, in0=in0, scalar=scalar, in1=in1, op0=o0, op1=o1, reads=reads, writes=writes)

        def mm(out, lhsT, rhs, start, stop, reads=(), writes=()):
            op("pe", PE.matmul, out, lhsT, rhs, start=start, stop=stop, reads=reads, writes=writes)

        def sigm(out, in_, reads, wb, scale=1.0, nbias=None):
            kw = {} if nbias is None else {"bias": nbias}
            act(out, in_, AF.Exp, reads=reads, writes=[wb], scale=-scale, **kw)
            act(out, out, AF.Ln, reads=[bf("cst")], writes=[wb], bias=cst[:, 1:2])
            act(out, out, AF.Exp, writes=[wb], scale=-1.0)

        MUL, ADD, SUB, MAX, MIN = ALU.mult, ALU.add, ALU.subtract, ALU.max, ALU.min

        cc = par[:, PC["c"]:PC["c"] + 16]
        sigm(ctmp[:], cc, [bf("par")], bf("ctmp"))
        for s_ in range(NSEQ):
            tt("dve", cact[:, :, s_], ctmp[:, s_ * 8:(s_ + 1) * 8], par[:, PC["c"] + s_ * 8:PC["c"] + (s_ + 1) * 8], MUL,
               reads=[bf("ctmp"), bf("par")], writes=[bf("cact")])
        wcount = [0]

        def next_slot():
            s = wcount[0] % 2
            wcount[0] += 1
            return s

        for l in range(DEPTH):
            for pc_ in range(12):
                slot = next_slot()
                wv = wsl[slot][:, 0:4096].rearrange("p (k n) -> p k n", k=8)
                dma("pool", P.dma_start, out=wv, in_=wmod_d[l, :, :, pc_ * 512:(pc_ + 1) * 512], writes=[wB[slot]])
                for oc in range(4):
                    pb = (pc_ * 4 + oc) % 8
                    for kc in range(8):
                        mm(PS[pb][:, 0:NSEQ], wv[:, kc, oc * 128:(oc + 1) * 128], cact[:, kc, :], kc == 0, kc == 7,
                           reads=[wB[slot], bf("cact")], writes=[psB[pb]])
                    ch = pc_ * 4 + oc
                    ts("dve", modT[:, l, ch, :], PS[pb][:, 0:NSEQ], col(f"bmod{l}", ch), ADD, reads=[psB[pb], bf("par")], writes=[bf("modT")])

        def norm_stats(t, nchunk_src, scale):
            sq = arena[:, 0:1024].bitcast(BF16).rearrange("p (b n) -> p b n", b=4)
            rs = arena[:, 1024:2048].rearrange("p (b n) -> p b n", b=2)
            tsl = slice(t * 512, (t + 1) * 512)
            pb = t % 2
            for k in range(8):
                j = (t * 8 + k) % 4
                act(sq[:, j, :], hT[:, k, tsl], AF.Square, reads=[hB[k][t]], writes=[bf(f"nsq{j}")])
                mm(PS[pb][:], ones_bf[:], sq[:, j, :], k == 0, k == 7, reads=[bf(f"nsq{j}"), bf("ones_bf")], writes=[psB[pb]])
            rb = bf(f"nrs{t % 2}")
            act(rs[:, t % 2, :], PS[pb][:], AF.Ln, reads=[psB[pb], bf("cst")], writes=[rb], scale=1.0 / D, bias=cst[:, 0:1])
            act(rs[:, t % 2, :], rs[:, t % 2, :], AF.Exp, writes=[rb], scale=-0.5)
            return rs[:, t % 2, :], rb

        def modnorm(l, s_, which):
            sc.barrier()
            gname = f"nmg{l}" if which == 0 else f"nfg{l}"
            sh0, sc0 = (0, 8) if which == 0 else (24, 32)
            stt(gmix[:, which, :], modT[:, l, sc0:sc0 + 8, s_], 1.0, par[:, PC[gname]:PC[gname] + 8], ADD, MUL,
                reads=[bf("modT"), bf("par")], writes=[bf("gmix")])
            tm = arena[:, 2048:3072].rearrange("p (b n) -> p b n", b=2)
            for t in range(4):
                tsl = slice(t * 512, (t + 1) * 512)
                rs, rb = norm_stats(t, 8, 1.0 / D)
                for k in range(8):
                    tb = bf(f"ntm{k % 2}")
                    tt("dve", tm[:, k % 2, :], hT[:, k, tsl], rs, MUL, reads=[hB[k][t], rb], writes=[tb])
                    act(uT[:, k, tsl], tm[:, k % 2, :], AF.Identity, reads=[tb, bf("gmix"), bf("modT")], writes=[uB[k][t]],
                        scale=gmix[:, which, k:k + 1], bias=modT[:, l, sh0 + k, s_:s_ + 1])

        def final_norm(s_):
            ot = arena[:, 2048:4096].rearrange("p (b n) -> p b n", b=4)
            for t in range(4):
                tsl = slice(t * 512, (t + 1) * 512)
                rs, rb = norm_stats(t, 8, 1.0 / D)
                for k in range(8):
                    ob = bf(f"fot{k % 4}")
                    stt(ot[:, k % 4, :], hT[:, k, tsl], col("fng", k), rs, MUL, MUL, reads=[hB[k][t], rb, bf("par")], writes=[ob])
                    dma("sp", nc.sync.dma_start, out=yT[s_, k * 128:(k + 1) * 128, tsl], in_=ot[:, k % 4, :], reads=[ob], writes=[bf("yout")])

        prefetched = {}

        def load_mlp0(l):
            slot = next_slot()
            w1v = wsl[slot][:, 0:4096].rearrange("p (k n) -> p k n", k=8)
            w2v = wsl[slot][:, 4096:8192].rearrange("p (k n) -> p k n", k=4)
            dma("pool", P.dma_start, out=w1v, in_=w1_d[l, :, :, 0:512], writes=[wB[slot]])
            dma("pool", P.dma_start, out=w2v, in_=w2_d[l, :, 0:4, :], writes=[wB[slot]])
            return (slot, w1v, w2v)

        def load_rg(l):
            slot = next_slot()
            wv = wsl[slot][:, 0:8192].rearrange("p (k n) -> p k n", k=8)
            wo = wsl[slot][:, 8192:12288].rearrange("p (k n) -> p k n", k=4)
            dma("pool", P.dma_start, out=wv, in_=win_d[l, :, :, 1032:2056], writes=[wB[slot]])
            dma("pool", P.dma_start, out=wo, in_=wout_d[l, :, 2:6, :], writes=[wB[slot]])
            return (slot, wv, wo)

        def load_mla(l):
            slot = next_slot()
            W = wsl[slot]
            wm = W[:, 0:3328].rearrange("p (k n) -> p k n", k=8)
            wom = W[:, 3584:7680].rearrange("p (h n) -> p h n", h=4)
            wq = W[:, 7680:8448].rearrange("p (k n) -> p k n", k=2)
            wkv = W[:, 8704:9216]
            dma("pool", P.dma_start, out=wm, in_=win_d[l, :, :, 2056:2472], writes=[wB[slot]])
            dma("pool", P.dma_start, out=wom[0:64], in_=woutm_d[l], writes=[wB[slot]])
            dma("pool", P.dma_start, out=wq, in_=wqb_d[l], writes=[wB[slot]])
            dma("pool", P.dma_start, out=wkv, in_=wkvb_d[l], writes=[wB[slot]])
            return slot

        def mlp(l, s_):
            fbuf = arena[:, 4096:6144].bitcast(BF16).rearrange("p (b n) -> p b n", b=8)
            rl = arena[:, 6144:8192].rearrange("p (b n) -> p b n", b=4)
            views = {}
            if ("mlp0", l) in prefetched:
                views[0] = prefetched.pop(("mlp0", l))
            cnt = [0]

            def load(fc):
                slot = next_slot()
                w1v = wsl[slot][:, 0:4096].rearrange("p (k n) -> p k n", k=8)
                w2v = wsl[slot][:, 4096:8192].rearrange("p (k n) -> p k n", k=4)
                dma("pool", P.dma_start, out=w1v, in_=w1_d[l, :, :, fc * 512:(fc + 1) * 512], writes=[wB[slot]])
                dma("pool", P.dma_start, out=w2v, in_=w2_d[l, :, fc * 4:(fc + 1) * 4, :], writes=[wB[slot]])
                views[fc] = (slot, w1v, w2v)

            def w1_stage(i):
                fc, t = divmod(i, 4)
                if fc not in views:
                    load(fc)
                if t == 1 and fc + 1 < 8 and (fc + 1) not in views:
                    load(fc + 1)
                slot, w1v, w2v = views[fc]
                tsl = slice(t * 512, (t + 1) * 512)
                par_ = i % 2
                for fs in range(4):
                    pb = cnt[0] % 4
                    cnt[0] += 1
                    for kc in range(8):
                        mm(PS[pb][:], w1v[:, kc, fs * 128:(fs + 1) * 128], uT[:, kc, tsl], kc == 0, kc == 7,
                           reads=[wB[slot], uB[kc][t]], writes=[psB[pb]])
                    rb = bf(f"rl{pb}")
                    fb = bf(f"f1_{par_}_{fs}")
                    act(rl[:, pb, :], PS[pb][:], AF.Relu, reads=[psB[pb]], writes=[rb])
                    tt("dve" if fs % 2 == 0 else "pool", fbuf[:, par_ * 4 + fs, :], rl[:, pb, :], rl[:, pb, :], MUL, reads=[rb], writes=[fb])

            def w2_stage(i):
                fc, t = divmod(i, 4)
                slot, w1v, w2v = views[fc]
                tsl = slice(t * 512, (t + 1) * 512)
                par_ = i % 2
                for oc in range(8):
                    pb = 4 + oc % 4
                    for fs in range(4):
                        mm(PS[pb][:], w2v[:, fs, oc * 128:(oc + 1) * 128], fbuf[:, par_ * 4 + fs, :], fs == 0, fs == 3,
                           reads=[wB[slot], bf(f"f1_{par_}_{fs}")], writes=[psB[pb]])
                    stt(hT[:, oc, tsl], PS[pb][:], modT[:, l, 40 + oc, s_:s_ + 1], hT[:, oc, tsl], MUL, ADD,
                        reads=[psB[pb], bf("modT"), hB[oc][t]], writes=[hB[oc][t]])

            NI = 32
            w1_stage(0)
            for i in range(NI):
                if i + 1 < NI:
                    w1_stage(i + 1)
                w2_stage(i)

        def accum_wout(l, s_, t0, n, rhs_fn, nk, wv_fn, rd):
            tq = t0 // 512
            for oc in range(8):
                pb = 4 + oc % 4
                for kc in range(nk):
                    mm(PS[pb][:, 0:n], wv_fn(kc, oc), rhs_fn(kc), kc == 0, kc == nk - 1, reads=rd, writes=[psB[pb]])
                stt(hT[:, oc, t0:t0 + n], PS[pb][:, 0:n], modT[:, l, 16 + oc, s_:s_ + 1], hT[:, oc, t0:t0 + n], MUL, ADD,
                    reads=[psB[pb], bf("modT"), hB[oc][tq]], writes=[hB[oc][tq]])

        def rg_group(l, s_):
            sc.barrier()
            T = 256
            slot, wv, wo = prefetched.pop(("rg", l)) if ("rg", l) in prefetched else load_rg(l)
            if G["mla"]:
                prefetched[("mla", l)] = load_mla(l)
            rgw = arena[:, 0:1024].rearrange("p (k n) -> p k n", k=8)
            dma("sp", nc.sync.dma_start, out=rgw, in_=rgw_d[l], writes=[bf("rgw")])
            prm = arena[:, 1024:1040]
            hst = arena[:, 1040:1044]
            xin = arena[:, 1048:1048 + 4 * 259].rearrange("p (j n) -> p j n", j=4)
            o0 = 1048 + 4 * 259 + 4
            gate = arena[:, o0:o0 + 4 * T].rearrange("p (j n) -> p j n", j=4)
            o1 = o0 + 4 * T
            tmpv = arena[:, o1:o1 + 28 * T].rearrange("p (j q n) -> p j q n", j=4, q=7)
            o2 = o1 + 28 * T
            ob = arena[:, o2:o2 + 2 * T].bitcast(BF16).rearrange("p (j n) -> p j n", j=4)
            assert o2 + 2 * T <= 13000
            pb_ = bf("rgprm")
            lam = par[:, PC[f"rlam{l}"]:PC[f"rlam{l}"] + 4]
            act(prm[:, 0:4], lam, AF.Exp, reads=[bf("par")], writes=[pb_], scale=-1.0)
            act(prm[:, 0:4], prm[:, 0:4], AF.Ln, reads=[bf("cst")], writes=[pb_], bias=cst[:, 1:2])
            ts("dve", prm[:, 4:8], prm[:, 0:4], -16.0, MUL, writes=[pb_])
            ts("dve", prm[:, 0:4], prm[:, 0:4], -8.0, MUL, writes=[pb_])
            ts("dve", prm[:, 8:12], par[:, PC[f"rba{l}"]:PC[f"rba{l}"] + 4], -1.0, MUL, reads=[bf("par")], writes=[pb_])
            ts("dve", prm[:, 12:16], par[:, PC[f"rbx{l}"]:PC[f"rbx{l}"] + 4], -1.0, MUL, reads=[bf("par")], writes=[pb_])
            hB_ = [bf(f"rgh{j}") for j in range(4)]
            xB_ = [bf(f"rgx{j}") for j in range(4)]
            gB_ = [bf(f"rgg{j}") for j in range(4)]
            oB_ = [bf(f"rgo{j}") for j in range(4)]
            tB_ = [[bf(f"rgt{j}_{q}") for q in range(7)] for j in range(4)]
            op("dve", V.memset, hst, 0.0, writes=hB_)
            op("dve", V.memset, xin[:, :, 0:3], 0.0, writes=xB_)
            C_G = 0.7978845608028654

            def chunk_gen(j, t0):
                pbx, pbg = 2 * j, 2 * j + 1
                ubs = [uB[kc][t0 // 512] for kc in range(8)]
                xc, r_, i_, a_, m_, hs_, g_ = (tmpv[:, j, q, :] for q in range(7))
                bxc, br, bi, ba, bm, bh, bg_ = tB_[j]
                for kc in range(8):
                    mm(PS[pbx][:, 0:T], wv[:, kc, j * 128:(j + 1) * 128], uT[:, kc, t0:t0 + T], kc == 0, kc == 7,
                       reads=[wB[slot]] + ubs, writes=[psB[pbx]])
                for kc in range(8):
                    mm(PS[pbg][:, 0:T], wv[:, kc, 512 + j * 128:512 + (j + 1) * 128], uT[:, kc, t0:t0 + T], kc == 0, kc == 7,
                       reads=[wB[slot]] + ubs, writes=[psB[pbg]])
                yield
                act(xin[:, j, 3:3 + T], PS[pbx][:, 0:T], AF.Copy, reads=[psB[pbx]], writes=[xB_[j]])
                act(gate[:, j, :], PS[pbg][:, 0:T], AF.Copy, reads=[psB[pbg]], writes=[gB_[j]])
                ts("dve", xc, xin[:, j, 0:T], col(f"rcw{l}", j), MUL, col(f"rcb{l}", j), ADD, reads=[xB_[j], bf("par")], writes=[bxc])
                for tap in range(1, 4):
                    stt(xc, xin[:, j, tap:tap + T], col(f"rcw{l}", tap * 4 + j), xc, MUL, ADD, reads=[xB_[j], bf("par")], writes=[bxc])
                op("pool", P.tensor_copy, out=xin[:, j, 0:3], in_=xin[:, j, T:T + 3], reads=[xB_[j]], writes=[xB_[j]])
                mm(PS[pbx][:, 0:T], rgw[:, j, :], xc, True, True, reads=[bf("rgw"), bxc], writes=[psB[pbx]])
                mm(PS[pbg][:, 0:T], rgw[:, 4 + j, :], xc, True, True, reads=[bf("rgw"), bxc], writes=[psB[pbg]])
                act(g_, gate[:, j, :], AF.Square, reads=[gB_[j]], writes=[bg_])
                ts("pool", g_, g_, 0.044715, MUL, 1.0, ADD, writes=[bg_])
                tt("pool", g_, g_, gate[:, j, :], MUL, reads=[gB_[j]], writes=[bg_])
                yield
                sigm(r_, PS[pbx][:, 0:T], [psB[pbx], pb_], br, nbias=prm[:, 8 + j:9 + j])
                sigm(i_, PS[pbg][:, 0:T], [psB[pbg], pb_], bi, nbias=prm[:, 12 + j:13 + j])
                yield
                act(a_, r_, AF.Exp, reads=[pb_, br], writes=[ba], scale=prm[:, j:j + 1])
                act(m_, r_, AF.Exp, reads=[pb_, br], writes=[bm], scale=prm[:, 4 + j:5 + j])
                act(m_, m_, AF.Ln, reads=[bf("cst")], writes=[bm], scale=-1.0, bias=cst[:, 1:2])
                act(m_, m_, AF.Exp, writes=[bm], scale=0.5)
                tt("pool", i_, i_, xc, MUL, reads=[bxc], writes=[bi])
                yield
                tt("dve", i_, i_, m_, MUL, reads=[bm], writes=[bi])
                op("dve", V.tensor_tensor_scan, out=hs_, data0=a_, data1=i_, initial=hst[:, j:j + 1], op0=MUL, op1=ADD,
                   reads=[hB_[j], ba, bi], writes=[bh])
                op("pool", P.tensor_copy, out=hst[:, j:j + 1], in_=hs_[:, T - 1:T], reads=[bh], writes=[hB_[j]])
                sigm(g_, g_, [bg_], bg_, scale=2.0 * C_G)
                yield
                tt("pool", g_, g_, gate[:, j, :], MUL, reads=[gB_[j]], writes=[bg_])
                tt("dve", ob[:, j, :], hs_, g_, MUL, reads=[bh, bg_], writes=[oB_[j]])

            for tq in range(S // T):
                t0 = tq * T
                alive = [chunk_gen(j, t0) for j in range(4)]
                while alive:
                    nxt = []
                    for g2_ in alive:
                        try:
                            next(g2_)
                            nxt.append(g2_)
                        except StopIteration:
                            pass
                    alive = nxt
                accum_wout(l, s_, t0, T, lambda kc: ob[:, kc, :], 4, lambda kc, oc: wo[:, kc, oc * 128:(oc + 1) * 128], [wB[slot]] + oB_)

        def rope_tables(s_):
            sc.barrier()
            pi_ = arena[:, 0:2048].bitcast(I32)
            ang = arena[:, 2048:4096]
            kf = arena[:, 4096:6144]
            ki = arena[:, 6144:8192].bitcast(I32)
            r0 = arena[:, 8192:10240]
            m_ = arena[:, 10240:12288]
            rb = bf("ropew")
            dma("sp", nc.sync.dma_start, out=pi_[0:32, :], in_=pos_d[s_, 0:32, :], writes=[rb])
            op("dve", V.tensor_copy, out=ang[0:32, :], in_=pi_[0:32, :], reads=[rb], writes=[rb])
            ts("dve", ang[0:32, :], ang[0:32, :], par[0:32, PC["invf"]:PC["invf"] + 1], MUL, reads=[bf("par")], writes=[rb])
            TWO_PI = 2.0 * np.pi
            C1 = 6.28125
            C2 = TWO_PI - C1
            for which, shift in ((0, np.pi / 2.0), (1, 0.0)):
                ts("dve", kf[0:32, :], ang[0:32, :], float(shift), ADD, 1.0 / TWO_PI, MUL, writes=[rb])
                op("dve", V.tensor_copy, out=ki[0:32, :], in_=kf[0:32, :], writes=[rb])
                op("dve", V.tensor_copy, out=kf[0:32, :], in_=ki[0:32, :], writes=[rb])
                ts("dve", r0[0:32, :], ang[0:32, :], float(shift), ADD, writes=[rb])
                stt(r0[0:32, :], kf[0:32, :], -C1, r0[0:32, :], MUL, ADD, writes=[rb])
                stt(r0[0:32, :], kf[0:32, :], -C2, r0[0:32, :], MUL, ADD, writes=[rb])
                ts("dve", m_[0:32, :], r0[0:32, :], float(np.pi), ALU.is_gt, -TWO_PI, MUL, writes=[rb])
                tt("dve", r0[0:32, :], r0[0:32, :], m_[0:32, :], ADD, writes=[rb])
                ts("dve", m_[0:32, :], r0[0:32, :], float(-np.pi), ALU.is_lt, TWO_PI, MUL, writes=[rb])
                tt("dve", r0[0:32, :], r0[0:32, :], m_[0:32, :], ADD, writes=[rb])
                ts("dve", r0[0:32, :], r0[0:32, :], 3.1415925, MIN, -3.1415925, MAX, writes=[rb])
                act(m_[0:32, :], r0[0:32, :], AF.Sin, writes=[rb])
                dma("sp", nc.sync.dma_start, out=rope_d[s_, which], in_=m_[0:32, :], reads=[rb], writes=[bf(f"roped{which}")])

        def mla_group(l, s_):
            sc.barrier()
            T = 512
            slot = prefetched.pop(("mla", l)) if ("mla", l) in prefetched else load_mla(l)
            if G["mlp"]:
                prefetched[("mlp0", l)] = load_mlp0(l)
            W = wsl[slot]
            wm = W[:, 0:3328].rearrange("p (k n) -> p k n", k=8)
            wks = W[:, 3328:3584].rearrange("p (k n) -> p k n", k=8)
            wom = W[:, 3584:7680].rearrange("p (h n) -> p h n", h=4)
            wq = W[:, 7680:8448].rearrange("p (k n) -> p k n", k=2)
            wqs = W[:, 8448:8704].rearrange("p (k h n) -> p k h n", k=2, h=4)
            wkv = W[:, 8704:9216]
            wb_ = wB[slot]
            op("act", A.mul, out=wks[:, :, 0:16], in_=wm[:, :, 400:416], mul=-1.0, reads=[wb_], writes=[wb_])
            op("act", A.copy, out=wks[:, :, 16:32], in_=wm[:, :, 384:400], reads=[wb_], writes=[wb_])
            wq4 = wq.rearrange("p k (h n) -> p k h n", h=4)
            for kc in range(2):
                op("act", A.mul, out=wqs[:, kc, :, 0:16], in_=wq4[:, kc, :, 80:96], mul=-1.0, reads=[wb_], writes=[wb_])
                op("act", A.copy, out=wqs[:, kc, :, 16:32], in_=wq4[:, kc, :, 64:80], reads=[wb_], writes=[wb_])
            a0 = 0
            Kc = arena[:, 0:4096].bitcast(BF16).rearrange("p (h n) -> p h n", h=4)
            Vc = arena[:, 4096:6176].bitcast(BF16).rearrange("p (j h n) -> p j h n", j=16, h=4)
            o = 6176

            def take(n):
                nonlocal o
                r = arena[:, o:o + n]
                o += n
                return r
            qn = take(512).bitcast(BF16).rearrange("p (k n) -> p k n", k=2)
            kvn = take(256).bitcast(BF16)
            Qh = take(256).bitcast(BF16)
            ET = take(512).bitcast(BF16).rearrange("p (k n) -> p k n", k=2)
            cos_ = take(512)
            sin_ = take(512)
            rs = take(512)
            sq = take(256).bitcast(BF16)
            t1 = take(512)
            t2 = take(512)
            osb = take(512)
            rec = take(512)
            oT = take(1024).bitcast(BF16).rearrange("p (h n) -> p h n", h=4)
            assert o <= 13000
            kb, vb_ = bf("mlaK"), bf("mlaV")
            bsq, brs, bqn, bkvn, bt1, bt2 = (bf(f"mla_{n}") for n in ("sq", "rs", "qn", "kvn", "t1", "t2"))
            bosb, brec, bQ = [bf("mla_osb0"), bf("mla_osb1")], [bf("mla_rec0"), bf("mla_rec1")], [bf("mla_Q0"), bf("mla_Q1")]
            boT = [bf(f"mla_oT{h}") for h in range(4)]
            op("pool", P.memset, Vc[:, :, :, 64:65], 1.0, writes=[vb_])
            SCALE = float(96 ** -0.5)
            Qh2 = [Qh, sq]
            osb2 = [osb, t1]
            rec2 = [rec, t2]
            for t in range(4):
                tsl = slice(t * T, (t + 1) * T)
                ub = [uB[kc][t] for kc in range(8)]
                dma("sp", nc.sync.dma_start, out=cos_[64:96, :], in_=rope_d[s_, 0, :, tsl], reads=[bf("roped0")], writes=[bf("mlacs")])
                dma("sp", nc.sync.dma_start, out=sin_[64:96, :], in_=rope_d[s_, 1, :, tsl], reads=[bf("roped1")], writes=[bf("mlacs")])
                for c in range(2):
                    for kc in range(8):
                        mm(PS[c][:], wm[:, kc, c * 128:(c + 1) * 128], uT[:, kc, tsl], kc == 0, kc == 7, reads=[wb_] + ub, writes=[psB[c]])
                for kc in range(8):
                    mm(PS[2][:], wm[:, kc, 256:384], uT[:, kc, tsl], kc == 0, kc == 7, reads=[wb_] + ub, writes=[psB[2]])
                for kc in range(8):
                    mm(PS[3][64:96, :], wm[:, kc, 384:416], uT[:, kc, tsl], kc == 0, kc == 7, reads=[wb_] + ub, writes=[psB[3]])
                for kc in range(8):
                    mm(PS[4][64:96, :], wks[:, kc, :], uT[:, kc, tsl], kc == 0, kc == 7, reads=[wb_] + ub, writes=[psB[4]])
                for c in range(2):
                    act(sq, PS[c][:], AF.Square, reads=[psB[c]], writes=[bsq] + bQ)
                    mm(PS[5][:], ones_bf[:], sq, c == 0, c == 1, reads=[bsq, bf("ones_bf")], writes=[psB[5]])
                act(rs, PS[5][:], AF.Ln, reads=[psB[5], bf("cst")], writes=[brs], scale=1.0 / 256, bias=cst[:, 0:1])
                act(rs, rs, AF.Exp, writes=[brs], scale=-0.5)
                for c in range(2):
                    stt(qn[:, c, :], PS[c][:], col(f"mqg{l}", c), rs, MUL, MUL, reads=[psB[c], bf("par"), brs], writes=[bqn])
                act(sq, PS[2][:], AF.Square, reads=[psB[2]], writes=[bsq] + bQ)
                mm(PS[5][:], ones_bf[:], sq, True, True, reads=[bsq, bf("ones_bf")], writes=[psB[5]])
                act(rs, PS[5][:], AF.Ln, reads=[psB[5], bf("cst")], writes=[brs], scale=1.0 / 128, bias=cst[:, 0:1])
                act(rs, rs, AF.Exp, writes=[brs], scale=-0.5)
                stt(kvn, PS[2][:], col(f"mkvg{l}", 0), rs, MUL, MUL, reads=[psB[2], bf("par"), brs], writes=[bkvn])
                tt("dve", t1[64:96, :], PS[3][64:96, :], cos_[64:96, :], MUL, reads=[psB[3], bf("mlacs")], writes=[bt1] + bosb)
                tt("dve", t2[64:96, :], PS[4][64:96, :], sin_[64:96, :], MUL, reads=[psB[4], bf("mlacs")], writes=[bt2] + brec)
                for h in range(4):
                    tt("pool", Kc[64:96, h, tsl], t1[64:96, :], t2[64:96, :], ADD, reads=[bt1, bt2], writes=[kb])
                for h in range(4):
                    pb = 6 + h % 2
                    mm(PS[pb][0:64, :], wkv[:, h * 128:h * 128 + 64], kvn, True, True, reads=[wb_, bkvn], writes=[psB[pb]])
                    act(Kc[0:64, h, tsl], PS[pb][0:64, :], AF.Copy, reads=[psB[pb]], writes=[kb])
                for kt in range(4):
                    pb = 6 + kt % 2
                    for h in range(4):
                        mm(PS[pb][:, h * 64:(h + 1) * 64], kvn[:, kt * 128:(kt + 1) * 128], wkv[:, h * 128 + 64:h * 128 + 128], True, True,
                           reads=[wb_, bkvn], writes=[psB[pb]])
                    op("dve", V.tensor_copy, out=Vc[:, t * 4 + kt, :, 0:64], in_=PS[pb][:, 0:256].rearrange("p (h n) -> p h n", h=4),
                       reads=[psB[pb]], writes=[vb_])
                def stage_a(h):
                    hh = h % 2
                    Qh_ = Qh2[hh]
                    pa, pbq = 0, 1
                    for c in range(2):
                        mm(PS[pa][0:96, :], wq[:, c, h * 96:(h + 1) * 96], qn[:, c, :], c == 0, c == 1, reads=[wb_, bqn], writes=[psB[pa]])
                    for c in range(2):
                        mm(PS[pbq][64:96, :], wqs[:, c, h, :], qn[:, c, :], c == 0, c == 1, reads=[wb_, bqn], writes=[psB[pbq]])
                    qb = bQ[hh]
                    act(Qh_[0:64, :], PS[pa][0:64, :], AF.Copy, reads=[psB[pa]], writes=[qb, bsq] if hh else [qb])
                    rsb = rs.bitcast(BF16)
                    scr = rsb[64:96, hh * 512:(hh + 1) * 512]
                    tt("dve", scr, PS[pbq][64:96, :], sin_[64:96, :], MUL, reads=[psB[pbq], bf("mlacs")], writes=[brs])
                    tt("dve", Qh_[64:96, :], PS[pa][64:96, :], cos_[64:96, :], MUL, reads=[psB[pa], bf("mlacs")], writes=[qb, bsq] if hh else [qb])
                    tt("pool", Qh_[64:96, :], Qh_[64:96, :], scr, ADD, reads=[brs], writes=[qb])

                def stage_b(h):
                    hh = h % 2
                    Qh_, osb_, rec_ = Qh2[hh], osb2[hh], rec2[hh]
                    qb = bQ[hh]
                    nkt = 4 * t + 4
                    pacc = 4 + hh
                    def qk_exp(j):
                        r = j - 4 * t
                        qs = 0 if r < 0 else r * 128
                        n = T - qs
                        pst = 2 + j % 2
                        e = j % 2
                        eb = bf(f"mlaE{e}")
                        mm(PS[pst][:, 0:n], Kc[0:96, h, j * 128:(j + 1) * 128], Qh_[0:96, qs:T], True, True, reads=[kb, qb], writes=[psB[pst]])
                        act(ET[:, e, 0:n], PS[pst][:, 0:n], AF.Exp, reads=[psB[pst]], writes=[eb], scale=SCALE)
                        if r >= 0:
                            op("pool", P.memset, ET[64:128, e, 0:64], 0.0, writes=[eb])

                    def pv(j):
                        r = j - 4 * t
                        qs = 0 if r < 0 else r * 128
                        n = T - qs
                        e = j % 2
                        mm(PS[pacc][0:65, qs:T], Vc[:, j, h, :], ET[:, e, 0:n], j == 0, j == nkt - 1, reads=[vb_, bf(f"mlaE{e}")], writes=[psB[pacc]])
                    qk_exp(0)
                    for j in range(nkt):
                        if j + 1 < nkt:
                            qk_exp(j + 1)
                        pv(j)
                    wr = [bosb[hh]] + ([bt1] if hh else [])
                    op("dve", V.tensor_copy, out=osb_[0:65, :], in_=PS[pacc][0:65, :], reads=[psB[pacc]], writes=wr)
                    wr2 = [brec[hh]] + ([bt2] if hh else [])
                    op("dve", V.reciprocal, out=rec_[64:65, :], in_=osb_[64:65, :], reads=[bosb[hh]], writes=wr2)
                    pbc = 6 + hh
                    mm(PS[pbc][0:64, :], ones_f[64:65, 0:64], rec_[64:65, :], True, True, reads=[brec[hh], bf("ones_f")], writes=[psB[pbc]])
                    tt("dve", oT[0:64, h, :], osb_[0:64, :], PS[pbc][0:64, :], MUL, reads=[bosb[hh], psB[pbc]], writes=[boT[h]])
                stage_a(0)
                for h in range(4):
                    if h + 1 < 4:
                        stage_a(h + 1)
                    stage_b(h)
                accum_wout(l, s_, t * T, T, lambda kc: oT[0:64, kc, :], 4, lambda kc, oc: wom[0:64, kc, oc * 128:(oc + 1) * 128], [wb_] + boT)

        def gdn_group(l, s_):
            sc.barrier()
            T = 256
            slot = next_slot()
            wg = wsl[slot][:, 0:8256].rearrange("p (k n) -> p k n", k=8)
            wo = wsl[slot][:, 8256:10304].rearrange("p (k n) -> p k n", k=2)
            wb_ = wB[slot]
            dma("pool", P.dma_start, out=wg, in_=win_d[l, :, :, 0:1032], writes=[wb_])
            dma("pool", P.dma_start, out=wo, in_=wout_d[l, :, 0:2, :], writes=[wb_])
            if G["rg"]:
                prefetched[("rg", l)] = load_rg(l)
            o = [0]

            def take(n):
                r = arena[:, o[0]:o[0] + n]
                o[0] += n
                return r
            Ubd, BD1, Mls, MuiT = take(128), take(128), take(128), take(128)
            xin = take(6 * 259).rearrange("p (j n) -> p j n", j=6)
            yq = take(6 * T).rearrange("p (j n) -> p j n", j=6)
            zs = take(2 * T).rearrange("p (j n) -> p j n", j=2)
            tmp = take(4 * T).rearrange("p (j n) -> p j n", j=4)
            Sst = take(128).rearrange("p (c n) -> p c n", c=2)
            gt_ = take(40)
            HM = [[take(128) for _ in range(9)] for _ in range(4)]
            HS = [[take(64) for _ in range(5)] for _ in range(4)]
            otok = take(256).rearrange("p (h n) -> p h n", h=4)
            on_ = take(256)
            oT = take(T).bitcast(BF16).rearrange("p (c n) -> p c n", c=2)
            assert o[0] <= 13000, o[0]
            cb, sb_ = bf("gdnc"), [bf(f"gdnS{h}") for h in range(4)]
            gB = bf("gdngate")
            op("pool", P.memset, BD1, 0.0, writes=[cb])
            op("pool", P.memset, BD1[0:64, 0:64], 1.0, writes=[cb])
            op("pool", P.memset, BD1[64:128, 64:128], 1.0, writes=[cb])
            op("pool", P.affine_select, out=Ubd, in_=BD1, pattern=[[1, 128]], compare_op=ALU.is_ge, fill=0.0, base=0, channel_multiplier=-1,
               reads=[cb], writes=[cb])
            op("pool", P.tensor_copy, out=MuiT, in_=Ubd, reads=[cb], writes=[cb])
            op("pool", P.affine_select, out=Mls, in_=BD1, pattern=[[-1, 128]], compare_op=ALU.is_gt, fill=0.0, base=0, channel_multiplier=1,
               reads=[cb], writes=[cb])
            op("pool", P.memset, Sst, 0.0, writes=sb_)
            xb = [bf(f"gdnx{j}") for j in range(6)]
            yb = [bf(f"gdny{j}") for j in range(6)]
            zb = [bf(f"gdnz{j}") for j in range(2)]
            tb4 = [bf(f"gdnt{j}") for j in range(4)]
            op("pool", P.memset, xin[:, :, 0:3], 0.0, writes=xb)
            nA = gt_[:, 36:40]
            act(nA, par[:, PC[f"galog{l}"]:PC[f"galog{l}"] + 4], AF.Exp, reads=[bf("par")], writes=[cb])
            ts("dve", nA, nA, -1.0, MUL, writes=[cb])
            otB, onB, oTB = bf("gdnotok"), bf("gdnon"), bf("gdnoT")

            def head_gen(h, bs):
                c2, po = h // 2, (h % 2) * 64
                hp = slice(po, po + 64)
                pa, pbk = 2 * h, 2 * h + 1
                qT, kT, vT = yq[hp, c2, bs], yq[hp, 2 + c2, bs], yq[hp, 4 + c2, bs]
                rq, rk, rv = [yb[c2]], [yb[2 + c2]], [yb[4 + c2]]
                NQ, E1, E2, eGr, A_, B_, Y_, wT, qdT = HM[h]
                ktail, kbg, vbt, u_, vnew = HS[h]
                bN = [bf(f"gh{h}_{i}") for i in range(9)]
                bNQ, bE1, bE2, beG, bA, bB, bY, bwT, bqd = bN
                bkt, bkb, bvb, bu, bvn = [bf(f"gs{h}_{i}") for i in range(5)]
                ts("dve", NQ, ones_f[:], gt_[:, h:h + 1], MUL, -1.0, MUL, reads=[bf("ones_f"), gB], writes=[bNQ])
                mm(PS[pa][:, 0:128], NQ, Ubd, True, True, reads=[bNQ, cb], writes=[psB[pa]])
                yield
                ts("dve", E2, PS[pa][:, 0:128], gt_[:, 12 + h:13 + h], ADD, 0.0, MIN, reads=[psB[pa], gB], writes=[bE2])
                act(E2, E2, AF.Exp, writes=[bE2])
                tt("pool", E2, E2, Mls, MUL, reads=[cb], writes=[bE2])
                ts("dve", E1, PS[pa][:, 0:128], gt_[:, 12 + h:13 + h], ADD, 0.0, MAX, reads=[psB[pa], gB], writes=[bE1])
                act(E1, E1, AF.Exp, writes=[bE1], scale=-1.0)
                tt("pool", E1, E1, MuiT, MUL, reads=[cb], writes=[bE1])
                act(eGr, PS[pa][:, 0:128], AF.Exp, reads=[psB[pa]], writes=[beG], scale=-1.0)
                mm(PS[pbk][:, 0:128], kT, kT, True, True, reads=rk, writes=[psB[pbk]])
                yield
                stt(A_, PS[pbk][:, 0:128], gt_[:, 8 + h:9 + h], E2, MUL, MUL, reads=[psB[pbk], gB, bE2], writes=[bA])
                mm(PS[pa][:, 0:128], kT, qT, True, True, reads=rk + rq, writes=[psB[pa]])
                op("pe", PE.transpose, PS[pbk][:, 0:128], A_, ident[:], reads=[bA, bf("ident")], writes=[psB[pbk]])
                yield
                tt("dve", NQ, PS[pa][:, 0:128], E1, MUL, reads=[psB[pa], bE1], writes=[bNQ])
                op("act", A.copy, out=B_, in_=PS[pbk][:, 0:128], reads=[psB[pbk]], writes=[bB])
                tt("dve", Y_, B_, ident[:], ADD, reads=[bB, bf("ident")], writes=[bY])
                yield
                Am, Bm, An, Bn = A_, B_, E2, E1
                bAm, bBm, bAn, bBn = bA, bB, bE2, bE1
                for lvl in range(5):
                    mm(PS[pa][:, 0:128], Bm, Am, True, True, reads=[bAm, bBm], writes=[psB[pa]])
                    if lvl < 4:
                        mm(PS[pbk][:, 0:128], Am, Bm, True, True, reads=[bAm, bBm], writes=[psB[pbk]])
                    yield
                    op("act", A.copy, out=An, in_=PS[pa][:, 0:128], reads=[psB[pa]], writes=[bAn])
                    if lvl < 4:
                        op("dve", V.tensor_copy, out=Bn, in_=PS[pbk][:, 0:128], reads=[psB[pbk]], writes=[bBn])
                    mm(PS[pa][:, 0:128], An, Y_, True, True, reads=[bAn, bY], writes=[psB[pa]])
                    yield
                    tt("dve", Y_, Y_, PS[pa][:, 0:128], ADD, reads=[psB[pa]], writes=[bY])
                    Am, Bm, An, Bn = An, Bn, Am, Bm
                    bAm, bBm, bAn, bBn = bAn, bBn, bAm, bBm
                idh = ident[hp, po:po + 64]
                op("pe", PE.transpose, PS[pbk][:, 0:64], kT, idh, reads=rk + [bf("ident")], writes=[psB[pbk]])
                op("pe", PE.transpose, PS[pa][:, 0:64], vT, idh, reads=rv + [bf("ident")], writes=[psB[pa]])
                yield
                ts("dve", ktail, PS[pbk][:, 0:64], gt_[:, 20 + h:21 + h], MUL, reads=[psB[pbk], gB], writes=[bkt])
                ts("dve", kbg, PS[pbk][:, 0:64], gt_[:, 24 + h:25 + h], MUL, reads=[psB[pbk], gB], writes=[bkb])
                ts("dve", vbt, PS[pa][:, 0:64], gt_[:, 4 + h:5 + h], MUL, reads=[psB[pa], gB], writes=[bvb])
                mm(PS[pbk][:, 0:64], Y_, vbt, True, True, reads=[bY, bvb], writes=[psB[pbk]])
                mm(PS[pa][hp, 0:128], kbg, Y_, True, True, reads=[bkb, bY], writes=[psB[pa]])
                tt("dve", qdT[hp, :], qT, eGr[hp, :], MUL, reads=rq + [beG], writes=[bqd])
                yield
                op("act", A.copy, out=u_, in_=PS[pbk][:, 0:64], reads=[psB[pbk]], writes=[bu])
                op("act", A.copy, out=wT[hp, :], in_=PS[pa][hp, 0:128], reads=[psB[pa]], writes=[bwT])
                Sh = Sst[hp, c2, :]
                for c in range(2):
                    tc = slice(c * 64, c * 64 + 64)
                    mm(PS[pa][tc, 0:64], wT[hp, tc], Sh, True, True, reads=[bwT, sb_[h]], writes=[psB[pa]])
                    mm(PS[pbk][tc, 0:64], qdT[hp, tc], Sh, True, True, reads=[bqd, sb_[h]], writes=[psB[pbk]])
                    yield
                    tt("dve", vnew[tc, :], u_[tc, :], PS[pa][tc, 0:64], SUB, reads=[psB[pa], bu], writes=[bvn])
                    op("act", A.copy, out=otok[tc, h, :], in_=PS[pbk][tc, 0:64], reads=[psB[pbk]], writes=[otB])
                    mm(PS[pa][tc, 0:64], NQ[tc, tc], vnew[tc, :], True, True, reads=[bNQ, bvn], writes=[psB[pa]])
                    mm(PS[pbk][hp, 0:64], ktail[tc, :], vnew[tc, :], True, True, reads=[bkt, bvn], writes=[psB[pbk]])
                    yield
                    tt("dve", otok[tc, h, :], otok[tc, h, :], PS[pa][tc, 0:64], ADD, reads=[psB[pa]], writes=[otB])
                    stt(Sh, Sh, eGr[hp, c * 64 + 63:c * 64 + 64], PS[pbk][hp, 0:64], MUL, ADD, reads=[psB[pbk], beG], writes=[sb_[h]])
                    yield

            for tq in range(lim.get('gdn_tiles', S // T)):
                t0 = tq * T
                ub = [uB[kc][t0 // 512] for kc in range(8)]
                for j in range(8):
                    pb = j % 4
                    for kc in range(8):
                        mm(PS[pb][:, 0:T], wg[:, kc, j * 128:(j + 1) * 128], uT[:, kc, t0:t0 + T], kc == 0, kc == 7, reads=[wb_] + ub, writes=[psB[pb]])
                    tj = tmp[:, j % 2, :]
                    tjb = tb4[j % 2]
                    if j < 6:
                        act(xin[:, j, 3:3 + T], PS[pb][:, 0:T], AF.Copy, reads=[psB[pb]], writes=[xb[j]])
                        y = yq[:, j, :]
                        ts("dve", y, xin[:, j, 0:T], col(f"gcw{l}", j), MUL, reads=[xb[j], bf("par")], writes=[yb[j]])
                        for tap in range(1, 4):
                            stt(y, xin[:, j, tap:tap + T], col(f"gcw{l}", tap * 6 + j), y, MUL, ADD, reads=[xb[j], bf("par")], writes=[yb[j]])
                        op("pool", P.tensor_copy, out=xin[:, j, 0:3], in_=xin[:, j, T:T + 3], reads=[xb[j]], writes=[xb[j]])
                        sigm(tj, y, [yb[j]], tjb)
                        tt("dve", y, y, tj, MUL, reads=[tjb], writes=[yb[j]])
                        if j < 4:
                            t2_ = tmp[:, 2 + j % 2, :]
                            t2b = tb4[2 + j % 2]
                            act(t2_.bitcast(BF16)[:, 0:T], y, AF.Square, reads=[yb[j]], writes=[t2b])
                            mm(PS[4 + j][:, 0:T], bones_bf[:], t2_.bitcast(BF16)[:, 0:T], True, True, reads=[t2b, bf("bones_bf")], writes=[psB[4 + j]])
                            act(tj, PS[4 + j][:, 0:T], AF.Ln, reads=[psB[4 + j], bf("cst")], writes=[tjb], bias=cst[:, 0:1])
                            if j < 2:
                                act(tj, tj, AF.Exp, reads=[bf("cst")], writes=[tjb], scale=-0.5, bias=cst[:, 2:3])
                            else:
                                act(tj, tj, AF.Exp, writes=[tjb], scale=-0.5)
                            tt("dve", y, y, tj, MUL, reads=[tjb], writes=[yb[j]])
                    else:
                        zc = zs[:, j - 6, :]
                        act(zc, PS[pb][:, 0:T], AF.Copy, reads=[psB[pb]], writes=[zb[j - 6]])
                        sigm(tj, zc, [zb[j - 6]], tjb)
                        tt("dve", zc, zc, tj, MUL, reads=[tjb], writes=[zb[j - 6]])
                for blk in range(2):
                    b0 = blk * 128
                    bs = slice(b0, b0 + 128)
                    for kc in range(8):
                        mm(PS[5][:, 0:8], uT[:, kc, t0 + b0:t0 + b0 + 128], wg[:, kc, 1024:1032], kc == 0, kc == 7, reads=[wb_] + ub, writes=[psB[5]])
                    gcol, beta, nbeta, Gc, eG, eTl, bg, sp = (gt_[:, 4 * i:4 * i + 4] for i in range(8))
                    tt("dve", sp, PS[5][:, 0:4], par[:, PC[f"gdtb{l}"]:PC[f"gdtb{l}"] + 4], ADD, reads=[psB[5], bf("par")], writes=[gB])
                    act(sp, sp, AF.Exp, writes=[gB])
                    act(sp, sp, AF.Ln, reads=[bf("cst")], writes=[gB], bias=cst[:, 1:2])
                    tt("dve", gcol, sp, nA, MUL, reads=[cb], writes=[gB])
                    sigm(beta, PS[5][:, 4:8], [psB[5]], gB)
                    ts("dve", nbeta, beta, -1.0, MUL, writes=[gB])
                    mm(PS[6][:, 0:4], Ubd, gcol, True, True, reads=[cb, gB], writes=[psB[6]])
                    mm(PS[6][:, 4:8], BD1, gcol, True, True, reads=[cb, gB], writes=[psB[6]])
                    op("dve", V.tensor_copy, out=Gc, in_=PS[6][:, 0:4], reads=[psB[6]], writes=[gB])
                    act(eG, PS[6][:, 0:4], AF.Exp, reads=[psB[6]], writes=[gB])
                    tt("dve", eTl, PS[6][:, 4:8], Gc, SUB, reads=[psB[6]], writes=[gB])
                    act(eTl, eTl, AF.Exp, writes=[gB])
                    tt("dve", bg, beta, eG, MUL, writes=[gB])
                    for grp in lim.get('gdn_il', [[0, 1, 2, 3]]):
                      alive = [head_gen(h, bs) for h in grp]
                      nsteps = 0
                      while alive and nsteps < lim.get('gdn_steps', 999):
                          nsteps += 1
                          nxt = []
                          for g_ in alive:
                              try:
                                  next(g_)
                                  nxt.append(g_)
                              except StopIteration:
                                  pass
                          alive = nxt
                    ss = gt_[:, 32:36]
                    for h in range(4):
                        tt("pool", on_[:, h * 64:(h + 1) * 64], otok[:, h, :], otok[:, h, :], MUL, reads=[otB], writes=[onB])
                    op("dve", V.tensor_reduce, out=ss, in_=on_.rearrange("p (h n) -> p h n", h=4), axis=mybir.AxisListType.X, op=ADD,
                       reads=[onB], writes=[gB])
                    act(ss, ss, AF.Ln, reads=[bf("cst")], writes=[gB], scale=1.0 / 64, bias=cst[:, 0:1])
                    act(ss, ss, AF.Exp, writes=[gB], scale=-0.5)
                    for h in range(4):
                        ts("dve", on_[:, h * 64:(h + 1) * 64], otok[:, h, :], gt_[:, 32 + h:33 + h], MUL, reads=[otB, gB], writes=[onB])
                    for c2 in range(2):
                        op("pe", PE.transpose, PS[2 + c2][:, 0:128], on_[:, c2 * 128:(c2 + 1) * 128], ident[:], reads=[onB, bf("ident")], writes=[psB[2 + c2]])
                        stt(oT[:, c2, bs], PS[2 + c2][:, 0:128], col(f"gng{l}"), zs[:, c2, bs], MUL, MUL, reads=[psB[2 + c2], bf("par"), zb[c2]], writes=[oTB])
                accum_wout(l, s_, t0, T, lambda kc: oT[:, kc, :], 2, lambda kc, oc: wo[:, kc, oc * 128:(oc + 1) * 128], [wb_, oTB])

        for s_ in range(lim.get('nseq', NSEQ)):
            for k in range(8):
                for t in range(4):
                    dma("sp", nc.sync.dma_start, out=hT[:, k, t * 512:(t + 1) * 512], in_=xT[s_, k * 128:(k + 1) * 128, t * 512:(t + 1) * 512],
                        writes=[hB[k][t]])
            if G["mla"]:
                rope_tables(s_)
            for l in range(lim.get('depth', DEPTH)):
                modnorm(l, s_, 0)
                if G["gdn"]:
                    gdn_group(l, s_)
                if G["rg"]:
                    rg_group(l, s_)
                if G["mla"]:
                    mla_group(l, s_)
                if G["mlp"]:
                    modnorm(l, s_, 1)
                    mlp(l, s_)
            final_norm(s_)
        if debug_stop == "mod":
            dma("sp", nc.sync.dma_start, out=yT[1, 0:128, 0:DEPTH * 48 * NSEQ], in_=modT[:].rearrange("p l c s -> p (l c s)"),
                reads=[bf("modT")], writes=[bf("yout")])
        sc.final_wait("sp", [bf("yout")])
        build_nc.last_counts = dict(sc.cnt)

        with nc.Block() as block:
            @block.tensor
            def _(e):
                sc.replay("pe", e)

            @block.vector
            def _(e):
                sc.replay("dve", e)

            @block.scalar
            def _(e):
                sc.replay("act", e)

            @block.gpsimd
            def _(e):
                sc.replay("pool", e)

            @block.sync
            def _(e):
                sc.replay("sp", e)
    return nc


def _pack_params(inp, core):
    par = np.zeros((128, NPAR), np.float32)
    f = lambda v: np.asarray(v, np.float32)
    cm = lambda v: f(v).reshape(-1, 128).T
    for s_ in range(NSEQ):
        par[:, PC["c"] + 8 * s_:PC["c"] + 8 * (s_ + 1)] = cm(inp["c"][core * NSEQ + s_])
    par[:, PC["fng"]:PC["fng"] + 8] = cm(inp["final_norm_g"])
    invf = (np.float32(10000.0) ** (-np.arange(0, 32, 2, dtype=np.float32) / np.float32(32))).astype(np.float32)
    par[:, PC["invf"]] = np.tile(invf, 8)
    for l in range(DEPTH):
        par[:, PC[f"bmod{l}"]:PC[f"bmod{l}"] + 48] = cm(inp["b_mod"][l])
        par[:, PC[f"nmg{l}"]:PC[f"nmg{l}"] + 8] = cm(inp["norm_mix_g"][l])
        par[:, PC[f"nfg{l}"]:PC[f"nfg{l}"] + 8] = cm(inp["norm_mlp_g"][l])
        gcw = f(inp["gdn_conv_w"][l])
        for tap in range(4):
            par[:, PC[f"gcw{l}"] + tap * 6:PC[f"gcw{l}"] + tap * 6 + 6] = cm(gcw[tap])
        par[:, PC[f"galog{l}"]:PC[f"galog{l}"] + 4] = f(inp["gdn_a_log"][l])[None, :]
        par[:, PC[f"gdtb{l}"]:PC[f"gdtb{l}"] + 4] = f(inp["gdn_dt_bias"][l])[None, :]
        par[:, PC[f"gng{l}"]] = np.tile(f(inp["gdn_norm_g"][l]), 2)
        rcw = f(inp["rg_conv_w"][l])
        for tap in range(4):
            par[:, PC[f"rcw{l}"] + tap * 4:PC[f"rcw{l}"] + tap * 4 + 4] = cm(rcw[tap])
        par[:, PC[f"rcb{l}"]:PC[f"rcb{l}"] + 4] = cm(inp["rg_conv_b"][l])
        par[:, PC[f"rba{l}"]:PC[f"rba{l}"] + 4] = cm(inp["rg_b_a"][l])
        par[:, PC[f"rbx{l}"]:PC[f"rbx{l}"] + 4] = cm(inp["rg_b_x"][l])
        par[:, PC[f"rlam{l}"]:PC[f"rlam{l}"] + 4] = cm(inp["rg_lambda"][l])
        par[:, PC[f"mqg{l}"]:PC[f"mqg{l}"] + 2] = cm(inp["mla_q_norm_g"][l])
        par[:, PC[f"mkvg{l}"]:PC[f"mkvg{l}"] + 1] = cm(inp["mla_kv_norm_g"][l])
    return par


def _shared_weights(inp):
    f = lambda v: np.ascontiguousarray(np.asarray(v, np.float32))
    kp = lambda w, k: f(w.reshape(w.shape[0], k, 128, w.shape[2]).transpose(0, 2, 1, 3))
    sh = {}
    sh["wmod"] = kp(inp["w_mod"], 8)
    sh["win"] = kp(inp["w_in"], 8)
    sh["wout"] = kp(inp["w_out"], 8)
    wo = np.asarray(inp["w_out"], np.float32)
    sh["woutm"] = f(wo[:, 768:1024, :].reshape(DEPTH, 4, 64, 1024).transpose(0, 2, 1, 3))
    sh["w1"] = kp(inp["w_mlp_in"], 8)
    sh["w2"] = kp(inp["w_mlp_out"], 32)
    rgw = np.zeros((DEPTH, 128, 8, 128), np.float32)
    for l in range(DEPTH):
        for wi_, nm in enumerate(("rg_w_a", "rg_w_x")):
            w = np.asarray(inp[nm][l], np.float32)
            for j in range(4):
                rgw[l, 0:64, wi_ * 4 + j, 0:64] = w[2 * j]
                rgw[l, 64:128, wi_ * 4 + j, 64:128] = w[2 * j + 1]
    sh["rgw"] = rgw
    sh["wqb"] = kp(inp["mla_w_qb"], 2)
    sh["wkvb"] = f(inp["mla_w_kvb"])
    return sh


def make_in_maps(inp, cores):
    sh = _shared_weights(inp)
    maps = []
    for c in cores:
        m = dict(sh)
        xs = np.asarray(inp["x"][c * NSEQ:(c + 1) * NSEQ], np.float32)
        m["xT"] = np.ascontiguousarray(xs.transpose(0, 2, 1))
        m["par"] = _pack_params(inp, c)
        pos = np.asarray(inp["positions"][c * NSEQ:(c + 1) * NSEQ], np.int32)
        m["pos"] = np.ascontiguousarray(np.broadcast_to(pos[:, None, :], (NSEQ, 128, S)))
        maps.append(m)
    return maps


def kernel(**inputs):
    nc = build_nc()
    cores = list(range(8))
    in_maps = make_in_maps(inputs, cores)
    res = run_bass_kernel_spmd(nc, in_maps, core_ids=cores)
    out = np.empty((16, S, D), np.float32)
    for c in cores:
        y = res.results[c]["yT"]
        for s_ in range(NSEQ):
            out[c * NSEQ + s_] = y[s_].T
    return out
```

```python
import numpy as np
import concourse.bass as bass
import concourse.mybir as mybir
from concourse.bass_utils import run_bass_kernel_spmd

F32 = mybir.dt.float32
BF16 = mybir.dt.bfloat16
I32 = mybir.dt.int32
AF = mybir.ActivationFunctionType
ALU = mybir.AluOpType

S = 2048
D = 1024
NSEQ = 2
DEPTH = 2
EPS = 1e-6
GROUPS = {"gdn": True, "rg": True, "mla": True, "mlp": True}

PC = {}
_off = 0


def _pc(name, n):
    global _off
    PC[name] = _off
    _off += n


_pc("c", 16)
_pc("fng", 8)
_pc("invf", 1)
for _l in range(DEPTH):
    _pc(f"bmod{_l}", 48)
    _pc(f"nmg{_l}", 8)
    _pc(f"nfg{_l}", 8)
    _pc(f"gcw{_l}", 24)
    _pc(f"galog{_l}", 4)
    _pc(f"gdtb{_l}", 4)
    _pc(f"gng{_l}", 1)
    _pc(f"rcw{_l}", 16)
    _pc(f"rcb{_l}", 4)
    _pc(f"rba{_l}", 4)
    _pc(f"rbx{_l}", 4)
    _pc(f"rlam{_l}", 4)
    _pc(f"mqg{_l}", 2)
    _pc(f"mkvg{_l}", 1)
NPAR = _off


class Buf:
    __slots__ = ("w", "r", "excl")

    def __init__(self):
        self.w = []
        self.r = {}
        self.excl = False


class Sched:
    EPOCH = 20000

    def __init__(self, nc, sems):
        self.nc = nc
        self.sems = sems
        self.eng = {"pe": nc.tensor, "dve": nc.vector, "act": nc.scalar, "pool": nc.gpsimd, "sp": nc.sync}
        self.q = {e: [] for e in self.eng}
        self.cnt = {e: 0 for e in self.eng}
        self.esem = {e: [] for e in self.eng}
        self.seen = {e: {} for e in self.eng}
        self.pending = {e: [] for e in self.eng}
        self.last = {e: None for e in self.eng}
        self.dsem = {}
        self.dcnt = {}
        self.dval = {}
        for qn in ("sp", "pool"):
            self.dsem[qn] = [self._newsem() for _ in range(6)]
            self.dval[qn] = [0] * 6
            self.dcnt[qn] = 0

    def _newsem(self):
        return self.sems.pop()

    def _deps(self, eng, reads, writes, extra=(), same=True):
        evs = list(extra)
        need = {}
        for b in reads:
            evs.extend(b.w)
            if b.excl:
                evs.extend(b.r.values())
        for b in writes:
            evs.extend(b.w)
            evs.extend(b.r.values())
        if self.pending[eng]:
            evs.extend(self.pending[eng])
            self.pending[eng] = []
        for (src, sem, val) in evs:
            if src == eng:
                if eng == "pe" or not same or sem is not (self.esem[eng][-1] if self.esem[eng] else None):
                    continue
                if self.cnt[eng] % self.EPOCH - (val - 1) > 5:
                    continue
                k = id(sem)
                if k not in need or need[k][1] < val:
                    need[k] = (sem, val)
                continue
            k = id(sem)
            if self.seen[eng].get(k, 0) >= val:
                continue
            if k not in need or need[k][1] < val:
                need[k] = (sem, val)
        for k, (sem, val) in need.items():
            self.seen[eng][k] = val
        return list(need.values())

    mute = False

    def op(self, eng, f, *args, reads=(), writes=(), **kw):
        if self.mute:
            return
        fn = (lambda: f(*args, **kw))
        waits = self._deps(eng, reads, writes)
        idx = self.cnt[eng]
        self.cnt[eng] += 1
        ep, v = idx // self.EPOCH, idx % self.EPOCH + 1
        while len(self.esem[eng]) <= ep:
            self.esem[eng].append(self._newsem())
        sem = self.esem[eng][ep]
        self.q[eng].append((waits, fn, sem, 1))
        ev = (eng, sem, v)
        self.last[eng] = ev
        for b in reads:
            b.r[eng] = ev
        for b in writes:
            b.w = [ev]
            b.r = {}

    def dma(self, qn, f, *args, reads=(), writes=(), **kw):
        if self.mute:
            return None
        fn = (lambda: f(*args, **kw))
        i = self.dcnt[qn] % len(self.dsem[qn])
        self.dcnt[qn] += 1
        sem = self.dsem[qn][i]
        prev = ("dma" + qn, sem, self.dval[qn][i])
        waits = self._deps(qn, reads, writes, extra=[prev] if prev[2] > 0 else [])
        self.dval[qn][i] += 16
        self.q[qn].append((waits, fn, sem, 16))
        ev = ("dma" + qn, sem, self.dval[qn][i])
        for b in reads:
            b.r["dma" + qn + str(i)] = ev
        for b in writes:
            b.w = [e for e in b.w if e[0].startswith("dma")][-40:] + [ev]
            b.r = {}
        return ev

    def barrier(self):
        evs = [ev for ev in self.last.values() if ev is not None]
        for qn in self.dsem:
            for i, sem in enumerate(self.dsem[qn]):
                if self.dval[qn][i] > 0:
                    evs.append(("dma" + qn, sem, self.dval[qn][i]))
        for e in self.eng:
            self.pending[e] = list(evs)

    def final_wait(self, eng, bufs):
        waits = self._deps(eng, bufs, bufs)
        self.q[eng].append((waits, None, None, 0))

    def replay(self, eng, h):
        for waits, fn, sem, inc in self.q[eng]:
            for (s, v) in waits:
                h.wait_ge(s, v)
            if fn is not None:
                fn().then_inc(sem, inc)


def build_nc(groups=None, debug_stop=None, lim=None):
    lim = lim or {}
    G = dict(GROUPS)
    if groups:
        G.update(groups)
    nc = bass.Bass("TRN2", target_bir_lowering=False)
    dt_ = nc.dram_tensor
    xT = dt_("xT", [NSEQ, D, S], F32, kind="ExternalInput").ap()
    par_d = dt_("par", [128, NPAR], F32, kind="ExternalInput").ap()
    pos_d = dt_("pos", [NSEQ, 128, S], I32, kind="ExternalInput").ap()
    wmod_d = dt_("wmod", [DEPTH, 128, 8, 6144], F32, kind="ExternalInput").ap()
    win_d = dt_("win", [DEPTH, 128, 8, 2472], F32, kind="ExternalInput").ap()
    wout_d = dt_("wout", [DEPTH, 128, 8, 1024], F32, kind="ExternalInput").ap()
    woutm_d = dt_("woutm", [DEPTH, 64, 4, 1024], F32, kind="ExternalInput").ap()
    w1_d = dt_("w1", [DEPTH, 128, 8, 4096], F32, kind="ExternalInput").ap()
    w2_d = dt_("w2", [DEPTH, 128, 32, 1024], F32, kind="ExternalInput").ap()
    rgw_d = dt_("rgw", [DEPTH, 128, 8, 128], F32, kind="ExternalInput").ap()
    wqb_d = dt_("wqb", [DEPTH, 128, 2, 384], F32, kind="ExternalInput").ap()
    wkvb_d = dt_("wkvb", [DEPTH, 128, 512], F32, kind="ExternalInput").ap()
    yT = dt_("yT", [NSEQ, D, S], F32, kind="ExternalOutput").ap()
    rope_d = dt_("ropescr", [NSEQ, 2, 32, S], F32, kind="Internal").ap()

    from contextlib import ExitStack
    es = ExitStack()
    sb = lambda n, shp, d=F32: es.enter_context(nc.sbuf_tensor(n, shp, d))
    ps = lambda n: es.enter_context(nc.psum_tensor(n, [128, 512], F32))
    with es:
        sems = [es.enter_context(nc.semaphore(f"s{i}")) for i in range(40)]
        sc = Sched(nc, sems)
        op, dma = sc.op, sc.dma
        V, A, P, PE = nc.vector, nc.scalar, nc.gpsimd, nc.tensor

        hT = sb("hT", [128, 8, S])
        uT = sb("uT", [128, 8, S], BF16)
        par = sb("par_sb", [128, NPAR])
        cst = sb("cst", [128, 8])
        ident = sb("ident", [128, 128])
        ones_bf = sb("ones_bf", [128, 128], BF16)
        ones_f = sb("ones_f", [128, 128])
        bones_bf = sb("bones_bf", [128, 128], BF16)
        modT = sb("modT", [128, DEPTH, 48, NSEQ])
        gmix = sb("gmix", [128, 4, 8])
        cact = sb("cact", [128, 8, NSEQ], BF16)
        ctmp = sb("ctmp", [128, 16])
        wsl = [sb("wslA", [128, 12288], BF16), sb("wslB", [128, 12288], BF16)]
        arena = sb("arena", [128, 13000])
        PS = [ps(f"ps{i}") for i in range(8)]

        B = {}

        def bf(name):
            if name not in B:
                B[name] = Buf()
            return B[name]

        hB = [[bf(f"h{k}_{t}") for t in range(4)] for k in range(8)]
        uB = [[bf(f"u{k}_{t}") for t in range(4)] for k in range(8)]
        psB = [bf(f"ps{i}") for i in range(8)]
        for b_ in psB:
            b_.excl = True
        wB = [bf("wslA"), bf("wslB")]

        def col(name, j=0):
            c0 = PC[name] + j
            return par[:, c0:c0 + 1]

        op("pool", P.memset, cst[:, 0:1], EPS, writes=[bf("cst")])
        op("pool", P.memset, cst[:, 1:2], 1.0, writes=[bf("cst")])
        op("pool", P.memset, cst[:, 2:3], float(np.log(0.125)), writes=[bf("cst")])
        op("pool", P.memset, cst[:, 3:4], 0.0, writes=[bf("cst")])
        op("pool", P.memset, ones_f[:], 1.0, writes=[bf("ones_f")])
        op("pool", P.memset, ones_bf[:], 1.0, writes=[bf("ones_bf")])
        op("pool", P.memset, bones_bf[:], 0.0, writes=[bf("bones_bf")])
        op("pool", P.memset, bones_bf[0:64, 0:64], 1.0, writes=[bf("bones_bf")])
        op("pool", P.memset, bones_bf[64:128, 64:128], 1.0, writes=[bf("bones_bf")])
        op("pool", P.affine_select, out=ident[:], in_=ones_f[:], pattern=[[-1, 128]], compare_op=ALU.is_equal,
           fill=0.0, base=0, channel_multiplier=1, reads=[bf("ones_f")], writes=[bf("ident")])
        dma("sp", nc.sync.dma_start, out=par[:], in_=par_d, writes=[bf("par")])

        def col(name, j=0):
            c0 = PC[name] + j
            return par[:, c0:c0 + 1]

        def act(out, in_, func, reads=(), writes=(), **kw):
            op("act", A.activation, out=out, in_=in_, func=func, reads=reads, writes=writes, **kw)

        def tt(eng, out, in0, in1, o, reads=(), writes=()):
            op(eng, (V if eng == "dve" else P).tensor_tensor, out=out, in0=in0, in1=in1, op=o, reads=reads, writes=writes)

        def ts(eng, out, in0, s1, o0, s2=None, o1=None, reads=(), writes=()):
            kw = {} if o1 is None else {"op1": o1}
            op(eng, (V if eng == "dve" else P).tensor_scalar, out=out, in0=in0, scalar1=s1, scalar2=s2, op0=o0, reads=reads, writes=writes, **kw)

        def stt(out, in0, scalar, in1, o0, o1, reads=(), writes=()):
            op("dve", V.scalar_tensor_tensor, out=out, in0=in0, scalar=scalar, in1=in1, op0=o0, op1=o1, reads=reads, writes=writes)

        def mm(out, lhsT, rhs, start, stop, reads=(), writes=()):
            op("pe", PE.matmul, out, lhsT, rhs, start=start, stop=stop, reads=reads, writes=writes)

        def sigm(out, in_, reads, wb, scale=1.0, nbias=None):
            kw = {} if nbias is None else {"bias": nbias}
            act(out, in_, AF.Exp, reads=reads, writes=[wb], scale=-scale, **kw)
            act(out, out, AF.Ln, reads=[bf("cst")], writes=[wb], bias=cst[:, 1:2])
            act(out, out, AF.Exp, writes=[wb], scale=-1.0)

        MUL, ADD, SUB, MAX, MIN = ALU.mult, ALU.add, ALU.subtract, ALU.max, ALU.min

        cc = par[:, PC["c"]:PC["c"] + 16]
        sigm(ctmp[:], cc, [bf("par")], bf("ctmp"))
        for s_ in range(NSEQ):
            tt("dve", cact[:, :, s_], ctmp[:, s_ * 8:(s_ + 1) * 8], par[:, PC["c"] + s_ * 8:PC["c"] + (s_ + 1) * 8], MUL,
               reads=[bf("ctmp"), bf("par")], writes=[bf("cact")])
        wcount = [0]

        def next_slot():
            s = wcount[0] % 2
            wcount[0] += 1
            return s

        for l in range(DEPTH):
            for pc_ in range(12):
                slot = next_slot()
                wv = wsl[slot][:, 0:4096].rearrange("p (k n) -> p k n", k=8)
                dma("pool", P.dma_start, out=wv, in_=wmod_d[l, :, :, pc_ * 512:(pc_ + 1) * 512], writes=[wB[slot]])
                for oc in range(4):
                    pb = (pc_ * 4 + oc) % 8
                    for kc in range(8):
                        mm(PS[pb][:, 0:NSEQ], wv[:, kc, oc * 128:(oc + 1) * 128], cact[:, kc, :], kc == 0, kc == 7,
                           reads=[wB[slot], bf("cact")], writes=[psB[pb]])
                    ch = pc_ * 4 + oc
                    ts("dve", modT[:, l, ch, :], PS[pb][:, 0:NSEQ], col(f"bmod{l}", ch), ADD, reads=[psB[pb], bf("par")], writes=[bf("modT")])

        def norm_stats(t, nchunk_src, scale):
            sq = arena[:, 0:1024].bitcast(BF16).rearrange("p (b n) -> p b n", b=4)
            rs = arena[:, 1024:2048].rearrange("p (b n) -> p b n", b=2)
            tsl = slice(t * 512, (t + 1) * 512)
            pb = t % 2
            for k in range(8):
                j = (t * 8 + k) % 4
                act(sq[:, j, :], hT[:, k, tsl], AF.Square, reads=[hB[k][t]], writes=[bf(f"nsq{j}")])
                mm(PS[pb][:], ones_bf[:], sq[:, j, :], k == 0, k == 7, reads=[bf(f"nsq{j}"), bf("ones_bf")], writes=[psB[pb]])
            rb = bf(f"nrs{t % 2}")
            act(rs[:, t % 2, :], PS[pb][:], AF.Ln, reads=[psB[pb], bf("cst")], writes=[rb], scale=1.0 / D, bias=cst[:, 0:1])
            act(rs[:, t % 2, :], rs[:, t % 2, :], AF.Exp, writes=[rb], scale=-0.5)
            return rs[:, t % 2, :], rb

        def modnorm(l, s_, which):
            sc.barrier()
            gname = f"nmg{l}" if which == 0 else f"nfg{l}"
            sh0, sc0 = (0, 8) if which == 0 else (24, 32)
            stt(gmix[:, which, :], modT[:, l, sc0:sc0 + 8, s_], 1.0, par[:, PC[gname]:PC[gname] + 8], ADD, MUL,
                reads=[bf("modT"), bf("par")], writes=[bf("gmix")])
            tm = arena[:, 2048:3072].rearrange("p (b n) -> p b n", b=2)
            for t in range(4):
                tsl = slice(t * 512, (t + 1) * 512)
                rs, rb = norm_stats(t, 8, 1.0 / D)
                for k in range(8):
                    tb = bf(f"ntm{k % 2}")
                    tt("dve", tm[:, k % 2, :], hT[:, k, tsl], rs, MUL, reads=[hB[k][t], rb], writes=[tb])
                    act(uT[:, k, tsl], tm[:, k % 2, :], AF.Identity, reads=[tb, bf("gmix"), bf("modT")], writes=[uB[k][t]],
                        scale=gmix[:, which, k:k + 1], bias=modT[:, l, sh0 + k, s_:s_ + 1])

        def final_norm(s_):
            ot = arena[:, 2048:4096].rearrange("p (b n) -> p b n", b=4)
            for t in range(4):
                tsl = slice(t * 512, (t + 1) * 512)
                rs, rb = norm_stats(t, 8, 1.0 / D)
                for k in range(8):
                    ob = bf(f"fot{k % 4}")
                    stt(ot[:, k % 4, :], hT[:, k, tsl], col("fng", k), rs, MUL, MUL, reads=[hB[k][t], rb, bf("par")], writes=[ob])
                    dma("sp", nc.sync.dma_start, out=yT[s_, k * 128:(k + 1) * 128, tsl], in_=ot[:, k % 4, :], reads=[ob], writes=[bf("yout")])

        prefetched = {}

        def load_mlp0(l):
            slot = next_slot()
            w1v = wsl[slot][:, 0:4096].rearrange("p (k n) -> p k n", k=8)
            w2v = wsl[slot][:, 4096:8192].rearrange("p (k n) -> p k n", k=4)
            dma("pool", P.dma_start, out=w1v, in_=w1_d[l, :, :, 0:512], writes=[wB[slot]])
            dma("pool", P.dma_start, out=w2v, in_=w2_d[l, :, 0:4, :], writes=[wB[slot]])
            return (slot, w1v, w2v)

        def load_rg(l):
            slot = next_slot()
            wv = wsl[slot][:, 0:8192].rearrange("p (k n) -> p k n", k=8)
            wo = wsl[slot][:, 8192:12288].rearrange("p (k n) -> p k n", k=4)
            dma("pool", P.dma_start, out=wv, in_=win_d[l, :, :, 1032:2056], writes=[wB[slot]])
            dma("pool", P.dma_start, out=wo, in_=wout_d[l, :, 2:6, :], writes=[wB[slot]])
            return (slot, wv, wo)

        def load_mla(l):
            slot = next_slot()
            W = wsl[slot]
            wm = W[:, 0:3328].rearrange("p (k n) -> p k n", k=8)
            wom = W[:, 3584:7680].rearrange("p (h n) -> p h n", h=4)
            wq = W[:, 7680:8448].rearrange("p (k n) -> p k n", k=2)
            wkv = W[:, 8704:9216]
            dma("pool", P.dma_start, out=wm, in_=win_d[l, :, :, 2056:2472], writes=[wB[slot]])
            dma("pool", P.dma_start, out=wom[0:64], in_=woutm_d[l], writes=[wB[slot]])
            dma("pool", P.dma_start, out=wq, in_=wqb_d[l], writes=[wB[slot]])
            dma("pool", P.dma_start, out=wkv, in_=wkvb_d[l], writes=[wB[slot]])
            return slot

        def mlp(l, s_):
            fbuf = arena[:, 4096:6144].bitcast(BF16).rearrange("p (b n) -> p b n", b=8)
            rl = arena[:, 6144:8192].rearrange("p (b n) -> p b n", b=4)
            views = {}
            if ("mlp0", l) in prefetched:
                views[0] = prefetched.pop(("mlp0", l))
            cnt = [0]

            def load(fc):
                slot = next_slot()
                w1v = wsl[slot][:, 0:4096].rearrange("p (k n) -> p k n", k=8)
                w2v = wsl[slot][:, 4096:8192].rearrange("p (k n) -> p k n", k=4)
                dma("pool", P.dma_start, out=w1v, in_=w1_d[l, :, :, fc * 512:(fc + 1) * 512], writes=[wB[slot]])
                dma("pool", P.dma_start, out=w2v, in_=w2_d[l, :, fc * 4:(fc + 1) * 4, :], writes=[wB[slot]])
                views[fc] = (slot, w1v, w2v)

            def w1_stage(i):
                fc, t = divmod(i, 4)
                if fc not in views:
                    load(fc)
                if t == 1 and fc + 1 < 8 and (fc + 1) not in views:
                    load(fc + 1)
                slot, w1v, w2v = views[fc]
                tsl = slice(t * 512, (t + 1) * 512)
                par_ = i % 2
                for fs in range(4):
                    pb = cnt[0] % 4
                    cnt[0] += 1
                    for kc in range(8):
                        mm(PS[pb][:], w1v[:, kc, fs * 128:(fs + 1) * 128], uT[:, kc, tsl], kc == 0, kc == 7,
                           reads=[wB[slot], uB[kc][t]], writes=[psB[pb]])
                    rb = bf(f"rl{pb}")
                    fb = bf(f"f1_{par_}_{fs}")
                    act(rl[:, pb, :], PS[pb][:], AF.Relu, reads=[psB[pb]], writes=[rb])
                    tt("dve", fbuf[:, par_ * 4 + fs, :], rl[:, pb, :], rl[:, pb, :], MUL, reads=[rb], writes=[fb])

            def w2_stage(i):
                fc, t = divmod(i, 4)
                slot, w1v, w2v = views[fc]
                tsl = slice(t * 512, (t + 1) * 512)
                par_ = i % 2
                for oc in range(8):
                    pb = 4 + oc % 4
                    for fs in range(4):
                        mm(PS[pb][:], w2v[:, fs, oc * 128:(oc + 1) * 128], fbuf[:, par_ * 4 + fs, :], fs == 0, fs == 3,
                           reads=[wB[slot], bf(f"f1_{par_}_{fs}")], writes=[psB[pb]])
                    stt(hT[:, oc, tsl], PS[pb][:], modT[:, l, 40 + oc, s_:s_ + 1], hT[:, oc, tsl], MUL, ADD,
                        reads=[psB[pb], bf("modT"), hB[oc][t]], writes=[hB[oc][t]])

            NI = 32
            w1_stage(0)
            for i in range(NI):
                if i + 1 < NI:
                    w1_stage(i + 1)
                w2_stage(i)

        def accum_wout(l, s_, t0, n, rhs_fn, nk, wv_fn, rd):
            tq = t0 // 512
            for oc in range(8):
                pb = 4 + oc % 4
                for kc in range(nk):
                    mm(PS[pb][:, 0:n], wv_fn(kc, oc), rhs_fn(kc), kc == 0, kc == nk - 1, reads=rd, writes=[psB[pb]])
                stt(hT[:, oc, t0:t0 + n], PS[pb][:, 0:n], modT[:, l, 16 + oc, s_:s_ + 1], hT[:, oc, t0:t0 + n], MUL, ADD,
                    reads=[psB[pb], bf("modT"), hB[oc][tq]], writes=[hB[oc][tq]])

        def rg_group(l, s_):
            sc.barrier()
            T = 256
            slot, wv, wo = prefetched.pop(("rg", l)) if ("rg", l) in prefetched else load_rg(l)
            if G["mla"]:
                prefetched[("mla", l)] = load_mla(l)
            rgw = arena[:, 0:1024].rearrange("p (k n) -> p k n", k=8)
            dma("sp", nc.sync.dma_start, out=rgw, in_=rgw_d[l], writes=[bf("rgw")])
            prm = arena[:, 1024:1040]
            hst = arena[:, 1040:1044]
            xin = arena[:, 1048:1048 + 4 * 259].rearrange("p (j n) -> p j n", j=4)
            o0 = 1048 + 4 * 259 + 4
            gate = arena[:, o0:o0 + 4 * T].rearrange("p (j n) -> p j n", j=4)
            o1 = o0 + 4 * T
            tmpv = arena[:, o1:o1 + 28 * T].rearrange("p (j q n) -> p j q n", j=4, q=7)
            o2 = o1 + 28 * T
            ob = arena[:, o2:o2 + 2 * T].bitcast(BF16).rearrange("p (j n) -> p j n", j=4)
            assert o2 + 2 * T <= 13000
            pb_ = bf("rgprm")
            lam = par[:, PC[f"rlam{l}"]:PC[f"rlam{l}"] + 4]
            act(prm[:, 0:4], lam, AF.Exp, reads=[bf("par")], writes=[pb_], scale=-1.0)
            act(prm[:, 0:4], prm[:, 0:4], AF.Ln, reads=[bf("cst")], writes=[pb_], bias=cst[:, 1:2])
            ts("dve", prm[:, 4:8], prm[:, 0:4], -16.0, MUL, writes=[pb_])
            ts("dve", prm[:, 0:4], prm[:, 0:4], -8.0, MUL, writes=[pb_])
            ts("dve", prm[:, 8:12], par[:, PC[f"rba{l}"]:PC[f"rba{l}"] + 4], -1.0, MUL, reads=[bf("par")], writes=[pb_])
            ts("dve", prm[:, 12:16], par[:, PC[f"rbx{l}"]:PC[f"rbx{l}"] + 4], -1.0, MUL, reads=[bf("par")], writes=[pb_])
            hB_ = [bf(f"rgh{j}") for j in range(4)]
            xB_ = [bf(f"rgx{j}") for j in range(4)]
            gB_ = [bf(f"rgg{j}") for j in range(4)]
            oB_ = [bf(f"rgo{j}") for j in range(4)]
            tB_ = [[bf(f"rgt{j}_{q}") for q in range(7)] for j in range(4)]
            op("dve", V.memset, hst, 0.0, writes=hB_)
            op("dve", V.memset, xin[:, :, 0:3], 0.0, writes=xB_)
            C_G = 0.7978845608028654

            def chunk_gen(j, t0):
                pbx, pbg = 2 * j, 2 * j + 1
                ubs = [uB[kc][t0 // 512] for kc in range(8)]
                xc, r_, i_, a_, m_, hs_, g_ = (tmpv[:, j, q, :] for q in range(7))
                bxc, br, bi, ba, bm, bh, bg_ = tB_[j]
                for kc in range(8):
                    mm(PS[pbx][:, 0:T], wv[:, kc, j * 128:(j + 1) * 128], uT[:, kc, t0:t0 + T], kc == 0, kc == 7,
                       reads=[wB[slot]] + ubs, writes=[psB[pbx]])
                for kc in range(8):
                    mm(PS[pbg][:, 0:T], wv[:, kc, 512 + j * 128:512 + (j + 1) * 128], uT[:, kc, t0:t0 + T], kc == 0, kc == 7,
                       reads=[wB[slot]] + ubs, writes=[psB[pbg]])
                yield
                act(xin[:, j, 3:3 + T], PS[pbx][:, 0:T], AF.Copy, reads=[psB[pbx]], writes=[xB_[j]])
                act(gate[:, j, :], PS[pbg][:, 0:T], AF.Copy, reads=[psB[pbg]], writes=[gB_[j]])
                ts("dve", xc, xin[:, j, 0:T], col(f"rcw{l}", j), MUL, col(f"rcb{l}", j), ADD, reads=[xB_[j], bf("par")], writes=[bxc])
                for tap in range(1, 4):
                    stt(xc, xin[:, j, tap:tap + T], col(f"rcw{l}", tap * 4 + j), xc, MUL, ADD, reads=[xB_[j], bf("par")], writes=[bxc])
                op("pool", P.tensor_copy, out=xin[:, j, 0:3], in_=xin[:, j, T:T + 3], reads=[xB_[j]], writes=[xB_[j]])
                mm(PS[pbx][:, 0:T], rgw[:, j, :], xc, True, True, reads=[bf("rgw"), bxc], writes=[psB[pbx]])
                mm(PS[pbg][:, 0:T], rgw[:, 4 + j, :], xc, True, True, reads=[bf("rgw"), bxc], writes=[psB[pbg]])
                act(g_, gate[:, j, :], AF.Square, reads=[gB_[j]], writes=[bg_])
                ts("pool", g_, g_, 0.044715, MUL, 1.0, ADD, writes=[bg_])
                tt("pool", g_, g_, gate[:, j, :], MUL, reads=[gB_[j]], writes=[bg_])
                yield
                sigm(r_, PS[pbx][:, 0:T], [psB[pbx], pb_], br, nbias=prm[:, 8 + j:9 + j])
                sigm(i_, PS[pbg][:, 0:T], [psB[pbg], pb_], bi, nbias=prm[:, 12 + j:13 + j])
                yield
                act(a_, r_, AF.Exp, reads=[pb_, br], writes=[ba], scale=prm[:, j:j + 1])
                act(m_, r_, AF.Exp, reads=[pb_, br], writes=[bm], scale=prm[:, 4 + j:5 + j])
                act(m_, m_, AF.Ln, reads=[bf("cst")], writes=[bm], scale=-1.0, bias=cst[:, 1:2])
                act(m_, m_, AF.Exp, writes=[bm], scale=0.5)
                tt("pool", i_, i_, xc, MUL, reads=[bxc], writes=[bi])
                yield
                tt("dve", i_, i_, m_, MUL, reads=[bm], writes=[bi])
                op("dve", V.tensor_tensor_scan, out=hs_, data0=a_, data1=i_, initial=hst[:, j:j + 1], op0=MUL, op1=ADD,
                   reads=[hB_[j], ba, bi], writes=[bh])
                op("pool", P.tensor_copy, out=hst[:, j:j + 1], in_=hs_[:, T - 1:T], reads=[bh], writes=[hB_[j]])
                sigm(g_, g_, [bg_], bg_, scale=2.0 * C_G)
                yield
                tt("pool", g_, g_, gate[:, j, :], MUL, reads=[gB_[j]], writes=[bg_])
                tt("dve", ob[:, j, :], hs_, g_, MUL, reads=[bh, bg_], writes=[oB_[j]])

            for tq in range(S // T):
                t0 = tq * T
                alive = [chunk_gen(j, t0) for j in range(4)]
                while alive:
                    nxt = []
                    for g2_ in alive:
                        try:
                            next(g2_)
                            nxt.append(g2_)
                        except StopIteration:
                            pass
                    alive = nxt
                accum_wout(l, s_, t0, T, lambda kc: ob[:, kc, :], 4, lambda kc, oc: wo[:, kc, oc * 128:(oc + 1) * 128], [wB[slot]] + oB_)

        def rope_tables(s_):
            sc.barrier()
            pi_ = arena[:, 0:2048].bitcast(I32)
            ang = arena[:, 2048:4096]
            kf = arena[:, 4096:6144]
            ki = arena[:, 6144:8192].bitcast(I32)
            r0 = arena[:, 8192:10240]
            m_ = arena[:, 10240:12288]
            rb = bf("ropew")
            dma("sp", nc.sync.dma_start, out=pi_[0:32, :], in_=pos_d[s_, 0:32, :], writes=[rb])
            op("dve", V.tensor_copy, out=ang[0:32, :], in_=pi_[0:32, :], reads=[rb], writes=[rb])
            ts("dve", ang[0:32, :], ang[0:32, :], par[0:32, PC["invf"]:PC["invf"] + 1], MUL, reads=[bf("par")], writes=[rb])
            TWO_PI = 2.0 * np.pi
            C1 = 6.28125
            C2 = TWO_PI - C1
            for which, shift in ((0, np.pi / 2.0), (1, 0.0)):
                ts("dve", kf[0:32, :], ang[0:32, :], float(shift), ADD, 1.0 / TWO_PI, MUL, writes=[rb])
                op("dve", V.tensor_copy, out=ki[0:32, :], in_=kf[0:32, :], writes=[rb])
                op("dve", V.tensor_copy, out=kf[0:32, :], in_=ki[0:32, :], writes=[rb])
                ts("dve", r0[0:32, :], ang[0:32, :], float(shift), ADD, writes=[rb])
                stt(r0[0:32, :], kf[0:32, :], -C1, r0[0:32, :], MUL, ADD, writes=[rb])
                stt(r0[0:32, :], kf[0:32, :], -C2, r0[0:32, :], MUL, ADD, writes=[rb])
                ts("dve", m_[0:32, :], r0[0:32, :], float(np.pi), ALU.is_gt, -TWO_PI, MUL, writes=[rb])
                tt("dve", r0[0:32, :], r0[0:32, :], m_[0:32, :], ADD, writes=[rb])
                ts("dve", m_[0:32, :], r0[0:32, :], float(-np.pi), ALU.is_lt, TWO_PI, MUL, writes=[rb])
                tt("dve", r0[0:32, :], r0[0:32, :], m_[0:32, :], ADD, writes=[rb])
                ts("dve", r0[0:32, :], r0[0:32, :], 3.1415925, MIN, -3.1415925, MAX, writes=[rb])
                act(m_[0:32, :], r0[0:32, :], AF.Sin, writes=[rb])
                dma("sp", nc.sync.dma_start, out=rope_d[s_, which], in_=m_[0:32, :], reads=[rb], writes=[bf(f"roped{which}")])

        def mla_group(l, s_):
            sc.barrier()
            T = 512
            slot = prefetched.pop(("mla", l)) if ("mla", l) in prefetched else load_mla(l)
            if G["mlp"]:
                prefetched[("mlp0", l)] = load_mlp0(l)
            W = wsl[slot]
            wm = W[:, 0:3328].rearrange("p (k n) -> p k n", k=8)
            wks = W[:, 3328:3584].rearrange("p (k n) -> p k n", k=8)
            wom = W[:, 3584:7680].rearrange("p (h n) -> p h n", h=4)
            wq = W[:, 7680:8448].rearrange("p (k n) -> p k n", k=2)
            wqs = W[:, 8448:8704].rearrange("p (k h n) -> p k h n", k=2, h=4)
            wkv = W[:, 8704:9216]
            wb_ = wB[slot]
            op("act", A.mul, out=wks[:, :, 0:16], in_=wm[:, :, 400:416], mul=-1.0, reads=[wb_], writes=[wb_])
            op("act", A.copy, out=wks[:, :, 16:32], in_=wm[:, :, 384:400], reads=[wb_], writes=[wb_])
            wq4 = wq.rearrange("p k (h n) -> p k h n", h=4)
            for kc in range(2):
                op("act", A.mul, out=wqs[:, kc, :, 0:16], in_=wq4[:, kc, :, 80:96], mul=-1.0, reads=[wb_], writes=[wb_])
                op("act", A.copy, out=wqs[:, kc, :, 16:32], in_=wq4[:, kc, :, 64:80], reads=[wb_], writes=[wb_])
            a0 = 0
            Kc = arena[:, 0:4096].bitcast(BF16).rearrange("p (h n) -> p h n", h=4)
            Vc = arena[:, 4096:6176].bitcast(BF16).rearrange("p (j h n) -> p j h n", j=16, h=4)
            o = 6176

            def take(n):
                nonlocal o
                r = arena[:, o:o + n]
                o += n
                return r
            qn = take(512).bitcast(BF16).rearrange("p (k n) -> p k n", k=2)
            kvn = take(256).bitcast(BF16)
            Qh = take(256).bitcast(BF16)
            ET = take(512).bitcast(BF16).rearrange("p (k n) -> p k n", k=2)
            cos_ = take(512)
            sin_ = take(512)
            rs = take(512)
            sq = take(256).bitcast(BF16)
            t1 = take(512)
            t2 = take(512)
            osb = take(512)
            rec = take(512)
            oT = take(1024).bitcast(BF16).rearrange("p (h n) -> p h n", h=4)
            assert o <= 13000
            kb, vb_ = bf("mlaK"), bf("mlaV")
            bsq, brs, bqn, bkvn, bt1, bt2 = (bf(f"mla_{n}") for n in ("sq", "rs", "qn", "kvn", "t1", "t2"))
            bosb, brec, bQ = [bf("mla_osb0"), bf("mla_osb1")], [bf("mla_rec0"), bf("mla_rec1")], [bf("mla_Q0"), bf("mla_Q1")]
            boT = [bf(f"mla_oT{h}") for h in range(4)]
            op("pool", P.memset, Vc[:, :, :, 64:65], 1.0, writes=[vb_])
            SCALE = float(96 ** -0.5)
            Qh2 = [Qh, sq]
            osb2 = [osb, t1]
            rec2 = [rec, t2]
            for t in range(4):
                tsl = slice(t * T, (t + 1) * T)
                ub = [uB[kc][t] for kc in range(8)]
                dma("sp", nc.sync.dma_start, out=cos_[64:96, :], in_=rope_d[s_, 0, :, tsl], reads=[bf("roped0")], writes=[bf("mlacs")])
                dma("sp", nc.sync.dma_start, out=sin_[64:96, :], in_=rope_d[s_, 1, :, tsl], reads=[bf("roped1")], writes=[bf("mlacs")])
                for c in range(2):
                    for kc in range(8):
                        mm(PS[c][:], wm[:, kc, c * 128:(c + 1) * 128], uT[:, kc, tsl], kc == 0, kc == 7, reads=[wb_] + ub, writes=[psB[c]])
                for kc in range(8):
                    mm(PS[2][:], wm[:, kc, 256:384], uT[:, kc, tsl], kc == 0, kc == 7, reads=[wb_] + ub, writes=[psB[2]])
                for kc in range(8):
                    mm(PS[3][64:96, :], wm[:, kc, 384:416], uT[:, kc, tsl], kc == 0, kc == 7, reads=[wb_] + ub, writes=[psB[3]])
                for kc in range(8):
                    mm(PS[4][64:96, :], wks[:, kc, :], uT[:, kc, tsl], kc == 0, kc == 7, reads=[wb_] + ub, writes=[psB[4]])
                for c in range(2):
                    act(sq, PS[c][:], AF.Square, reads=[psB[c]], writes=[bsq] + bQ)
                    mm(PS[5][:], ones_bf[:], sq, c == 0, c == 1, reads=[bsq, bf("ones_bf")], writes=[psB[5]])
                act(rs, PS[5][:], AF.Ln, reads=[psB[5], bf("cst")], writes=[brs], scale=1.0 / 256, bias=cst[:, 0:1])
                act(rs, rs, AF.Exp, writes=[brs], scale=-0.5)
                for c in range(2):
                    stt(qn[:, c, :], PS[c][:], col(f"mqg{l}", c), rs, MUL, MUL, reads=[psB[c], bf("par"), brs], writes=[bqn])
                act(sq, PS[2][:], AF.Square, reads=[psB[2]], writes=[bsq] + bQ)
                mm(PS[5][:], ones_bf[:], sq, True, True, reads=[bsq, bf("ones_bf")], writes=[psB[5]])
                act(rs, PS[5][:], AF.Ln, reads=[psB[5], bf("cst")], writes=[brs], scale=1.0 / 128, bias=cst[:, 0:1])
                act(rs, rs, AF.Exp, writes=[brs], scale=-0.5)
                stt(kvn, PS[2][:], col(f"mkvg{l}", 0), rs, MUL, MUL, reads=[psB[2], bf("par"), brs], writes=[bkvn])
                tt("dve", t1[64:96, :], PS[3][64:96, :], cos_[64:96, :], MUL, reads=[psB[3], bf("mlacs")], writes=[bt1] + bosb)
                tt("dve", t2[64:96, :], PS[4][64:96, :], sin_[64:96, :], MUL, reads=[psB[4], bf("mlacs")], writes=[bt2] + brec)
                for h in range(4):
                    tt("pool", Kc[64:96, h, tsl], t1[64:96, :], t2[64:96, :], ADD, reads=[bt1, bt2], writes=[kb])
                for h in range(4):
                    pb = 6 + h % 2
                    mm(PS[pb][0:64, :], wkv[:, h * 128:h * 128 + 64], kvn, True, True, reads=[wb_, bkvn], writes=[psB[pb]])
                    act(Kc[0:64, h, tsl], PS[pb][0:64, :], AF.Copy, reads=[psB[pb]], writes=[kb])
                for kt in range(4):
                    pb = 6 + kt % 2
                    for h in range(4):
                        mm(PS[pb][:, h * 64:(h + 1) * 64], kvn[:, kt * 128:(kt + 1) * 128], wkv[:, h * 128 + 64:h * 128 + 128], True, True,
                           reads=[wb_, bkvn], writes=[psB[pb]])
                    op("dve", V.tensor_copy, out=Vc[:, t * 4 + kt, :, 0:64], in_=PS[pb][:, 0:256].rearrange("p (h n) -> p h n", h=4),
                       reads=[psB[pb]], writes=[vb_])
                def stage_a(h):
                    hh = h % 2
                    Qh_ = Qh2[hh]
                    pa, pbq = 0, 1
                    for c in range(2):
                        mm(PS[pa][0:96, :], wq[:, c, h * 96:(h + 1) * 96], qn[:, c, :], c == 0, c == 1, reads=[wb_, bqn], writes=[psB[pa]])
                    for c in range(2):
                        mm(PS[pbq][64:96, :], wqs[:, c, h, :], qn[:, c, :], c == 0, c == 1, reads=[wb_, bqn], writes=[psB[pbq]])
                    qb = bQ[hh]
                    act(Qh_[0:64, :], PS[pa][0:64, :], AF.Copy, reads=[psB[pa]], writes=[qb, bsq] if hh else [qb])
                    rsb = rs.bitcast(BF16)
                    scr = rsb[64:96, hh * 512:(hh + 1) * 512]
                    tt("dve", scr, PS[pbq][64:96, :], sin_[64:96, :], MUL, reads=[psB[pbq], bf("mlacs")], writes=[brs])
                    tt("dve", Qh_[64:96, :], PS[pa][64:96, :], cos_[64:96, :], MUL, reads=[psB[pa], bf("mlacs")], writes=[qb, bsq] if hh else [qb])
                    tt("pool", Qh_[64:96, :], Qh_[64:96, :], scr, ADD, reads=[brs], writes=[qb])

                def stage_b(h):
                    hh = h % 2
                    Qh_, osb_, rec_ = Qh2[hh], osb2[hh], rec2[hh]
                    qb = bQ[hh]
                    nkt = 4 * t + 4
                    pacc = 4 + hh
                    def qk_exp(j):
                        r = j - 4 * t
                        qs = 0 if r < 0 else r * 128
                        n = T - qs
                        pst = 2 + j % 2
                        e = j % 2
                        eb = bf(f"mlaE{e}")
                        mm(PS[pst][:, 0:n], Kc[0:96, h, j * 128:(j + 1) * 128], Qh_[0:96, qs:T], True, True, reads=[kb, qb], writes=[psB[pst]])
                        act(ET[:, e, 0:n], PS[pst][:, 0:n], AF.Exp, reads=[psB[pst]], writes=[eb], scale=SCALE)
                        if r >= 0:
                            op("pool", P.memset, ET[64:128, e, 0:64], 0.0, writes=[eb])

                    def pv(j):
                        r = j - 4 * t
                        qs = 0 if r < 0 else r * 128
                        n = T - qs
                        e = j % 2
                        mm(PS[pacc][0:65, qs:T], Vc[:, j, h, :], ET[:, e, 0:n], j == 0, j == nkt - 1, reads=[vb_, bf(f"mlaE{e}")], writes=[psB[pacc]])
                    qk_exp(0)
                    for j in range(nkt):
                        if j + 1 < nkt:
                            qk_exp(j + 1)
                        pv(j)
                    wr = [bosb[hh]] + ([bt1] if hh else [])
                    op("dve", V.tensor_copy, out=osb_[0:65, :], in_=PS[pacc][0:65, :], reads=[psB[pacc]], writes=wr)
                    wr2 = [brec[hh]] + ([bt2] if hh else [])
                    op("dve", V.reciprocal, out=rec_[64:65, :], in_=osb_[64:65, :], reads=[bosb[hh]], writes=wr2)
                    pbc = 6 + hh
                    mm(PS[pbc][0:64, :], ones_f[64:65, 0:64], rec_[64:65, :], True, True, reads=[brec[hh], bf("ones_f")], writes=[psB[pbc]])
                    tt("dve", oT[0:64, h, :], osb_[0:64, :], PS[pbc][0:64, :], MUL, reads=[bosb[hh], psB[pbc]], writes=[boT[h]])
                stage_a(0)
                for h in range(4):
                    if h + 1 < 4:
                        stage_a(h + 1)
                    stage_b(h)
                accum_wout(l, s_, t * T, T, lambda kc: oT[0:64, kc, :], 4, lambda kc, oc: wom[0:64, kc, oc * 128:(oc + 1) * 128], [wb_] + boT)

        def gdn_group(l, s_):
            sc.barrier()
            T = 256
            slot = next_slot()
            wg = wsl[slot][:, 0:8256].rearrange("p (k n) -> p k n", k=8)
            wo = wsl[slot][:, 8256:10304].rearrange("p (k n) -> p k n", k=2)
            wb_ = wB[slot]
            dma("pool", P.dma_start, out=wg, in_=win_d[l, :, :, 0:1032], writes=[wb_])
            dma("pool", P.dma_start, out=wo, in_=wout_d[l, :, 0:2, :], writes=[wb_])
            if G["rg"]:
                prefetched[("rg", l)] = load_rg(l)
            o = [0]

            def take(n):
                r = arena[:, o[0]:o[0] + n]
                o[0] += n
                return r
            Ubd, BD1, Mls, MuiT = take(128), take(128), take(128), take(128)
            xin = take(6 * 259).rearrange("p (j n) -> p j n", j=6)
            yq = take(6 * T).rearrange("p (j n) -> p j n", j=6)
            zs = take(2 * T).rearrange("p (j n) -> p j n", j=2)
            tmp = take(4 * T).rearrange("p (j n) -> p j n", j=4)
            Sst = take(128).rearrange("p (c n) -> p c n", c=2)
            gt_ = take(40)
            HM = [[take(128) for _ in range(9)] for _ in range(4)]
            HS = [[take(64) for _ in range(5)] for _ in range(4)]
            otok = take(256).rearrange("p (h n) -> p h n", h=4)
            on_ = take(256)
            oT = take(T).bitcast(BF16).rearrange("p (c n) -> p c n", c=2)
            assert o[0] <= 13000, o[0]
            cb, sb_ = bf("gdnc"), [bf(f"gdnS{h}") for h in range(4)]
            gB = bf("gdngate")
            op("pool", P.memset, BD1, 0.0, writes=[cb])
            op("pool", P.memset, BD1[0:64, 0:64], 1.0, writes=[cb])
            op("pool", P.memset, BD1[64:128, 64:128], 1.0, writes=[cb])
            op("pool", P.affine_select, out=Ubd, in_=BD1, pattern=[[1, 128]], compare_op=ALU.is_ge, fill=0.0, base=0, channel_multiplier=-1,
               reads=[cb], writes=[cb])
            op("pool", P.tensor_copy, out=MuiT, in_=Ubd, reads=[cb], writes=[cb])
            op("pool", P.affine_select, out=Mls, in_=BD1, pattern=[[-1, 128]], compare_op=ALU.is_gt, fill=0.0, base=0, channel_multiplier=1,
               reads=[cb], writes=[cb])
            op("pool", P.memset, Sst, 0.0, writes=sb_)
            xb = [bf(f"gdnx{j}") for j in range(6)]
            yb = [bf(f"gdny{j}") for j in range(6)]
            zb = [bf(f"gdnz{j}") for j in range(2)]
            tb4 = [bf(f"gdnt{j}") for j in range(4)]
            op("pool", P.memset, xin[:, :, 0:3], 0.0, writes=xb)
            nA = gt_[:, 36:40]
            act(nA, par[:, PC[f"galog{l}"]:PC[f"galog{l}"] + 4], AF.Exp, reads=[bf("par")], writes=[cb])
            ts("dve", nA, nA, -1.0, MUL, writes=[cb])
            otB, onB, oTB = bf("gdnotok"), bf("gdnon"), bf("gdnoT")

            def head_gen(h, bs):
                c2, po = h // 2, (h % 2) * 64
                hp = slice(po, po + 64)
                pa, pbk = 2 * h, 2 * h + 1
                qT, kT, vT = yq[hp, c2, bs], yq[hp, 2 + c2, bs], yq[hp, 4 + c2, bs]
                rq, rk, rv = [yb[c2]], [yb[2 + c2]], [yb[4 + c2]]
                NQ, E1, E2, eGr, A_, B_, Y_, wT, qdT = HM[h]
                ktail, kbg, vbt, u_, vnew = HS[h]
                bN = [bf(f"gh{h}_{i}") for i in range(9)]
                bNQ, bE1, bE2, beG, bA, bB, bY, bwT, bqd = bN
                bkt, bkb, bvb, bu, bvn = [bf(f"gs{h}_{i}") for i in range(5)]
                ts("dve", NQ, ones_f[:], gt_[:, h:h + 1], MUL, -1.0, MUL, reads=[bf("ones_f"), gB], writes=[bNQ])
                mm(PS[pa][:, 0:128], NQ, Ubd, True, True, reads=[bNQ, cb], writes=[psB[pa]])
                yield
                ts("dve", E2, PS[pa][:, 0:128], gt_[:, 12 + h:13 + h], ADD, 0.0, MIN, reads=[psB[pa], gB], writes=[bE2])
                act(E2, E2, AF.Exp, writes=[bE2])
                tt("pool", E2, E2, Mls, MUL, reads=[cb], writes=[bE2])
                ts("dve", E1, PS[pa][:, 0:128], gt_[:, 12 + h:13 + h], ADD, 0.0, MAX, reads=[psB[pa], gB], writes=[bE1])
                act(E1, E1, AF.Exp, writes=[bE1], scale=-1.0)
                tt("pool", E1, E1, MuiT, MUL, reads=[cb], writes=[bE1])
                act(eGr, PS[pa][:, 0:128], AF.Exp, reads=[psB[pa]], writes=[beG], scale=-1.0)
                mm(PS[pbk][:, 0:128], kT, kT, True, True, reads=rk, writes=[psB[pbk]])
                yield
                stt(A_, PS[pbk][:, 0:128], gt_[:, 8 + h:9 + h], E2, MUL, MUL, reads=[psB[pbk], gB, bE2], writes=[bA])
                mm(PS[pa][:, 0:128], kT, qT, True, True, reads=rk + rq, writes=[psB[pa]])
                op("pe", PE.transpose, PS[pbk][:, 0:128], A_, ident[:], reads=[bA, bf("ident")], writes=[psB[pbk]])
                yield
                tt("dve", NQ, PS[pa][:, 0:128], E1, MUL, reads=[psB[pa], bE1], writes=[bNQ])
                op("act", A.copy, out=B_, in_=PS[pbk][:, 0:128], reads=[psB[pbk]], writes=[bB])
                tt("dve", Y_, B_, ident[:], ADD, reads=[bB, bf("ident")], writes=[bY])
                yield
                Am, Bm, An, Bn = A_, B_, E2, E1
                bAm, bBm, bAn, bBn = bA, bB, bE2, bE1
                for lvl in range(5):
                    mm(PS[pa][:, 0:128], Bm, Am, True, True, reads=[bAm, bBm], writes=[psB[pa]])
                    if lvl < 4:
                        mm(PS[pbk][:, 0:128], Am, Bm, True, True, reads=[bAm, bBm], writes=[psB[pbk]])
                    yield
                    op("act", A.copy, out=An, in_=PS[pa][:, 0:128], reads=[psB[pa]], writes=[bAn])
                    if lvl < 4:
                        op("dve", V.tensor_copy, out=Bn, in_=PS[pbk][:, 0:128], reads=[psB[pbk]], writes=[bBn])
                    mm(PS[pa][:, 0:128], An, Y_, True, True, reads=[bAn, bY], writes=[psB[pa]])
                    yield
                    tt("dve", Y_, Y_, PS[pa][:, 0:128], ADD, reads=[psB[pa]], writes=[bY])
                    Am, Bm, An, Bn = An, Bn, Am, Bm
                    bAm, bBm, bAn, bBn = bAn, bBn, bAm, bBm
                idh = ident[hp, po:po + 64]
                op("pe", PE.transpose, PS[pbk][:, 0:64], kT, idh, reads=rk + [bf("ident")], writes=[psB[pbk]])
                op("pe", PE.transpose, PS[pa][:, 0:64], vT, idh, reads=rv + [bf("ident")], writes=[psB[pa]])
                yield
                ts("dve", ktail, PS[pbk][:, 0:64], gt_[:, 20 + h:21 + h], MUL, reads=[psB[pbk], gB], writes=[bkt])
                ts("dve", kbg, PS[pbk][:, 0:64], gt_[:, 24 + h:25 + h], MUL, reads=[psB[pbk], gB], writes=[bkb])
                ts("dve", vbt, PS[pa][:, 0:64], gt_[:, 4 + h:5 + h], MUL, reads=[psB[pa], gB], writes=[bvb])
                mm(PS[pbk][:, 0:64], Y_, vbt, True, True, reads=[bY, bvb], writes=[psB[pbk]])
                mm(PS[pa][hp, 0:128], kbg, Y_, True, True, reads=[bkb, bY], writes=[psB[pa]])
                tt("dve", qdT[hp, :], qT, eGr[hp, :], MUL, reads=rq + [beG], writes=[bqd])
                yield
                op("act", A.copy, out=u_, in_=PS[pbk][:, 0:64], reads=[psB[pbk]], writes=[bu])
                op("act", A.copy, out=wT[hp, :], in_=PS[pa][hp, 0:128], reads=[psB[pa]], writes=[bwT])
                Sh = Sst[hp, c2, :]
                for c in range(2):
                    tc = slice(c * 64, c * 64 + 64)
                    mm(PS[pa][tc, 0:64], wT[hp, tc], Sh, True, True, reads=[bwT, sb_[h]], writes=[psB[pa]])
                    mm(PS[pbk][tc, 0:64], qdT[hp, tc], Sh, True, True, reads=[bqd, sb_[h]], writes=[psB[pbk]])
                    yield
                    tt("dve", vnew[tc, :], u_[tc, :], PS[pa][tc, 0:64], SUB, reads=[psB[pa], bu], writes=[bvn])
                    op("act", A.copy, out=otok[tc, h, :], in_=PS[pbk][tc, 0:64], reads=[psB[pbk]], writes=[otB])
                    mm(PS[pa][tc, 0:64], NQ[tc, tc], vnew[tc, :], True, True, reads=[bNQ, bvn], writes=[psB[pa]])
                    mm(PS[pbk][hp, 0:64], ktail[tc, :], vnew[tc, :], True, True, reads=[bkt, bvn], writes=[psB[pbk]])
                    yield
                    tt("dve", otok[tc, h, :], otok[tc, h, :], PS[pa][tc, 0:64], ADD, reads=[psB[pa]], writes=[otB])
                    stt(Sh, Sh, eGr[hp, c * 64 + 63:c * 64 + 64], PS[pbk][hp, 0:64], MUL, ADD, reads=[psB[pbk], beG], writes=[sb_[h]])
                    yield

            for tq in range(lim.get('gdn_tiles', S // T)):
                t0 = tq * T
                ub = [uB[kc][t0 // 512] for kc in range(8)]
                for j in range(8):
                    pb = j % 4
                    for kc in range(8):
                        mm(PS[pb][:, 0:T], wg[:, kc, j * 128:(j + 1) * 128], uT[:, kc, t0:t0 + T], kc == 0, kc == 7, reads=[wb_] + ub, writes=[psB[pb]])
                    tj = tmp[:, j % 2, :]
                    tjb = tb4[j % 2]
                    if j < 6:
                        act(xin[:, j, 3:3 + T], PS[pb][:, 0:T], AF.Copy, reads=[psB[pb]], writes=[xb[j]])
                        y = yq[:, j, :]
                        ts("dve", y, xin[:, j, 0:T], col(f"gcw{l}", j), MUL, reads=[xb[j], bf("par")], writes=[yb[j]])
                        for tap in range(1, 4):
                            stt(y, xin[:, j, tap:tap + T], col(f"gcw{l}", tap * 6 + j), y, MUL, ADD, reads=[xb[j], bf("par")], writes=[yb[j]])
                        op("pool", P.tensor_copy, out=xin[:, j, 0:3], in_=xin[:, j, T:T + 3], reads=[xb[j]], writes=[xb[j]])
                        sigm(tj, y, [yb[j]], tjb)
                        tt("dve", y, y, tj, MUL, reads=[tjb], writes=[yb[j]])
                        if j < 4:
                            t2_ = tmp[:, 2 + j % 2, :]
                            t2b = tb4[2 + j % 2]
                            act(t2_.bitcast(BF16)[:, 0:T], y, AF.Square, reads=[yb[j]], writes=[t2b])
                            mm(PS[4 + j][:, 0:T], bones_bf[:], t2_.bitcast(BF16)[:, 0:T], True, True, reads=[t2b, bf("bones_bf")], writes=[psB[4 + j]])
                            act(tj, PS[4 + j][:, 0:T], AF.Ln, reads=[psB[4 + j], bf("cst")], writes=[tjb], bias=cst[:, 0:1])
                            if j < 2:
                                act(tj, tj, AF.Exp, reads=[bf("cst")], writes=[tjb], scale=-0.5, bias=cst[:, 2:3])
                            else:
                                act(tj, tj, AF.Exp, writes=[tjb], scale=-0.5)
                            tt("dve", y, y, tj, MUL, reads=[tjb], writes=[yb[j]])
                    else:
                        zc = zs[:, j - 6, :]
                        act(zc, PS[pb][:, 0:T], AF.Copy, reads=[psB[pb]], writes=[zb[j - 6]])
                        sigm(tj, zc, [zb[j - 6]], tjb)
                        tt("dve", zc, zc, tj, MUL, reads=[tjb], writes=[zb[j - 6]])
                for blk in range(2):
                    b0 = blk * 128
                    bs = slice(b0, b0 + 128)
                    for kc in range(8):
                        mm(PS[5][:, 0:8], uT[:, kc, t0 + b0:t0 + b0 + 128], wg[:, kc, 1024:1032], kc == 0, kc == 7, reads=[wb_] + ub, writes=[psB[5]])
                    gcol, beta, nbeta, Gc, eG, eTl, bg, sp = (gt_[:, 4 * i:4 * i + 4] for i in range(8))
                    tt("dve", sp, PS[5][:, 0:4], par[:, PC[f"gdtb{l}"]:PC[f"gdtb{l}"] + 4], ADD, reads=[psB[5], bf("par")], writes=[gB])
                    act(sp, sp, AF.Exp, writes=[gB])
                    act(sp, sp, AF.Ln, reads=[bf("cst")], writes=[gB], bias=cst[:, 1:2])
                    tt("dve", gcol, sp, nA, MUL, reads=[cb], writes=[gB])
                    sigm(beta, PS[5][:, 4:8], [psB[5]], gB)
                    ts("dve", nbeta, beta, -1.0, MUL, writes=[gB])
                    mm(PS[6][:, 0:4], Ubd, gcol, True, True, reads=[cb, gB], writes=[psB[6]])
                    mm(PS[6][:, 4:8], BD1, gcol, True, True, reads=[cb, gB], writes=[psB[6]])
                    op("dve", V.tensor_copy, out=Gc, in_=PS[6][:, 0:4], reads=[psB[6]], writes=[gB])
                    act(eG, PS[6][:, 0:4], AF.Exp, reads=[psB[6]], writes=[gB])
                    tt("dve", eTl, PS[6][:, 4:8], Gc, SUB, reads=[psB[6]], writes=[gB])
                    act(eTl, eTl, AF.Exp, writes=[gB])
                    tt("dve", bg, beta, eG, MUL, writes=[gB])
                    for grp in lim.get('gdn_il', [[0, 1, 2, 3]]):
                      alive = [head_gen(h, bs) for h in grp]
                      nsteps = 0
                      while alive and nsteps < lim.get('gdn_steps', 999):
                          nsteps += 1
                          nxt = []
                          for g_ in alive:
                              try:
                                  next(g_)
                                  nxt.append(g_)
                              except StopIteration:
                                  pass
                          alive = nxt
                    ss = gt_[:, 32:36]
                    for h in range(4):
                        tt("pool", on_[:, h * 64:(h + 1) * 64], otok[:, h, :], otok[:, h, :], MUL, reads=[otB], writes=[onB])
                    op("dve", V.tensor_reduce, out=ss, in_=on_.rearrange("p (h n) -> p h n", h=4), axis=mybir.AxisListType.X, op=ADD,
                       reads=[onB], writes=[gB])
                    act(ss, ss, AF.Ln, reads=[bf("cst")], writes=[gB], scale=1.0 / 64, bias=cst[:, 0:1])
                    act(ss, ss, AF.Exp, writes=[gB], scale=-0.5)
                    for h in range(4):
                        ts("dve", on_[:, h * 64:(h + 1) * 64], otok[:, h, :], gt_[:, 32 + h:33 + h], MUL, reads=[otB, gB], writes=[onB])
                    for c2 in range(2):
                        op("pe", PE.transpose, PS[2 + c2][:, 0:128], on_[:, c2 * 128:(c2 + 1) * 128], ident[:], reads=[onB, bf("ident")], writes=[psB[2 + c2]])
                        stt(oT[:, c2, bs], PS[2 + c2][:, 0:128], col(f"gng{l}"), zs[:, c2, bs], MUL, MUL, reads=[psB[2 + c2], bf("par"), zb[c2]], writes=[oTB])
                accum_wout(l, s_, t0, T, lambda kc: oT[:, kc, :], 2, lambda kc, oc: wo[:, kc, oc * 128:(oc + 1) * 128], [wb_, oTB])

        for s_ in range(lim.get('nseq', NSEQ)):
            for k in range(8):
                for t in range(4):
                    dma("sp", nc.sync.dma_start, out=hT[:, k, t * 512:(t + 1) * 512], in_=xT[s_, k * 128:(k + 1) * 128, t * 512:(t + 1) * 512],
                        writes=[hB[k][t]])
            if G["mla"]:
                rope_tables(s_)
            for l in range(lim.get('depth', DEPTH)):
                modnorm(l, s_, 0)
                if G["gdn"]:
                    gdn_group(l, s_)
                if G["rg"]:
                    rg_group(l, s_)
                if G["mla"]:
                    mla_group(l, s_)
                if G["mlp"]:
                    modnorm(l, s_, 1)
                    mlp(l, s_)
            final_norm(s_)
        if debug_stop == "mod":
            dma("sp", nc.sync.dma_start, out=yT[1, 0:128, 0:DEPTH * 48 * NSEQ], in_=modT[:].rearrange("p l c s -> p (l c s)"),
                reads=[bf("modT")], writes=[bf("yout")])
        sc.final_wait("sp", [bf("yout")])
        build_nc.last_counts = dict(sc.cnt)

        with nc.Block() as block:
            @block.tensor
            def _(e):
                sc.replay("pe", e)

            @block.vector
            def _(e):
                sc.replay("dve", e)

            @block.scalar
            def _(e):
                sc.replay("act", e)

            @block.gpsimd
            def _(e):
                sc.replay("pool", e)

            @block.sync
            def _(e):
                sc.replay("sp", e)
    return nc


def _pack_params(inp, core):
    par = np.zeros((128, NPAR), np.float32)
    f = lambda v: np.asarray(v, np.float32)
    cm = lambda v: f(v).reshape(-1, 128).T
    for s_ in range(NSEQ):
        par[:, PC["c"] + 8 * s_:PC["c"] + 8 * (s_ + 1)] = cm(inp["c"][core * NSEQ + s_])
    par[:, PC["fng"]:PC["fng"] + 8] = cm(inp["final_norm_g"])
    invf = (np.float32(10000.0) ** (-np.arange(0, 32, 2, dtype=np.float32) / np.float32(32))).astype(np.float32)
    par[:, PC["invf"]] = np.tile(invf, 8)
    for l in range(DEPTH):
        par[:, PC[f"bmod{l}"]:PC[f"bmod{l}"] + 48] = cm(inp["b_mod"][l])
        par[:, PC[f"nmg{l}"]:PC[f"nmg{l}"] + 8] = cm(inp["norm_mix_g"][l])
        par[:, PC[f"nfg{l}"]:PC[f"nfg{l}"] + 8] = cm(inp["norm_mlp_g"][l])
        gcw = f(inp["gdn_conv_w"][l])
        for tap in range(4):
            par[:, PC[f"gcw{l}"] + tap * 6:PC[f"gcw{l}"] + tap * 6 + 6] = cm(gcw[tap])
        par[:, PC[f"galog{l}"]:PC[f"galog{l}"] + 4] = f(inp["gdn_a_log"][l])[None, :]
        par[:, PC[f"gdtb{l}"]:PC[f"gdtb{l}"] + 4] = f(inp["gdn_dt_bias"][l])[None, :]
        par[:, PC[f"gng{l}"]] = np.tile(f(inp["gdn_norm_g"][l]), 2)
        rcw = f(inp["rg_conv_w"][l])
        for tap in range(4):
            par[:, PC[f"rcw{l}"] + tap * 4:PC[f"rcw{l}"] + tap * 4 + 4] = cm(rcw[tap])
        par[:, PC[f"rcb{l}"]:PC[f"rcb{l}"] + 4] = cm(inp["rg_conv_b"][l])
        par[:, PC[f"rba{l}"]:PC[f"rba{l}"] + 4] = cm(inp["rg_b_a"][l])
        par[:, PC[f"rbx{l}"]:PC[f"rbx{l}"] + 4] = cm(inp["rg_b_x"][l])
        par[:, PC[f"rlam{l}"]:PC[f"rlam{l}"] + 4] = cm(inp["rg_lambda"][l])
        par[:, PC[f"mqg{l}"]:PC[f"mqg{l}"] + 2] = cm(inp["mla_q_norm_g"][l])
        par[:, PC[f"mkvg{l}"]:PC[f"mkvg{l}"] + 1] = cm(inp["mla_kv_norm_g"][l])
    return par


def _shared_weights(inp):
    f = lambda v: np.ascontiguousarray(np.asarray(v, np.float32))
    kp = lambda w, k: f(w.reshape(w.shape[0], k, 128, w.shape[2]).transpose(0, 2, 1, 3))
    sh = {}
    sh["wmod"] = kp(inp["w_mod"], 8)
    sh["win"] = kp(inp["w_in"], 8)
    sh["wout"] = kp(inp["w_out"], 8)
    wo = np.asarray(inp["w_out"], np.float32)
    sh["woutm"] = f(wo[:, 768:1024, :].reshape(DEPTH, 4, 64, 1024).transpose(0, 2, 1, 3))
    sh["w1"] = kp(inp["w_mlp_in"], 8)
    sh["w2"] = kp(inp["w_mlp_out"], 32)
    rgw = np.zeros((DEPTH, 128, 8, 128), np.float32)
    for l in range(DEPTH):
        for wi_, nm in enumerate(("rg_w_a", "rg_w_x")):
            w = np.asarray(inp[nm][l], np.float32)
            for j in range(4):
                rgw[l, 0:64, wi_ * 4 + j, 0:64] = w[2 * j]
                rgw[l, 64:128, wi_ * 4 + j, 64:128] = w[2 * j + 1]
    sh["rgw"] = rgw
    sh["wqb"] = kp(inp["mla_w_qb"], 2)
    sh["wkvb"] = f(inp["mla_w_kvb"])
    return sh


def make_in_maps(inp, cores):
    sh = _shared_weights(inp)
    maps = []
    for c in cores:
        m = dict(sh)
        xs = np.asarray(inp["x"][c * NSEQ:(c + 1) * NSEQ], np.float32)
        m["xT"] = np.ascontiguousarray(xs.transpose(0, 2, 1))
        m["par"] = _pack_params(inp, c)
        pos = np.asarray(inp["positions"][c * NSEQ:(c + 1) * NSEQ], np.int32)
        m["pos"] = np.ascontiguousarray(np.broadcast_to(pos[:, None, :], (NSEQ, 128, S)))
        maps.append(m)
    return maps


def kernel(**inputs):
    nc = build_nc()
    cores = list(range(8))
    in_maps = make_in_maps(inputs, cores)
    res = run_bass_kernel_spmd(nc, in_maps, core_ids=cores)
    out = np.empty((16, S, D), np.float32)
    for c in cores:
        y = res.results[c]["yT"]
        for s_ in range(NSEQ):
            out[c * NSEQ + s_] = y[s_].T
    return out
```
